# Optimizing a Trainium2 kernel written in Bass

```python
import math
import jax, jax.numpy as jnp
from jax import lax
import numpy as np

D_MODEL = 1024
BATCH = 32
SEQ = 256
DEPTH = 2
DEC_BATCH = 4
DEC_SEQ = 4096
PAST_LEN = 256

F32 = jnp.float32
GRID_W = 64
N_EVEN = (DEPTH + 1) // 2
N_ODD = DEPTH // 2
N_MOD = 9
D_FF = 2816
FFN_RES = 0.5
EPS = 1e-6
H_A = 4
DK_A = 128
DV_A = 128
CONV_W = 5
GDN_CHUNK = 64
H_B = 4
Q_LORA = 256
KV_LORA = 128
DN_B = 128
DR_B = 64
DV_B = 128
ROPE_BASE = 10000.0
Q_BLOCK = 128
H_C = 4
DK_C = 128
DV_C = 256
GATE_RANK = 16
GATE_NORM = 16.0
GLA_CHUNK = 32
EV_SIZES = (2 * H_A * DK_A + H_A * DV_A, H_A * DV_A, 2 * H_A, 2 * H_A, Q_LORA, KV_LORA, DR_B)
EV_IN = 2 * H_A * DK_A + 2 * H_A * DV_A + 4 * H_A + Q_LORA + KV_LORA + DR_B
EV_MIX = H_A * DV_A + H_B * DV_B
OD_SIZES = (H_C * DK_C, H_C * DK_C, H_C * DV_C, H_C * DV_C, 2 * GATE_RANK)
OD_IN = 2 * H_C * DK_C + 2 * H_C * DV_C + 2 * GATE_RANK
OD_MIX = H_C * DV_C

kernel_name = 'bidir_gdn_mla_gla_prefix_trunk'


def _split(x, sizes):
    out, off = [], 0
    for s in sizes:
        out.append(x[..., off:off + s])
        off += s
    return out


def rmsnorm(x, w):
    x32 = x.astype(F32)
    y = x32 * lax.rsqrt(jnp.mean(x32 * x32, axis=-1, keepdims=True) + EPS)
    return (y * w.astype(F32)).astype(x.dtype)


def l2norm(x):
    x32 = x.astype(F32)
    return (x32 * lax.rsqrt(jnp.sum(x32 * x32, axis=-1, keepdims=True) + EPS)).astype(x.dtype)


def adaln(cond, w, b):
    m = (jax.nn.silu(cond) @ w + b).reshape(cond.shape[0], 1, N_MOD, D_MODEL)
    return [m[:, :, j] for j in range(N_MOD)]


def modulate(x, g, shift, scale):
    return rmsnorm(x, g) * (1 + scale) + shift


def swiglu(h, w_in, w_out):
    u, v = jnp.split(h @ w_in, 2, axis=-1)
    return (jax.nn.silu(u) * v) @ w_out


def ffn_branch(x, mods, g_pre, g_post, w_in, w_out):
    shift, scale, gate = mods
    y = swiglu(modulate(x, g_pre, shift, scale), w_in, w_out)
    return x + FFN_RES * gate * rmsnorm(y, g_post)


def short_conv(x, w):
    return lax.conv_general_dilated(x, w.astype(x.dtype)[:, None, :], window_strides=(1,),
                                    padding=((CONV_W // 2, CONV_W // 2),),
                                    dimension_numbers=('NWC', 'WIO', 'NWC'),
                                    feature_group_count=x.shape[-1])


def grid_rope(n_tokens):
    rows = n_tokens // GRID_W
    row = jnp.repeat(jnp.arange(rows, dtype=F32), GRID_W)
    col = jnp.tile(jnp.arange(GRID_W, dtype=F32), rows)
    nf = DR_B // 4
    inv = ROPE_BASE ** (-jnp.arange(nf, dtype=F32) / nf)
    ang = jnp.stack([row[:, None] * inv, col[:, None] * inv], axis=1)
    return jnp.cos(ang), jnp.sin(ang)


def rope_2d(x, cos, sin):
    B, T, H, R = x.shape
    xr = x.reshape(B, T, H, 2, 2, R // 4)
    x1, x2 = xr[..., 0, :], xr[..., 1, :]
    c = cos[None, :, None].astype(x.dtype)
    s = sin[None, :, None].astype(x.dtype)
    return jnp.stack([x1 * c - x2 * s, x2 * c + x1 * s], axis=-2).reshape(B, T, H, R)


def _to_chunks(x, c):
    B, T = x.shape[:2]
    x = x.astype(F32).reshape((B, T // c, c) + x.shape[2:])
    return jnp.swapaxes(jnp.moveaxis(x, 1, 0), 2, 3)


def _from_chunks(o):
    n, B, H, c, E = o.shape
    return jnp.moveaxis(jnp.swapaxes(o, 2, 3), 0, 1).reshape(B, n * c, H, E)


def gdn_chunk_scan(q, k, v, g, beta, s0):
    C = GDN_CHUNK
    DV = v.shape[-1]
    qc, kc, vc = _to_chunks(q, C), _to_chunks(k, C), _to_chunks(v, C)
    gc = jnp.cumsum(_to_chunks(g, C), axis=-1)
    bc = _to_chunks(beta, C)
    idx = jnp.arange(C)
    causal = idx[:, None] >= idx[None, :]
    strict = idx[:, None] > idx[None, :]
    decay = jnp.exp(jnp.where(causal, gc[..., :, None] - gc[..., None, :], -jnp.inf))
    kb = kc * bc[..., None]
    a_mat = jnp.where(strict, jnp.einsum('nbhik,nbhjk->nbhij', kb, kc) * decay, 0.0)
    rhs = jnp.concatenate([vc * bc[..., None], kb * jnp.exp(gc)[..., None]], axis=-1)
    sol = lax.linalg.triangular_solve(a_mat + jnp.eye(C, dtype=F32), rhs, left_side=True, lower=True)
    u_all, w_all = sol[..., :DV], sol[..., DV:]
    qk = jnp.einsum('nbhik,nbhjk->nbhij', qc, kc) * decay

    def step(S, inp):
        q_i, k_i, u_i, w_i, qk_i, g_i = inp
        v_new = u_i - jnp.einsum('bhck,bhkv->bhcv', w_i, S)
        o = (jnp.einsum('bhck,bhkv->bhcv', q_i * jnp.exp(g_i)[..., None], S)
             + jnp.einsum('bhij,bhjv->bhiv', qk_i, v_new))
        g_last = g_i[..., -1:]
        S = (jnp.exp(g_last)[..., None] * S
             + jnp.einsum('bhck,bhcv->bhkv', k_i * jnp.exp(g_last - g_i)[..., None], v_new))
        return S, o

    S, o = lax.scan(step, s0.astype(F32), (qc, kc, u_all, w_all, qk, gc))
    return _from_chunks(o), S


def gla_chunk_scan(q, k, v, g, s0):
    C = GLA_CHUNK
    qc, kc, vc = _to_chunks(q, C), _to_chunks(k, C), _to_chunks(v, C)
    gc = jnp.cumsum(_to_chunks(g, C), axis=-2)
    idx = jnp.arange(C)
    causal = (idx[:, None] >= idx[None, :])[..., None]

    def step(S, inp):
        q_i, k_i, v_i, b_i = inp
        b_last = b_i[..., -1:, :]
        o_inter = jnp.einsum('bhck,bhkv->bhcv', q_i * jnp.exp(b_i), S)
        dec = jnp.exp(jnp.where(causal, b_i[..., :, None, :] - b_i[..., None, :, :], -jnp.inf))
        att = jnp.einsum('bhik,bhjk,bhijk->bhij', q_i, k_i, dec)
        o = o_inter + jnp.einsum('bhij,bhjv->bhiv', att, v_i)
        S = (jnp.swapaxes(jnp.exp(b_last), -1, -2) * S
             + jnp.einsum('bhck,bhcv->bhkv', k_i * jnp.exp(b_last - b_i), v_i))
        return S, o

    S, o = lax.scan(step, s0.astype(F32), (qc, kc, vc, gc))
    return _from_chunks(o), S


def bidirectional(scan_fn, fwd_inputs, bwd_inputs, s0):
    o_f, s_f = scan_fn(*fwd_inputs, s0[:, 0])
    o_b, s_b = scan_fn(*[jnp.flip(t, 1) for t in bwd_inputs], s0[:, 1])
    return o_f + jnp.flip(o_b, 1), jnp.stack([s_f, s_b], axis=1)


def blocked_attention(q_nope, q_rope, k_nope, k_rope, v):
    B, T, H, _ = q_nope.shape
    nb = T // Q_BLOCK
    scale = (DN_B + DR_B) ** -0.5

    def blocks(x):
        return jnp.moveaxis(x.reshape((B, nb, Q_BLOCK) + x.shape[2:]), 1, 0)

    def attend(qs):
        qn, qr = qs
        s = jnp.einsum('bqhd,bkhd->bhqk', qn, k_nope) + jnp.einsum('bqhr,bkr->bhqk', qr, k_rope)
        p = jax.nn.softmax(s.astype(F32) * scale, axis=-1).astype(v.dtype)
        return jnp.einsum('bhqk,bkhd->bqhd', p, v)

    o = lax.map(attend, (blocks(q_nope), blocks(q_rope)))
    return jnp.moveaxis(o, 0, 1).reshape(B, T, H, v.shape[-1])


def even_mixer(h, w_in, conv_w, a_log, dt_bias, gdn_norm, q_norm, w_uq, kv_norm, w_ukv, w_out,
               gdn_s0, rope, ctx_kv):
    B, T, _ = h.shape
    qkv, z, a, b, cq, ckv, k_rope = _split(h @ w_in, EV_SIZES)
    qkv = jax.nn.silu(short_conv(qkv, conv_w))
    q, k, v = _split(qkv, (H_A * DK_A, H_A * DK_A, H_A * DV_A))
    q = l2norm(q.reshape(B, T, H_A, DK_A)) * (DK_A ** -0.5)
    k = l2norm(k.reshape(B, T, H_A, DK_A))
    v = v.reshape(B, T, H_A, DV_A)
    a = a.reshape(B, T, 2, H_A).astype(F32)
    b = b.reshape(B, T, 2, H_A).astype(F32)
    g = -jnp.exp(a_log.astype(F32)) * jax.nn.softplus(a + dt_bias.astype(F32))
    beta = jax.nn.sigmoid(b)
    o_a, s_fin = bidirectional(gdn_chunk_scan, (q, k, v, g[:, :, 0], beta[:, :, 0]),
                               (q, k, v, g[:, :, 1], beta[:, :, 1]), gdn_s0)
    o_a = (rmsnorm(o_a.astype(h.dtype), gdn_norm)
           * jax.nn.silu(z.reshape(B, T, H_A, DV_A))).reshape(B, T, H_A * DV_A)
    qh = (rmsnorm(cq, q_norm) @ w_uq).reshape(B, T, H_B, DN_B + DR_B)
    q_nope, q_rope = qh[..., :DN_B], qh[..., DN_B:]
    ckv = rmsnorm(ckv, kv_norm)
    k_rope = k_rope[:, :, None, :]
    if rope is not None:
        q_rope = rope_2d(q_rope, rope[0], rope[1])
        k_rope = rope_2d(k_rope, rope[0], rope[1])
    k_rope = k_rope[:, :, 0]
    ckv_all, krope_all = ckv, k_rope
    if ctx_kv is not None:
        ckv_all = jnp.concatenate([ctx_kv[0].astype(h.dtype), ckv], axis=1)
        krope_all = jnp.concatenate([ctx_kv[1].astype(h.dtype), k_rope], axis=1)
    S = ckv_all.shape[1]
    kv = (ckv_all @ w_ukv).reshape(B, S, H_B, DN_B + DV_B)
    o_b = blocked_attention(q_nope, q_rope, kv[..., :DN_B], krope_all, kv[..., DN_B:])
    o_b = o_b.reshape(B, T, H_B * DV_B)
    y = jnp.concatenate([o_a, o_b], axis=-1) @ w_out
    return y, s_fin, ckv, k_rope


def odd_mixer(h, w_in, w_gup, b_g, gla_norm, w_out, gla_s0):
    B, T, _ = h.shape
    q, k, v, r, gdown = _split(h @ w_in, OD_SIZES)
    q = q.reshape(B, T, H_C, DK_C) * (DK_C ** -0.5)
    k = k.reshape(B, T, H_C, DK_C)
    v = v.reshape(B, T, H_C, DV_C)
    gdown = gdown.reshape(B, T, 2, GATE_RANK).astype(F32)
    glog = jax.nn.log_sigmoid(jnp.einsum('btdr,drk->btdk', gdown, w_gup.astype(F32))
                              + b_g.astype(F32)) / GATE_NORM
    glog = glog.reshape(B, T, 2, H_C, DK_C)
    o, s_fin = bidirectional(gla_chunk_scan, (q, k, v, glog[:, :, 0]), (q, k, v, glog[:, :, 1]), gla_s0)
    o = (rmsnorm(o.astype(h.dtype), gla_norm)
         * jax.nn.silu(r.reshape(B, T, H_C, DV_C))).reshape(B, T, H_C * DV_C)
    return o @ w_out, s_fin


def setup_inputs(seed: int = 0) -> dict:
    ks = iter(jax.random.split(jax.random.key(seed), 32))

    def nrm(shape, s):
        return jax.random.normal(next(ks), shape, F32) * s

    def gain(shape):
        return 1.0 + 0.02 * jax.random.normal(next(ks), shape, F32)

    a_log = jnp.log(jax.random.uniform(next(ks), (N_EVEN, 2, H_A), F32, 1.0, 16.0))
    dt = jnp.exp(jax.random.uniform(next(ks), (N_EVEN, 2, H_A), F32, math.log(1e-3), math.log(0.1)))
    dt_bias = dt + jnp.log(-jnp.expm1(-dt))
    return {
        'x_prompt': nrm((BATCH, SEQ, D_MODEL), 1.0),
        'x_sample': nrm((DEC_BATCH, DEC_SEQ, D_MODEL), 1.0),
        'cache_mla_ckv': nrm((DEC_BATCH, N_EVEN, PAST_LEN, KV_LORA), 1.0),
        'cache_mla_krope': nrm((DEC_BATCH, N_EVEN, PAST_LEN, DR_B), 1.0),
        'state_gdn': nrm((DEC_BATCH, N_EVEN, 2, H_A, DK_A, DV_A), 0.5),
        'state_gla': nrm((DEC_BATCH, N_ODD, 2, H_C, DK_C, DV_C), 1.0),
        'c': nrm((DEC_BATCH, D_MODEL), 1.0),
        'c_ctx': nrm((D_MODEL,), 1.0),
        'mod_w': nrm((DEPTH, D_MODEL, N_MOD * D_MODEL), D_MODEL ** -0.5),
        'mod_b': nrm((DEPTH, N_MOD * D_MODEL), 0.02),
        'norm_pre': gain((DEPTH, 3, D_MODEL)),
        'norm_post': gain((DEPTH, 3, D_MODEL)),
        'ffn_w_in': nrm((DEPTH, 2, D_MODEL, 2 * D_FF), D_MODEL ** -0.5),
        'ffn_w_out': nrm((DEPTH, 2, D_FF, D_MODEL), D_FF ** -0.5),
        'ev_w_in': nrm((N_EVEN, D_MODEL, EV_IN), D_MODEL ** -0.5),
        'ev_conv_w': nrm((N_EVEN, CONV_W, 2 * H_A * DK_A + H_A * DV_A), CONV_W ** -0.5),
        'ev_gdn_a_log': a_log,
        'ev_gdn_dt_bias': dt_bias,
        'ev_gdn_norm': gain((N_EVEN, DV_A)),
        'ev_mla_q_norm': gain((N_EVEN, Q_LORA)),
        'ev_mla_w_uq': nrm((N_EVEN, Q_LORA, H_B * (DN_B + DR_B)), Q_LORA ** -0.5),
        'ev_mla_kv_norm': gain((N_EVEN, KV_LORA)),
        'ev_mla_w_ukv': nrm((N_EVEN, KV_LORA, H_B * (DN_B + DV_B)), KV_LORA ** -0.5),
        'ev_w_out': nrm((N_EVEN, EV_MIX, D_MODEL), EV_MIX ** -0.5),
        'od_w_in': nrm((N_ODD, D_MODEL, OD_IN), D_MODEL ** -0.5),
        'od_gla_w_gup': nrm((N_ODD, 2, GATE_RANK, H_C * DK_C), GATE_RANK ** -0.5),
        'od_gla_b_g': nrm((N_ODD, 2, H_C * DK_C), 0.1),
        'od_gla_norm': gain((N_ODD, DV_C)),
        'od_w_out': nrm((N_ODD, OD_MIX, D_MODEL), OD_MIX ** -0.5),
    }


def reference(x_prompt, x_sample, cache_mla_ckv, cache_mla_krope, state_gdn, state_gla, c, c_ctx,
              mod_w, mod_b, norm_pre, norm_post, ffn_w_in, ffn_w_out,
              ev_w_in, ev_conv_w, ev_gdn_a_log, ev_gdn_dt_bias, ev_gdn_norm,
              ev_mla_q_norm, ev_mla_w_uq, ev_mla_kv_norm, ev_mla_w_ukv, ev_w_out,
              od_w_in, od_gla_w_gup, od_gla_b_g, od_gla_norm, od_w_out):
    cos, sin = grid_rope(x_sample.shape[1])
    xc, xs = x_prompt, x_sample
    new_ckv, new_krope, new_gdn, new_gla = [], [], [], []
    for i in range(DEPTH):
        m_ctx = adaln(c_ctx[None, :], mod_w[i], mod_b[i])
        m_lat = adaln(c, mod_w[i], mod_b[i])
        xc = ffn_branch(xc, m_ctx[0:3], norm_pre[i, 0], norm_post[i, 0], ffn_w_in[i, 0], ffn_w_out[i, 0])
        xs = ffn_branch(xs, m_lat[0:3], norm_pre[i, 0], norm_post[i, 0], ffn_w_in[i, 0], ffn_w_out[i, 0])
        hc = modulate(xc, norm_pre[i, 1], m_ctx[3], m_ctx[4])
        hs = modulate(xs, norm_pre[i, 1], m_lat[3], m_lat[4])
        if i % 2 == 0:
            e = i // 2
            w = (ev_w_in[e], ev_conv_w[e], ev_gdn_a_log[e], ev_gdn_dt_bias[e], ev_gdn_norm[e],
                 ev_mla_q_norm[e], ev_mla_w_uq[e], ev_mla_kv_norm[e], ev_mla_w_ukv[e], ev_w_out[e])
            s0 = jnp.zeros((xc.shape[0], 2, H_A, DK_A, DV_A), F32)
            yc, s_ctx, ckv_ctx, krope_ctx = even_mixer(hc, *w, s0, None, None)
            ys = even_mixer(hs, *w, state_gdn[:, e], (cos, sin),
                            (cache_mla_ckv[:, e], cache_mla_krope[:, e]))[0]
            new_ckv.append(ckv_ctx)
            new_krope.append(krope_ctx)
            new_gdn.append(s_ctx.astype(xc.dtype))
        else:
            o = i // 2
            w = (od_w_in[o], od_gla_w_gup[o], od_gla_b_g[o], od_gla_norm[o], od_w_out[o])
            s0 = jnp.zeros((xc.shape[0], 2, H_C, DK_C, DV_C), F32)
            yc, s_ctx = odd_mixer(hc, *w, s0)
            ys = odd_mixer(hs, *w, state_gla[:, o])[0]
            new_gla.append(s_ctx.astype(xc.dtype))
        xc = xc + m_ctx[5] * rmsnorm(yc, norm_post[i, 1])
        xs = xs + m_lat[5] * rmsnorm(ys, norm_post[i, 1])
        xc = ffn_branch(xc, m_ctx[6:9], norm_pre[i, 2], norm_post[i, 2], ffn_w_in[i, 1], ffn_w_out[i, 1])
        xs = ffn_branch(xs, m_lat[6:9], norm_pre[i, 2], norm_post[i, 2], ffn_w_in[i, 1], ffn_w_out[i, 1])
    return (xc, xs, jnp.stack(new_ckv, axis=1), jnp.stack(new_krope, axis=1),
            jnp.stack(new_gdn, axis=1), jnp.stack(new_gla, axis=1))
```

```python
import numpy as np
from contextlib import ExitStack
import concourse.bass as bass
import concourse.mybir as mybir
from concourse.bass_utils import run_bass_kernel_spmd

F32 = mybir.dt.float32
BF16 = mybir.dt.bfloat16
F32R = mybir.dt.float32r
AF = mybir.ActivationFunctionType
ALU = mybir.AluOpType
AX = mybir.AxisListType

ENGS = ("pe", "act", "dve", "pool", "sp")
NDMA = 20
D = 1024
DFF = 2816
EPS = 1e-6


class T:
    _n = 0

    def __init__(self, h, name, psum=False):
        self.h = h
        self.name = name
        self.recs = {}
        self.psum = psum

    def __getitem__(self, idx):
        return self.h[idx]


class Prog:
    def __init__(self, nc):
        self.nc = nc
        self.es = ExitStack()
        self.sem = {}
        for e in ENGS:
            self.sem["E:" + e] = self.es.enter_context(nc.semaphore("s_" + e))
        self.dsem = {}
        for q in ("sp", "pool", "act"):
            self.dsem[q] = []
            for i in range(56 if q == "pool" else (NDMA if q == "sp" else 2)):
                k = "D:%s%d" % (q, i)
                self.sem[k] = self.es.enter_context(nc.semaphore("d_%s%d" % (q, i)))
                self.dsem[q].append(k)
        self.csem = []
        for i in range(6):
            k = "C:%d" % i
            self.sem[k] = self.es.enter_context(nc.semaphore("c_%d" % i))
            self.csem.append(k)
        self.cctr = 0
        self.val = {k: 0 for k in self.sem}
        self.drr = {q: 0 for q in self.dsem}
        self.known = {e: {} for e in ENGS}
        self.ops = {e: [] for e in ENGS}
        self.nops = 0
        self.phase_es = None
        self.uid = 0

    def begin_phase(self):
        self.phase_es = ExitStack()

    def sb(self, name, shape, dt=F32, persistent=False, mid=False):
        es = self.es if persistent else (self.mid_es if mid else self.phase_es)
        self.uid += 1
        h = es.enter_context(self.nc.sbuf_tensor("%s_%d" % (name, self.uid), list(shape), dt))
        return T(h, name)

    def ps(self, name, shape, dt=F32, persistent=False):
        es = self.es if persistent else self.phase_es
        self.uid += 1
        h = es.enter_context(self.nc.psum_tensor("%s_%d" % (name, self.uid), list(shape), dt))
        return T(h, name, psum=True)

    def _conf(self, t, key):
        if key is None or t.psum:
            return list(t.recs.values())
        out = []
        if key in t.recs:
            out.append(t.recs[key])
        if None in t.recs:
            out.append(t.recs[None])
        return out

    def _deps(self, reads, writes, eng=None):
        deps = []
        for (t, key) in reads:
            for r in self._conf(t, key):
                if r["w"] is not None:
                    deps.append(r["w"])
                if t.psum and eng is not None:
                    for tok in r["r"]:
                        if tok[0] != "E:" + eng:
                            deps.append(tok)
        for (t, key) in writes:
            for r in self._conf(t, key):
                if r["w"] is not None:
                    deps.append(r["w"])
                deps.extend(r["r"])
        return deps

    def _commit(self, tok, reads, writes):
        for (t, key) in reads:
            r = t.recs.get(key)
            if r is None:
                r = t.recs[key] = {"w": None, "r": []}
            r["r"].append(tok)
            if len(r["r"]) > 48:
                best = {}
                for (s, v) in r["r"]:
                    if best.get(s, -1) < v:
                        best[s] = v
                r["r"] = list(best.items())
        for (t, key) in writes:
            if key is None:
                t.recs = {None: {"w": tok, "r": []}}
            else:
                t.recs[key] = {"w": tok, "r": []}

    def _waits(self, eng, deps):
        best = {}
        for (s, v) in deps:
            if best.get(s, -1) < v:
                best[s] = v
        kn = self.known[eng]
        out = []
        for s, v in best.items():
            if kn.get(s, 0) >= v:
                continue
            kn[s] = v
            out.append((s, v))
        return out

    @staticmethod
    def _norm(lst):
        out = []
        for a in lst:
            if isinstance(a, T):
                out.append((a, None))
            else:
                out.append(a)
        return out

    def op(self, eng, fn, reads=(), writes=()):
        reads = self._norm(reads)
        writes = self._norm(writes)
        deps = self._deps(reads, writes, eng)
        if eng == "pe":
            deps = [d for d in deps if d[0] != "E:pe"]
        waits = self._waits(eng, deps)
        sk = "E:" + eng
        self.val[sk] += 1
        tok = (sk, self.val[sk])
        self.ops[eng].append((waits, fn, (sk, 1)))
        self._commit(tok, reads, writes)
        self.nops += 1
        return tok

    def dma(self, q, fn, reads=(), writes=()):
        reads = self._norm(reads)
        writes = self._norm(writes)
        deps = self._deps(reads, writes)
        i = self.drr[q]
        self.drr[q] = (i + 1) % len(self.dsem[q])
        sk = self.dsem[q][i]
        if self.val[sk] > 0:
            deps.append((sk, self.val[sk]))
        waits = self._waits(q, deps)
        self.val[sk] += 16
        tok = (sk, self.val[sk])
        self.ops[q].append((waits, fn, (sk, 16)))
        self._commit(tok, reads, writes)
        self.nops += 1
        return tok

    def coll(self, fn, reads=(), writes=()):
        reads = self._norm(reads)
        writes = self._norm(writes)
        deps = self._deps(reads, writes)
        sk = self.csem[self.cctr % len(self.csem)]
        self.cctr += 1
        if self.val[sk] > 0:
            deps.append((sk, self.val[sk]))
        waits = self._waits("pool", deps)
        self.val[sk] += 1
        tok = (sk, self.val[sk])
        self.ops["pool"].append((waits, fn, (sk, 1)))
        self._commit(tok, reads, writes)
        self.nops += 1
        return tok

    def _replay(self, eng, e):
        for (waits, fn, inc) in self.ops[eng]:
            if fn is None:
                for (s, v) in waits:
                    e.wait_ge(self.sem[s], v)
                continue
            for (s, v) in waits[1:]:
                e.wait_ge(self.sem[s], v)
            ins = fn(e)
            if waits:
                ins._wait_ge(self.sem[waits[0][0]], waits[0][1])
            if inc is not None:
                ins.then_inc(self.sem[inc[0]], inc[1])

    def end_phase(self, final=False):
        allv = [(s, v) for s, v in self.val.items() if v > 0 and (final or not s.startswith("D:pool"))]
        for eng in ENGS:
            waits = self._waits(eng, allv)
            if waits:
                self.ops[eng].append((waits, None, None))
        nc = self.nc
        with nc.Block() as block:
            @block.tensor
            def _(e):
                self._replay("pe", e)

            @block.scalar
            def _(e):
                self._replay("act", e)

            @block.vector
            def _(e):
                self._replay("dve", e)

            @block.gpsimd
            def _(e):
                self._replay("pool", e)

            @block.sync
            def _(e):
                self._replay("sp", e)
        self.ops = {e: [] for e in ENGS}
        if self.phase_es is not None:
            self.phase_es.close()
            self.phase_es = None

    def close(self):
        self.es.close()


class Cfg:
    def __init__(self, TS=2048, NPS=4, TP=256, phases=None, dbg=False, groups=((0, 1), (2, 3), (4, 5), (6, 7))):
        self.rg = [list(g) for g in groups]
        self.TS = TS
        self.NPS = NPS
        self.TP = TP
        self.NP = NPS * TP
        self.NT = self.NP + TS
        self.phases = phases
        self.dbg = dbg


INPUT_SPECS = {
    "mod_w": (2, 1024, 9216), "mod_b": (2, 9216), "norm_pre": (2, 3, 1024), "norm_post": (2, 3, 1024),
    "ffn_w_in": (2, 2, 1024, 5632), "ffn_w_out": (2, 2, 2816, 1024),
    "ev_w_in": (1024, 2512), "ev_w_in_perm": (1024, 64), "ev_conv_w": (5, 1536), "ev_gdn_a_log": (8,), "ev_gdn_dt_bias": (8,),
    "ev_gdn_norm": (128,), "ev_mla_q_norm": (256,), "ev_mla_w_uq": (256, 768), "ev_mla_w_uq_perm": (256, 256),
    "ev_mla_kv_norm": (128,), "ev_mla_w_ukv": (128, 1024), "ev_w_out": (1024, 1024),
    "od_w_in": (1024, 3104), "od_gla_w_gup": (2, 16, 512), "od_gla_b_g": (2, 512), "od_gla_norm": (256,),
    "od_w_out": (1024, 1024),
}


class Builder:
    def __init__(self, cfg):
        self.cfg = cfg
        nc = self.nc = bass.Bass("TRN2", target_bir_lowering=False)
        self.P = Prog(nc)
        self.din = {}
        self.dout = {}
        self.dtr = {}
        c = cfg
        self.inp("xin", (c.NT, D))
        self.inp("cond", (2, D))
        self.inp("cache_ckv", (256, 128))
        self.inp("cache_krope", (256, 64))
        self.inp("state_gdn", (2, 4, 128, 128))
        self.inp("state_gla", (2, 4, 128, 256))
        self.inp("rope_cos", (64, c.TS))
        self.inp("rope_sin", (64, c.TS))
        self.inp("selw", (128, 2))
        for k, shp in INPUT_SPECS.items():
            self.inp(k, shp)
        self.out("y", (c.NT, D))
        self.out("ckv_out", (c.NP, 128))
        self.out("krope_out", (c.NP, 64))
        self.out("gdn_out", (c.NPS, 2, 4, 128, 128))
        self.out("gla_out", (c.NPS, 2, 4, 128, 256))

    def inp(self, name, shape):
        self.din[name] = self.nc.dram_tensor(name, list(shape), F32, kind="ExternalInput").ap()
        self.dtr[name] = T(None, name)

    def out(self, name, shape):
        self.dout[name] = self.nc.dram_tensor(name, list(shape), F32, kind="ExternalOutput").ap()
        self.dtr[name] = T(None, name)

    def scratch(self, name, shape, dt=F32):
        ap = self.nc.dram_tensor(name, list(shape), dt).ap()
        self.dtr[name] = T(None, name)
        self.din[name] = ap
        return ap

    def build(self):
        c = self.cfg
        P = self.P
        self.XS = self.scratch("XS", (c.NT, D))
        self.wbf = {}
        self.phase0()
        src = "xin"
        plan = [("ffn", 0, 0), ("mix", 0), ("ffn", 0, 1), ("ffn", 1, 0), ("mix", 1), ("ffn", 1, 1)]
        if c.phases is not None:
            plan = c.phases
        for i, ph in enumerate(plan):
            last = (i == len(plan) - 1)
            dst = "y" if last else "XS"
            if ph[0] == "ffn":
                self.ffn_phase(ph[1], ph[2], src, dst)
            else:
                self.mix_phase(ph[1], src, dst)
            src = dst
        P.begin_phase()
        P.end_phase(final=True)
        P.close()
        return self.nc

    def dr(self, name):
        return self.din[name] if name in self.din else self.dout[name]

    def cast_begin(self, name, ap2d, piece=64):
        R, C = ap2d.shape
        dst = self.scratch(name + "_bf", (R, C), BF16)
        self.wbf[name] = dst
        q = []
        for r0 in range(0, R, piece):
            q.append((name, ap2d, dst, r0, min(R, r0 + piece)))
        return q

    def cast_piece(self, q):
        if not q:
            return
        name, ap2d, dst, r0, r1 = q.pop(0)
        tr = self.dtr[name + "_bf"]
        self.P.dma("pool", lambda e: e.dma_start(out=dst[r0:r1, :], in_=ap2d[r0:r1, :]), writes=[(tr, r0)])

    def flush_deferred(self, upto=1):
        for k in sorted(getattr(self, "deferred", {})):
            if k <= upto:
                for (nm_, ap_, rp_) in self.deferred.pop(k):
                    self.cast_weight(nm_, ap_, rows_per=rp_)

    def cast_weight(self, name, ap2d, rows_per=256):
        P = self.P
        R, C = ap2d.shape
        dst = self.scratch(name + "_bf", (R, C), BF16)
        tr = self.dtr[name + "_bf"]
        r0 = 0
        while r0 < R:
            r1 = min(R, r0 + rows_per)
            P.dma("pool", lambda e, r0=r0, r1=r1: e.dma_start(out=dst[r0:r1, :], in_=ap2d[r0:r1, :]),
                  writes=[(tr, r0)])
            r0 = r1
        self.wbf[name] = dst
        return dst

    def phase0(self):
        c = self.cfg
        P = self.P
        nc = self.nc
        di = self.din
        P.begin_phase()
        self.ident = P.sb("ident", [128, 128], F32, persistent=True)
        self.ones = P.sb("ones", [128, 128], F32, persistent=True)
        self.epsc = P.sb("epsc", [128, 1], F32, persistent=True)
        ident, ones = self.ident, self.ones
        P.op("pool", lambda e: e.memset(ident[:], 1.0), writes=[ident])
        P.op("pool", lambda e: e.affine_select(out=ident[:], in_=ident[:], pattern=[[-1, 128]],
                                               compare_op=ALU.is_equal, fill=0.0, base=0, channel_multiplier=1),
             reads=[ident], writes=[ident])
        P.op("pool", lambda e: e.memset(ones[:], 1.0), writes=[ones])
        P.op("pool", lambda e: e.memset(self.epsc[:], EPS), writes=[self.epsc])
        self.cast_weight("ffn_in_00", di["ffn_w_in"][0, 0])
        self.castq = {}
        self.npre = P.sb("npre", [128, 48], F32, persistent=True)
        self.npost = P.sb("npost", [128, 48], F32, persistent=True)
        self.modbT = P.sb("modbT", [128, 144], F32, persistent=True)
        self.modsT = P.sb("modsT", [128, 2, 72, 2], F32, persistent=True)
        self.ABC = P.sb("ABC", [128, 2, 3, 2, 3, 8], F32, persistent=True)
        ld = P.sb("ld", [128, 128], F32)
        tps = P.ps("tps", [128, 128], F32)

        def load_T(src_ap, nrows, dst_tile, dst_off):
            P.dma("sp", lambda e: e.dma_start(out=ld[0:nrows, :], in_=src_ap), writes=[ld])
            P.op("pe", lambda e: e.transpose(tps[:, 0:nrows], ld[0:nrows, :], ident[0:nrows, 0:nrows]),
                 reads=[ld, ident], writes=[tps])
            P.op("dve", lambda e: e.tensor_copy(out=dst_tile[:, dst_off:dst_off + nrows], in_=tps[:, 0:nrows]),
                 reads=[tps], writes=[dst_tile])

        load_T(di["norm_pre"].rearrange("l s (c p) -> (l s c) p", p=128), 48, self.npre, 0)
        load_T(di["norm_post"].rearrange("l s (c p) -> (l s c) p", p=128), 48, self.npost, 0)
        for l in range(2):
            load_T(di["mod_b"][l].rearrange("(m p) -> m p", p=128), 72, self.modbT, l * 72)
        scT = P.sb("scT", [128, 16], F32)
        sc2 = P.sb("sc2", [128, 8, 2], F32)
        load_T(di["cond"].rearrange("r (c p) -> (r c) p", p=128), 16, scT, 0)
        P.op("act", lambda e: e.activation(out=scT[:], in_=scT[:], func=AF.Silu), reads=[scT], writes=[scT])
        for r in range(2):
            P.op("dve", lambda e, r=r: e.tensor_copy(out=sc2[:, :, r], in_=scT[:, r * 8:(r + 1) * 8]),
                 reads=[scT], writes=[sc2])
        wmb = [P.sb("wmb%d" % i, [128, 8, 512], F32) for i in range(2)]
        mps = [P.ps("mps%d" % l, [128, 144], F32) for l in range(2)]
        mrow = [P.sb("mrow%d" % l, [2, 9216], F32) for l in range(2)]
        pacc = [P.ps("pacc%d" % i, [2, 512], F32) for i in range(2)]
        bi = 0
        for l in range(2):
            for cb in range(18):
                w = wmb[bi % 2]
                pa = pacc[bi % 2]
                bi += 1
                for h2 in range(2):
                    P.dma("sp", lambda e, w=w, l=l, cb=cb, h2=h2: e.dma_start(
                        out=w[:, h2 * 4:(h2 + 1) * 4, :],
                        in_=di["mod_w"][l, h2 * 512:(h2 + 1) * 512, cb * 512:(cb + 1) * 512].rearrange(
                            "(kc p) n -> p kc n", p=128)), writes=[(w, h2)])
                for k in range(8):
                    P.op("pe", lambda e, w=w, k=k, pa=pa: e.matmul(pa[0:2, :], sc2[:, k, :], w[:, k, :], start=(k == 0), stop=(k == 7)),
                         reads=[(w, k // 4), sc2], writes=[pa])
                P.op("act", lambda e, l=l, cb=cb, pa=pa: e.activation(out=mrow[l][0:2, cb * 512:(cb + 1) * 512], in_=pa[0:2, :], func=AF.Copy),
                     reads=[pa], writes=[(mrow[l], cb)])
        for l in range(2):
            for m in range(72):
                P.op("pe", lambda e, l=l, m=m: e.transpose(mps[l][:, m * 2:m * 2 + 2], mrow[l][0:2, m * 128:(m + 1) * 128], ident[0:2, 0:2]),
                     reads=[(mrow[l], m // 4), ident], writes=[mps[l]])
        modsT, modbT = self.modsT, self.modbT
        for l in range(2):
            P.op("dve", lambda e, l=l: e.tensor_tensor(
                out=modsT[:, l], in0=mps[l][:].rearrange("p (m r) -> p m r", r=2),
                in1=modbT[:, l * 72:(l + 1) * 72].unsqueeze(2).to_broadcast([128, 72, 2]), op=ALU.add),
                reads=[mps[l], modbT], writes=[modsT])
        ABC = self.ABC
        for l in range(2):
            for s in range(3):
                res = 1.0 if s == 1 else 0.5
                for r in range(2):
                    gp = self.npre[:, (l * 3 + s) * 8:(l * 3 + s) * 8 + 8]
                    gq = self.npost[:, (l * 3 + s) * 8:(l * 3 + s) * 8 + 8]
                    sh = modsT[:, l, (3 * s) * 8:(3 * s) * 8 + 8, r]
                    scl = modsT[:, l, (3 * s + 1) * 8:(3 * s + 1) * 8 + 8, r]
                    gt = modsT[:, l, (3 * s + 2) * 8:(3 * s + 2) * 8 + 8, r]
                    P.op("dve", lambda e, l=l, s=s, r=r, gp=gp, scl=scl: e.scalar_tensor_tensor(
                        out=ABC[:, l, s, r, 0, :], in0=scl, scalar=1.0, in1=gp, op0=ALU.add, op1=ALU.mult),
                        reads=[modsT, self.npre], writes=[(ABC, (l, s, r, 0))])
                    P.op("dve", lambda e, l=l, s=s, r=r, sh=sh: e.tensor_copy(out=ABC[:, l, s, r, 1, :], in_=sh),
                         reads=[modsT], writes=[(ABC, (l, s, r, 1))])
                    P.op("dve", lambda e, l=l, s=s, r=r, gt=gt, gq=gq, res=res: e.scalar_tensor_tensor(
                        out=ABC[:, l, s, r, 2, :], in0=gt, scalar=res, in1=gq, op0=ALU.mult, op1=ALU.mult),
                        reads=[modsT, self.npost], writes=[(ABC, (l, s, r, 2))])
        P.end_phase()

    def groups(self):
        c = self.cfg
        gs = []
        for r0 in range(0, c.NP, 512):
            gs.append((r0, 0))
        for r0 in range(c.NP, c.NT, 512):
            gs.append((r0, 1))
        return gs

    def make_cbc(self, l, s, PB):
        P = self.P
        ABC, ident, ones = self.ABC, self.ident, self.ones
        out = []
        dg = P.sb("dg", [128, 128], F32)
        for r in range(2):
            cb = P.sb("cbc%d" % r, [128, 1024], F32)
            pb = [PB[2 * r], PB[2 * r + 1]]
            for cc in range(8):
                P.op("dve", lambda e, cc=cc, r=r: e.tensor_scalar(
                    out=dg[:], in0=ident[:], scalar1=ABC[:, l, s, r, 2, cc:cc + 1], scalar2=None, op0=ALU.mult),
                    reads=[ident, ABC], writes=[dg])
                P.op("pe", lambda e, cc=cc, r=r, pb=pb: e.matmul(
                    pb[cc // 4][:, (cc % 4) * 128:(cc % 4 + 1) * 128], ones[:], dg[:], start=True, stop=True),
                    reads=[ones, dg], writes=[(pb[cc // 4], cc % 4)])
            for i in range(2):
                P.op("act", lambda e, i=i, pb=pb, cb=cb: e.activation(
                    out=cb[:, i * 512:(i + 1) * 512], in_=pb[i][:], func=AF.Copy), reads=[pb[i]], writes=[(cb, i)])
            out.append(cb)
        return out

    def ffn_phase(self, l, f, src, dst):
        c = self.cfg
        P = self.P
        s = 0 if f == 0 else 2
        ABC, ident = self.ABC, self.ident
        srcap, dstap = self.dr(src), self.dr(dst)
        srct, dstt = self.dtr[src], self.dtr[dst]
        win = self.wbf["ffn_in_%d%d" % (l, f)]
        wint = self.dtr["ffn_in_%d%d_bf" % (l, f)]
        P.begin_phase()
        PB = [P.ps("pb%d" % i, [128, 512], F32) for i in range(8)]
        import os
        parts = os.environ.get("FFN_PARTS", "cbc,wo,load,pro,in,out").split(",")
        if "cbc" in parts:
            cbc = self.make_cbc(l, s, PB)
        wo = P.sb("wo", [128, 22, 1024], BF16)
        wout32 = self.din["ffn_w_out"][l, f]
        for q in range(2 if "wo" in parts else 0):
            P.dma("pool", lambda e, q=q: e.dma_start(
                out=wo[:, q * 11:(q + 1) * 11, :],
                in_=wout32[q * 1408:(q + 1) * 1408, :].rearrange("(j p) n -> p j n", p=128)),
                writes=[(wo, q)])
        wi = [P.sb("wi%d" % i, [128, 8, 512], BF16) for i in range(3)]
        xg = [P.sb("xg%d" % i, [128, 4, 1024], F32) for i in range(2)]
        hT = [P.sb("hT%d" % i, [128, 8, 512], BF16) for i in range(2)]
        gT = [P.sb("gT%d" % i, [128, 22, 512], BF16) for i in range(2)]
        xn = [P.sb("xn%d" % i, [128, 1024], F32) for i in range(2)]
        su = [P.sb("su%d" % i, [128, 512], F32) for i in range(2)]
        junk = P.sb("junk", [128, 1024], F32)
        junk2 = P.sb("junk2", [128, 1024], F32)
        tmp = [P.sb("tmp%d" % i, [128, 1024], F32) for i in range(2)]
        ss = [P.sb("ss%d" % i, [128, 24], F32) for i in range(2)]
        gs = self.groups()
        wctr = [0]

        def load(gi):
            r0, r = gs[gi]
            x = xg[gi % 2]
            P.dma("sp", lambda e: e.dma_start(out=x[:], in_=srcap[r0:r0 + 512, :].rearrange("(t p) d -> p t d", p=128)),
                  reads=[srct], writes=[x])

        def pro_a(gi):
            r0, r = gs[gi]
            x = xg[gi % 2]
            sst = ss[gi % 2]
            P.op("dve", lambda e: e.memset(sst[:], 0.0), writes=[sst])
            for t in range(4):
                P.op("act", lambda e, t=t: e.activation(out=junk2[:], in_=x[:, t, :], func=AF.Square,
                                                        accum_out=sst[:, t:t + 1]),
                     reads=[(x, None)], writes=[junk2, (sst, t)])
            P.op("act", lambda e: e.activation(out=sst[:, 4:8], in_=sst[:, 0:4], func=AF.Sqrt, scale=1.0 / D,
                                               bias=self.epsc[:, 0:1]), reads=[sst, self.epsc], writes=[(sst, "sq")])
            P.op("dve", lambda e: e.reciprocal(out=sst[:, 8:12], in_=sst[:, 4:8]), reads=[(sst, "sq")],
                 writes=[(sst, "rs")])
            for t in range(2):
                xx = xn[t % 2]
                P.op("act", lambda e, t=t, xx=xx: e.activation(out=xx[:], in_=x[:, t, :], func=AF.Identity,
                                                               scale=sst[:, 8 + t:9 + t]),
                     reads=[(x, None), (sst, "rs")], writes=[xx])

        def pro_b(gi):
            r0, r = gs[gi]
            x = xg[gi % 2]
            sst = ss[gi % 2]
            h = hT[gi % 2]
            for t in range(4):
                xx = xn[t % 2]
                if t >= 2:
                    P.op("act", lambda e, t=t, xx=xx: e.activation(out=xx[:], in_=x[:, t, :], func=AF.Identity,
                                                                   scale=sst[:, 8 + t:9 + t]),
                         reads=[(x, None), (sst, "rs")], writes=[xx])
                for half in range(2):
                    pb = PB[4 + half]
                    for q in range(4):
                        cc = half * 4 + q
                        P.op("pe", lambda e, xx=xx, cc=cc, q=q, pb=pb: e.transpose(
                            pb[:, q * 128:(q + 1) * 128], xx[:, cc * 128:(cc + 1) * 128], ident[:]),
                            reads=[xx, ident], writes=[(pb, q)])
                    for q in range(4):
                        cc = half * 4 + q
                        if half == 0:
                            P.op("act", lambda e, cc=cc, q=q, pb=pb, t=t: e.activation(
                                out=h[:, cc, t * 128:(t + 1) * 128], in_=pb[:, q * 128:(q + 1) * 128],
                                func=AF.Identity, scale=ABC[:, l, s, r, 0, cc:cc + 1],
                                bias=ABC[:, l, s, r, 1, cc:cc + 1]),
                                reads=[(pb, q), ABC], writes=[(h, (cc, t))])
                        else:
                            P.op("dve", lambda e, cc=cc, q=q, pb=pb, t=t: e.tensor_scalar(
                                out=h[:, cc, t * 128:(t + 1) * 128], in0=pb[:, q * 128:(q + 1) * 128],
                                scalar1=ABC[:, l, s, r, 0, cc:cc + 1], scalar2=ABC[:, l, s, r, 1, cc:cc + 1],
                                op0=ALU.mult, op1=ALU.add),
                                reads=[(pb, q), ABC], writes=[(h, (cc, t))])

        wmap = {}

        def wload(gi, b):
            if (gi, b) in wmap:
                return
            w = wi[wctr[0] % 3]
            wctr[0] += 1
            wmap[(gi, b)] = w
            for uv in range(2):
                P.dma("sp", lambda e, w=w, b=b, uv=uv: e.dma_start(
                    out=w[:, :, uv * 256:(uv + 1) * 256],
                    in_=win[:, uv * DFF + b * 256: uv * DFF + (b + 1) * 256].rearrange("(kc p) n -> p kc n", p=128)),
                    reads=[wint], writes=[(w, uv)])

        def inproj(gi, hook=None):
            h = hT[gi % 2]
            g = gT[gi % 2]
            for b in range(11):
                if b == 4 and hook is not None:
                    hook()
                wload(gi, b)
                w = wmap.pop((gi, b))
                self.cast_piece(self.cq_cur)
                for jj in range(2):
                    j = 2 * b + jj
                    pu = PB[(j % 2) * 2]
                    pv = PB[(j % 2) * 2 + 1]
                    for k in range(8):
                        P.op("pe", lambda e, w=w, k=k, jj=jj, pu=pu: e.matmul(
                            pu[:], w[:, k, jj * 128:(jj + 1) * 128], h[:, k, :], start=(k == 0), stop=(k == 7)),
                            reads=[(w, 0), h], writes=[pu])
                    for k in range(8):
                        P.op("pe", lambda e, w=w, k=k, jj=jj, pv=pv: e.matmul(
                            pv[:], w[:, k, 256 + jj * 128:256 + (jj + 1) * 128], h[:, k, :], start=(k == 0),
                            stop=(k == 7)),
                            reads=[(w, 1), h], writes=[pv])
                    sj = su[j % 2]
                    P.op("act", lambda e, sj=sj, pu=pu: e.activation(out=sj[:], in_=pu[:], func=AF.Silu),
                         reads=[pu], writes=[sj])
                    P.op("dve", lambda e, sj=sj, pv=pv, j=j: e.tensor_tensor(out=g[:, j, :], in0=sj[:], in1=pv[:],
                                                                             op=ALU.mult),
                         reads=[sj, pv], writes=[(g, j)])

        def outproj(gi):
            r0, r = gs[gi]
            x = xg[gi % 2]
            g = gT[gi % 2]
            sst = ss[gi % 2]
            for t in range(4):
                py = [PB[6], PB[7]] if t % 2 == 0 else [PB[4], PB[5]]
                for nb in range(2):
                    for j in range(22):
                        P.op("pe", lambda e, nb=nb, j=j, t=t, py=py: e.matmul(
                            py[nb][:], g[:, j, t * 128:(t + 1) * 128], wo[:, j, nb * 512:(nb + 1) * 512],
                            start=(j == 0), stop=(j == 21)),
                            reads=[g, (wo, j // 11)], writes=[py[nb]])
                for nb in range(2):
                    P.op("act", lambda e, nb=nb, t=t, py=py: e.activation(
                        out=junk[:, 0:512], in_=py[nb][:], func=AF.Square, accum_out=sst[:, 12 + 2 * t + nb:13 + 2 * t + nb]),
                        reads=[py[nb]], writes=[junk, (sst, ("y", t, nb))])
                P.op("dve", lambda e, t=t: e.tensor_tensor(out=sst[:, 20 + (t % 2) * 2:21 + (t % 2) * 2], in0=sst[:, 12 + 2 * t:13 + 2 * t],
                                                           in1=sst[:, 13 + 2 * t:14 + 2 * t], op=ALU.add),
                     reads=[(sst, ("y", t, 0)), (sst, ("y", t, 1))], writes=[(sst, ("ys", t % 2))])
                P.op("act", lambda e, t=t: e.activation(out=sst[:, 20 + (t % 2) * 2:21 + (t % 2) * 2], in_=sst[:, 20 + (t % 2) * 2:21 + (t % 2) * 2],
                                                        func=AF.Sqrt, scale=1.0 / D, bias=self.epsc[:, 0:1]),
                     reads=[(sst, ("ys", t % 2)), self.epsc], writes=[(sst, ("ys", t % 2))])
                P.op("dve", lambda e, t=t: e.reciprocal(out=sst[:, 21 + (t % 2) * 2:22 + (t % 2) * 2], in_=sst[:, 20 + (t % 2) * 2:21 + (t % 2) * 2]),
                     reads=[(sst, ("ys", t % 2))], writes=[(sst, ("yr", t % 2))])
                tm = tmp[t % 2]
                for nb in range(2):
                    P.op("dve", lambda e, nb=nb, t=t, py=py, tm=tm: e.scalar_tensor_tensor(
                        out=tm[:, nb * 512:(nb + 1) * 512], in0=py[nb][:], scalar=sst[:, 21 + (t % 2) * 2:22 + (t % 2) * 2],
                        in1=cbc[r][:, nb * 512:(nb + 1) * 512], op0=ALU.mult, op1=ALU.mult),
                        reads=[py[nb], (sst, ("yr", t % 2)), cbc[r]], writes=[(tm, nb)])
                P.op("pool", lambda e, t=t, tm=tm: e.tensor_tensor(out=tm[:], in0=tm[:], in1=x[:, t, :], op=ALU.add),
                     reads=[tm, (x, None)], writes=[tm])
                P.dma("sp", lambda e, t=t, tm=tm: e.dma_start(out=dstap[r0 + t * 128:r0 + (t + 1) * 128, :], in_=tm[:]),
                      reads=[tm], writes=[(dstt, r0 + t * 128)])

        ng = len(gs)
        di_ = self.din
        cq = []
        if (l, f) == (0, 0):
            cq = self.cast_begin("ffn_in_01", di_["ffn_w_in"][0, 1])
        elif (l, f) == (0, 1):
            cq = self.cast_begin("ffn_in_10", di_["ffn_w_in"][1, 0]) + self.cast_begin("ffn_in_11", di_["ffn_w_in"][1, 1])
        self.cq_cur = cq
        load(0)
        pro_a(0)
        pro_b(0)
        for gi in range(ng):
            if gi + 1 < ng:
                load(gi + 1)
                inproj(gi, hook=lambda gi=gi: pro_a(gi + 1))
                pro_b(gi + 1)
                for b_ in range(3):
                    wload(gi + 1, b_)
            else:
                inproj(gi)
            outproj(gi)
        while self.cq_cur:
            self.cast_piece(self.cq_cur)
        P.end_phase()

    def mix_phase(self, l, src, dst):
        if l == 0:
            self.even_mixer(src, dst)
        else:
            self.odd_mixer(src, dst)

    def pro_hT(self, x, sst, h, l, s, r, pbs, xn, junk):
        P = self.P
        ABC, ident = self.ABC, self.ident
        P.op("pool", lambda e: e.memset(sst[:], 0.0), writes=[sst])
        for t in range(4):
            P.op("act", lambda e, t=t: e.activation(out=junk[:], in_=x[:, t, :], func=AF.Square,
                                                    accum_out=sst[:, t:t + 1]),
                 reads=[(x, None)], writes=[junk, (sst, t)])
        P.op("act", lambda e: e.activation(out=sst[:, 4:8], in_=sst[:, 0:4], func=AF.Sqrt, scale=1.0 / D,
                                           bias=self.epsc[:, 0:1]), reads=[sst, self.epsc], writes=[(sst, "sq")])
        P.op("dve", lambda e: e.reciprocal(out=sst[:, 8:12], in_=sst[:, 4:8]), reads=[(sst, "sq")],
             writes=[(sst, "rs")])
        for t in range(4):
            xx = xn[t % 2]
            P.op("act", lambda e, t=t, xx=xx: e.activation(out=xx[:], in_=x[:, t, :], func=AF.Identity,
                                                           scale=sst[:, 8 + t:9 + t]),
                 reads=[(x, None), (sst, "rs")], writes=[xx])
            for half in range(2):
                pb = pbs[half]
                for q in range(4):
                    cc = half * 4 + q
                    P.op("pe", lambda e, xx=xx, cc=cc, q=q, pb=pb: e.transpose(
                        pb[:, q * 128:(q + 1) * 128], xx[:, cc * 128:(cc + 1) * 128], ident[:]),
                        reads=[xx, ident], writes=[(pb, q)])
                for q in range(4):
                    cc = half * 4 + q
                    P.op("act", lambda e, cc=cc, q=q, pb=pb, t=t: e.activation(
                        out=h[:, cc, t * 128:(t + 1) * 128], in_=pb[:, q * 128:(q + 1) * 128],
                        func=AF.Identity, scale=ABC[:, l, s, r, 0, cc:cc + 1],
                        bias=ABC[:, l, s, r, 1, cc:cc + 1]),
                        reads=[(pb, q), ABC], writes=[(h, (cc, t))])

    def epi(self, py, xap, xdeps, r, sst, c0, cbc, tm, junk, dstap, dstt, row0):
        P = self.P
        for nb in range(2):
            P.op("act", lambda e, nb=nb: e.activation(out=junk[:, 0:512], in_=py[nb][:], func=AF.Square,
                                                      accum_out=sst[:, c0 + nb:c0 + nb + 1]),
                 reads=[py[nb]], writes=[junk, (sst, ("y", c0, nb))])
        P.op("dve", lambda e: e.tensor_tensor(out=sst[:, c0 + 2:c0 + 3], in0=sst[:, c0:c0 + 1], in1=sst[:, c0 + 1:c0 + 2],
                                              op=ALU.add),
             reads=[(sst, ("y", c0, 0)), (sst, ("y", c0, 1))], writes=[(sst, ("ys", c0))])
        P.op("act", lambda e: e.activation(out=sst[:, c0 + 2:c0 + 3], in_=sst[:, c0 + 2:c0 + 3], func=AF.Sqrt,
                                           scale=1.0 / D, bias=self.epsc[:, 0:1]),
             reads=[(sst, ("ys", c0)), self.epsc], writes=[(sst, ("ys", c0))])
        P.op("dve", lambda e: e.reciprocal(out=sst[:, c0 + 3:c0 + 4], in_=sst[:, c0 + 2:c0 + 3]),
             reads=[(sst, ("ys", c0))], writes=[(sst, ("yr", c0))])
        for nb in range(2):
            P.op("dve", lambda e, nb=nb: e.scalar_tensor_tensor(
                out=tm[:, nb * 512:(nb + 1) * 512], in0=py[nb][:], scalar=sst[:, c0 + 3:c0 + 4],
                in1=cbc[r][:, nb * 512:(nb + 1) * 512], op0=ALU.mult, op1=ALU.mult),
                reads=[py[nb], (sst, ("yr", c0)), cbc[r]], writes=[(tm, nb)])
        P.op("pool", lambda e: e.tensor_tensor(out=tm[:], in0=tm[:], in1=xap, op=ALU.add),
             reads=[tm] + xdeps, writes=[tm])
        P.dma("sp", lambda e: e.dma_start(out=dstap[row0:row0 + 128, :], in_=tm[:]),
              reads=[tm], writes=[(dstt, row0)])

    def pair_buf(self, name, rows, cols, dt=F32):
        gin = self.scratch(name + "_in", (rows, cols), dt)
        gout = self.scratch(name + "_out", (2 * rows, cols), dt)
        return gin, gout, self.dtr[name + "_in"], self.dtr[name + "_out"]

    def pair_gather(self, gin, gout, tin, tout):
        rg = self.cfg.rg
        self.P.coll(lambda e: e.collective_compute("AllGather", ALU.bypass, replica_groups=rg, ins=[gin.opt()], outs=[gout.opt()]),
                    reads=[tin], writes=[tout])

    def kcol(self, row):
        return row if row < self.cfg.NP else row + 256

    def seqs(self):
        c = self.cfg
        out = [(i * c.TP, c.TP, 0, "p") for i in range(c.NPS)]
        out.append((c.NP, c.TS, 1, "s"))
        return out

    def even_mixer(self, src, dst):
        c = self.cfg
        NT, NP = c.NT, c.NP
        NKV = NT + 256
        self.QKVT = self.scratch("QKVT", (1536, NT))
        self.ZS = self.scratch("ZS", (NT, 512))
        self.GB = self.scratch("GB", (NT, 16))
        self.QT = self.scratch("QT", (4, 192, NT), BF16)
        self.CKVT = self.scratch("CKVT", (128, NKV), BF16)
        self.KRT = self.scratch("KRT", (65, NKV), BF16)
        self.OT = self.scratch("OT", (1024, NT), BF16)
        self.OD = self.scratch("OD", (2, NT, 512))
        import os
        parts = os.environ.get("EV_PARTS", "e1,e2,e3,e4").split(",")
        if "e1" in parts:
            self.ev_e1(src)
        if "e2" in parts:
            self.ev_e2()
        if "e3" in parts:
            self.ev_e3()
        if "e4" in parts:
            self.mix_out(0, src, dst, "ev_w_out")

    def ev_e1(self, src):
        c = self.cfg
        P = self.P
        di = self.din
        l, s = 0, 1
        NT, NP = c.NT, c.NP
        ident = self.ident
        srcap, srct = self.dr(src), self.dtr[src]
        win = di["ev_w_in"]
        winp = di["ev_w_in_perm"]
        QKVT, ZS, GB, QT, CKVT, KRT = self.QKVT, self.ZS, self.GB, self.QT, self.CKVT, self.KRT
        tQKVT, tZS, tGB, tQT, tCKVT, tKRT = [self.dtr[n] for n in ("QKVT", "ZS", "GB", "QT", "CKVT", "KRT")]
        P.begin_phase()
        PB = [P.ps("pb%d" % i, [128, 512], F32) for i in range(8)]
        wsb = P.sb("wsb", [128, 8, 2512], BF16)
        for q in range(4):
            P.dma("pool", lambda e, q=q: e.dma_start(
                out=wsb[:, q * 2:(q + 1) * 2, :], in_=win[q * 256:(q + 1) * 256, :].rearrange("(kc p) n -> p kc n", p=128)),
                writes=[(wsb, q)])
        wpsb = P.sb("wpsb", [128, 8, 64], BF16)
        P.dma("pool", lambda e: e.dma_start(out=wpsb[:], in_=winp.rearrange("(kc p) n -> p kc n", p=128)),
              writes=[wpsb])
        ld = P.sb("ld", [128, 128], F32)
        qnT = P.sb("qnT", [128, 2], F32)
        P.dma("sp", lambda e: e.dma_start(out=ld[0:2, :], in_=di["ev_mla_q_norm"].rearrange("(c p) -> c p", p=128)),
              writes=[ld])
        P.op("pe", lambda e: e.transpose(PB[0][:, 0:2], ld[0:2, :], ident[0:2, 0:2]), reads=[ld, ident], writes=[PB[0]])
        P.op("dve", lambda e: e.tensor_copy(out=qnT[:], in_=PB[0][:, 0:2]), reads=[PB[0]], writes=[qnT])
        wq32 = P.sb("wq32", [128, 2, 1024], F32)
        P.dma("sp", lambda e: e.dma_start(out=wq32[:, :, 0:768], in_=di["ev_mla_w_uq"].rearrange("(kc p) n -> p kc n", p=128)),
              writes=[(wq32, 0)])
        P.dma("sp", lambda e: e.dma_start(out=wq32[:, :, 768:1024], in_=di["ev_mla_w_uq_perm"].rearrange("(kc p) n -> p kc n", p=128)),
              writes=[(wq32, 1)])
        wuq = P.sb("wuq", [128, 2, 1024], BF16)
        for kc in range(2):
            P.op("dve", lambda e, kc=kc: e.tensor_scalar(out=wuq[:, kc, :], in0=wq32[:, kc, :], scalar1=qnT[:, kc:kc + 1],
                                                         scalar2=None, op0=ALU.mult),
                 reads=[wq32, qnT], writes=[(wuq, kc)])
        dtb = P.sb("dtb", [128, 8], F32)
        nea = P.sb("nea", [128, 8], F32)
        kvn = P.sb("kvn", [128, 128], F32)
        P.dma("sp", lambda e: e.dma_start(out=dtb[:], in_=di["ev_gdn_dt_bias"].partition_broadcast(128)), writes=[dtb])
        P.dma("sp", lambda e: e.dma_start(out=nea[:], in_=di["ev_gdn_a_log"].partition_broadcast(128)), writes=[nea])
        P.dma("sp", lambda e: e.dma_start(out=kvn[:], in_=di["ev_mla_kv_norm"].partition_broadcast(128)), writes=[kvn])
        P.op("act", lambda e: e.activation(out=nea[:], in_=nea[:], func=AF.Exp), reads=[nea], writes=[nea])
        P.op("dve", lambda e: e.tensor_scalar(out=nea[:], in0=nea[:], scalar1=-1.0, scalar2=None, op0=ALU.mult),
             reads=[nea], writes=[nea])
        cst = P.sb("cst", [128, 256], BF16)
        for i in range(2):
            P.dma("sp", lambda e, i=i: e.dma_start(out=ld[:], in_=di["cache_ckv"][i * 128:(i + 1) * 128, :]), writes=[ld])
            P.op("pe", lambda e: e.transpose(PB[1][:, 0:128], ld[:], ident[:]), reads=[ld, ident], writes=[PB[1]])
            P.op("dve", lambda e, i=i: e.tensor_copy(out=cst[:, i * 128:(i + 1) * 128], in_=PB[1][:, 0:128]),
                 reads=[PB[1]], writes=[(cst, i)])
        P.dma("sp", lambda e: e.dma_start(out=CKVT[:, NP:NP + 256], in_=cst[:]), reads=[cst], writes=[(tCKVT, "c")])
        cst2 = P.sb("cst2", [64, 256], BF16)
        for i in range(2):
            P.dma("sp", lambda e, i=i: e.dma_start(out=ld[:, 0:64], in_=di["cache_krope"][i * 128:(i + 1) * 128, :]),
                  writes=[ld])
            P.op("pe", lambda e: e.transpose(PB[1][0:64, 0:128], ld[:, 0:64], ident[:]), reads=[ld, ident],
                 writes=[PB[1]])
            P.op("dve", lambda e, i=i: e.tensor_copy(out=cst2[:, i * 128:(i + 1) * 128], in_=PB[1][0:64, 0:128]),
                 reads=[PB[1]], writes=[(cst2, i)])
        P.dma("sp", lambda e: e.dma_start(out=KRT[0:64, NP:NP + 256], in_=cst2[:]), reads=[cst2], writes=[(tKRT, "c")])
        onesrow = P.sb("onesrow", [1, NT + 256], BF16)
        P.op("pool", lambda e: e.memset(onesrow[:], 1.0), writes=[onesrow])
        P.dma("sp", lambda e: e.dma_start(out=KRT[64:65, :], in_=onesrow[:]), reads=[onesrow], writes=[(tKRT, "o")])

        xg = [P.sb("xg%d" % i, [128, 4, 1024], F32) for i in range(2)]
        hT = [P.sb("hT%d" % i, [128, 8, 512], BF16) for i in range(2)]
        xn = [P.sb("xn%d" % i, [128, 1024], F32) for i in range(2)]
        junk = P.sb("junk", [128, 1024], F32)
        ss = [P.sb("ss%d" % i, [128, 24], F32) for i in range(2)]
        stg = [P.sb("stg%d" % i, [128, 512], F32) for i in range(3)]
        zst = [P.sb("zst%d" % i, [128, 512], F32) for i in range(2)]
        sm = [P.sb("sm%d" % i, [128, 64], F32) for i in range(2)]
        gb = [P.sb("gb%d" % i, [128, 16], F32) for i in range(2)]
        cqn = [P.sb("cqn%d" % i, [128, 256], F32) for i in range(2)]
        ckn = [P.sb("ckn%d" % i, [128, 128], F32) for i in range(2)]
        krt = [P.sb("krt%d" % i, [128, 64], F32) for i in range(2)]
        cqnT = P.sb("cqnT", [128, 2, 512], BF16)
        ckT = P.sb("ckT", [128, 512], BF16)
        cs = [P.sb("cos%d" % i, [64, 512], F32) for i in range(2)]
        sn = [P.sb("sin%d" % i, [64, 512], F32) for i in range(2)]
        rt = [P.sb("rt%d" % i, [64, 512], F32) for i in range(2)]
        rb = [P.sb("rb%d" % i, [128, 512], BF16) for i in range(3)]
        gs = self.groups()
        ctr = {"stg": 0, "rb": 0, "pb": 0}

        def rope_or_copy(pa, pbp, kind, gi, outbf, nrow=64):
            if kind == "p":
                P.op("act", lambda e: e.activation(out=outbf[0:64, :], in_=pa[0:64, :], func=AF.Copy), reads=[pa],
                     writes=[outbf])
            else:
                a, b = rt[0], rt[1]
                P.op("dve", lambda e: e.tensor_tensor(out=a[:], in0=pa[0:64, :], in1=cs[gi % 2][:], op=ALU.mult),
                     reads=[pa, cs[gi % 2]], writes=[a])
                P.op("dve", lambda e: e.tensor_tensor(out=b[:], in0=pbp[0:64, :], in1=sn[gi % 2][:], op=ALU.mult),
                     reads=[pbp, sn[gi % 2]], writes=[b])
                P.op("pool", lambda e: e.tensor_tensor(out=outbf[0:64, :], in0=a[:], in1=b[:], op=ALU.add),
                     reads=[a, b], writes=[outbf])

        def load(gi):
            r0, r = gs[gi]
            x = xg[gi % 2]
            P.dma("sp", lambda e: e.dma_start(out=x[:], in_=srcap[r0:r0 + 512, :].rearrange("(t p) d -> p t d", p=128)),
                  reads=[srct], writes=[x])
            if r == 1:
                t0 = r0 - NP
                P.dma("sp", lambda e: e.dma_start(out=cs[gi % 2][:], in_=di["rope_cos"][:, t0:t0 + 512]), writes=[cs[gi % 2]])
                P.dma("sp", lambda e: e.dma_start(out=sn[gi % 2][:], in_=di["rope_sin"][:, t0:t0 + 512]), writes=[sn[gi % 2]])

        def body(gi):
            r0, r = gs[gi]
            kind = "p" if r == 0 else "s"
            x = xg[gi % 2]
            h = hT[gi % 2]
            sst = ss[gi % 2]
            kc0 = self.kcol(r0)
            for m in range(12):
                pb = PB[m % 4]
                for k in range(8):
                    P.op("pe", lambda e, m=m, k=k, pb=pb: e.matmul(pb[:], wsb[:, k, m * 128:(m + 1) * 128], h[:, k, :],
                                                                     start=(k == 0), stop=(k == 7)),
                         reads=[wsb, h], writes=[pb])
                st = stg[ctr["stg"] % 3]
                ctr["stg"] += 1
                if m % 2 == 0:
                    P.op("act", lambda e, st=st, pb=pb: e.activation(out=st[:], in_=pb[:], func=AF.Copy), reads=[pb], writes=[st])
                else:
                    P.op("dve", lambda e, st=st, pb=pb: e.tensor_copy(out=st[:], in_=pb[:]), reads=[pb], writes=[st])
                P.dma("sp", lambda e, st=st, m=m: e.dma_start(out=QKVT[m * 128:(m + 1) * 128, r0:r0 + 512], in_=st[:]),
                      reads=[st], writes=[(tQKVT, (m, r0))])
            pa, pbp = PB[0], PB[1]
            for k in range(8):
                P.op("pe", lambda e, k=k, pa=pa: e.matmul(pa[0:64, :], wsb[:, k, 2448:2512], h[:, k, :], start=(k == 0), stop=(k == 7)),
                     reads=[wsb, h], writes=[pa])
            if kind == "s":
                for k in range(8):
                    P.op("pe", lambda e, k=k, pbp=pbp: e.matmul(pbp[0:64, :], wpsb[:, k, :], h[:, k, :], start=(k == 0), stop=(k == 7)),
                         reads=[wpsb, h], writes=[pbp])
            ob = rb[ctr["rb"] % 3]
            ctr["rb"] += 1
            rope_or_copy(pa, pbp, kind, gi, ob)
            P.dma("sp", lambda e, ob=ob: e.dma_start(out=KRT[0:64, kc0:kc0 + 512], in_=ob[0:64, :]), reads=[ob],
                  writes=[(tKRT, r0)])
            for t in range(4):
                pz, pm = PB[6], PB[7]
                for k in range(8):
                    P.op("pe", lambda e, k=k, t=t: e.matmul(pz[:], h[:, k, t * 128:(t + 1) * 128], wsb[:, k, 1536:2048],
                                                            start=(k == 0), stop=(k == 7)), reads=[wsb, h], writes=[pz])
                for k in range(8):
                    P.op("pe", lambda e, k=k, t=t: e.matmul(pm[:, 0:464], h[:, k, t * 128:(t + 1) * 128], wsb[:, k, 2048:2512],
                                                            start=(k == 0), stop=(k == 7)), reads=[wsb, h], writes=[pm])
                row = r0 + t * 128
                zt = zst[t % 2]
                P.op("act", lambda e, zt=zt: e.activation(out=zt[:], in_=pz[:], func=AF.Silu), reads=[pz], writes=[zt])
                P.dma("sp", lambda e, zt=zt, row=row: e.dma_start(out=ZS[row:row + 128, :], in_=zt[:]), reads=[zt],
                      writes=[(tZS, row)])
                s_ = sm[t % 2]
                g_ = gb[t % 2]
                P.op("dve", lambda e, s_=s_: e.tensor_tensor(out=s_[:, 0:8], in0=pm[:, 0:8], in1=dtb[:], op=ALU.add),
                     reads=[pm, dtb], writes=[(s_, "a")])
                P.op("act", lambda e, s_=s_: e.activation(out=s_[:, 0:8], in_=s_[:, 0:8], func=AF.Exp), reads=[(s_, "a")],
                     writes=[(s_, "a")])
                P.op("act", lambda e, s_=s_: e.activation(out=s_[:, 0:8], in_=s_[:, 0:8], func=AF.Ln, bias=self.ones[:, 0:1]),
                     reads=[(s_, "a"), self.ones], writes=[(s_, "a")])
                P.op("dve", lambda e, s_=s_, g_=g_: e.tensor_tensor(out=g_[:, 0:8], in0=s_[:, 0:8], in1=nea[:], op=ALU.mult),
                     reads=[(s_, "a"), nea], writes=[(g_, 0)])
                P.op("act", lambda e, g_=g_: e.activation(out=g_[:, 8:16], in_=pm[:, 8:16], func=AF.Sigmoid), reads=[pm],
                     writes=[(g_, 1)])
                P.dma("sp", lambda e, g_=g_, row=row: e.dma_start(out=GB[row:row + 128, :], in_=g_[:]), reads=[g_],
                      writes=[(tGB, row)])
                cq = cqn[t % 2]
                P.op("pool", lambda e, s_=s_: e.memset(s_[:, 16:24], 0.0), writes=[(s_, "n")])
                P.op("act", lambda e, s_=s_: e.activation(out=junk[:, 0:256], in_=pm[:, 16:272], func=AF.Square,
                                                          accum_out=s_[:, 16:17]), reads=[pm, (s_, "n")],
                     writes=[junk, (s_, "n")])
                P.op("act", lambda e, s_=s_: e.activation(out=junk[:, 256:384], in_=pm[:, 272:400], func=AF.Square,
                                                          accum_out=s_[:, 17:18]), reads=[pm, (s_, "n")],
                     writes=[junk, (s_, "n")])
                P.op("act", lambda e, s_=s_: e.activation(out=s_[:, 18:19], in_=s_[:, 16:17], func=AF.Sqrt, scale=1.0 / 256,
                                                          bias=self.epsc[:, 0:1]), reads=[(s_, "n"), self.epsc],
                     writes=[(s_, "n")])
                P.op("act", lambda e, s_=s_: e.activation(out=s_[:, 19:20], in_=s_[:, 17:18], func=AF.Sqrt, scale=1.0 / 128,
                                                          bias=self.epsc[:, 0:1]), reads=[(s_, "n"), self.epsc],
                     writes=[(s_, "n")])
                P.op("dve", lambda e, s_=s_: e.reciprocal(out=s_[:, 20:22], in_=s_[:, 18:20]), reads=[(s_, "n")],
                     writes=[(s_, "n")])
                P.op("act", lambda e, s_=s_, cq=cq: e.activation(out=cq[:], in_=pm[:, 16:272], func=AF.Identity,
                                                                 scale=s_[:, 20:21]), reads=[pm, (s_, "n")], writes=[cq])
                ck = ckn[t % 2]
                P.op("dve", lambda e, s_=s_, ck=ck: e.scalar_tensor_tensor(out=ck[:], in0=pm[:, 272:400], scalar=s_[:, 21:22],
                                                                           in1=kvn[:], op0=ALU.mult, op1=ALU.mult),
                     reads=[pm, (s_, "n"), kvn], writes=[ck])
                if kind == "p":
                    P.dma("sp", lambda e, ck=ck, row=row: e.dma_start(out=self.dout["ckv_out"][row:row + 128, :], in_=ck[:]),
                          reads=[ck], writes=[(self.dtr["ckv_out"], row)])
                    kr = krt[t % 2]
                    P.op("dve", lambda e, kr=kr: e.tensor_copy(out=kr[:], in_=pm[:, 400:464]), reads=[pm], writes=[kr])
                    P.dma("sp", lambda e, kr=kr, row=row: e.dma_start(out=self.dout["krope_out"][row:row + 128, :], in_=kr[:]),
                          reads=[kr], writes=[(self.dtr["krope_out"], row)])
                pt = PB[4 + (t % 2)]
                for kc in range(2):
                    P.op("pe", lambda e, kc=kc, cq=cq, pt=pt: e.transpose(pt[:, kc * 128:(kc + 1) * 128],
                                                                           cq[:, kc * 128:(kc + 1) * 128], ident[:]),
                         reads=[cq, ident], writes=[pt])
                P.op("pe", lambda e, ck=ck, pt=pt: e.transpose(pt[:, 256:384], ck[:], ident[:]), reads=[ck, ident], writes=[pt])
                P.op("act", lambda e, pt=pt, t=t: e.activation(out=cqnT[:, :, t * 128:(t + 1) * 128],
                                                               in_=pt[:, 0:256].rearrange("p (a b) -> p a b", a=2),
                                                               func=AF.Copy), reads=[pt], writes=[(cqnT, t)])
                P.op("dve", lambda e, pt=pt, t=t: e.tensor_copy(out=ckT[:, t * 128:(t + 1) * 128], in_=pt[:, 256:384]),
                     reads=[pt], writes=[(ckT, t)])
            P.dma("sp", lambda e: e.dma_start(out=CKVT[:, kc0:kc0 + 512], in_=ckT[:]), reads=[ckT], writes=[(tCKVT, r0)])
            for hh in range(4):
                pq = PB[hh % 4]
                for kc in range(2):
                    P.op("pe", lambda e, kc=kc, hh=hh, pq=pq: e.matmul(pq[:], wuq[:, kc, hh * 192:hh * 192 + 128], cqnT[:, kc, :],
                                                                         start=(kc == 0), stop=(kc == 1)),
                         reads=[wuq, cqnT], writes=[pq])
                ob = rb[ctr["rb"] % 3]
                ctr["rb"] += 1
                P.op("act", lambda e, ob=ob, pq=pq: e.activation(out=ob[:], in_=pq[:], func=AF.Copy), reads=[pq], writes=[ob])
                P.dma("sp", lambda e, ob=ob, hh=hh: e.dma_start(out=QT[hh, 0:128, r0:r0 + 512], in_=ob[:]), reads=[ob],
                      writes=[(tQT, (hh, 0, r0))])
            for hh in range(4):
                pa, pbp = PB[(2 * hh) % 4], PB[(2 * hh + 1) % 4]
                for kc in range(2):
                    P.op("pe", lambda e, kc=kc, hh=hh, pa=pa: e.matmul(pa[0:64, :], wuq[:, kc, hh * 192 + 128:hh * 192 + 192],
                                                                         cqnT[:, kc, :], start=(kc == 0), stop=(kc == 1)),
                         reads=[wuq, cqnT], writes=[pa])
                if kind == "s":
                    for kc in range(2):
                        P.op("pe", lambda e, kc=kc, hh=hh, pbp=pbp: e.matmul(pbp[0:64, :], wuq[:, kc, 768 + hh * 64:768 + (hh + 1) * 64],
                                                                               cqnT[:, kc, :], start=(kc == 0), stop=(kc == 1)),
                             reads=[wuq, cqnT], writes=[pbp])
                ob = rb[ctr["rb"] % 3]
                ctr["rb"] += 1
                rope_or_copy(pa, pbp, kind, gi, ob)
                P.dma("sp", lambda e, ob=ob, hh=hh: e.dma_start(out=QT[hh, 128:192, r0:r0 + 512], in_=ob[0:64, :]), reads=[ob],
                      writes=[(tQT, (hh, 1, r0))])

        ng = len(gs)
        load(0)
        for gi in range(ng):
            r0, r = gs[gi]
            self.pro_hT(xg[gi % 2], ss[gi % 2], hT[gi % 2], l, s, r, [PB[4], PB[5]], xn, junk)
            if gi + 1 < ng:
                load(gi + 1)
            body(gi)
        P.end_phase()

    def ev_e2(self):
        c = self.cfg
        P = self.P
        di = self.din
        NT, NP = c.NT, c.NP
        NTILE = NT // 128
        ident, ones = self.ident, self.ones
        QKVT, GB, ZS, OT, OD = self.QKVT, self.GB, self.ZS, self.OT, self.OD
        tQKVT, tGB, tZS, tOT, tOD = [self.dtr[n] for n in ("QKVT", "GB", "ZS", "OT", "OD")]
        TBT = self.scratch("TBT", (NTILE, 8, 128, 128))
        QKT = self.scratch("QKT", (NTILE, 8, 128, 128))
        KDEC = self.scratch("KDEC", (NTILE, 8, 128, 128))
        VTM = self.scratch("VTM", (NTILE, 4, 128, 128))
        KNT = self.scratch("KNT", (4, 128, NT))
        QNT = self.scratch("QNT", (4, 128, NT))
        tTBT, tQKT, tKDEC, tVTM, tKNT, tQNT = [self.dtr[n] for n in ("TBT", "QKT", "KDEC", "VTM", "KNT", "QNT")]
        BIG = 30000.0

        for (row0, T, r, kind) in self.seqs():
            NTl = T // 128
            tile0 = row0 // 128
            P.mid_es = ExitStack()
            egc = P.sb("egc", [128, 2, NTl, 4], F32, mid=True)
            negc = P.sb("negc", [128, 2, NTl, 4], F32, mid=True)
            egl = P.sb("egl", [128, 2, NTl, 4], F32, mid=True)
            P.begin_phase()
            PB = [P.ps("pb%d" % i, [128, 512], F32) for i in range(8)]
            def mask(name, op, fill_where_false, flip=False):
                m = P.sb(name, [128, 128], F32)
                P.op("pool", lambda e: e.memset(m[:], 0.0 if fill_where_false != 0.0 else 1.0), writes=[m])
                P.op("pool", lambda e: e.affine_select(out=m[:], in_=m[:], pattern=[[1 if flip else -1, 128]], compare_op=op,
                                                       fill=fill_where_false, base=0, channel_multiplier=(-1 if flip else 1)),
                     reads=[m], writes=[m])
                return m
            Uincl = mask("Uincl", ALU.is_ge, 0.0, flip=True)
            Lincl = mask("Lincl", ALU.is_ge, 0.0)
            MA = [mask("MAf", ALU.is_gt, -BIG), mask("MAb", ALU.is_gt, -BIG, flip=True)]
            MB = [mask("MBf", ALU.is_ge, -BIG, flip=True), mask("MBb", ALU.is_ge, -BIG)]
            ld = P.sb("ld", [128, 128], F32)
            cw = P.sb("cw", [128, 60], F32)
            P.dma("sp", lambda e: e.dma_start(out=ld[0:60, :], in_=di["ev_conv_w"].rearrange("t (m p) -> (t m) p", p=128)),
                  writes=[ld])
            P.op("pe", lambda e: e.transpose(PB[0][:, 0:60], ld[0:60, :], ident[0:60, 0:60]), reads=[ld, ident], writes=[PB[0]])
            P.op("dve", lambda e: e.tensor_copy(out=cw[:], in_=PB[0][:, 0:60]), reads=[PB[0]], writes=[cw])
            GBt = P.sb("GBt", [128, NTl, 16], F32)
            nq = max(1, NTl // 8)
            for q in range(0, NTl, nq):
                P.dma("sp", lambda e, q=q: e.dma_start(
                    out=GBt[:, q:q + nq, :], in_=GB[row0 + q * 128:row0 + (q + nq) * 128, :].rearrange("(c p) f -> p c f", p=128)),
                    reads=[tGB], writes=[(GBt, q)])
            gfb = P.sb("gfb", [128, 2, NTl, 4], F32)
            for d in range(2):
                P.op("dve", lambda e, d=d: e.tensor_copy(out=gfb[:, d], in_=GBt[:, :, d * 4:(d + 1) * 4]), reads=[GBt],
                     writes=[(gfb, d)])
            gc = P.sb("gc", [128, 2, NTl, 4], F32)
            ngc = P.sb("ngc", [128, 2, NTl, 4], F32)
            kds = P.sb("kds", [128, 2, NTl, 4], F32)
            nbeta = P.sb("nbeta", [128, NTl, 8], F32)
            W = NTl * 4
            for d in range(2):
                P.op("pe", lambda e, d=d: e.matmul(PB[0][:, d * W:(d + 1) * W], (Uincl if d == 0 else Lincl)[:],
                                                   gfb[:, d].rearrange("p c h -> p (c h)"), start=True, stop=True),
                     reads=[Uincl, Lincl, gfb], writes=[PB[0]])
            P.op("pe", lambda e: e.matmul(PB[1][:, 0:2 * W], ones[:], gfb[:].rearrange("p d c h -> p (d c h)"), start=True,
                                          stop=True), reads=[ones, gfb], writes=[PB[1]])
            fl = lambda t_: t_[:].rearrange("p d c h -> p (d c h)")
            P.op("dve", lambda e: e.tensor_copy(out=fl(gc), in_=PB[0][:, 0:2 * W]), reads=[PB[0]], writes=[gc])
            P.op("dve", lambda e: e.tensor_scalar(out=fl(ngc), in0=PB[0][:, 0:2 * W], scalar1=-1.0, scalar2=None, op0=ALU.mult),
                 reads=[PB[0]], writes=[ngc])
            P.op("act", lambda e: e.activation(out=fl(egc), in_=PB[0][:, 0:2 * W], func=AF.Exp), reads=[PB[0]], writes=[egc])
            P.op("dve", lambda e: e.tensor_scalar(out=fl(negc), in0=fl(egc), scalar1=-1.0, scalar2=None, op0=ALU.mult),
                 reads=[egc], writes=[negc])
            P.op("act", lambda e: e.activation(out=fl(egl), in_=PB[1][:, 0:2 * W], func=AF.Exp), reads=[PB[1]], writes=[egl])
            P.op("dve", lambda e: e.tensor_tensor(out=fl(kds), in0=PB[1][:, 0:2 * W], in1=fl(gc), op=ALU.subtract),
                 reads=[PB[1], gc], writes=[kds])
            P.op("act", lambda e: e.activation(out=fl(kds), in_=fl(kds), func=AF.Exp), reads=[kds], writes=[kds])
            P.op("dve", lambda e: e.tensor_scalar(out=nbeta[:], in0=GBt[:, :, 8:16], scalar1=-1.0, scalar2=None, op0=ALU.mult),
                 reads=[GBt], writes=[nbeta])

            XT = [P.sb("XT%d" % i, [128, T + 4], F32) for i in range(2)]
            AC = [P.sb("AC%d" % i, [128, T], F32) for i in range(3)]
            sq = P.sb("sq", [128, 512], F32)
            rin = P.sb("rin", [128, 512], F32)
            TB = 2
            NU = 2 * TB
            dgs = [P.sb("dg%d" % i, [128, 256], F32) for i in range(TB)]
            Nt = [P.sb("Nt%d" % i, [128, 128], F32R) for i in range(2 * NU)]
            identr = P.sb("identr", [128, 128], F32R)
            P.op("dve", lambda e: e.tensor_copy(out=identr[:], in_=ident[:]), reads=[ident], writes=[identr])
            Qf = [P.sb("Qf%d" % i, [128, 128], F32R) for i in range(NU)]
            Nl = [[P.sb("Nl%d_%d" % (i, j), [128, 128], F32R) for j in range(2)] for i in range(NU)]
            Ql = [[P.sb("Ql%d_%d" % (i, j), [128, 128], F32R) for j in range(2)] for i in range(NU)]
            Dm = [P.sb("Dm%d" % i, [128, 128], F32R) for i in range(NU)]
            Em = [P.sb("Em%d" % i, [128, 128], F32R) for i in range(NU)]
            Wt = [P.sb("Wt%d" % i, [128, 128], F32R) for i in range(NU)]
            Zt = [P.sb("Zt%d" % i, [128, 128], F32R) for i in range(NU)]
            No = [P.sb("No%d" % i, [128, 128], F32R) for i in range(NU)]
            Qo = [P.sb("Qo%d" % i, [128, 128], F32R) for i in range(NU)]
            SAME = {}
            for msz in (16, 32, 64):
                nb_ = 128 // msz
                bt = P.sb("bt%d" % msz, [8, 128], F32)
                P.op("pool", lambda e, bt=bt: e.memset(bt[:], 1.0), writes=[bt])
                P.op("pool", lambda e, bt=bt, msz=msz: e.affine_select(out=bt[:], in_=bt[:], pattern=[[1, 128]], compare_op=ALU.is_ge,
                                                                     fill=0.0, base=0, channel_multiplier=-msz), reads=[bt], writes=[bt])
                P.op("pool", lambda e, bt=bt, msz=msz: e.affine_select(out=bt[:], in_=bt[:], pattern=[[-1, 128]], compare_op=ALU.is_ge,
                                                                     fill=0.0, base=msz - 1, channel_multiplier=msz), reads=[bt], writes=[bt])
                sm_ = P.sb("same%d" % msz, [128, 128], F32)
                P.op("pe", lambda e, bt=bt, nb_=nb_: e.matmul(PB[0][:, 0:128], bt[0:nb_, :], bt[0:nb_, :], start=True, stop=True),
                     reads=[bt], writes=[PB[0]])
                P.op("dve", lambda e, sm_=sm_: e.tensor_copy(out=sm_[:], in_=PB[0][:, 0:128]), reads=[PB[0]], writes=[sm_])
                SAME[msz] = sm_
            SYM = {}
            for msz, big in ((16, 32), (32, 64)):
                t_ = P.sb("sym%d" % msz, [128, 128], F32)
                P.op("dve", lambda e, t_=t_, msz=msz, big=big: e.tensor_tensor(out=t_[:], in0=SAME[big][:], in1=SAME[msz][:], op=ALU.subtract),
                     reads=[SAME[big], SAME[msz]], writes=[t_])
                SYM[msz] = t_
            t_ = P.sb("sym64", [128, 128], F32)
            P.op("dve", lambda e, t_=t_: e.tensor_tensor(out=t_[:], in0=ones[:], in1=SAME[64][:], op=ALU.subtract),
                 reads=[ones, SAME[64]], writes=[t_])
            SYM[64] = t_
            dA = [P.sb("dA%d" % i, [128, 128], F32) for i in range(NU)]
            dB = [P.sb("dB%d" % i, [128, 128], F32) for i in range(NU)]
            qk = [P.sb("qk%d" % i, [128, 128], F32) for i in range(NU)]
            tb = [P.sb("tb%d" % i, [128, 128], F32) for i in range(NU)]
            kd = [P.sb("kd%d" % i, [128, 128], F32) for i in range(NU)]
            vt = [P.sb("vt%d" % i, [128, 128], F32) for i in range(2)]
            uc = [0]
            if kind == "s":
                hin, hout, thin, thout = self.pair_buf("HALO", 1536, 2)
                P.dma("sp", lambda e: e.dma_start(out=hin, in_=QKVT[:, row0 + T - 2:row0 + T]), reads=[tQKVT], writes=[thin])
                self.pair_gather(hin, hout, thin, thout)
                hal = P.sb("hal", [128, 12, 2, 2], F32)
                for rk in range(2):
                    P.dma("sp", lambda e, rk=rk: e.dma_start(out=hal[:, :, rk, :], in_=hout[rk * 1536:(rk + 1) * 1536, :].rearrange("(m p) c -> p m c", p=128)),
                          reads=[thout], writes=[(hal, rk)])
                selw = P.sb("selw", [128, 2], F32)
                P.dma("sp", lambda e: e.dma_start(out=selw[:], in_=di["selw"]), writes=[selw])
                hsel = P.sb("hsel", [128, 12, 2], F32)
                P.op("dve", lambda e: e.tensor_scalar(out=hsel[:], in0=hal[:, :, 0, :], scalar1=selw[:, 0:1], scalar2=None, op0=ALU.mult),
                     reads=[hal, selw], writes=[hsel])
                P.op("dve", lambda e: e.scalar_tensor_tensor(out=hsel[:], in0=hal[:, :, 1, :], scalar=selw[:, 1:2], in1=hsel[:], op0=ALU.mult,
                                                             op1=ALU.add), reads=[hal, selw, hsel], writes=[hsel])
            for hh in range(4):
                for part in range(3):
                    X = XT[part % 2]
                    m = part * 4 + hh
                    P.op("pool", lambda e, X=X: e.memset(X[:, 0:2], 0.0), writes=[(X, "h0")])
                    if kind == "s":
                        P.op("pool", lambda e, X=X, m=m: e.tensor_copy(out=X[:, T + 2:T + 3], in_=hsel[:, m, 1:2]), reads=[hsel], writes=[(X, "h1")])
                        P.op("pool", lambda e, X=X, m=m: e.tensor_copy(out=X[:, T + 3:T + 4], in_=hsel[:, m, 0:1]), reads=[hsel], writes=[(X, "h2")])
                    else:
                        P.op("pool", lambda e, X=X: e.memset(X[:, T + 2:T + 4], 0.0), writes=[(X, "h1")])
                    P.dma("sp", lambda e, X=X, m=m: e.dma_start(out=X[:, 2:T + 2], in_=QKVT[m * 128:(m + 1) * 128, row0:row0 + T]),
                          reads=[tQKVT], writes=[(X, "b")])
                    A = AC[part]
                    for tap in range(5):
                        if tap == 0:
                            P.op("dve", lambda e, X=X, A=A, m=m: e.tensor_scalar(out=A[:], in0=X[:, 0:T], scalar1=cw[:, m:m + 1],
                                                                                 scalar2=None, op0=ALU.mult),
                                 reads=[X, cw], writes=[A])
                        else:
                            P.op("dve", lambda e, X=X, A=A, m=m, tap=tap: e.scalar_tensor_tensor(
                                out=A[:], in0=X[:, tap:tap + T], scalar=cw[:, tap * 12 + m:tap * 12 + m + 1], in1=A[:],
                                op0=ALU.mult, op1=ALU.add), reads=[X, cw, A], writes=[A])
                    P.op("act", lambda e, A=A: e.activation(out=A[:], in_=A[:], func=AF.Silu), reads=[A], writes=[A])
                    if part < 2:
                        for b0 in range(0, T, 512):
                            bw = min(512, T - b0)
                            P.op("act", lambda e, A=A, b0=b0, bw=bw: e.activation(out=sq[:, 0:bw], in_=A[:, b0:b0 + bw], func=AF.Square),
                                 reads=[A], writes=[sq])
                            P.op("pe", lambda e, bw=bw: e.matmul(PB[0][:, 0:bw], ones[:], sq[:, 0:bw], start=True, stop=True),
                                 reads=[ones, sq], writes=[PB[0]])
                            P.op("act", lambda e, bw=bw: e.activation(out=rin[:, 0:bw], in_=PB[0][:, 0:bw], func=AF.Sqrt,
                                                                      bias=self.epsc[:, 0:1]), reads=[PB[0], self.epsc], writes=[rin])
                            P.op("dve", lambda e, bw=bw: e.reciprocal(out=rin[:, 0:bw], in_=rin[:, 0:bw]), reads=[rin], writes=[rin])
                            if part == 0:
                                P.op("dve", lambda e, A=A, b0=b0, bw=bw: e.scalar_tensor_tensor(
                                    out=A[:, b0:b0 + bw], in0=A[:, b0:b0 + bw], scalar=128.0 ** -0.5, in1=rin[:, 0:bw],
                                    op0=ALU.mult, op1=ALU.mult), reads=[A, rin], writes=[A])
                            else:
                                P.op("dve", lambda e, A=A, b0=b0, bw=bw: e.tensor_tensor(
                                    out=A[:, b0:b0 + bw], in0=A[:, b0:b0 + bw], in1=rin[:, 0:bw], op=ALU.mult),
                                    reads=[A, rin], writes=[A])
                qn, kn, vv = AC
                P.dma("sp", lambda e, hh=hh: e.dma_start(out=QNT[hh, :, row0:row0 + T], in_=qn[:]), reads=[qn],
                      writes=[(tQNT, (hh, row0))])
                P.dma("sp", lambda e, hh=hh: e.dma_start(out=KNT[hh, :, row0:row0 + T], in_=kn[:]), reads=[kn],
                      writes=[(tKNT, (hh, row0))])
                for cc0 in range(0, NTl, TB):
                    units = []
                    for bi, cc in enumerate(range(cc0, min(NTl, cc0 + TB))):
                        dg = dgs[bi]
                        tl = tile0 + cc
                        csl = slice(cc * 128, (cc + 1) * 128)
                        pmisc = PB[1 + bi]
                        P.op("pe", lambda e, csl=csl, pmisc=pmisc: e.transpose(pmisc[:, 0:128], kn[:, csl], ident[:]), reads=[kn, ident], writes=[pmisc])
                        P.op("pe", lambda e, csl=csl, pmisc=pmisc: e.transpose(pmisc[:, 128:256], vv[:, csl], ident[:]), reads=[vv, ident], writes=[pmisc])
                        P.op("pe", lambda e, csl=csl, pmisc=pmisc: e.matmul(pmisc[:, 256:384], kn[:, csl], kn[:, csl], start=True, stop=True),
                             reads=[kn], writes=[pmisc])
                        P.op("pe", lambda e, csl=csl, pmisc=pmisc: e.matmul(pmisc[:, 384:512], kn[:, csl], qn[:, csl], start=True, stop=True),
                             reads=[kn, qn], writes=[pmisc])
                        v_ = vt[bi]
                        P.op("dve", lambda e, v_=v_, pmisc=pmisc: e.tensor_copy(out=v_[:], in_=pmisc[:, 128:256]), reads=[pmisc], writes=[v_])
                        P.dma("sp", lambda e, v_=v_, tl=tl, hh=hh: e.dma_start(out=VTM[tl, hh], in_=v_[:]), reads=[v_],
                              writes=[(tVTM, (tl, hh))])
                        for d in range(2):
                            P.op("dve", lambda e, d=d, cc=cc, hh=hh, dg=dg: e.tensor_scalar(out=dg[:, d * 128:(d + 1) * 128], in0=ident[:],
                                                                                     scalar1=gc[:, d, cc, hh:hh + 1], scalar2=None,
                                                                                     op0=ALU.mult), reads=[ident, gc], writes=[(dg, d)])
                        pgb = PB[3]
                        pg0 = bi * 256
                        P.op("pe", lambda e, dg=dg, pg0=pg0: e.matmul(pgb[:, pg0:pg0 + 256], ones[:], dg[:], start=True, stop=True), reads=[ones, dg], writes=[pgb])
                        for d in range(2):
                            u = bi * 2 + d
                            dh = d * 4 + hh
                            col = lambda t_, d=d, cc=cc, hh=hh: t_[:, d, cc, hh:hh + 1]
                            P.op("act", lambda e, u=u, d=d, cc=cc, hh=hh, pmisc=pmisc: e.activation(out=kd[u][:], in_=pmisc[:, 0:128], func=AF.Identity,
                                                                                       scale=kds[:, d, cc, hh:hh + 1]),
                                 reads=[pmisc, kds], writes=[kd[u]])
                            P.dma("sp", lambda e, u=u, tl=tl, dh=dh: e.dma_start(out=KDEC[tl, dh], in_=kd[u][:]), reads=[kd[u]],
                                  writes=[(tKDEC, (tl, dh))])
                            P.op("dve", lambda e, u=u, d=d, pg0=pg0: e.scalar_tensor_tensor(out=dA[u][:], in0=pgb[:, pg0 + d * 128:pg0 + (d + 1) * 128], scalar=-1.0,
                                                                                   in1=MA[d][:], op0=ALU.mult, op1=ALU.add),
                                 reads=[pgb, MA[d]], writes=[dA[u]])
                            P.op("act", lambda e, u=u, d=d, cc=cc, hh=hh: e.activation(out=dA[u][:], in_=dA[u][:], func=AF.Exp,
                                                                                       bias=gc[:, d, cc, hh:hh + 1]),
                                 reads=[dA[u], gc], writes=[dA[u]])
                            P.op("dve", lambda e, u=u, d=d, pg0=pg0: e.tensor_tensor(out=dB[u][:], in0=pgb[:, pg0 + d * 128:pg0 + (d + 1) * 128], in1=MB[d][:],
                                                                            op=ALU.add), reads=[pgb, MB[d]], writes=[dB[u]])
                            P.op("act", lambda e, u=u, d=d, cc=cc, hh=hh: e.activation(out=dB[u][:], in_=dB[u][:], func=AF.Exp,
                                                                                       bias=ngc[:, d, cc, hh:hh + 1]),
                                 reads=[dB[u], ngc], writes=[dB[u]])
                            n0 = Nt[2 * u]
                            P.op("dve", lambda e, u=u, n0=n0, cc=cc, dh=dh, pmisc=pmisc: e.scalar_tensor_tensor(
                                out=n0[:], in0=pmisc[:, 256:384], scalar=nbeta[:, cc, dh:dh + 1], in1=dA[u][:], op0=ALU.mult, op1=ALU.mult),
                                reads=[pmisc, nbeta, dA[u]], writes=[n0])
                            P.op("dve", lambda e, u=u, pmisc=pmisc: e.tensor_tensor(out=qk[u][:], in0=pmisc[:, 384:512], in1=dB[u][:], op=ALU.mult),
                                 reads=[pmisc, dB[u]], writes=[qk[u]])
                            P.dma("sp", lambda e, u=u, tl=tl, dh=dh: e.dma_start(out=QKT[tl, dh], in_=qk[u][:]), reads=[qk[u]],
                                  writes=[(tQKT, (tl, dh))])
                            units.append((u, d, dh, cc, tl))
                    pw = {u_[0]: PB[4 + u_[0]] for u_ in units}
                    U = [(u, pw[u], Nt[2 * u], cc, tl, dh) for (u, d, dh, cc, tl) in units]
                    for (u, p_, Nf, cc, tl, dh) in U:
                        P.op("pe", lambda e, p_=p_, Nf=Nf: e.transpose(p_[:, 0:128].bitcast(F32R), Nf[:], identr[:]), reads=[Nf, identr], writes=[p_])
                    for (u, p_, Nf, cc, tl, dh) in U:
                        P.op("act", lambda e, u=u, p_=p_: e.activation(out=Qf[u][:], in_=p_[:, 0:128], func=AF.Copy), reads=[p_], writes=[Qf[u]])
                        P.op("dve", lambda e, u=u, p_=p_: e.tensor_tensor(out=Ql[u][0][:], in0=p_[:, 0:128], in1=SAME[16][:], op=ALU.mult),
                             reads=[p_, SAME[16]], writes=[Ql[u][0]])
                        P.op("pool", lambda e, u=u, Nf=Nf: e.tensor_tensor(out=Nl[u][0][:], in0=Nf[:], in1=SAME[16][:], op=ALU.mult),
                             reads=[Nf, SAME[16]], writes=[Nl[u][0]])
                    for (u, p_, Nf, cc, tl, dh) in U:
                        P.op("pool", lambda e, u=u: e.tensor_tensor(out=Dm[u][:], in0=Nl[u][0][:], in1=ident[:], op=ALU.add),
                             reads=[Nl[u][0], ident], writes=[Dm[u]])
                        P.op("dve", lambda e, u=u: e.tensor_tensor(out=Em[u][:], in0=Ql[u][0][:], in1=ident[:], op=ALU.add),
                             reads=[Ql[u][0], ident], writes=[Em[u]])
                    for lvl in range(1, 4):
                        for (u, p_, Nf, cc, tl, dh) in U:
                            nc_, qc_ = Nl[u][(lvl - 1) % 2], Ql[u][(lvl - 1) % 2]
                            P.op("pe", lambda e, p_=p_, nc_=nc_, qc_=qc_: e.matmul(p_[:, 0:128], qc_[:], nc_[:], start=True, stop=True),
                                 reads=[nc_, qc_], writes=[p_])
                            P.op("pe", lambda e, p_=p_, nc_=nc_, qc_=qc_: e.matmul(p_[:, 128:256], nc_[:], qc_[:], start=True, stop=True),
                                 reads=[nc_, qc_], writes=[p_])
                        for (u, p_, Nf, cc, tl, dh) in U:
                            nn_, qn_ = Nl[u][lvl % 2], Ql[u][lvl % 2]
                            P.op("act", lambda e, p_=p_, nn_=nn_: e.activation(out=nn_[:], in_=p_[:, 0:128], func=AF.Copy), reads=[p_], writes=[nn_])
                            P.op("dve", lambda e, p_=p_, qn_=qn_: e.tensor_copy(out=qn_[:], in_=p_[:, 128:256]), reads=[p_], writes=[qn_])
                        for (u, p_, Nf, cc, tl, dh) in U:
                            nn_, qn_ = Nl[u][lvl % 2], Ql[u][lvl % 2]
                            P.op("pe", lambda e, p_=p_, nn_=nn_, u=u: e.matmul(p_[:, 256:384], nn_[:], Em[u][:], start=True, stop=True),
                                 reads=[nn_, Em[u]], writes=[p_])
                            P.op("pe", lambda e, p_=p_, qn_=qn_, u=u: e.matmul(p_[:, 384:512], qn_[:], Dm[u][:], start=True, stop=True),
                                 reads=[qn_, Dm[u]], writes=[p_])
                        for (u, p_, Nf, cc, tl, dh) in U:
                            P.op("dve", lambda e, p_=p_, u=u: e.tensor_tensor(out=Em[u][:], in0=Em[u][:], in1=p_[:, 256:384], op=ALU.add),
                                 reads=[p_, Em[u]], writes=[Em[u]])
                            P.op("dve", lambda e, p_=p_, u=u: e.tensor_tensor(out=Dm[u][:], in0=Dm[u][:], in1=p_[:, 384:512], op=ALU.add),
                                 reads=[p_, Dm[u]], writes=[Dm[u]])
                    for mi, msz in enumerate((16, 32, 64)):
                        last = (mi == 2)
                        for (u, p_, Nf, cc, tl, dh) in U:
                            P.op("pool", lambda e, u=u, Nf=Nf, msz=msz: e.tensor_tensor(out=No[u][:], in0=Nf[:], in1=SYM[msz][:], op=ALU.mult),
                                 reads=[Nf, SYM[msz]], writes=[No[u]])
                            if not last:
                                P.op("pool", lambda e, u=u, msz=msz: e.tensor_tensor(out=Qo[u][:], in0=Qf[u][:], in1=SYM[msz][:], op=ALU.mult),
                                     reads=[Qf[u], SYM[msz]], writes=[Qo[u]])
                        for (u, p_, Nf, cc, tl, dh) in U:
                            if not last:
                                P.op("pe", lambda e, p_=p_, u=u: e.matmul(p_[:, 0:128], Qo[u][:], Dm[u][:], start=True, stop=True),
                                     reads=[Qo[u], Dm[u]], writes=[p_])
                            P.op("pe", lambda e, p_=p_, u=u: e.matmul(p_[:, 128:256], No[u][:], Em[u][:], start=True, stop=True),
                                 reads=[No[u], Em[u]], writes=[p_])
                        for (u, p_, Nf, cc, tl, dh) in U:
                            if not last:
                                P.op("act", lambda e, p_=p_, u=u: e.activation(out=Wt[u][:], in_=p_[:, 0:128], func=AF.Copy), reads=[p_], writes=[Wt[u]])
                            P.op("dve", lambda e, p_=p_, u=u: e.tensor_copy(out=Zt[u][:], in_=p_[:, 128:256]), reads=[p_], writes=[Zt[u]])
                        for (u, p_, Nf, cc, tl, dh) in U:
                            if not last:
                                P.op("pe", lambda e, p_=p_, u=u: e.matmul(p_[:, 256:384], Em[u][:], Wt[u][:], start=True, stop=True),
                                     reads=[Em[u], Wt[u]], writes=[p_])
                            P.op("pe", lambda e, p_=p_, u=u: e.matmul(p_[:, 384:512], Dm[u][:], Zt[u][:], start=True, stop=True),
                                 reads=[Dm[u], Zt[u]], writes=[p_])
                        for (u, p_, Nf, cc, tl, dh) in U:
                            if not last:
                                P.op("dve", lambda e, p_=p_, u=u: e.tensor_tensor(out=Dm[u][:], in0=Dm[u][:], in1=p_[:, 256:384], op=ALU.add),
                                     reads=[p_, Dm[u]], writes=[Dm[u]])
                            P.op("dve", lambda e, p_=p_, u=u: e.tensor_tensor(out=Em[u][:], in0=Em[u][:], in1=p_[:, 384:512], op=ALU.add),
                                 reads=[p_, Em[u]], writes=[Em[u]])
                    for (u, p_, Nf, cc, tl, dh) in U:
                        P.op("act", lambda e, u=u, cc=cc, dh=dh: e.activation(out=tb[u][:], in_=Em[u][:], func=AF.Identity,
                                                                              scale=GBt[:, cc, 8 + dh:9 + dh]), reads=[Em[u], GBt],
                             writes=[tb[u]])
                        P.dma("sp", lambda e, u=u, tl=tl, dh=dh: e.dma_start(out=TBT[tl, dh], in_=tb[u][:]), reads=[tb[u]],
                              writes=[(tTBT, (tl, dh))])
            P.end_phase()
            P.begin_phase()
            PB = [P.ps("pb%d" % i, [128, 512], F32) for i in range(8)]
            S = [[P.sb("S%d_%d" % (dh, i), [128, 128], F32) for i in range(2)] for dh in range(8)]
            for dh in range(8):
                if kind == "p":
                    P.op("pool", lambda e, dh=dh: e.memset(S[dh][0][:], 0.0), writes=[S[dh][0]])
                else:
                    P.dma("sp", lambda e, dh=dh: e.dma_start(out=S[dh][0][:], in_=di["state_gdn"][dh // 4, dh % 4]), writes=[S[dh][0]])
            NB = 2
            L_tb = [[P.sb("Ltb%d_%d" % (dh, i), [128, 128], F32) for i in range(NB)] for dh in range(8)]
            L_qk = [[P.sb("Lqk%d_%d" % (dh, i), [128, 128], F32) for i in range(NB)] for dh in range(8)]
            L_kd = [[P.sb("Lkd%d_%d" % (dh, i), [128, 128], F32) for i in range(NB)] for dh in range(8)]
            L_kn = [[P.sb("Lkn%d_%d" % (dh, i), [128, 128], F32) for i in range(NB)] for dh in range(8)]
            L_qn = [[P.sb("Lqn%d_%d" % (dh, i), [128, 128], F32) for i in range(NB)] for dh in range(8)]
            L_v = [[P.sb("Lv%d_%d" % (dh, i), [128, 128], F32) for i in range(NB)] for dh in range(8)]
            Rt = [P.sb("Rt%d" % dh, [128, 128], F32) for dh in range(8)]
            vn = [P.sb("vn%d" % dh, [128, 128], F32) for dh in range(8)]
            o1 = [P.sb("o1_%d" % dh, [128, 128], F32) for dh in range(8)]
            ob = [[P.sb("ob%d_%d" % (dh, i), [128, 128], F32) for i in range(2)] for dh in range(8)]

            def sload(i, dirs):
                for d in dirs:
                    cc = i if d == 0 else NTl - 1 - i
                    tl = tile0 + cc
                    rows = slice(row0 + cc * 128, row0 + (cc + 1) * 128)
                    for hh in range(4):
                        dh = d * 4 + hh
                        b = i % NB
                        P.dma("sp", lambda e, dh=dh, b=b, tl=tl: e.dma_start(out=L_tb[dh][b][:], in_=TBT[tl, dh]), reads=[tTBT],
                              writes=[L_tb[dh][b]])
                        P.dma("sp", lambda e, dh=dh, b=b, tl=tl: e.dma_start(out=L_qk[dh][b][:], in_=QKT[tl, dh]), reads=[tQKT],
                              writes=[L_qk[dh][b]])
                        P.dma("sp", lambda e, dh=dh, b=b, tl=tl: e.dma_start(out=L_kd[dh][b][:], in_=KDEC[tl, dh]), reads=[tKDEC],
                              writes=[L_kd[dh][b]])
                        P.dma("sp", lambda e, dh=dh, b=b, hh=hh, rows=rows: e.dma_start(out=L_kn[dh][b][:], in_=KNT[hh, :, rows]),
                              reads=[tKNT], writes=[L_kn[dh][b]])
                        P.dma("sp", lambda e, dh=dh, b=b, hh=hh, rows=rows: e.dma_start(out=L_qn[dh][b][:], in_=QNT[hh, :, rows]),
                              reads=[tQNT], writes=[L_qn[dh][b]])
                        P.dma("sp", lambda e, dh=dh, b=b, tl=tl, hh=hh: e.dma_start(out=L_v[dh][b][:], in_=VTM[tl, hh]), reads=[tVTM],
                              writes=[L_v[dh][b]])

            def scan_pass(dirs):
                sload(0, dirs)
                for i in range(NTl):
                    if i + 1 < NTl:
                        sload(i + 1, dirs)
                    b = i % NB
                    CH = []
                    for d in dirs:
                        cc = i if d == 0 else NTl - 1 - i
                        for hh in range(4):
                            dh = d * 4 + hh
                            CH.append((d, cc, hh, dh, S[dh][i % 2], S[dh][(i + 1) % 2], PB[dh], row0 + cc * 128))
                    for (d, cc, hh, dh, Sc, Sn, pp, rows0) in CH:
                        P.op("pe", lambda e, dh=dh, b=b, Sc=Sc, pp=pp: e.matmul(pp[:, 0:128], L_kn[dh][b][:], Sc[:], start=True, stop=True),
                             reads=[L_kn[dh][b], Sc], writes=[pp])
                        P.op("pe", lambda e, dh=dh, b=b, Sc=Sc, pp=pp: e.matmul(pp[:, 128:256], L_qn[dh][b][:], Sc[:], start=True, stop=True),
                             reads=[L_qn[dh][b], Sc], writes=[pp])
                    for (d, cc, hh, dh, Sc, Sn, pp, rows0) in CH:
                        P.op("dve", lambda e, dh=dh, b=b, pp=pp, d=d, cc=cc, hh=hh: e.scalar_tensor_tensor(
                            out=Rt[dh][:], in0=pp[:, 0:128], scalar=negc[:, d, cc, hh:hh + 1], in1=L_v[dh][b][:], op0=ALU.mult, op1=ALU.add),
                            reads=[pp, negc, L_v[dh][b]], writes=[Rt[dh]])
                    for (d, cc, hh, dh, Sc, Sn, pp, rows0) in CH:
                        P.op("pe", lambda e, dh=dh, b=b, pp=pp: e.matmul(pp[:, 256:384], L_tb[dh][b][:], Rt[dh][:], start=True, stop=True),
                             reads=[L_tb[dh][b], Rt[dh]], writes=[pp])
                    for (d, cc, hh, dh, Sc, Sn, pp, rows0) in CH:
                        P.op("act", lambda e, dh=dh, pp=pp: e.activation(out=vn[dh][:], in_=pp[:, 256:384], func=AF.Copy), reads=[pp],
                             writes=[vn[dh]])
                        P.op("act", lambda e, dh=dh, pp=pp, d=d, cc=cc, hh=hh: e.activation(out=o1[dh][:], in_=pp[:, 128:256], func=AF.Identity,
                                                                                            scale=egc[:, d, cc, hh:hh + 1]),
                             reads=[pp, egc], writes=[o1[dh]])
                    for (d, cc, hh, dh, Sc, Sn, pp, rows0) in CH:
                        P.op("pe", lambda e, dh=dh, b=b, pp=pp: e.matmul(pp[:, 0:128], L_kd[dh][b][:], vn[dh][:], start=True, stop=True),
                             reads=[L_kd[dh][b], vn[dh]], writes=[pp])
                        P.op("pe", lambda e, dh=dh, b=b, pp=pp: e.matmul(pp[:, 384:512], L_qk[dh][b][:], vn[dh][:], start=True, stop=True),
                             reads=[L_qk[dh][b], vn[dh]], writes=[pp])
                    for (d, cc, hh, dh, Sc, Sn, pp, rows0) in CH:
                        P.op("dve", lambda e, dh=dh, pp=pp, Sc=Sc, Sn=Sn, d=d, cc=cc, hh=hh: e.scalar_tensor_tensor(
                            out=Sn[:], in0=Sc[:], scalar=egl[:, d, cc, hh:hh + 1], in1=pp[:, 0:128], op0=ALU.mult, op1=ALU.add),
                            reads=[pp, Sc, egl], writes=[Sn])
                    for (d, cc, hh, dh, Sc, Sn, pp, rows0) in CH:
                        o_ = ob[dh][i % 2]
                        P.op("dve", lambda e, dh=dh, pp=pp, o_=o_: e.tensor_tensor(out=o_[:], in0=o1[dh][:], in1=pp[:, 384:512], op=ALU.add),
                             reads=[pp, o1[dh]], writes=[o_])
                        P.dma("sp", lambda e, o_=o_, d=d, hh=hh, rows0=rows0: e.dma_start(
                            out=OD[d, rows0:rows0 + 128, hh * 128:(hh + 1) * 128], in_=o_[:]), reads=[o_], writes=[(tOD, (d, rows0, hh))])
            if kind == "p":
                scan_pass((0, 1))
            else:
                scan_pass((0,))
                gin, gout, tgin, tgout = self.pair_buf("GST", 512, 128)
                for hh in range(4):
                    Sf = S[hh][NTl % 2]
                    P.dma("sp", lambda e, hh=hh, Sf=Sf: e.dma_start(out=gin[hh * 128:(hh + 1) * 128, :], in_=Sf[:]), reads=[Sf],
                          writes=[(tgin, hh)])
                self.pair_gather(gin, gout, tgin, tgout)
                cand = P.sb("cand", [128, 2, 4, 128], F32)
                for rk in range(2):
                    P.dma("sp", lambda e, rk=rk: e.dma_start(out=cand[:, rk], in_=gout[rk * 512:(rk + 1) * 512, :].rearrange("(h p) c -> p h c", p=128)),
                          reads=[tgout], writes=[(cand, rk)])
                selw2 = P.sb("selw2", [128, 2], F32)
                P.dma("sp", lambda e: e.dma_start(out=selw2[:], in_=di["selw"]), writes=[selw2])
                for hh in range(4):
                    P.op("dve", lambda e, hh=hh: e.tensor_scalar(out=S[4 + hh][0][:], in0=cand[:, 0, hh, :], scalar1=selw2[:, 0:1], scalar2=None,
                                                                 op0=ALU.mult), reads=[cand, selw2], writes=[S[4 + hh][0]])
                    P.op("dve", lambda e, hh=hh: e.scalar_tensor_tensor(out=S[4 + hh][0][:], in0=cand[:, 1, hh, :], scalar=selw2[:, 1:2],
                                                                        in1=S[4 + hh][0][:], op0=ALU.mult, op1=ALU.add),
                         reads=[cand, selw2, S[4 + hh][0]], writes=[S[4 + hh][0]])
                scan_pass((1,))
            if kind == "p":
                si = row0 // c.TP
                for dh in range(8):
                    Sf = S[dh][NTl % 2]
                    P.dma("sp", lambda e, dh=dh, Sf=Sf, si=si: e.dma_start(out=self.dout["gdn_out"][si, dh // 4, dh % 4], in_=Sf[:]),
                          reads=[Sf], writes=[(self.dtr["gdn_out"], (si, dh))])
            P.end_phase()
            P.mid_es.close()

        P.begin_phase()
        PB = [P.ps("pb%d" % i, [128, 512], F32) for i in range(8)]
        gn = P.sb("gn", [128, 128], F32)
        P.dma("sp", lambda e: e.dma_start(out=gn[:], in_=di["ev_gdn_norm"].partition_broadcast(128)), writes=[gn])
        oa = [P.sb("oa%d" % i, [128, 512], F32) for i in range(2)]
        obb = [P.sb("obb%d" % i, [128, 512], F32) for i in range(2)]
        zz = [P.sb("zz%d" % i, [128, 512], F32) for i in range(2)]
        st8 = [P.sb("st8%d" % i, [128, 16], F32) for i in range(2)]
        junk = P.sb("junk", [128, 512], F32)
        otb = [P.sb("otb%d" % i, [128, 4, 128], BF16) for i in range(2)]
        for tl in range(NTILE):
            rows = slice(tl * 128, (tl + 1) * 128)
            a, b_, z_, s8, ot_ = oa[tl % 2], obb[tl % 2], zz[tl % 2], st8[tl % 2], otb[tl % 2]
            P.dma("sp", lambda e, a=a, rows=rows: e.dma_start(out=a[:], in_=OD[0, rows, :]), reads=[tOD], writes=[a])
            P.dma("sp", lambda e, b_=b_, rows=rows: e.dma_start(out=b_[:], in_=OD[1, rows, :]), reads=[tOD], writes=[b_])
            P.dma("sp", lambda e, z_=z_, rows=rows: e.dma_start(out=z_[:], in_=ZS[rows, :]), reads=[tZS], writes=[z_])
            P.op("dve", lambda e, a=a, b_=b_: e.tensor_tensor(out=a[:], in0=a[:], in1=b_[:], op=ALU.add), reads=[a, b_], writes=[a])
            P.op("pool", lambda e, s8=s8: e.memset(s8[:], 0.0), writes=[s8])
            for hh in range(4):
                P.op("act", lambda e, a=a, s8=s8, hh=hh: e.activation(out=junk[:, 0:128], in_=a[:, hh * 128:(hh + 1) * 128], func=AF.Square,
                                                                      accum_out=s8[:, hh:hh + 1]), reads=[a, s8], writes=[junk, s8])
            P.op("act", lambda e, s8=s8: e.activation(out=s8[:, 4:8], in_=s8[:, 0:4], func=AF.Sqrt, scale=1.0 / 128, bias=self.epsc[:, 0:1]),
                 reads=[s8, self.epsc], writes=[s8])
            P.op("dve", lambda e, s8=s8: e.reciprocal(out=s8[:, 8:12], in_=s8[:, 4:8]), reads=[s8], writes=[s8])
            for hh in range(4):
                P.op("dve", lambda e, a=a, s8=s8, hh=hh: e.scalar_tensor_tensor(
                    out=a[:, hh * 128:(hh + 1) * 128], in0=a[:, hh * 128:(hh + 1) * 128], scalar=s8[:, 8 + hh:9 + hh], in1=gn[:],
                    op0=ALU.mult, op1=ALU.mult), reads=[a, s8, gn], writes=[a])
            P.op("pool", lambda e, a=a, z_=z_: e.tensor_tensor(out=a[:], in0=a[:], in1=z_[:], op=ALU.mult), reads=[a, z_], writes=[a])
            pt = PB[tl % 2]
            for hh in range(4):
                P.op("pe", lambda e, a=a, hh=hh, pt=pt: e.transpose(pt[:, hh * 128:(hh + 1) * 128], a[:, hh * 128:(hh + 1) * 128], ident[:]),
                     reads=[a, ident], writes=[pt])
            P.op("act", lambda e, pt=pt, ot_=ot_: e.activation(out=ot_[:], in_=pt[:].rearrange("p (a b) -> p a b", a=4), func=AF.Copy),
                 reads=[pt], writes=[ot_])
            P.dma("sp", lambda e, ot_=ot_, tl=tl: e.dma_start(out=OT[0:512, tl * 128:(tl + 1) * 128].rearrange("(a p) t -> p a t", p=128),
                                                             in_=ot_[:]), reads=[ot_], writes=[(tOT, ("a", tl))])
        P.end_phase()

    def ev_e3(self):
        c = self.cfg
        P = self.P
        NT, NP = c.NT, c.NP
        ones = self.ones
        QT, CKVT, KRT, OT = self.QT, self.CKVT, self.KRT, self.OT
        tQT, tCKVT, tKRT, tOT = [self.dtr[n] for n in ("QT", "CKVT", "KRT", "OT")]
        wukv = self.din["ev_mla_w_ukv"]
        SCALE = 192.0 ** -0.5
        SMAX = 2 * c.TS + 256
        P.begin_phase()
        PB = [P.ps("pb%d" % i, [128, 512], F32) for i in range(8)]
        wsb = P.sb("wukv", [128, 1024], BF16)
        P.dma("pool", lambda e: e.dma_start(out=wsb[:], in_=wukv), writes=[wsb])
        sel = P.sb("sel65", [128, 65], F32)
        P.op("pool", lambda e: e.memset(sel[:], 0.0), writes=[sel])
        P.op("pool", lambda e: e.memset(sel[:, 64:65], 1.0), reads=[sel], writes=[sel])
        onesb = P.sb("onesb", [128, 128], BF16)
        P.op("pool", lambda e: e.memset(onesb[:], 1.0), writes=[onesb])
        CK = P.sb("CK", [128, SMAX], BF16)
        KR = P.sb("KR", [65, SMAX], BF16)
        KN = P.sb("KN", [128, SMAX], BF16)
        Vp = P.sb("Vp", [128, SMAX // 128, 128], BF16)
        sq1 = P.sb("sq1", [128, 512], F32)
        sq2 = P.sb("sq2", [64, 512], F32)
        kkm = P.sb("kkm", [65, 16], F32)
        row = P.sb("row", [65, 512], F32)
        QN = [P.sb("QN%d" % i, [128, 512], BF16) for i in range(2)]
        QR = [P.sb("QR%d" % i, [65, 512], BF16) for i in range(2)]
        PT = [P.sb("PT%d" % i, [128, 512], BF16) for i in range(3)]
        rec = P.sb("rec", [128, 512], F32)
        obf = [P.sb("obf%d" % i, [128, 512], BF16) for i in range(2)]
        qctr = [0]
        pctr = [0]
        for (row0, T, r, kind) in self.seqs():
            if kind == "p":
                k0 = self.kcol(row0)
                S = T
                P.dma("sp", lambda e, k0=k0, S=S: e.dma_start(out=CK[:, 0:S], in_=CKVT[:, k0:k0 + S]), reads=[tCKVT], writes=[CK])
                P.dma("sp", lambda e, k0=k0, S=S: e.dma_start(out=KR[:, 0:S], in_=KRT[:, k0:k0 + S]), reads=[tKRT], writes=[KR])
            else:
                S = 256 + 2 * T
                own0 = self.kcol(row0)
                kin, kout, tkin, tkout = self.pair_buf("KVX", 193, T, BF16)
                P.dma("sp", lambda e, own0=own0, T=T: e.dma_start(out=kin[0:128, :], in_=CKVT[:, own0:own0 + T]), reads=[tCKVT], writes=[(tkin, 0)])
                P.dma("sp", lambda e, own0=own0, T=T: e.dma_start(out=kin[128:193, :], in_=KRT[:, own0:own0 + T]), reads=[tKRT], writes=[(tkin, 1)])
                self.pair_gather(kin, kout, tkin, tkout)
                P.dma("sp", lambda e: e.dma_start(out=CK[:, 0:256], in_=CKVT[:, NP:NP + 256]), reads=[tCKVT], writes=[(CK, "c")])
                P.dma("sp", lambda e: e.dma_start(out=KR[:, 0:256], in_=KRT[:, NP:NP + 256]), reads=[tKRT], writes=[(KR, "c")])
                for rk in range(2):
                    P.dma("sp", lambda e, rk=rk, T=T: e.dma_start(out=CK[:, 256 + rk * T:256 + (rk + 1) * T], in_=kout[rk * 193:rk * 193 + 128, :]),
                          reads=[tkout], writes=[(CK, rk)])
                    P.dma("sp", lambda e, rk=rk, T=T: e.dma_start(out=KR[:, 256 + rk * T:256 + (rk + 1) * T], in_=kout[rk * 193 + 128:rk * 193 + 193, :]),
                          reads=[tkout], writes=[(KR, rk)])
            nkt = S // 128
            for hh in range(4):
                for b0 in range(0, S, 512):
                    bw = min(512, S - b0)
                    pk = PB[0]
                    P.op("pe", lambda e, hh=hh, b0=b0, bw=bw, pk=pk: e.matmul(pk[:, 0:bw], wsb[:, hh * 256:hh * 256 + 128], CK[:, b0:b0 + bw],
                                                                              start=True, stop=True), reads=[wsb, CK], writes=[pk])
                    P.op("act", lambda e, b0=b0, bw=bw, pk=pk: e.activation(out=KN[:, b0:b0 + bw], in_=pk[:, 0:bw], func=AF.Copy),
                         reads=[pk], writes=[(KN, b0)])
                    P.op("act", lambda e, bw=bw, pk=pk: e.activation(out=sq1[:, 0:bw], in_=pk[:, 0:bw], func=AF.Square), reads=[pk],
                         writes=[sq1])
                    P.op("act", lambda e, b0=b0, bw=bw: e.activation(out=sq2[:, 0:bw], in_=KR[0:64, b0:b0 + bw], func=AF.Square),
                         reads=[KR], writes=[sq2])
                    pr = PB[1]
                    P.op("pe", lambda e, bw=bw, pr=pr: e.matmul(pr[0:65, 0:bw], sel[:], sq1[:, 0:bw], start=True, stop=False),
                         reads=[sel, sq1], writes=[pr])
                    P.op("pe", lambda e, bw=bw, pr=pr: e.matmul(pr[0:65, 0:bw], sel[0:64, :], sq2[:, 0:bw], start=False, stop=True),
                         reads=[sel, sq2], writes=[pr])
                    P.op("dve", lambda e, b0=b0, bw=bw, pr=pr: e.reduce_max(out=kkm[64:65, b0 // 512:b0 // 512 + 1], in_=pr[64:65, 0:bw],
                                                                            axis=AX.X), reads=[pr], writes=[(kkm, b0)])
                    for k4 in range(0, bw // 128, 4):
                        pv = PB[2]
                        n4 = min(4, bw // 128 - k4)
                        for q in range(n4):
                            kt = b0 // 128 + k4 + q
                            P.op("pe", lambda e, hh=hh, kt=kt, q=q, pv=pv: e.matmul(
                                pv[:, q * 128:(q + 1) * 128], CK[:, kt * 128:(kt + 1) * 128], wsb[:, hh * 256 + 128:hh * 256 + 256],
                                start=True, stop=True), reads=[wsb, CK], writes=[pv])
                        kt0 = b0 // 128 + k4
                        P.op("dve", lambda e, kt0=kt0, n4=n4, pv=pv: e.tensor_copy(
                            out=Vp[:, kt0:kt0 + n4, :], in_=pv[:, 0:n4 * 128].rearrange("p (a b) -> p a b", a=n4)),
                            reads=[pv], writes=[(Vp, kt0)])
                nblk = (S + 511) // 512
                P.op("dve", lambda e, nblk=nblk: e.reduce_max(out=kkm[64:65, 15:16], in_=kkm[64:65, 0:nblk], axis=AX.X),
                     reads=[kkm], writes=[(kkm, "max")])
                for q0 in range(0, T, 512):
                    qw = min(512, T - q0)
                    qn_, qr_ = QN[qctr[0] % 2], QR[qctr[0] % 2]
                    pacc, psm = PB[4 + (qctr[0] % 2) * 2], PB[5 + (qctr[0] % 2) * 2]
                    qctr[0] += 1
                    rows = slice(row0 + q0, row0 + q0 + qw)
                    P.dma("sp", lambda e, hh=hh, rows=rows, qw=qw, qn_=qn_: e.dma_start(out=qn_[:, 0:qw], in_=QT[hh, 0:128, rows]),
                          reads=[tQT], writes=[qn_])
                    P.dma("sp", lambda e, hh=hh, rows=rows, qw=qw, qr_=qr_: e.dma_start(out=qr_[0:64, 0:qw], in_=QT[hh, 128:192, rows]),
                          reads=[tQT], writes=[(qr_, "a")])
                    P.op("act", lambda e, qw=qw, qn_=qn_: e.activation(out=sq1[:, 0:qw], in_=qn_[:, 0:qw], func=AF.Square), reads=[qn_],
                         writes=[sq1])
                    P.op("act", lambda e, qw=qw, qr_=qr_: e.activation(out=sq2[:, 0:qw], in_=qr_[0:64, 0:qw], func=AF.Square),
                         reads=[(qr_, "a")], writes=[sq2])
                    pr = PB[1]
                    P.op("pe", lambda e, qw=qw, pr=pr: e.matmul(pr[0:65, 0:qw], sel[:], sq1[:, 0:qw], start=True, stop=False),
                         reads=[sel, sq1], writes=[pr])
                    P.op("pe", lambda e, qw=qw, pr=pr: e.matmul(pr[0:65, 0:qw], sel[0:64, :], sq2[:, 0:qw], start=False, stop=True),
                         reads=[sel, sq2], writes=[pr])
                    P.op("dve", lambda e, qw=qw, pr=pr: e.tensor_scalar(out=row[64:65, 0:qw], in0=pr[64:65, 0:qw], scalar1=kkm[64:65, 15:16],
                                                                        scalar2=None, op0=ALU.mult), reads=[pr, (kkm, "max")], writes=[row])
                    P.op("act", lambda e, qw=qw: e.activation(out=row[64:65, 0:qw], in_=row[64:65, 0:qw], func=AF.Sqrt), reads=[row],
                         writes=[row])
                    P.op("dve", lambda e, qw=qw, qr_=qr_: e.tensor_scalar(out=qr_[64:65, 0:qw], in0=row[64:65, 0:qw], scalar1=-1.0,
                                                                          scalar2=None, op0=ALU.mult), reads=[row], writes=[(qr_, "b")])
                    ptmap = {}

                    def scores(kt, qw=qw, qn_=qn_, qr_=qr_, ptmap=ptmap):
                        ps = PB[2 + (pctr[0] % 2)]
                        pt_ = PT[pctr[0] % 3]
                        pctr[0] += 1
                        ptmap[kt] = pt_
                        ksl = slice(kt * 128, (kt + 1) * 128)
                        P.op("pe", lambda e: e.matmul(ps[:, 0:qw], KN[:, ksl], qn_[:, 0:qw], start=True, stop=False), reads=[KN, qn_], writes=[ps])
                        P.op("pe", lambda e: e.matmul(ps[:, 0:qw], KR[0:65, ksl], qr_[0:65, 0:qw], start=False, stop=True), reads=[KR, qr_],
                             writes=[ps])
                        P.op("act", lambda e: e.activation(out=pt_[:, 0:qw], in_=ps[:, 0:qw], func=AF.Exp, scale=SCALE), reads=[ps], writes=[pt_])

                    def accum(kt, qw=qw, pacc=pacc, psm=psm, nkt=nkt, ptmap=ptmap):
                        pt_ = ptmap[kt]
                        P.op("pe", lambda e: e.matmul(pacc[:, 0:qw], Vp[:, kt, :], pt_[:, 0:qw], start=(kt == 0), stop=(kt == nkt - 1)),
                             reads=[Vp, pt_], writes=[pacc])
                        P.op("pe", lambda e: e.matmul(psm[:, 0:qw], onesb[:], pt_[:, 0:qw], start=(kt == 0), stop=(kt == nkt - 1)),
                             reads=[onesb, pt_], writes=[psm])

                    scores(0)
                    for kt in range(nkt):
                        if kt + 1 < nkt:
                            scores(kt + 1)
                        accum(kt)
                    P.op("dve", lambda e, qw=qw, psm=psm: e.reciprocal(out=rec[:, 0:qw], in_=psm[:, 0:qw]), reads=[psm], writes=[rec])
                    ob_ = obf[qctr[0] % 2]
                    P.op("dve", lambda e, qw=qw, pacc=pacc, ob_=ob_: e.tensor_tensor(out=ob_[:, 0:qw], in0=pacc[:, 0:qw], in1=rec[:, 0:qw],
                                                                                     op=ALU.mult), reads=[pacc, rec], writes=[ob_])
                    P.dma("sp", lambda e, hh=hh, rows=rows, qw=qw, ob_=ob_: e.dma_start(out=OT[512 + hh * 128:512 + (hh + 1) * 128, rows],
                                                                                       in_=ob_[:, 0:qw]), reads=[ob_],
                          writes=[(tOT, ("b", hh, row0 + q0))])
        P.end_phase()

    def mix_out(self, l, src, dst, wname):
        c = self.cfg
        P = self.P
        s = 1
        srcap, dstap = self.dr(src), self.dr(dst)
        srct, dstt = self.dtr[src], self.dtr[dst]
        OT, tOT = self.OT, self.dtr["OT"]
        wo_d = self.din[wname]
        P.begin_phase()
        PB = [P.ps("pb%d" % i, [128, 512], F32) for i in range(8)]
        cbc = self.make_cbc(l, s, PB)
        wo = P.sb("wo", [128, 8, 1024], BF16)
        P.dma("pool", lambda e: e.dma_start(out=wo[:], in_=wo_d.rearrange("(k p) n -> p k n", p=128)), writes=[wo])
        xg = [P.sb("xg%d" % i, [128, 4, 1024], F32) for i in range(2)]
        og = [P.sb("og%d" % i, [128, 8, 512], BF16) for i in range(2)]
        junk = P.sb("junk", [128, 1024], F32)
        tmp = [P.sb("tmp%d" % i, [128, 1024], F32) for i in range(2)]
        ss = [P.sb("ss%d" % i, [128, 24], F32) for i in range(2)]
        gs = self.groups()
        for gi, (r0, r) in enumerate(gs):
            x, o_, sst = xg[gi % 2], og[gi % 2], ss[gi % 2]
            P.dma("sp", lambda e, x=x, r0=r0: e.dma_start(out=x[:], in_=srcap[r0:r0 + 512, :].rearrange("(t p) d -> p t d", p=128)),
                  reads=[srct], writes=[x])
            for hf in range(2):
                P.dma("sp", lambda e, o_=o_, r0=r0, hf=hf: e.dma_start(
                    out=o_[:, hf * 4:(hf + 1) * 4, :], in_=OT[hf * 512:(hf + 1) * 512, r0:r0 + 512].rearrange("(k p) t -> p k t", p=128)),
                    reads=[tOT], writes=[(o_, hf)])
            P.op("pool", lambda e, sst=sst: e.memset(sst[:], 0.0), writes=[sst])
            for t in range(4):
                py = [PB[4 + (t % 2) * 2], PB[5 + (t % 2) * 2]]
                for nb in range(2):
                    for k in range(8):
                        P.op("pe", lambda e, nb=nb, k=k, t=t, py=py, o_=o_: e.matmul(
                            py[nb][:], o_[:, k, t * 128:(t + 1) * 128], wo[:, k, nb * 512:(nb + 1) * 512], start=(k == 0), stop=(k == 7)),
                            reads=[o_, wo], writes=[py[nb]])
                self.epi(py, x[:, t, :], [(x, None)], r, sst, 4 * t, cbc, tmp[t % 2], junk, dstap, dstt, r0 + t * 128)
        P.end_phase()

    def odd_mixer(self, src, dst):
        c = self.cfg
        NT = c.NT
        if not hasattr(self, "OT"):
            self.OT = self.scratch("OT", (1024, NT), BF16)
        self.QK2 = self.scratch("QK2", (NT, 1024))
        self.V2 = self.scratch("V2", (NT, 1024), BF16)
        self.RS = self.scratch("RS", (NT, 1024))
        self.G2 = self.scratch("G2", (2, NT, 512))
        self.ODG = self.scratch("ODG", (2, NT, 1024))
        self.od_o1(src)
        self.od_o2()
        self.mix_out(1, src, dst, "od_w_out")

    def od_o1(self, src):
        c = self.cfg
        P = self.P
        di = self.din
        l, s = 1, 1
        NT, NP = c.NT, c.NP
        srcap, srct = self.dr(src), self.dtr[src]
        win = di["od_w_in"]
        QK2, V2, RS, G2 = self.QK2, self.V2, self.RS, self.G2
        tQK2, tV2, tRS, tG2 = [self.dtr[n] for n in ("QK2", "V2", "RS", "G2")]
        P.begin_phase()
        PB = [P.ps("pb%d" % i, [128, 512], F32) for i in range(8)]
        wsb = P.sb("wsb", [128, 8, 3104], BF16)
        for q in range(4):
            P.dma("pool", lambda e, q=q: e.dma_start(
                out=wsb[:, q * 2:(q + 1) * 2, :], in_=win[q * 256:(q + 1) * 256, :].rearrange("(kc p) n -> p kc n", p=128)),
                writes=[(wsb, q)])
        wg = [P.sb("wg%d" % d, [17, 512], F32) for d in range(2)]
        gdT = [P.sb("gdT%d" % d, [17, 512], F32) for d in range(2)]
        for d in range(2):
            P.dma("sp", lambda e, d=d: e.dma_start(out=wg[d][0:16, :], in_=di["od_gla_w_gup"][d]), writes=[(wg[d], 0)])
            P.dma("sp", lambda e, d=d: e.dma_start(out=wg[d][16:17, :], in_=di["od_gla_b_g"][d:d + 1, :]), writes=[(wg[d], 1)])
            P.op("pool", lambda e, d=d: e.memset(gdT[d][:], 1.0), writes=[gdT[d]])
        xg = [P.sb("xg%d" % i, [128, 4, 1024], F32) for i in range(2)]
        hT = [P.sb("hT%d" % i, [128, 8, 512], BF16) for i in range(2)]
        xn = [P.sb("xn%d" % i, [128, 1024], F32) for i in range(2)]
        junk = P.sb("junk", [128, 1024], F32)
        ss = [P.sb("ss%d" % i, [128, 24], F32) for i in range(2)]
        sqk = [P.sb("sqk%d" % i, [128, 1024], F32) for i in range(2)]
        sv = [P.sb("sv%d" % i, [128, 1024], BF16) for i in range(2)]
        sr = [P.sb("sr%d" % i, [128, 1024], F32) for i in range(2)]
        sg = [P.sb("sg%d" % i, [128, 512], F32) for i in range(2)]
        gs = self.groups()

        def load(gi):
            r0, r = gs[gi]
            x = xg[gi % 2]
            P.dma("sp", lambda e: e.dma_start(out=x[:], in_=srcap[r0:r0 + 512, :].rearrange("(t p) d -> p t d", p=128)),
                  reads=[srct], writes=[x])

        def body(gi):
            r0, r = gs[gi]
            h = hT[gi % 2]
            for d in range(2):
                pg = PB[2 + d]
                for k in range(8):
                    P.op("pe", lambda e, k=k, d=d, pg=pg: e.matmul(pg[0:16, :], wsb[:, k, 3072 + 16 * d:3072 + 16 * (d + 1)], h[:, k, :],
                                                                     start=(k == 0), stop=(k == 7)), reads=[wsb, h], writes=[pg])
                P.op("dve", lambda e, d=d, pg=pg: e.tensor_copy(out=gdT[d][0:16, :], in_=pg[0:16, :]), reads=[pg], writes=[gdT[d]])
            for t in range(4):
                row = r0 + t * 128
                tsl = slice(t * 128, (t + 1) * 128)
                banks = {"qk": (PB[0], PB[1], 0), "v": (PB[2], PB[3], 1024), "r": (PB[6], PB[7], 2048)}
                for name in ("qk", "v", "r"):
                    pa, pb_, c0 = banks[name]
                    for hf, pp in enumerate((pa, pb_)):
                        for k in range(8):
                            P.op("pe", lambda e, k=k, pp=pp, c0=c0, hf=hf, tsl=tsl: e.matmul(
                                pp[:], h[:, k, tsl], wsb[:, k, c0 + hf * 512:c0 + (hf + 1) * 512], start=(k == 0), stop=(k == 7)),
                                reads=[wsb, h], writes=[pp])
                    if name == "qk":
                        st = sqk[t % 2]
                        P.op("act", lambda e, st=st, pa=pa: e.activation(out=st[:, 0:512], in_=pa[:], func=AF.Copy), reads=[pa], writes=[(st, 0)])
                        P.op("dve", lambda e, st=st, pb_=pb_: e.tensor_copy(out=st[:, 512:1024], in_=pb_[:]), reads=[pb_], writes=[(st, 1)])
                        P.dma("sp", lambda e, st=st, row=row: e.dma_start(out=QK2[row:row + 128, :], in_=st[:]), reads=[st], writes=[(tQK2, row)])
                    elif name == "v":
                        st = sv[t % 2]
                        P.op("act", lambda e, st=st, pa=pa: e.activation(out=st[:, 0:512], in_=pa[:], func=AF.Copy), reads=[pa], writes=[(st, 0)])
                        P.op("dve", lambda e, st=st, pb_=pb_: e.tensor_copy(out=st[:, 512:1024], in_=pb_[:]), reads=[pb_], writes=[(st, 1)])
                        P.dma("sp", lambda e, st=st, row=row: e.dma_start(out=V2[row:row + 128, :], in_=st[:]), reads=[st], writes=[(tV2, row)])
                    else:
                        st = sr[t % 2]
                        P.op("act", lambda e, st=st, pa=pa: e.activation(out=st[:, 0:512], in_=pa[:], func=AF.Silu), reads=[pa], writes=[(st, 0)])
                        P.op("act", lambda e, st=st, pb_=pb_: e.activation(out=st[:, 512:1024], in_=pb_[:], func=AF.Silu), reads=[pb_], writes=[(st, 1)])
                        P.dma("sp", lambda e, st=st, row=row: e.dma_start(out=RS[row:row + 128, :], in_=st[:]), reads=[st], writes=[(tRS, row)])
                for d in range(2):
                    pg = PB[d]
                    g_ = sg[d]
                    P.op("pe", lambda e, d=d, pg=pg, tsl=tsl: e.matmul(pg[:], gdT[d][0:17, tsl], wg[d][0:17, :], start=True, stop=True),
                         reads=[gdT[d], wg[d]], writes=[pg])
                    P.op("act", lambda e, pg=pg, g_=g_: e.activation(out=g_[:], in_=pg[:], func=AF.Exp, scale=-1.0), reads=[pg], writes=[g_])
                    P.op("act", lambda e, g_=g_: e.activation(out=g_[:], in_=g_[:], func=AF.Ln, bias=self.ones[:, 0:1]), reads=[g_, self.ones],
                         writes=[g_])
                    P.op("dve", lambda e, g_=g_: e.tensor_scalar(out=g_[:], in0=g_[:], scalar1=-1.0 / 16.0, scalar2=None, op0=ALU.mult),
                         reads=[g_], writes=[g_])
                    P.dma("sp", lambda e, g_=g_, d=d, row=row: e.dma_start(out=G2[d, row:row + 128, :], in_=g_[:]), reads=[g_],
                          writes=[(tG2, (d, row))])

        ng = len(gs)
        load(0)
        for gi in range(ng):
            r0, r = gs[gi]
            self.pro_hT(xg[gi % 2], ss[gi % 2], hT[gi % 2], l, s, r, [PB[4], PB[5]], xn, junk)
            if gi + 1 < ng:
                load(gi + 1)
            body(gi)
        P.end_phase()

    def od_o2(self):
        c = self.cfg
        P = self.P
        di = self.din
        NT, NP = c.NT, c.NP
        NTILE = NT // 128
        ident, ones = self.ident, self.ones
        QK2, V2, RS, G2, ODG, OT = self.QK2, self.V2, self.RS, self.G2, self.ODG, self.OT
        tQK2, tV2, tRS, tG2, tODG, tOT = [self.dtr[n] for n in ("QK2", "V2", "RS", "G2", "ODG", "OT")]
        P.begin_phase()
        PB = [P.ps("pb%d" % i, [128, 512], F32) for i in range(8)]

        def mask(name, op, flip):
            m = P.sb(name, [128, 128], F32)
            P.op("pool", lambda e: e.memset(m[:], 1.0), writes=[m])
            P.op("pool", lambda e: e.affine_select(out=m[:], in_=m[:], pattern=[[1 if flip else -1, 128]], compare_op=op,
                                                   fill=0.0, base=0, channel_multiplier=(-1 if flip else 1)), reads=[m], writes=[m])
            return m
        Uincl = mask("Uincl", ALU.is_ge, True)
        Lincl = mask("Lincl", ALU.is_ge, False)
        CUM = [Uincl, Lincl]
        AM = [Uincl, Lincl]
        qk_t = [P.sb("qk_t%d" % i, [128, 1024], F32) for i in range(2)]
        v_t = [P.sb("v_t%d" % i, [128, 1024], BF16) for i in range(2)]
        g_t = [P.sb("g_t%d" % i, [128, 512], F32) for i in range(2)]
        bsb = P.sb("bsb", [128, 512], F32)
        eb = P.sb("eb", [128, 512], F32)
        enb = P.sb("enb", [128, 512], F32)
        ekd = P.sb("ekd", [128, 512], F32)
        qe = P.sb("qe", [128, 512], F32)
        ke = P.sb("ke", [128, 512], F32)
        kd2 = [P.sb("kd%d" % i, [128, 512], BF16) for i in range(2)]
        qeT2 = [P.sb("qeT%d" % i, [128, 4, 128], BF16) for i in range(2)]
        keT2 = [P.sb("keT%d" % i, [128, 4, 128], BF16) for i in range(2)]
        attT = [P.sb("attT%d" % i, [128, 128], BF16) for i in range(4)]
        ebl2 = [P.sb("ebl%d" % i, [128, 8], F32) for i in range(2)]
        osb = [P.sb("osb%d" % i, [128, 1024], F32) for i in range(2)]
        S = [[[P.sb("S%d_%d_%d" % (d, hh, i), [128, 256], F32) for i in range(2)] for hh in range(4)] for d in range(2)]
        Sb = [[P.sb("Sb%d_%d" % (d, hh), [128, 256], BF16) for hh in range(4)] for d in range(2)]
        cnt = [0]
        for (row0, T, r, kind) in self.seqs():
            NTl = T // 128
            for d in range(2):
                for hh in range(4):
                    if kind == "p":
                        P.op("pool", lambda e, d=d, hh=hh: e.memset(S[d][hh][0][:], 0.0), writes=[S[d][hh][0]])
                    else:
                        P.dma("sp", lambda e, d=d, hh=hh: e.dma_start(out=S[d][hh][0][:], in_=di["state_gla"][d, hh]), writes=[S[d][hh][0]])
                    P.op("pool", lambda e, d=d, hh=hh: e.tensor_copy(out=Sb[d][hh][:], in_=S[d][hh][0][:]), reads=[S[d][hh][0]],
                         writes=[Sb[d][hh]])
            def gla_pass(dirs):
                its = [(i, d) for i in range(NTl) for d in dirs]
                ctx = {}

                def front(n):
                    i, d = its[n]
                    cc = i if d == 0 else NTl - 1 - i
                    rows = slice(row0 + cc * 128, row0 + (cc + 1) * 128)
                    k_ = cnt[0] % 2
                    cnt[0] += 1
                    qk_, vv_, gg_ = qk_t[k_], v_t[k_], g_t[k_]
                    qeT_, keT_, kd_, ebl_ = qeT2[k_], keT2[k_], kd2[k_], ebl2[k_]
                    ctx[n] = (i, d, cc, rows, k_, vv_, qeT_, keT_, kd_, ebl_)
                    P.dma("sp", lambda e: e.dma_start(out=qk_[:], in_=QK2[rows, :]), reads=[tQK2], writes=[qk_])
                    P.dma("sp", lambda e: e.dma_start(out=vv_[:], in_=V2[rows, :]), reads=[tV2], writes=[vv_])
                    P.dma("sp", lambda e: e.dma_start(out=gg_[:], in_=G2[d, rows, :]), reads=[tG2], writes=[gg_])
                    pb_c, pb_t = PB[0], PB[1]
                    P.op("pe", lambda e: e.matmul(pb_c[:], CUM[d][:], gg_[:], start=True, stop=True), reads=[CUM[d], gg_], writes=[pb_c])
                    P.op("pe", lambda e: e.matmul(pb_t[:], ones[:], gg_[:], start=True, stop=True), reads=[ones, gg_], writes=[pb_t])
                    for hh in range(4):
                        P.op("pe", lambda e, hh=hh: e.matmul(PB[2][:, hh:hh + 1], gg_[:, hh * 128:(hh + 1) * 128], ones[:, 0:1],
                                                             start=True, stop=True), reads=[gg_, ones], writes=[PB[2]])
                    P.op("act", lambda e: e.activation(out=ebl_[:, 0:4], in_=PB[2][:, 0:4], func=AF.Exp), reads=[PB[2]], writes=[ebl_])
                    P.op("dve", lambda e: e.tensor_copy(out=bsb[:], in_=pb_c[:]), reads=[pb_c], writes=[bsb])
                    P.op("act", lambda e: e.activation(out=eb[:], in_=bsb[:], func=AF.Exp), reads=[bsb], writes=[eb])
                    P.op("act", lambda e: e.activation(out=enb[:], in_=bsb[:], func=AF.Exp, scale=-1.0), reads=[bsb], writes=[enb])
                    P.op("dve", lambda e: e.tensor_tensor(out=ekd[:], in0=pb_t[:], in1=bsb[:], op=ALU.subtract), reads=[pb_t, bsb], writes=[ekd])
                    P.op("act", lambda e: e.activation(out=ekd[:], in_=ekd[:], func=AF.Exp), reads=[ekd], writes=[ekd])
                    P.op("dve", lambda e: e.scalar_tensor_tensor(out=qe[:], in0=qk_[:, 0:512], scalar=128.0 ** -0.5, in1=eb[:],
                                                                 op0=ALU.mult, op1=ALU.mult), reads=[qk_, eb], writes=[qe])
                    P.op("dve", lambda e: e.tensor_tensor(out=ke[:], in0=qk_[:, 512:1024], in1=enb[:], op=ALU.mult), reads=[qk_, enb], writes=[ke])
                    P.op("pool", lambda e: e.tensor_tensor(out=kd_[:], in0=qk_[:, 512:1024], in1=ekd[:], op=ALU.mult), reads=[qk_, ekd],
                         writes=[kd_])
                    for hh in range(4):
                        hs = slice(hh * 128, (hh + 1) * 128)
                        P.op("pe", lambda e, hs=hs: e.transpose(PB[3][:, hs], qe[:, hs], ident[:]), reads=[qe, ident], writes=[PB[3]])
                        P.op("pe", lambda e, hs=hs: e.transpose(PB[4][:, hs], ke[:, hs], ident[:]), reads=[ke, ident], writes=[PB[4]])
                    P.op("act", lambda e: e.activation(out=qeT_[:], in_=PB[3][:].rearrange("p (a b) -> p a b", a=4), func=AF.Copy), reads=[PB[3]],
                         writes=[qeT_])
                    P.op("dve", lambda e: e.tensor_copy(out=keT_[:], in_=PB[4][:].rearrange("p (a b) -> p a b", a=4)), reads=[PB[4]], writes=[keT_])

                def back(n):
                    (i, d, cc, rows, k_, vv_, qeT_, keT_, kd_, ebl_) = ctx.pop(n)
                    o_ = osb[k_]
                    for hh in range(4):
                        P.op("pe", lambda e, hh=hh: e.matmul(PB[5][:, hh * 128:(hh + 1) * 128], keT_[:, hh, :], qeT_[:, hh, :], start=True, stop=True),
                             reads=[keT_, qeT_], writes=[PB[5]])
                    for hh in range(4):
                        P.op("dve", lambda e, hh=hh: e.tensor_tensor(out=attT[hh][:], in0=PB[5][:, hh * 128:(hh + 1) * 128], in1=AM[d][:],
                                                                     op=ALU.mult), reads=[PB[5], AM[d]], writes=[attT[hh]])
                    for hh in range(4):
                        hs = slice(hh * 128, (hh + 1) * 128)
                        po = PB[6 + (hh // 2)]
                        osl = slice((hh % 2) * 256, (hh % 2 + 1) * 256)
                        pu = PB[1] if hh < 2 else PB[0]
                        usl = slice((hh % 2) * 256, (hh % 2 + 1) * 256)
                        P.op("pe", lambda e, hh=hh, hs=hs, pu=pu, usl=usl: e.matmul(pu[:, usl], kd_[:, hs], vv_[:, hh * 256:(hh + 1) * 256],
                                                                                     start=True, stop=True), reads=[kd_, vv_], writes=[pu])
                        P.op("pe", lambda e, hh=hh, po=po, osl=osl: e.matmul(po[:, osl], qeT_[:, hh, :], Sb[d][hh][:], start=True, stop=False),
                             reads=[qeT_, Sb[d][hh]], writes=[po])
                        P.op("pe", lambda e, hh=hh, po=po, osl=osl: e.matmul(po[:, osl], attT[hh][:], vv_[:, hh * 256:(hh + 1) * 256],
                                                                             start=False, stop=True), reads=[attT[hh], vv_], writes=[po])
                    for hh in range(4):
                        Sc, Sn = S[d][hh][i % 2], S[d][hh][(i + 1) % 2]
                        pu = PB[1] if hh < 2 else PB[0]
                        usl = slice((hh % 2) * 256, (hh % 2 + 1) * 256)
                        P.op("dve", lambda e, hh=hh, pu=pu, usl=usl, Sc=Sc, Sn=Sn: e.scalar_tensor_tensor(
                            out=Sn[:], in0=Sc[:], scalar=ebl_[:, hh:hh + 1], in1=pu[:, usl], op0=ALU.mult, op1=ALU.add),
                            reads=[pu, Sc, ebl_], writes=[Sn])
                        P.op("pool", lambda e, hh=hh, Sn=Sn: e.tensor_copy(out=Sb[d][hh][:], in_=Sn[:]), reads=[Sn], writes=[Sb[d][hh]])
                    P.op("act", lambda e: e.activation(out=o_[:, 0:512], in_=PB[6][:], func=AF.Copy), reads=[PB[6]], writes=[(o_, 0)])
                    P.op("act", lambda e: e.activation(out=o_[:, 512:1024], in_=PB[7][:], func=AF.Copy), reads=[PB[7]], writes=[(o_, 1)])
                    P.dma("sp", lambda e: e.dma_start(out=ODG[d, rows, :], in_=o_[:]), reads=[o_], writes=[(tODG, (d, rows.start))])

                front(0)
                for n in range(len(its)):
                    if n + 1 < len(its):
                        front(n + 1)
                    back(n)
            if kind == "p":
                gla_pass((0, 1))
            else:
                gla_pass((0,))
                gin, gout, tgin, tgout = self.pair_buf("LST", 512, 256)
                for hh in range(4):
                    Sf = S[0][hh][NTl % 2]
                    P.dma("sp", lambda e, hh=hh, Sf=Sf: e.dma_start(out=gin[hh * 128:(hh + 1) * 128, :], in_=Sf[:]), reads=[Sf],
                          writes=[(tgin, hh)])
                self.pair_gather(gin, gout, tgin, tgout)
                cand = P.sb("cand", [128, 2, 4, 256], F32)
                for rk in range(2):
                    P.dma("sp", lambda e, rk=rk: e.dma_start(out=cand[:, rk], in_=gout[rk * 512:(rk + 1) * 512, :].rearrange("(h p) c -> p h c", p=128)),
                          reads=[tgout], writes=[(cand, rk)])
                selw = P.sb("selw", [128, 2], F32)
                P.dma("sp", lambda e: e.dma_start(out=selw[:], in_=di["selw"]), writes=[selw])
                for hh in range(4):
                    P.op("dve", lambda e, hh=hh: e.tensor_scalar(out=S[1][hh][0][:], in0=cand[:, 0, hh, :], scalar1=selw[:, 0:1], scalar2=None,
                                                                 op0=ALU.mult), reads=[cand, selw], writes=[S[1][hh][0]])
                    P.op("dve", lambda e, hh=hh: e.scalar_tensor_tensor(out=S[1][hh][0][:], in0=cand[:, 1, hh, :], scalar=selw[:, 1:2],
                                                                        in1=S[1][hh][0][:], op0=ALU.mult, op1=ALU.add),
                         reads=[cand, selw, S[1][hh][0]], writes=[S[1][hh][0]])
                    P.op("pool", lambda e, hh=hh: e.tensor_copy(out=Sb[1][hh][:], in_=S[1][hh][0][:]), reads=[S[1][hh][0]], writes=[Sb[1][hh]])
                gla_pass((1,))
            if kind == "p":
                si = row0 // c.TP
                for d in range(2):
                    for hh in range(4):
                        Sf = S[d][hh][NTl % 2]
                        P.dma("sp", lambda e, d=d, hh=hh, Sf=Sf, si=si: e.dma_start(out=self.dout["gla_out"][si, d, hh], in_=Sf[:]), reads=[Sf],
                              writes=[(self.dtr["gla_out"], (si, d, hh))])
        P.end_phase()
        P.begin_phase()
        PB = [P.ps("pb%d" % i, [128, 512], F32) for i in range(8)]
        gn = P.sb("gn", [128, 256], F32)
        P.dma("sp", lambda e: e.dma_start(out=gn[:], in_=di["od_gla_norm"].partition_broadcast(128)), writes=[gn])
        oa = [P.sb("oa%d" % i, [128, 1024], F32) for i in range(2)]
        obb = [P.sb("obb%d" % i, [128, 1024], F32) for i in range(2)]
        zz = [P.sb("zz%d" % i, [128, 1024], F32) for i in range(2)]
        st8 = [P.sb("st8%d" % i, [128, 16], F32) for i in range(2)]
        junk = P.sb("junk", [128, 512], F32)
        otb = [P.sb("otb%d" % i, [128, 8, 128], BF16) for i in range(2)]
        for tl in range(NTILE):
            rows = slice(tl * 128, (tl + 1) * 128)
            a, b_, z_, s8, ot_ = oa[tl % 2], obb[tl % 2], zz[tl % 2], st8[tl % 2], otb[tl % 2]
            P.dma("sp", lambda e, a=a, rows=rows: e.dma_start(out=a[:], in_=ODG[0, rows, :]), reads=[tODG], writes=[a])
            P.dma("sp", lambda e, b_=b_, rows=rows: e.dma_start(out=b_[:], in_=ODG[1, rows, :]), reads=[tODG], writes=[b_])
            P.dma("sp", lambda e, z_=z_, rows=rows: e.dma_start(out=z_[:], in_=RS[rows, :]), reads=[tRS], writes=[z_])
            P.op("dve", lambda e, a=a, b_=b_: e.tensor_tensor(out=a[:], in0=a[:], in1=b_[:], op=ALU.add), reads=[a, b_], writes=[a])
            P.op("pool", lambda e, s8=s8: e.memset(s8[:], 0.0), writes=[s8])
            for hh in range(4):
                P.op("act", lambda e, a=a, s8=s8, hh=hh: e.activation(out=junk[:, 0:256], in_=a[:, hh * 256:(hh + 1) * 256], func=AF.Square,
                                                                      accum_out=s8[:, hh:hh + 1]), reads=[a, s8], writes=[junk, s8])
            P.op("act", lambda e, s8=s8: e.activation(out=s8[:, 4:8], in_=s8[:, 0:4], func=AF.Sqrt, scale=1.0 / 256, bias=self.epsc[:, 0:1]),
                 reads=[s8, self.epsc], writes=[s8])
            P.op("dve", lambda e, s8=s8: e.reciprocal(out=s8[:, 8:12], in_=s8[:, 4:8]), reads=[s8], writes=[s8])
            for hh in range(4):
                P.op("dve", lambda e, a=a, s8=s8, hh=hh: e.scalar_tensor_tensor(
                    out=a[:, hh * 256:(hh + 1) * 256], in0=a[:, hh * 256:(hh + 1) * 256], scalar=s8[:, 8 + hh:9 + hh], in1=gn[:],
                    op0=ALU.mult, op1=ALU.mult), reads=[a, s8, gn], writes=[a])
            P.op("pool", lambda e, a=a, z_=z_: e.tensor_tensor(out=a[:], in0=a[:], in1=z_[:], op=ALU.mult), reads=[a, z_], writes=[a])
            for hf in range(2):
                pt = PB[(tl % 2) * 2 + hf]
                for q in range(4):
                    cc = hf * 4 + q
                    P.op("pe", lambda e, a=a, q=q, cc=cc, pt=pt: e.transpose(pt[:, q * 128:(q + 1) * 128], a[:, cc * 128:(cc + 1) * 128], ident[:]),
                         reads=[a, ident], writes=[pt])
                if hf == 0:
                    P.op("act", lambda e, pt=pt, ot_=ot_: e.activation(out=ot_[:, 0:4, :], in_=pt[:].rearrange("p (a b) -> p a b", a=4),
                                                                       func=AF.Copy), reads=[pt], writes=[(ot_, 0)])
                else:
                    P.op("dve", lambda e, pt=pt, ot_=ot_: e.tensor_copy(out=ot_[:, 4:8, :], in_=pt[:].rearrange("p (a b) -> p a b", a=4)),
                         reads=[pt], writes=[(ot_, 1)])
            P.dma("sp", lambda e, ot_=ot_, tl=tl: e.dma_start(out=OT[:, tl * 128:(tl + 1) * 128].rearrange("(a p) t -> p a t", p=128),
                                                             in_=ot_[:]), reads=[ot_], writes=[(tOT, ("a", tl))])
        P.end_phase()


def rope_tables(pos):
    nf = 16
    inv = (10000.0 ** (-np.arange(nf, dtype=np.float32) / nf)).astype(np.float32)
    TS = len(pos)
    row = (pos // 64).astype(np.float32)
    col = (pos % 64).astype(np.float32)
    cos = np.zeros((64, TS), np.float32)
    sin = np.zeros((64, TS), np.float32)
    for a in range(2):
        p_ = row if a == 0 else col
        ang = p_[None, :] * inv[:, None]
        for half in range(2):
            d0 = a * 32 + half * 16
            cos[d0:d0 + 16] = np.cos(ang)
            sin[d0:d0 + 16] = np.sin(ang) * (-1.0 if half == 0 else 1.0)
    return cos, sin


def rope_perm():
    p = np.zeros(64, np.int64)
    for a in range(2):
        for half in range(2):
            for f in range(16):
                p[a * 32 + half * 16 + f] = a * 32 + (1 - half) * 16 + f
    return p


def shared_weights(inputs, mirror):
    f32 = lambda a: np.ascontiguousarray(np.asarray(a, dtype=np.float32))
    perm = rope_perm()
    ev_w_in = np.asarray(inputs["ev_w_in"][0])
    conv_w = np.asarray(inputs["ev_conv_w"][0])
    a_log = np.asarray(inputs["ev_gdn_a_log"][0])
    dtb = np.asarray(inputs["ev_gdn_dt_bias"][0])
    od_w_in = np.asarray(inputs["od_w_in"][0])
    w_gup = np.asarray(inputs["od_gla_w_gup"][0])
    b_g = np.asarray(inputs["od_gla_b_g"][0])
    if mirror:
        idx = np.arange(2512)
        idx[2048:2052], idx[2052:2056] = np.arange(2052, 2056), np.arange(2048, 2052)
        idx[2056:2060], idx[2060:2064] = np.arange(2060, 2064), np.arange(2056, 2060)
        ev_w_in = ev_w_in[:, idx]
        conv_w = conv_w[::-1]
        a_log = a_log[::-1]
        dtb = dtb[::-1]
        idx2 = np.arange(3104)
        idx2[3072:3088], idx2[3088:3104] = np.arange(3088, 3104), np.arange(3072, 3088)
        od_w_in = od_w_in[:, idx2]
        w_gup = w_gup[::-1]
        b_g = b_g[::-1]
    w_uq = np.asarray(inputs["ev_mla_w_uq"][0])
    return {
        "mod_w": f32(inputs["mod_w"]), "mod_b": f32(inputs["mod_b"]),
        "norm_pre": f32(inputs["norm_pre"]), "norm_post": f32(inputs["norm_post"]),
        "ffn_w_in": f32(inputs["ffn_w_in"]), "ffn_w_out": f32(inputs["ffn_w_out"]),
        "ev_w_in": f32(ev_w_in), "ev_w_in_perm": f32(ev_w_in[:, 2448:2512][:, perm]),
        "ev_conv_w": f32(conv_w), "ev_gdn_a_log": f32(a_log.reshape(8)), "ev_gdn_dt_bias": f32(dtb.reshape(8)),
        "ev_gdn_norm": f32(inputs["ev_gdn_norm"][0]), "ev_mla_q_norm": f32(inputs["ev_mla_q_norm"][0]),
        "ev_mla_w_uq": f32(w_uq),
        "ev_mla_w_uq_perm": f32(np.concatenate([w_uq[:, h * 192 + 128:h * 192 + 192][:, perm] for h in range(4)], axis=1)),
        "ev_mla_kv_norm": f32(inputs["ev_mla_kv_norm"][0]), "ev_mla_w_ukv": f32(inputs["ev_mla_w_ukv"][0]),
        "ev_w_out": f32(inputs["ev_w_out"][0]),
        "od_w_in": f32(od_w_in), "od_gla_w_gup": f32(w_gup), "od_gla_b_g": f32(b_g),
        "od_gla_norm": f32(inputs["od_gla_norm"][0]), "od_w_out": f32(inputs["od_w_out"][0]),
    }


def make_in_maps(cfg, inputs, cores):
    c = cfg
    f32 = lambda a: np.ascontiguousarray(np.asarray(a, dtype=np.float32))
    sh = [shared_weights(inputs, False), shared_weights(inputs, True)]
    maps = []
    for core in cores:
        sb, half = core // 2, core % 2
        mirror = (half == 1)
        xp = np.asarray(inputs["x_prompt"])[core * c.NPS:(core + 1) * c.NPS, :c.TP]
        xs = np.asarray(inputs["x_sample"])[sb, half * c.TS:(half + 1) * c.TS]
        pos = np.arange(half * c.TS, (half + 1) * c.TS)
        if mirror:
            xp = xp[:, ::-1]
            xs = xs[::-1]
            pos = pos[::-1]
        cos, sin = rope_tables(pos)
        m = dict(sh[half])
        m["rope_cos"], m["rope_sin"] = cos, sin
        m["xin"] = f32(np.concatenate([xp.reshape(c.NP, D), xs], axis=0))
        m["cond"] = f32(np.stack([np.asarray(inputs["c_ctx"]), np.asarray(inputs["c"])[sb]], axis=0))
        m["cache_ckv"] = f32(inputs["cache_mla_ckv"][sb, 0])
        m["cache_krope"] = f32(inputs["cache_mla_krope"][sb, 0])
        sg = np.asarray(inputs["state_gdn"][sb, 0])
        sl = np.asarray(inputs["state_gla"][sb, 0])
        m["state_gdn"] = f32(sg[::-1] if mirror else sg)
        m["state_gla"] = f32(sl[::-1] if mirror else sl)
        w = np.zeros((128, 2), np.float32)
        w[:, 1 - half] = 1.0
        m["selw"] = w
        maps.append(m)
    return maps


def assemble(cfg, rs, cores):
    c = cfg
    yp, ck, kr, gd, gl = [], [], [], [], []
    ys = {}
    for r, core in zip(rs, cores):
        sb, half = core // 2, core % 2
        mirror = (half == 1)
        y = r["y"]
        p_ = y[:c.NP].reshape(c.NPS, c.TP, D)
        s_ = y[c.NP:]
        a = r["ckv_out"].reshape(c.NPS, 1, c.TP, 128)
        b = r["krope_out"].reshape(c.NPS, 1, c.TP, 64)
        g1 = r["gdn_out"].reshape(c.NPS, 1, 2, 4, 128, 128)
        g2 = r["gla_out"].reshape(c.NPS, 1, 2, 4, 128, 256)
        if mirror:
            p_, s_ = p_[:, ::-1], s_[::-1]
            a, b = a[:, :, ::-1], b[:, :, ::-1]
            g1, g2 = g1[:, :, ::-1], g2[:, :, ::-1]
        yp.append(p_); ck.append(a); kr.append(b); gd.append(g1); gl.append(g2)
        ys.setdefault(sb, [None, None])[half] = s_
    y_sample = np.stack([np.concatenate(ys[k], axis=0) for k in sorted(ys)], axis=0)
    cat = lambda l: np.ascontiguousarray(np.concatenate(l, axis=0), dtype=np.float32)
    return (cat(yp), np.ascontiguousarray(y_sample, dtype=np.float32), cat(ck), cat(kr), cat(gd), cat(gl))


_NC_CACHE = {}


def kernel(**inputs):
    cfg = Cfg()
    cores = list(range(8))
    if "full" not in _NC_CACHE:
        _NC_CACHE["full"] = Builder(cfg).build()
    nc = _NC_CACHE["full"]
    maps = make_in_maps(cfg, inputs, cores)
    res = run_bass_kernel_spmd(nc, maps, core_ids=cores)
    return assemble(cfg, res.results, cores)
```

```python
import numpy as np
from contextlib import ExitStack
import concourse.bass as bass
import concourse.mybir as mybir
from concourse.bass_utils import run_bass_kernel_spmd

F32 = mybir.dt.float32
BF16 = mybir.dt.bfloat16
F32R = mybir.dt.float32r
AF = mybir.ActivationFunctionType
ALU = mybir.AluOpType
AX = mybir.AxisListType

ENGS = ("pe", "act", "dve", "pool", "sp")
NDMA = 20
D = 1024
DFF = 2816
EPS = 1e-6


class T:
    _n = 0

    def __init__(self, h, name, psum=False):
        self.h = h
        self.name = name
        self.recs = {}
        self.psum = psum

    def __getitem__(self, idx):
        return self.h[idx]


class Prog:
    def __init__(self, nc):
        self.nc = nc
        self.es = ExitStack()
        self.sem = {}
        for e in ENGS:
            self.sem["E:" + e] = self.es.enter_context(nc.semaphore("s_" + e))
        self.dsem = {}
        for q in ("sp", "pool", "act"):
            self.dsem[q] = []
            for i in range(56 if q == "pool" else (NDMA if q == "sp" else 2)):
                k = "D:%s%d" % (q, i)
                self.sem[k] = self.es.enter_context(nc.semaphore("d_%s%d" % (q, i)))
                self.dsem[q].append(k)
        self.csem = []
        for i in range(6):
            k = "C:%d" % i
            self.sem[k] = self.es.enter_context(nc.semaphore("c_%d" % i))
            self.csem.append(k)
        self.cctr = 0
        self.val = {k: 0 for k in self.sem}
        self.drr = {q: 0 for q in self.dsem}
        self.known = {e: {} for e in ENGS}
        self.ops = {e: [] for e in ENGS}
        self.nops = 0
        self.phase_es = None
        self.uid = 0

    def begin_phase(self):
        self.phase_es = ExitStack()

    def sb(self, name, shape, dt=F32, persistent=False, mid=False):
        es = self.es if persistent else (self.mid_es if mid else self.phase_es)
        self.uid += 1
        h = es.enter_context(self.nc.sbuf_tensor("%s_%d" % (name, self.uid), list(shape), dt))
        return T(h, name)

    def ps(self, name, shape, dt=F32, persistent=False):
        es = self.es if persistent else self.phase_es
        self.uid += 1
        h = es.enter_context(self.nc.psum_tensor("%s_%d" % (name, self.uid), list(shape), dt))
        return T(h, name, psum=True)

    def _conf(self, t, key):
        if key is None or t.psum:
            return list(t.recs.values())
        out = []
        if key in t.recs:
            out.append(t.recs[key])
        if None in t.recs:
            out.append(t.recs[None])
        return out

    def _deps(self, reads, writes, eng=None):
        deps = []
        for (t, key) in reads:
            for r in self._conf(t, key):
                if r["w"] is not None:
                    deps.append(r["w"])
                if t.psum and eng is not None:
                    for tok in r["r"]:
                        if tok[0] != "E:" + eng:
                            deps.append(tok)
        for (t, key) in writes:
            for r in self._conf(t, key):
                if r["w"] is not None:
                    deps.append(r["w"])
                deps.extend(r["r"])
        return deps

    def _commit(self, tok, reads, writes):
        for (t, key) in reads:
            r = t.recs.get(key)
            if r is None:
                r = t.recs[key] = {"w": None, "r": []}
            r["r"].append(tok)
            if len(r["r"]) > 48:
                best = {}
                for (s, v) in r["r"]:
                    if best.get(s, -1) < v:
                        best[s] = v
                r["r"] = list(best.items())
        for (t, key) in writes:
            if key is None:
                t.recs = {None: {"w": tok, "r": []}}
            else:
                t.recs[key] = {"w": tok, "r": []}

    def _waits(self, eng, deps):
        best = {}
        for (s, v) in deps:
            if best.get(s, -1) < v:
                best[s] = v
        kn = self.known[eng]
        out = []
        for s, v in best.items():
            if kn.get(s, 0) >= v:
                continue
            kn[s] = v
            out.append((s, v))
        return out

    @staticmethod
    def _norm(lst):
        out = []
        for a in lst:
            if isinstance(a, T):
                out.append((a, None))
            else:
                out.append(a)
        return out

    def op(self, eng, fn, reads=(), writes=()):
        reads = self._norm(reads)
        writes = self._norm(writes)
        deps = self._deps(reads, writes, eng)
        if eng == "pe":
            deps = [d for d in deps if d[0] != "E:pe"]
        waits = self._waits(eng, deps)
        sk = "E:" + eng
        self.val[sk] += 1
        tok = (sk, self.val[sk])
        self.ops[eng].append((waits, fn, (sk, 1)))
        self._commit(tok, reads, writes)
        self.nops += 1
        return tok

    def dma(self, q, fn, reads=(), writes=()):
        reads = self._norm(reads)
        writes = self._norm(writes)
        deps = self._deps(reads, writes)
        i = self.drr[q]
        self.drr[q] = (i + 1) % len(self.dsem[q])
        sk = self.dsem[q][i]
        if self.val[sk] > 0:
            deps.append((sk, self.val[sk]))
        waits = self._waits(q, deps)
        self.val[sk] += 16
        tok = (sk, self.val[sk])
        self.ops[q].append((waits, fn, (sk, 16)))
        self._commit(tok, reads, writes)
        self.nops += 1
        return tok

    def coll(self, fn, reads=(), writes=()):
        reads = self._norm(reads)
        writes = self._norm(writes)
        deps = self._deps(reads, writes)
        sk = self.csem[self.cctr % len(self.csem)]
        self.cctr += 1
        if self.val[sk] > 0:
            deps.append((sk, self.val[sk]))
        waits = self._waits("pool", deps)
        self.val[sk] += 1
        tok = (sk, self.val[sk])
        self.ops["pool"].append((waits, fn, (sk, 1)))
        self._commit(tok, reads, writes)
        self.nops += 1
        return tok

    def _replay(self, eng, e):
        for (waits, fn, inc) in self.ops[eng]:
            if fn is None:
                for (s, v) in waits:
                    e.wait_ge(self.sem[s], v)
                continue
            for (s, v) in waits[1:]:
                e.wait_ge(self.sem[s], v)
            ins = fn(e)
            if waits:
                ins._wait_ge(self.sem[waits[0][0]], waits[0][1])
            if inc is not None:
                ins.then_inc(self.sem[inc[0]], inc[1])

    def end_phase(self, final=False):
        allv = [(s, v) for s, v in self.val.items() if v > 0 and (final or not s.startswith("D:pool"))]
        for eng in ENGS:
            waits = self._waits(eng, allv)
            if waits:
                self.ops[eng].append((waits, None, None))
        nc = self.nc
        with nc.Block() as block:
            @block.tensor
            def _(e):
                self._replay("pe", e)

            @block.scalar
            def _(e):
                self._replay("act", e)

            @block.vector
            def _(e):
                self._replay("dve", e)

            @block.gpsimd
            def _(e):
                self._replay("pool", e)

            @block.sync
            def _(e):
                self._replay("sp", e)
        self.ops = {e: [] for e in ENGS}
        if self.phase_es is not None:
            self.phase_es.close()
            self.phase_es = None

    def close(self):
        self.es.close()


class Cfg:
    def __init__(self, TS=2048, NPS=4, TP=256, phases=None, dbg=False, groups=((0, 1), (2, 3), (4, 5), (6, 7))):
        self.rg = [list(g) for g in groups]
        self.TS = TS
        self.NPS = NPS
        self.TP = TP
        self.NP = NPS * TP
        self.NT = self.NP + TS
        self.phases = phases
        self.dbg = dbg


INPUT_SPECS = {
    "mod_w": (2, 1024, 9216), "mod_b": (2, 9216), "norm_pre": (2, 3, 1024), "norm_post": (2, 3, 1024),
    "ffn_w_in": (2, 2, 1024, 5632), "ffn_w_out": (2, 2, 2816, 1024),
    "ev_w_in": (1024, 2512), "ev_w_in_perm": (1024, 64), "ev_conv_w": (5, 1536), "ev_gdn_a_log": (8,), "ev_gdn_dt_bias": (8,),
    "ev_gdn_norm": (128,), "ev_mla_q_norm": (256,), "ev_mla_w_uq": (256, 768), "ev_mla_w_uq_perm": (256, 256),
    "ev_mla_kv_norm": (128,), "ev_mla_w_ukv": (128, 1024), "ev_w_out": (1024, 1024),
    "od_w_in": (1024, 3104), "od_gla_w_gup": (2, 16, 512), "od_gla_b_g": (2, 512), "od_gla_norm": (256,),
    "od_w_out": (1024, 1024),
}


class Builder:
    def __init__(self, cfg):
        self.cfg = cfg
        nc = self.nc = bass.Bass("TRN2", target_bir_lowering=False)
        self.P = Prog(nc)
        self.din = {}
        self.dout = {}
        self.dtr = {}
        c = cfg
        self.inp("xin", (c.NT, D))
        self.inp("cond", (2, D))
        self.inp("cache_ckv", (256, 128))
        self.inp("cache_krope", (256, 64))
        self.inp("state_gdn", (2, 4, 128, 128))
        self.inp("state_gla", (2, 4, 128, 256))
        self.inp("rope_cos", (64, c.TS))
        self.inp("rope_sin", (64, c.TS))
        self.inp("selw", (128, 2))
        for k, shp in INPUT_SPECS.items():
            self.inp(k, shp)
        self.out("y", (c.NT, D))
        self.out("ckv_out", (c.NP, 128))
        self.out("krope_out", (c.NP, 64))
        self.out("gdn_out", (c.NPS, 2, 4, 128, 128))
        self.out("gla_out", (c.NPS, 2, 4, 128, 256))

    def inp(self, name, shape):
        self.din[name] = self.nc.dram_tensor(name, list(shape), F32, kind="ExternalInput").ap()
        self.dtr[name] = T(None, name)

    def out(self, name, shape):
        self.dout[name] = self.nc.dram_tensor(name, list(shape), F32, kind="ExternalOutput").ap()
        self.dtr[name] = T(None, name)

    def scratch(self, name, shape, dt=F32):
        ap = self.nc.dram_tensor(name, list(shape), dt).ap()
        self.dtr[name] = T(None, name)
        self.din[name] = ap
        return ap

    def build(self):
        c = self.cfg
        P = self.P
        self.XS = self.scratch("XS", (c.NT, D))
        self.wbf = {}
        self.phase0()
        src = "xin"
        plan = [("ffn", 0, 0), ("mix", 0), ("ffn", 0, 1), ("ffn", 1, 0), ("mix", 1), ("ffn", 1, 1)]
        if c.phases is not None:
            plan = c.phases
        for i, ph in enumerate(plan):
            last = (i == len(plan) - 1)
            dst = "y" if last else "XS"
            if ph[0] == "ffn":
                self.ffn_phase(ph[1], ph[2], src, dst)
            else:
                self.mix_phase(ph[1], src, dst)
            src = dst
        P.begin_phase()
        P.end_phase(final=True)
        P.close()
        return self.nc

    def dr(self, name):
        return self.din[name] if name in self.din else self.dout[name]

    def cast_begin(self, name, ap2d, piece=64):
        R, C = ap2d.shape
        dst = self.scratch(name + "_bf", (R, C), BF16)
        self.wbf[name] = dst
        q = []
        for r0 in range(0, R, piece):
            q.append((name, ap2d, dst, r0, min(R, r0 + piece)))
        return q

    def cast_piece(self, q):
        if not q:
            return
        name, ap2d, dst, r0, r1 = q.pop(0)
        tr = self.dtr[name + "_bf"]
        self.P.dma("pool", lambda e: e.dma_start(out=dst[r0:r1, :], in_=ap2d[r0:r1, :]), writes=[(tr, r0)])

    def flush_deferred(self, upto=1):
        for k in sorted(getattr(self, "deferred", {})):
            if k <= upto:
                for (nm_, ap_, rp_) in self.deferred.pop(k):
                    self.cast_weight(nm_, ap_, rows_per=rp_)

    def cast_weight(self, name, ap2d, rows_per=256):
        P = self.P
        R, C = ap2d.shape
        dst = self.scratch(name + "_bf", (R, C), BF16)
        tr = self.dtr[name + "_bf"]
        r0 = 0
        while r0 < R:
            r1 = min(R, r0 + rows_per)
            P.dma("pool", lambda e, r0=r0, r1=r1: e.dma_start(out=dst[r0:r1, :], in_=ap2d[r0:r1, :]),
                  writes=[(tr, r0)])
            r0 = r1
        self.wbf[name] = dst
        return dst

    def phase0(self):
        c = self.cfg
        P = self.P
        nc = self.nc
        di = self.din
        P.begin_phase()
        self.ident = P.sb("ident", [128, 128], F32, persistent=True)
        self.ones = P.sb("ones", [128, 128], F32, persistent=True)
        self.epsc = P.sb("epsc", [128, 1], F32, persistent=True)
        ident, ones = self.ident, self.ones
        P.op("pool", lambda e: e.memset(ident[:], 1.0), writes=[ident])
        P.op("pool", lambda e: e.affine_select(out=ident[:], in_=ident[:], pattern=[[-1, 128]],
                                               compare_op=ALU.is_equal, fill=0.0, base=0, channel_multiplier=1),
             reads=[ident], writes=[ident])
        P.op("pool", lambda e: e.memset(ones[:], 1.0), writes=[ones])
        P.op("pool", lambda e: e.memset(self.epsc[:], EPS), writes=[self.epsc])
        self.cast_weight("ffn_in_00", di["ffn_w_in"][0, 0])
        self.castq = {}
        self.npre = P.sb("npre", [128, 48], F32, persistent=True)
        self.npost = P.sb("npost", [128, 48], F32, persistent=True)
        self.modbT = P.sb("modbT", [128, 144], F32, persistent=True)
        self.modsT = P.sb("modsT", [128, 2, 72, 2], F32, persistent=True)
        self.ABC = P.sb("ABC", [128, 2, 3, 2, 3, 8], F32, persistent=True)
        ld = P.sb("ld", [128, 128], F32)
        tps = P.ps("tps", [128, 128], F32)

        def load_T(src_ap, nrows, dst_tile, dst_off):
            P.dma("sp", lambda e: e.dma_start(out=ld[0:nrows, :], in_=src_ap), writes=[ld])
            P.op("pe", lambda e: e.transpose(tps[:, 0:nrows], ld[0:nrows, :], ident[0:nrows, 0:nrows]),
                 reads=[ld, ident], writes=[tps])
            P.op("dve", lambda e: e.tensor_copy(out=dst_tile[:, dst_off:dst_off + nrows], in_=tps[:, 0:nrows]),
                 reads=[tps], writes=[dst_tile])

        load_T(di["norm_pre"].rearrange("l s (c p) -> (l s c) p", p=128), 48, self.npre, 0)
        load_T(di["norm_post"].rearrange("l s (c p) -> (l s c) p", p=128), 48, self.npost, 0)
        for l in range(2):
            load_T(di["mod_b"][l].rearrange("(m p) -> m p", p=128), 72, self.modbT, l * 72)
        scT = P.sb("scT", [128, 16], F32)
        sc2 = P.sb("sc2", [128, 8, 2], F32)
        load_T(di["cond"].rearrange("r (c p) -> (r c) p", p=128), 16, scT, 0)
        P.op("act", lambda e: e.activation(out=scT[:], in_=scT[:], func=AF.Silu), reads=[scT], writes=[scT])
        for r in range(2):
            P.op("dve", lambda e, r=r: e.tensor_copy(out=sc2[:, :, r], in_=scT[:, r * 8:(r + 1) * 8]),
                 reads=[scT], writes=[sc2])
        wmb = [P.sb("wmb%d" % i, [128, 8, 512], F32) for i in range(2)]
        mps = [P.ps("mps%d" % l, [128, 144], F32) for l in range(2)]
        mrow = [P.sb("mrow%d" % l, [2, 9216], F32) for l in range(2)]
        pacc = [P.ps("pacc%d" % i, [2, 512], F32) for i in range(2)]
        bi = 0
        for l in range(2):
            for cb in range(18):
                w = wmb[bi % 2]
                pa = pacc[bi % 2]
                bi += 1
                for h2 in range(2):
                    P.dma("sp", lambda e, w=w, l=l, cb=cb, h2=h2: e.dma_start(
                        out=w[:, h2 * 4:(h2 + 1) * 4, :],
                        in_=di["mod_w"][l, h2 * 512:(h2 + 1) * 512, cb * 512:(cb + 1) * 512].rearrange(
                            "(kc p) n -> p kc n", p=128)), writes=[(w, h2)])
                for k in range(8):
                    P.op("pe", lambda e, w=w, k=k, pa=pa: e.matmul(pa[0:2, :], sc2[:, k, :], w[:, k, :], start=(k == 0), stop=(k == 7)),
                         reads=[(w, k // 4), sc2], writes=[pa])
                P.op("act", lambda e, l=l, cb=cb, pa=pa: e.activation(out=mrow[l][0:2, cb * 512:(cb + 1) * 512], in_=pa[0:2, :], func=AF.Copy),
                     reads=[pa], writes=[(mrow[l], cb)])
        for l in range(2):
            for m in range(72):
                P.op("pe", lambda e, l=l, m=m: e.transpose(mps[l][:, m * 2:m * 2 + 2], mrow[l][0:2, m * 128:(m + 1) * 128], ident[0:2, 0:2]),
                     reads=[(mrow[l], m // 4), ident], writes=[mps[l]])
        modsT, modbT = self.modsT, self.modbT
        for l in range(2):
            P.op("dve", lambda e, l=l: e.tensor_tensor(
                out=modsT[:, l], in0=mps[l][:].rearrange("p (m r) -> p m r", r=2),
                in1=modbT[:, l * 72:(l + 1) * 72].unsqueeze(2).to_broadcast([128, 72, 2]), op=ALU.add),
                reads=[mps[l], modbT], writes=[modsT])
        ABC = self.ABC
        for l in range(2):
            for s in range(3):
                res = 1.0 if s == 1 else 0.5
                for r in range(2):
                    gp = self.npre[:, (l * 3 + s) * 8:(l * 3 + s) * 8 + 8]
                    gq = self.npost[:, (l * 3 + s) * 8:(l * 3 + s) * 8 + 8]
                    sh = modsT[:, l, (3 * s) * 8:(3 * s) * 8 + 8, r]
                    scl = modsT[:, l, (3 * s + 1) * 8:(3 * s + 1) * 8 + 8, r]
                    gt = modsT[:, l, (3 * s + 2) * 8:(3 * s + 2) * 8 + 8, r]
                    P.op("dve", lambda e, l=l, s=s, r=r, gp=gp, scl=scl: e.scalar_tensor_tensor(
                        out=ABC[:, l, s, r, 0, :], in0=scl, scalar=1.0, in1=gp, op0=ALU.add, op1=ALU.mult),
                        reads=[modsT, self.npre], writes=[(ABC, (l, s, r, 0))])
                    P.op("dve", lambda e, l=l, s=s, r=r, sh=sh: e.tensor_copy(out=ABC[:, l, s, r, 1, :], in_=sh),
                         reads=[modsT], writes=[(ABC, (l, s, r, 1))])
                    P.op("dve", lambda e, l=l, s=s, r=r, gt=gt, gq=gq, res=res: e.scalar_tensor_tensor(
                        out=ABC[:, l, s, r, 2, :], in0=gt, scalar=res, in1=gq, op0=ALU.mult, op1=ALU.mult),
                        reads=[modsT, self.npost], writes=[(ABC, (l, s, r, 2))])
        P.end_phase()

    def groups(self):
        c = self.cfg
        gs = []
        for r0 in range(0, c.NP, 512):
            gs.append((r0, 0))
        for r0 in range(c.NP, c.NT, 512):
            gs.append((r0, 1))
        return gs

    def make_cbc(self, l, s, PB):
        P = self.P
        ABC, ident, ones = self.ABC, self.ident, self.ones
        out = []
        dg = P.sb("dg", [128, 128], F32)
        for r in range(2):
            cb = P.sb("cbc%d" % r, [128, 1024], F32)
            pb = [PB[2 * r], PB[2 * r + 1]]
            for cc in range(8):
                P.op("dve", lambda e, cc=cc, r=r: e.tensor_scalar(
                    out=dg[:], in0=ident[:], scalar1=ABC[:, l, s, r, 2, cc:cc + 1], scalar2=None, op0=ALU.mult),
                    reads=[ident, ABC], writes=[dg])
                P.op("pe", lambda e, cc=cc, r=r, pb=pb: e.matmul(
                    pb[cc // 4][:, (cc % 4) * 128:(cc % 4 + 1) * 128], ones[:], dg[:], start=True, stop=True),
                    reads=[ones, dg], writes=[(pb[cc // 4], cc % 4)])
            for i in range(2):
                P.op("act", lambda e, i=i, pb=pb, cb=cb: e.activation(
                    out=cb[:, i * 512:(i + 1) * 512], in_=pb[i][:], func=AF.Copy), reads=[pb[i]], writes=[(cb, i)])
            out.append(cb)
        return out

    def ffn_phase(self, l, f, src, dst):
        c = self.cfg
        P = self.P
        s = 0 if f == 0 else 2
        ABC, ident = self.ABC, self.ident
        srcap, dstap = self.dr(src), self.dr(dst)
        srct, dstt = self.dtr[src], self.dtr[dst]
        win = self.wbf["ffn_in_%d%d" % (l, f)]
        wint = self.dtr["ffn_in_%d%d_bf" % (l, f)]
        P.begin_phase()
        PB = [P.ps("pb%d" % i, [128, 512], F32) for i in range(8)]
        import os
        parts = os.environ.get("FFN_PARTS", "cbc,wo,load,pro,in,out").split(",")
        if "cbc" in parts:
            cbc = self.make_cbc(l, s, PB)
        wo = P.sb("wo", [128, 22, 1024], BF16)
        wout32 = self.din["ffn_w_out"][l, f]
        for q in range(2 if "wo" in parts else 0):
            P.dma("pool", lambda e, q=q: e.dma_start(
                out=wo[:, q * 11:(q + 1) * 11, :],
                in_=wout32[q * 1408:(q + 1) * 1408, :].rearrange("(j p) n -> p j n", p=128)),
                writes=[(wo, q)])
        wi = [P.sb("wi%d" % i, [128, 8, 512], BF16) for i in range(3)]
        xg = [P.sb("xg%d" % i, [128, 4, 1024], F32) for i in range(2)]
        hT = [P.sb("hT%d" % i, [128, 8, 512], BF16) for i in range(2)]
        gT = [P.sb("gT%d" % i, [128, 22, 512], BF16) for i in range(2)]
        xn = [P.sb("xn%d" % i, [128, 1024], F32) for i in range(2)]
        su = [P.sb("su%d" % i, [128, 512], F32) for i in range(2)]
        junk = P.sb("junk", [128, 1024], F32)
        junk2 = P.sb("junk2", [128, 1024], F32)
        tmp = [P.sb("tmp%d" % i, [128, 1024], F32) for i in range(2)]
        ss = [P.sb("ss%d" % i, [128, 24], F32) for i in range(2)]
        gs = self.groups()
        wctr = [0]

        def load(gi):
            r0, r = gs[gi]
            x = xg[gi % 2]
            P.dma("sp", lambda e: e.dma_start(out=x[:], in_=srcap[r0:r0 + 512, :].rearrange("(t p) d -> p t d", p=128)),
                  reads=[srct], writes=[x])

        def pro_a(gi):
            r0, r = gs[gi]
            x = xg[gi % 2]
            sst = ss[gi % 2]
            P.op("dve", lambda e: e.memset(sst[:], 0.0), writes=[sst])
            for t in range(4):
                P.op("act", lambda e, t=t: e.activation(out=junk2[:], in_=x[:, t, :], func=AF.Square,
                                                        accum_out=sst[:, t:t + 1]),
                     reads=[(x, None)], writes=[junk2, (sst, t)])
            P.op("act", lambda e: e.activation(out=sst[:, 4:8], in_=sst[:, 0:4], func=AF.Sqrt, scale=1.0 / D,
                                               bias=self.epsc[:, 0:1]), reads=[sst, self.epsc], writes=[(sst, "sq")])
            P.op("dve", lambda e: e.reciprocal(out=sst[:, 8:12], in_=sst[:, 4:8]), reads=[(sst, "sq")],
                 writes=[(sst, "rs")])
            for t in range(2):
                xx = xn[t % 2]
                P.op("act", lambda e, t=t, xx=xx: e.activation(out=xx[:], in_=x[:, t, :], func=AF.Identity,
                                                               scale=sst[:, 8 + t:9 + t]),
                     reads=[(x, None), (sst, "rs")], writes=[xx])

        def pro_b(gi):
            r0, r = gs[gi]
            x = xg[gi % 2]
            sst = ss[gi % 2]
            h = hT[gi % 2]
            for t in range(4):
                xx = xn[t % 2]
                if t >= 2:
                    P.op("act", lambda e, t=t, xx=xx: e.activation(out=xx[:], in_=x[:, t, :], func=AF.Identity,
                                                                   scale=sst[:, 8 + t:9 + t]),
                         reads=[(x, None), (sst, "rs")], writes=[xx])
                for half in range(2):
                    pb = PB[4 + half]
                    for q in range(4):
                        cc = half * 4 + q
                        P.op("pe", lambda e, xx=xx, cc=cc, q=q, pb=pb: e.transpose(
                            pb[:, q * 128:(q + 1) * 128], xx[:, cc * 128:(cc + 1) * 128], ident[:]),
                            reads=[xx, ident], writes=[(pb, q)])
                    for q in range(4):
                        cc = half * 4 + q
                        if half == 0:
                            P.op("act", lambda e, cc=cc, q=q, pb=pb, t=t: e.activation(
                                out=h[:, cc, t * 128:(t + 1) * 128], in_=pb[:, q * 128:(q + 1) * 128],
                                func=AF.Identity, scale=ABC[:, l, s, r, 0, cc:cc + 1],
                                bias=ABC[:, l, s, r, 1, cc:cc + 1]),
                                reads=[(pb, q), ABC], writes=[(h, (cc, t))])
                        else:
                            P.op("dve", lambda e, cc=cc, q=q, pb=pb, t=t: e.tensor_scalar(
                                out=h[:, cc, t * 128:(t + 1) * 128], in0=pb[:, q * 128:(q + 1) * 128],
                                scalar1=ABC[:, l, s, r, 0, cc:cc + 1], scalar2=ABC[:, l, s, r, 1, cc:cc + 1],
                                op0=ALU.mult, op1=ALU.add),
                                reads=[(pb, q), ABC], writes=[(h, (cc, t))])

        wmap = {}

        def wload(gi, b):
            if (gi, b) in wmap:
                return
            w = wi[wctr[0] % 3]
            wctr[0] += 1
            wmap[(gi, b)] = w
            for uv in range(2):
                P.dma("sp", lambda e, w=w, b=b, uv=uv: e.dma_start(
                    out=w[:, :, uv * 256:(uv + 1) * 256],
                    in_=win[:, uv * DFF + b * 256: uv * DFF + (b + 1) * 256].rearrange("(kc p) n -> p kc n", p=128)),
                    reads=[wint], writes=[(w, uv)])

        def inproj(gi, hook=None):
            h = hT[gi % 2]
            g = gT[gi % 2]
            for b in range(11):
                if b == 4 and hook is not None:
                    hook()
                wload(gi, b)
                w = wmap.pop((gi, b))
                self.cast_piece(self.cq_cur)
                for jj in range(2):
                    j = 2 * b + jj
                    pu = PB[(j % 2) * 2]
                    pv = PB[(j % 2) * 2 + 1]
                    for k in range(8):
                        P.op("pe", lambda e, w=w, k=k, jj=jj, pu=pu: e.matmul(
                            pu[:], w[:, k, jj * 128:(jj + 1) * 128], h[:, k, :], start=(k == 0), stop=(k == 7)),
                            reads=[(w, 0), h], writes=[pu])
                    for k in range(8):
                        P.op("pe", lambda e, w=w, k=k, jj=jj, pv=pv: e.matmul(
                            pv[:], w[:, k, 256 + jj * 128:256 + (jj + 1) * 128], h[:, k, :], start=(k == 0),
                            stop=(k == 7)),
                            reads=[(w, 1), h], writes=[pv])
                    sj = su[j % 2]
                    P.op("act", lambda e, sj=sj, pu=pu: e.activation(out=sj[:], in_=pu[:], func=AF.Silu),
                         reads=[pu], writes=[sj])
                    P.op("dve", lambda e, sj=sj, pv=pv, j=j: e.tensor_tensor(out=g[:, j, :], in0=sj[:], in1=pv[:],
                                                                             op=ALU.mult),
                         reads=[sj, pv], writes=[(g, j)])

        def outproj(gi):
            r0, r = gs[gi]
            x = xg[gi % 2]
            g = gT[gi % 2]
            sst = ss[gi % 2]
            for t in range(4):
                py = [PB[6], PB[7]] if t % 2 == 0 else [PB[4], PB[5]]
                for nb in range(2):
                    for j in range(22):
                        P.op("pe", lambda e, nb=nb, j=j, t=t, py=py: e.matmul(
                            py[nb][:], g[:, j, t * 128:(t + 1) * 128], wo[:, j, nb * 512:(nb + 1) * 512],
                            start=(j == 0), stop=(j == 21)),
                            reads=[g, (wo, j // 11)], writes=[py[nb]])
                for nb in range(2):
                    P.op("act", lambda e, nb=nb, t=t, py=py: e.activation(
                        out=junk[:, 0:512], in_=py[nb][:], func=AF.Square, accum_out=sst[:, 12 + 2 * t + nb:13 + 2 * t + nb]),
                        reads=[py[nb]], writes=[junk, (sst, ("y", t, nb))])
                P.op("dve", lambda e, t=t: e.tensor_tensor(out=sst[:, 20 + (t % 2) * 2:21 + (t % 2) * 2], in0=sst[:, 12 + 2 * t:13 + 2 * t],
                                                           in1=sst[:, 13 + 2 * t:14 + 2 * t], op=ALU.add),
                     reads=[(sst, ("y", t, 0)), (sst, ("y", t, 1))], writes=[(sst, ("ys", t % 2))])
                P.op("act", lambda e, t=t: e.activation(out=sst[:, 20 + (t % 2) * 2:21 + (t % 2) * 2], in_=sst[:, 20 + (t % 2) * 2:21 + (t % 2) * 2],
                                                        func=AF.Sqrt, scale=1.0 / D, bias=self.epsc[:, 0:1]),
                     reads=[(sst, ("ys", t % 2)), self.epsc], writes=[(sst, ("ys", t % 2))])
                P.op("dve", lambda e, t=t: e.reciprocal(out=sst[:, 21 + (t % 2) * 2:22 + (t % 2) * 2], in_=sst[:, 20 + (t % 2) * 2:21 + (t % 2) * 2]),
                     reads=[(sst, ("ys", t % 2))], writes=[(sst, ("yr", t % 2))])
                tm = tmp[t % 2]
                for nb in range(2):
                    P.op("dve", lambda e, nb=nb, t=t, py=py, tm=tm: e.scalar_tensor_tensor(
                        out=tm[:, nb * 512:(nb + 1) * 512], in0=py[nb][:], scalar=sst[:, 21 + (t % 2) * 2:22 + (t % 2) * 2],
                        in1=cbc[r][:, nb * 512:(nb + 1) * 512], op0=ALU.mult, op1=ALU.mult),
                        reads=[py[nb], (sst, ("yr", t % 2)), cbc[r]], writes=[(tm, nb)])
                P.op("pool", lambda e, t=t, tm=tm: e.tensor_tensor(out=tm[:], in0=tm[:], in1=x[:, t, :], op=ALU.add),
                     reads=[tm, (x, None)], writes=[tm])
                P.dma("sp", lambda e, t=t, tm=tm: e.dma_start(out=dstap[r0 + t * 128:r0 + (t + 1) * 128, :], in_=tm[:]),
                      reads=[tm], writes=[(dstt, r0 + t * 128)])

        ng = len(gs)
        di_ = self.din
        cq = []
        for (l_, f_) in ((0, 1), (1, 0), (1, 1)):
            if (l, f) == (l_, f_) and ("ffn_in_%d%d" % (l_, f_)) not in self.wbf:
                q_ = self.cast_begin("ffn_in_%d%d" % (l_, f_), di_["ffn_w_in"][l_, f_], piece=256)
                while q_:
                    self.cast_piece(q_)
        self.cq_cur = cq
        load(0)
        pro_a(0)
        pro_b(0)
        for gi in range(ng):
            if gi + 1 < ng:
                load(gi + 1)
                inproj(gi, hook=lambda gi=gi: pro_a(gi + 1))
                pro_b(gi + 1)
                for b_ in range(3):
                    wload(gi + 1, b_)
            else:
                inproj(gi)
            outproj(gi)
        while self.cq_cur:
            self.cast_piece(self.cq_cur)
        P.end_phase()

    def mix_phase(self, l, src, dst):
        if l == 0:
            self.even_mixer(src, dst)
        else:
            self.odd_mixer(src, dst)

    def pro_hT(self, x, sst, h, l, s, r, pbs, xn, junk):
        P = self.P
        ABC, ident = self.ABC, self.ident
        P.op("pool", lambda e: e.memset(sst[:], 0.0), writes=[sst])
        for t in range(4):
            P.op("act", lambda e, t=t: e.activation(out=junk[:], in_=x[:, t, :], func=AF.Square,
                                                    accum_out=sst[:, t:t + 1]),
                 reads=[(x, None)], writes=[junk, (sst, t)])
        P.op("act", lambda e: e.activation(out=sst[:, 4:8], in_=sst[:, 0:4], func=AF.Sqrt, scale=1.0 / D,
                                           bias=self.epsc[:, 0:1]), reads=[sst, self.epsc], writes=[(sst, "sq")])
        P.op("dve", lambda e: e.reciprocal(out=sst[:, 8:12], in_=sst[:, 4:8]), reads=[(sst, "sq")],
             writes=[(sst, "rs")])
        for t in range(4):
            xx = xn[t % 2]
            P.op("act", lambda e, t=t, xx=xx: e.activation(out=xx[:], in_=x[:, t, :], func=AF.Identity,
                                                           scale=sst[:, 8 + t:9 + t]),
                 reads=[(x, None), (sst, "rs")], writes=[xx])
            for half in range(2):
                pb = pbs[half]
                for q in range(4):
                    cc = half * 4 + q
                    P.op("pe", lambda e, xx=xx, cc=cc, q=q, pb=pb: e.transpose(
                        pb[:, q * 128:(q + 1) * 128], xx[:, cc * 128:(cc + 1) * 128], ident[:]),
                        reads=[xx, ident], writes=[(pb, q)])
                for q in range(4):
                    cc = half * 4 + q
                    P.op("act", lambda e, cc=cc, q=q, pb=pb, t=t: e.activation(
                        out=h[:, cc, t * 128:(t + 1) * 128], in_=pb[:, q * 128:(q + 1) * 128],
                        func=AF.Identity, scale=ABC[:, l, s, r, 0, cc:cc + 1],
                        bias=ABC[:, l, s, r, 1, cc:cc + 1]),
                        reads=[(pb, q), ABC], writes=[(h, (cc, t))])

    def epi(self, py, xap, xdeps, r, sst, c0, cbc, tm, junk, dstap, dstt, row0):
        P = self.P
        for nb in range(2):
            P.op("act", lambda e, nb=nb: e.activation(out=junk[:, 0:512], in_=py[nb][:], func=AF.Square,
                                                      accum_out=sst[:, c0 + nb:c0 + nb + 1]),
                 reads=[py[nb]], writes=[junk, (sst, ("y", c0, nb))])
        P.op("dve", lambda e: e.tensor_tensor(out=sst[:, c0 + 2:c0 + 3], in0=sst[:, c0:c0 + 1], in1=sst[:, c0 + 1:c0 + 2],
                                              op=ALU.add),
             reads=[(sst, ("y", c0, 0)), (sst, ("y", c0, 1))], writes=[(sst, ("ys", c0))])
        P.op("act", lambda e: e.activation(out=sst[:, c0 + 2:c0 + 3], in_=sst[:, c0 + 2:c0 + 3], func=AF.Sqrt,
                                           scale=1.0 / D, bias=self.epsc[:, 0:1]),
             reads=[(sst, ("ys", c0)), self.epsc], writes=[(sst, ("ys", c0))])
        P.op("dve", lambda e: e.reciprocal(out=sst[:, c0 + 3:c0 + 4], in_=sst[:, c0 + 2:c0 + 3]),
             reads=[(sst, ("ys", c0))], writes=[(sst, ("yr", c0))])
        for nb in range(2):
            P.op("dve", lambda e, nb=nb: e.scalar_tensor_tensor(
                out=tm[:, nb * 512:(nb + 1) * 512], in0=py[nb][:], scalar=sst[:, c0 + 3:c0 + 4],
                in1=cbc[r][:, nb * 512:(nb + 1) * 512], op0=ALU.mult, op1=ALU.mult),
                reads=[py[nb], (sst, ("yr", c0)), cbc[r]], writes=[(tm, nb)])
        P.op("pool", lambda e: e.tensor_tensor(out=tm[:], in0=tm[:], in1=xap, op=ALU.add),
             reads=[tm] + xdeps, writes=[tm])
        P.dma("sp", lambda e: e.dma_start(out=dstap[row0:row0 + 128, :], in_=tm[:]),
              reads=[tm], writes=[(dstt, row0)])

    def pair_buf(self, name, rows, cols, dt=F32):
        gin = self.scratch(name + "_in", (rows, cols), dt)
        gout = self.scratch(name + "_out", (2 * rows, cols), dt)
        return gin, gout, self.dtr[name + "_in"], self.dtr[name + "_out"]

    def pair_gather(self, gin, gout, tin, tout):
        rg = self.cfg.rg
        self.P.coll(lambda e: e.collective_compute("AllGather", ALU.bypass, replica_groups=rg, ins=[gin.opt()], outs=[gout.opt()]),
                    reads=[tin], writes=[tout])

    def kcol(self, row):
        return row if row < self.cfg.NP else row + 256

    def seqs(self):
        c = self.cfg
        out = [(i * c.TP, c.TP, 0, "p") for i in range(c.NPS)]
        out.append((c.NP, c.TS, 1, "s"))
        return out

    def even_mixer(self, src, dst):
        c = self.cfg
        NT, NP = c.NT, c.NP
        NKV = NT + 256
        self.QKVT = self.scratch("QKVT", (1536, NT))
        self.ZS = self.scratch("ZS", (NT, 512))
        self.GB = self.scratch("GB", (NT, 16))
        self.QT = self.scratch("QT", (4, 192, NT), BF16)
        self.CKVT = self.scratch("CKVT", (128, NKV), BF16)
        self.KRT = self.scratch("KRT", (65, NKV), BF16)
        self.OT = self.scratch("OT", (1024, NT), BF16)
        self.OD = self.scratch("OD", (2, NT, 512))
        import os
        parts = os.environ.get("EV_PARTS", "e1,e2,e3,e4").split(",")
        if "e1" in parts:
            self.ev_e1(src)
        if "e2" in parts:
            self.ev_e2()
        if "e3" in parts:
            self.ev_e3()
        if "e4" in parts:
            self.mix_out(0, src, dst, "ev_w_out")

    def ev_e1(self, src):
        c = self.cfg
        P = self.P
        di = self.din
        l, s = 0, 1
        NT, NP = c.NT, c.NP
        ident = self.ident
        srcap, srct = self.dr(src), self.dtr[src]
        win = di["ev_w_in"]
        winp = di["ev_w_in_perm"]
        QKVT, ZS, GB, QT, CKVT, KRT = self.QKVT, self.ZS, self.GB, self.QT, self.CKVT, self.KRT
        tQKVT, tZS, tGB, tQT, tCKVT, tKRT = [self.dtr[n] for n in ("QKVT", "ZS", "GB", "QT", "CKVT", "KRT")]
        P.begin_phase()
        PB = [P.ps("pb%d" % i, [128, 512], F32) for i in range(8)]
        wsb = P.sb("wsb", [128, 8, 2512], BF16)
        for q in range(4):
            P.dma("pool", lambda e, q=q: e.dma_start(
                out=wsb[:, q * 2:(q + 1) * 2, :], in_=win[q * 256:(q + 1) * 256, :].rearrange("(kc p) n -> p kc n", p=128)),
                writes=[(wsb, q)])
        wpsb = P.sb("wpsb", [128, 8, 64], BF16)
        P.dma("pool", lambda e: e.dma_start(out=wpsb[:], in_=winp.rearrange("(kc p) n -> p kc n", p=128)),
              writes=[wpsb])
        ld = P.sb("ld", [128, 128], F32)
        qnT = P.sb("qnT", [128, 2], F32)
        P.dma("sp", lambda e: e.dma_start(out=ld[0:2, :], in_=di["ev_mla_q_norm"].rearrange("(c p) -> c p", p=128)),
              writes=[ld])
        P.op("pe", lambda e: e.transpose(PB[0][:, 0:2], ld[0:2, :], ident[0:2, 0:2]), reads=[ld, ident], writes=[PB[0]])
        P.op("dve", lambda e: e.tensor_copy(out=qnT[:], in_=PB[0][:, 0:2]), reads=[PB[0]], writes=[qnT])
        wq32 = P.sb("wq32", [128, 2, 1024], F32)
        P.dma("sp", lambda e: e.dma_start(out=wq32[:, :, 0:768], in_=di["ev_mla_w_uq"].rearrange("(kc p) n -> p kc n", p=128)),
              writes=[(wq32, 0)])
        P.dma("sp", lambda e: e.dma_start(out=wq32[:, :, 768:1024], in_=di["ev_mla_w_uq_perm"].rearrange("(kc p) n -> p kc n", p=128)),
              writes=[(wq32, 1)])
        wuq = P.sb("wuq", [128, 2, 1024], BF16)
        for kc in range(2):
            P.op("dve", lambda e, kc=kc: e.tensor_scalar(out=wuq[:, kc, :], in0=wq32[:, kc, :], scalar1=qnT[:, kc:kc + 1],
                                                         scalar2=None, op0=ALU.mult),
                 reads=[wq32, qnT], writes=[(wuq, kc)])
        dtb = P.sb("dtb", [128, 8], F32)
        nea = P.sb("nea", [128, 8], F32)
        kvn = P.sb("kvn", [128, 128], F32)
        P.dma("sp", lambda e: e.dma_start(out=dtb[:], in_=di["ev_gdn_dt_bias"].partition_broadcast(128)), writes=[dtb])
        P.dma("sp", lambda e: e.dma_start(out=nea[:], in_=di["ev_gdn_a_log"].partition_broadcast(128)), writes=[nea])
        P.dma("sp", lambda e: e.dma_start(out=kvn[:], in_=di["ev_mla_kv_norm"].partition_broadcast(128)), writes=[kvn])
        P.op("act", lambda e: e.activation(out=nea[:], in_=nea[:], func=AF.Exp), reads=[nea], writes=[nea])
        P.op("dve", lambda e: e.tensor_scalar(out=nea[:], in0=nea[:], scalar1=-1.0, scalar2=None, op0=ALU.mult),
             reads=[nea], writes=[nea])
        cst = P.sb("cst", [128, 256], BF16)
        for i in range(2):
            P.dma("sp", lambda e, i=i: e.dma_start(out=ld[:], in_=di["cache_ckv"][i * 128:(i + 1) * 128, :]), writes=[ld])
            P.op("pe", lambda e: e.transpose(PB[1][:, 0:128], ld[:], ident[:]), reads=[ld, ident], writes=[PB[1]])
            P.op("dve", lambda e, i=i: e.tensor_copy(out=cst[:, i * 128:(i + 1) * 128], in_=PB[1][:, 0:128]),
                 reads=[PB[1]], writes=[(cst, i)])
        P.dma("sp", lambda e: e.dma_start(out=CKVT[:, NP:NP + 256], in_=cst[:]), reads=[cst], writes=[(tCKVT, "c")])
        cst2 = P.sb("cst2", [64, 256], BF16)
        for i in range(2):
            P.dma("sp", lambda e, i=i: e.dma_start(out=ld[:, 0:64], in_=di["cache_krope"][i * 128:(i + 1) * 128, :]),
                  writes=[ld])
            P.op("pe", lambda e: e.transpose(PB[1][0:64, 0:128], ld[:, 0:64], ident[:]), reads=[ld, ident],
                 writes=[PB[1]])
            P.op("dve", lambda e, i=i: e.tensor_copy(out=cst2[:, i * 128:(i + 1) * 128], in_=PB[1][0:64, 0:128]),
                 reads=[PB[1]], writes=[(cst2, i)])
        P.dma("sp", lambda e: e.dma_start(out=KRT[0:64, NP:NP + 256], in_=cst2[:]), reads=[cst2], writes=[(tKRT, "c")])
        onesrow = P.sb("onesrow", [1, NT + 256], BF16)
        P.op("pool", lambda e: e.memset(onesrow[:], 1.0), writes=[onesrow])
        P.dma("sp", lambda e: e.dma_start(out=KRT[64:65, :], in_=onesrow[:]), reads=[onesrow], writes=[(tKRT, "o")])

        xg = [P.sb("xg%d" % i, [128, 4, 1024], F32) for i in range(2)]
        hT = [P.sb("hT%d" % i, [128, 8, 512], BF16) for i in range(2)]
        xn = [P.sb("xn%d" % i, [128, 1024], F32) for i in range(2)]
        junk = P.sb("junk", [128, 1024], F32)
        ss = [P.sb("ss%d" % i, [128, 24], F32) for i in range(2)]
        stg = [P.sb("stg%d" % i, [128, 512], F32) for i in range(3)]
        zst = [P.sb("zst%d" % i, [128, 512], F32) for i in range(2)]
        sm = [P.sb("sm%d" % i, [128, 64], F32) for i in range(2)]
        gb = [P.sb("gb%d" % i, [128, 16], F32) for i in range(2)]
        cqn = [P.sb("cqn%d" % i, [128, 256], F32) for i in range(2)]
        ckn = [P.sb("ckn%d" % i, [128, 128], F32) for i in range(2)]
        krt = [P.sb("krt%d" % i, [128, 64], F32) for i in range(2)]
        cqnT = P.sb("cqnT", [128, 2, 512], BF16)
        ckT = P.sb("ckT", [128, 512], BF16)
        cs = [P.sb("cos%d" % i, [64, 512], F32) for i in range(2)]
        sn = [P.sb("sin%d" % i, [64, 512], F32) for i in range(2)]
        rt = [P.sb("rt%d" % i, [64, 512], F32) for i in range(2)]
        rb = [P.sb("rb%d" % i, [128, 512], BF16) for i in range(3)]
        gs = self.groups()
        ctr = {"stg": 0, "rb": 0, "pb": 0}

        def rope_or_copy(pa, pbp, kind, gi, outbf, nrow=64):
            if kind == "p":
                P.op("act", lambda e: e.activation(out=outbf[0:64, :], in_=pa[0:64, :], func=AF.Copy), reads=[pa],
                     writes=[outbf])
            else:
                a, b = rt[0], rt[1]
                P.op("dve", lambda e: e.tensor_tensor(out=a[:], in0=pa[0:64, :], in1=cs[gi % 2][:], op=ALU.mult),
                     reads=[pa, cs[gi % 2]], writes=[a])
                P.op("dve", lambda e: e.tensor_tensor(out=b[:], in0=pbp[0:64, :], in1=sn[gi % 2][:], op=ALU.mult),
                     reads=[pbp, sn[gi % 2]], writes=[b])
                P.op("pool", lambda e: e.tensor_tensor(out=outbf[0:64, :], in0=a[:], in1=b[:], op=ALU.add),
                     reads=[a, b], writes=[outbf])

        def load(gi):
            r0, r = gs[gi]
            x = xg[gi % 2]
            P.dma("sp", lambda e: e.dma_start(out=x[:], in_=srcap[r0:r0 + 512, :].rearrange("(t p) d -> p t d", p=128)),
                  reads=[srct], writes=[x])
            if r == 1:
                t0 = r0 - NP
                P.dma("sp", lambda e: e.dma_start(out=cs[gi % 2][:], in_=di["rope_cos"][:, t0:t0 + 512]), writes=[cs[gi % 2]])
                P.dma("sp", lambda e: e.dma_start(out=sn[gi % 2][:], in_=di["rope_sin"][:, t0:t0 + 512]), writes=[sn[gi % 2]])

        def body(gi):
            r0, r = gs[gi]
            kind = "p" if r == 0 else "s"
            x = xg[gi % 2]
            h = hT[gi % 2]
            sst = ss[gi % 2]
            kc0 = self.kcol(r0)
            for m in range(12):
                pb = PB[m % 4]
                for k in range(8):
                    P.op("pe", lambda e, m=m, k=k, pb=pb: e.matmul(pb[:], wsb[:, k, m * 128:(m + 1) * 128], h[:, k, :],
                                                                     start=(k == 0), stop=(k == 7)),
                         reads=[wsb, h], writes=[pb])
                st = stg[ctr["stg"] % 3]
                ctr["stg"] += 1
                if m % 2 == 0:
                    P.op("act", lambda e, st=st, pb=pb: e.activation(out=st[:], in_=pb[:], func=AF.Copy), reads=[pb], writes=[st])
                else:
                    P.op("dve", lambda e, st=st, pb=pb: e.tensor_copy(out=st[:], in_=pb[:]), reads=[pb], writes=[st])
                P.dma("sp", lambda e, st=st, m=m: e.dma_start(out=QKVT[m * 128:(m + 1) * 128, r0:r0 + 512], in_=st[:]),
                      reads=[st], writes=[(tQKVT, (m, r0))])
            pa, pbp = PB[0], PB[1]
            for k in range(8):
                P.op("pe", lambda e, k=k, pa=pa: e.matmul(pa[0:64, :], wsb[:, k, 2448:2512], h[:, k, :], start=(k == 0), stop=(k == 7)),
                     reads=[wsb, h], writes=[pa])
            if kind == "s":
                for k in range(8):
                    P.op("pe", lambda e, k=k, pbp=pbp: e.matmul(pbp[0:64, :], wpsb[:, k, :], h[:, k, :], start=(k == 0), stop=(k == 7)),
                         reads=[wpsb, h], writes=[pbp])
            ob = rb[ctr["rb"] % 3]
            ctr["rb"] += 1
            rope_or_copy(pa, pbp, kind, gi, ob)
            P.dma("sp", lambda e, ob=ob: e.dma_start(out=KRT[0:64, kc0:kc0 + 512], in_=ob[0:64, :]), reads=[ob],
                  writes=[(tKRT, r0)])
            for t in range(4):
                pz, pm = PB[6], PB[7]
                for k in range(8):
                    P.op("pe", lambda e, k=k, t=t: e.matmul(pz[:], h[:, k, t * 128:(t + 1) * 128], wsb[:, k, 1536:2048],
                                                            start=(k == 0), stop=(k == 7)), reads=[wsb, h], writes=[pz])
                for k in range(8):
                    P.op("pe", lambda e, k=k, t=t: e.matmul(pm[:, 0:464], h[:, k, t * 128:(t + 1) * 128], wsb[:, k, 2048:2512],
                                                            start=(k == 0), stop=(k == 7)), reads=[wsb, h], writes=[pm])
                row = r0 + t * 128
                zt = zst[t % 2]
                P.op("act", lambda e, zt=zt: e.activation(out=zt[:], in_=pz[:], func=AF.Silu), reads=[pz], writes=[zt])
                P.dma("sp", lambda e, zt=zt, row=row: e.dma_start(out=ZS[row:row + 128, :], in_=zt[:]), reads=[zt],
                      writes=[(tZS, row)])
                s_ = sm[t % 2]
                g_ = gb[t % 2]
                P.op("dve", lambda e, s_=s_: e.tensor_tensor(out=s_[:, 0:8], in0=pm[:, 0:8], in1=dtb[:], op=ALU.add),
                     reads=[pm, dtb], writes=[(s_, "a")])
                P.op("act", lambda e, s_=s_: e.activation(out=s_[:, 0:8], in_=s_[:, 0:8], func=AF.Exp), reads=[(s_, "a")],
                     writes=[(s_, "a")])
                P.op("act", lambda e, s_=s_: e.activation(out=s_[:, 0:8], in_=s_[:, 0:8], func=AF.Ln, bias=self.ones[:, 0:1]),
                     reads=[(s_, "a"), self.ones], writes=[(s_, "a")])
                P.op("dve", lambda e, s_=s_, g_=g_: e.tensor_tensor(out=g_[:, 0:8], in0=s_[:, 0:8], in1=nea[:], op=ALU.mult),
                     reads=[(s_, "a"), nea], writes=[(g_, 0)])
                P.op("act", lambda e, g_=g_: e.activation(out=g_[:, 8:16], in_=pm[:, 8:16], func=AF.Sigmoid), reads=[pm],
                     writes=[(g_, 1)])
                P.dma("sp", lambda e, g_=g_, row=row: e.dma_start(out=GB[row:row + 128, :], in_=g_[:]), reads=[g_],
                      writes=[(tGB, row)])
                cq = cqn[t % 2]
                P.op("pool", lambda e, s_=s_: e.memset(s_[:, 16:24], 0.0), writes=[(s_, "n")])
                P.op("act", lambda e, s_=s_: e.activation(out=junk[:, 0:256], in_=pm[:, 16:272], func=AF.Square,
                                                          accum_out=s_[:, 16:17]), reads=[pm, (s_, "n")],
                     writes=[junk, (s_, "n")])
                P.op("act", lambda e, s_=s_: e.activation(out=junk[:, 256:384], in_=pm[:, 272:400], func=AF.Square,
                                                          accum_out=s_[:, 17:18]), reads=[pm, (s_, "n")],
                     writes=[junk, (s_, "n")])
                P.op("act", lambda e, s_=s_: e.activation(out=s_[:, 18:19], in_=s_[:, 16:17], func=AF.Sqrt, scale=1.0 / 256,
                                                          bias=self.epsc[:, 0:1]), reads=[(s_, "n"), self.epsc],
                     writes=[(s_, "n")])
                P.op("act", lambda e, s_=s_: e.activation(out=s_[:, 19:20], in_=s_[:, 17:18], func=AF.Sqrt, scale=1.0 / 128,
                                                          bias=self.epsc[:, 0:1]), reads=[(s_, "n"), self.epsc],
                     writes=[(s_, "n")])
                P.op("dve", lambda e, s_=s_: e.reciprocal(out=s_[:, 20:22], in_=s_[:, 18:20]), reads=[(s_, "n")],
                     writes=[(s_, "n")])
                P.op("act", lambda e, s_=s_, cq=cq: e.activation(out=cq[:], in_=pm[:, 16:272], func=AF.Identity,
                                                                 scale=s_[:, 20:21]), reads=[pm, (s_, "n")], writes=[cq])
                ck = ckn[t % 2]
                P.op("dve", lambda e, s_=s_, ck=ck: e.scalar_tensor_tensor(out=ck[:], in0=pm[:, 272:400], scalar=s_[:, 21:22],
                                                                           in1=kvn[:], op0=ALU.mult, op1=ALU.mult),
                     reads=[pm, (s_, "n"), kvn], writes=[ck])
                if kind == "p":
                    P.dma("sp", lambda e, ck=ck, row=row: e.dma_start(out=self.dout["ckv_out"][row:row + 128, :], in_=ck[:]),
                          reads=[ck], writes=[(self.dtr["ckv_out"], row)])
                    kr = krt[t % 2]
                    P.op("dve", lambda e, kr=kr: e.tensor_copy(out=kr[:], in_=pm[:, 400:464]), reads=[pm], writes=[kr])
                    P.dma("sp", lambda e, kr=kr, row=row: e.dma_start(out=self.dout["krope_out"][row:row + 128, :], in_=kr[:]),
                          reads=[kr], writes=[(self.dtr["krope_out"], row)])
                pt = PB[4 + (t % 2)]
                for kc in range(2):
                    P.op("pe", lambda e, kc=kc, cq=cq, pt=pt: e.transpose(pt[:, kc * 128:(kc + 1) * 128],
                                                                           cq[:, kc * 128:(kc + 1) * 128], ident[:]),
                         reads=[cq, ident], writes=[pt])
                P.op("pe", lambda e, ck=ck, pt=pt: e.transpose(pt[:, 256:384], ck[:], ident[:]), reads=[ck, ident], writes=[pt])
                P.op("act", lambda e, pt=pt, t=t: e.activation(out=cqnT[:, :, t * 128:(t + 1) * 128],
                                                               in_=pt[:, 0:256].rearrange("p (a b) -> p a b", a=2),
                                                               func=AF.Copy), reads=[pt], writes=[(cqnT, t)])
                P.op("dve", lambda e, pt=pt, t=t: e.tensor_copy(out=ckT[:, t * 128:(t + 1) * 128], in_=pt[:, 256:384]),
                     reads=[pt], writes=[(ckT, t)])
            P.dma("sp", lambda e: e.dma_start(out=CKVT[:, kc0:kc0 + 512], in_=ckT[:]), reads=[ckT], writes=[(tCKVT, r0)])
            for hh in range(4):
                pq = PB[hh % 4]
                for kc in range(2):
                    P.op("pe", lambda e, kc=kc, hh=hh, pq=pq: e.matmul(pq[:], wuq[:, kc, hh * 192:hh * 192 + 128], cqnT[:, kc, :],
                                                                         start=(kc == 0), stop=(kc == 1)),
                         reads=[wuq, cqnT], writes=[pq])
                ob = rb[ctr["rb"] % 3]
                ctr["rb"] += 1
                P.op("act", lambda e, ob=ob, pq=pq: e.activation(out=ob[:], in_=pq[:], func=AF.Copy), reads=[pq], writes=[ob])
                P.dma("sp", lambda e, ob=ob, hh=hh: e.dma_start(out=QT[hh, 0:128, r0:r0 + 512], in_=ob[:]), reads=[ob],
                      writes=[(tQT, (hh, 0, r0))])
            for hh in range(4):
                pa, pbp = PB[(2 * hh) % 4], PB[(2 * hh + 1) % 4]
                for kc in range(2):
                    P.op("pe", lambda e, kc=kc, hh=hh, pa=pa: e.matmul(pa[0:64, :], wuq[:, kc, hh * 192 + 128:hh * 192 + 192],
                                                                         cqnT[:, kc, :], start=(kc == 0), stop=(kc == 1)),
                         reads=[wuq, cqnT], writes=[pa])
                if kind == "s":
                    for kc in range(2):
                        P.op("pe", lambda e, kc=kc, hh=hh, pbp=pbp: e.matmul(pbp[0:64, :], wuq[:, kc, 768 + hh * 64:768 + (hh + 1) * 64],
                                                                               cqnT[:, kc, :], start=(kc == 0), stop=(kc == 1)),
                             reads=[wuq, cqnT], writes=[pbp])
                ob = rb[ctr["rb"] % 3]
                ctr["rb"] += 1
                rope_or_copy(pa, pbp, kind, gi, ob)
                P.dma("sp", lambda e, ob=ob, hh=hh: e.dma_start(out=QT[hh, 128:192, r0:r0 + 512], in_=ob[0:64, :]), reads=[ob],
                      writes=[(tQT, (hh, 1, r0))])

        ng = len(gs)
        load(0)
        for gi in range(ng):
            r0, r = gs[gi]
            self.pro_hT(xg[gi % 2], ss[gi % 2], hT[gi % 2], l, s, r, [PB[4], PB[5]], xn, junk)
            if gi + 1 < ng:
                load(gi + 1)
            body(gi)
        P.end_phase()

    def ev_e2(self):
        c = self.cfg
        P = self.P
        di = self.din
        NT, NP = c.NT, c.NP
        NTILE = NT // 128
        ident, ones = self.ident, self.ones
        QKVT, GB, ZS, OT, OD = self.QKVT, self.GB, self.ZS, self.OT, self.OD
        tQKVT, tGB, tZS, tOT, tOD = [self.dtr[n] for n in ("QKVT", "GB", "ZS", "OT", "OD")]
        TBT = self.scratch("TBT", (NTILE, 8, 128, 128))
        QKT = self.scratch("QKT", (NTILE, 8, 128, 128))
        KDEC = self.scratch("KDEC", (NTILE, 8, 128, 128))
        VTM = self.scratch("VTM", (NTILE, 4, 128, 128))
        KNT = self.scratch("KNT", (4, 128, NT))
        QNT = self.scratch("QNT", (4, 128, NT))
        tTBT, tQKT, tKDEC, tVTM, tKNT, tQNT = [self.dtr[n] for n in ("TBT", "QKT", "KDEC", "VTM", "KNT", "QNT")]
        BIG = 30000.0
        e2cq = []
        for (l_, f_) in ((0, 1), (1, 0), (1, 1)):
            e2cq += self.cast_begin("ffn_in_%d%d" % (l_, f_), di["ffn_w_in"][l_, f_])

        for (row0, T, r, kind) in self.seqs():
            NTl = T // 128
            tile0 = row0 // 128
            P.mid_es = ExitStack()
            egc = P.sb("egc", [128, 2, NTl, 4], F32, mid=True)
            negc = P.sb("negc", [128, 2, NTl, 4], F32, mid=True)
            egl = P.sb("egl", [128, 2, NTl, 4], F32, mid=True)
            P.begin_phase()
            PB = [P.ps("pb%d" % i, [128, 512], F32) for i in range(8)]
            def mask(name, op, fill_where_false, flip=False):
                m = P.sb(name, [128, 128], F32)
                P.op("pool", lambda e: e.memset(m[:], 0.0 if fill_where_false != 0.0 else 1.0), writes=[m])
                P.op("pool", lambda e: e.affine_select(out=m[:], in_=m[:], pattern=[[1 if flip else -1, 128]], compare_op=op,
                                                       fill=fill_where_false, base=0, channel_multiplier=(-1 if flip else 1)),
                     reads=[m], writes=[m])
                return m
            Uincl = mask("Uincl", ALU.is_ge, 0.0, flip=True)
            Lincl = mask("Lincl", ALU.is_ge, 0.0)
            MA = [mask("MAf", ALU.is_gt, -BIG), mask("MAb", ALU.is_gt, -BIG, flip=True)]
            MB = [mask("MBf", ALU.is_ge, -BIG, flip=True), mask("MBb", ALU.is_ge, -BIG)]
            ld = P.sb("ld", [128, 128], F32)
            cw = P.sb("cw", [128, 60], F32)
            P.dma("sp", lambda e: e.dma_start(out=ld[0:60, :], in_=di["ev_conv_w"].rearrange("t (m p) -> (t m) p", p=128)),
                  writes=[ld])
            P.op("pe", lambda e: e.transpose(PB[0][:, 0:60], ld[0:60, :], ident[0:60, 0:60]), reads=[ld, ident], writes=[PB[0]])
            P.op("dve", lambda e: e.tensor_copy(out=cw[:], in_=PB[0][:, 0:60]), reads=[PB[0]], writes=[cw])
            GBt = P.sb("GBt", [128, NTl, 16], F32)
            nq = max(1, NTl // 8)
            for q in range(0, NTl, nq):
                P.dma("sp", lambda e, q=q: e.dma_start(
                    out=GBt[:, q:q + nq, :], in_=GB[row0 + q * 128:row0 + (q + nq) * 128, :].rearrange("(c p) f -> p c f", p=128)),
                    reads=[tGB], writes=[(GBt, q)])
            gfb = P.sb("gfb", [128, 2, NTl, 4], F32)
            for d in range(2):
                P.op("dve", lambda e, d=d: e.tensor_copy(out=gfb[:, d], in_=GBt[:, :, d * 4:(d + 1) * 4]), reads=[GBt],
                     writes=[(gfb, d)])
            gc = P.sb("gc", [128, 2, NTl, 4], F32)
            ngc = P.sb("ngc", [128, 2, NTl, 4], F32)
            kds = P.sb("kds", [128, 2, NTl, 4], F32)
            nbeta = P.sb("nbeta", [128, NTl, 8], F32)
            W = NTl * 4
            for d in range(2):
                P.op("pe", lambda e, d=d: e.matmul(PB[0][:, d * W:(d + 1) * W], (Uincl if d == 0 else Lincl)[:],
                                                   gfb[:, d].rearrange("p c h -> p (c h)"), start=True, stop=True),
                     reads=[Uincl, Lincl, gfb], writes=[PB[0]])
            P.op("pe", lambda e: e.matmul(PB[1][:, 0:2 * W], ones[:], gfb[:].rearrange("p d c h -> p (d c h)"), start=True,
                                          stop=True), reads=[ones, gfb], writes=[PB[1]])
            fl = lambda t_: t_[:].rearrange("p d c h -> p (d c h)")
            P.op("dve", lambda e: e.tensor_copy(out=fl(gc), in_=PB[0][:, 0:2 * W]), reads=[PB[0]], writes=[gc])
            P.op("dve", lambda e: e.tensor_scalar(out=fl(ngc), in0=PB[0][:, 0:2 * W], scalar1=-1.0, scalar2=None, op0=ALU.mult),
                 reads=[PB[0]], writes=[ngc])
            P.op("act", lambda e: e.activation(out=fl(egc), in_=PB[0][:, 0:2 * W], func=AF.Exp), reads=[PB[0]], writes=[egc])
            P.op("dve", lambda e: e.tensor_scalar(out=fl(negc), in0=fl(egc), scalar1=-1.0, scalar2=None, op0=ALU.mult),
                 reads=[egc], writes=[negc])
            P.op("act", lambda e: e.activation(out=fl(egl), in_=PB[1][:, 0:2 * W], func=AF.Exp), reads=[PB[1]], writes=[egl])
            P.op("dve", lambda e: e.tensor_tensor(out=fl(kds), in0=PB[1][:, 0:2 * W], in1=fl(gc), op=ALU.subtract),
                 reads=[PB[1], gc], writes=[kds])
            P.op("act", lambda e: e.activation(out=fl(kds), in_=fl(kds), func=AF.Exp), reads=[kds], writes=[kds])
            P.op("dve", lambda e: e.tensor_scalar(out=nbeta[:], in0=GBt[:, :, 8:16], scalar1=-1.0, scalar2=None, op0=ALU.mult),
                 reads=[GBt], writes=[nbeta])

            XT = [P.sb("XT%d" % i, [128, T + 4], F32) for i in range(2)]
            AC = [P.sb("AC%d" % i, [128, T], F32) for i in range(3)]
            sq = P.sb("sq", [128, 512], F32)
            rin = P.sb("rin", [128, 512], F32)
            TB = 2
            NU = 2 * TB
            dgs = [P.sb("dg%d" % i, [128, 256], F32) for i in range(TB)]
            Nt = [P.sb("Nt%d" % i, [128, 128], F32R) for i in range(2 * NU)]
            identr = P.sb("identr", [128, 128], F32R)
            P.op("dve", lambda e: e.tensor_copy(out=identr[:], in_=ident[:]), reads=[ident], writes=[identr])
            Qf = [P.sb("Qf%d" % i, [128, 128], F32R) for i in range(NU)]
            Nl = [[P.sb("Nl%d_%d" % (i, j), [128, 128], F32R) for j in range(2)] for i in range(NU)]
            Ql = [[P.sb("Ql%d_%d" % (i, j), [128, 128], F32R) for j in range(2)] for i in range(NU)]
            Dm = [P.sb("Dm%d" % i, [128, 128], F32R) for i in range(NU)]
            Em = [P.sb("Em%d" % i, [128, 128], F32R) for i in range(NU)]
            Wt = [P.sb("Wt%d" % i, [128, 128], F32R) for i in range(NU)]
            Zt = [P.sb("Zt%d" % i, [128, 128], F32R) for i in range(NU)]
            No = [P.sb("No%d" % i, [128, 128], F32R) for i in range(NU)]
            Qo = [P.sb("Qo%d" % i, [128, 128], F32R) for i in range(NU)]
            SAME = {}
            for msz in (16, 32, 64):
                nb_ = 128 // msz
                bt = P.sb("bt%d" % msz, [8, 128], F32)
                P.op("pool", lambda e, bt=bt: e.memset(bt[:], 1.0), writes=[bt])
                P.op("pool", lambda e, bt=bt, msz=msz: e.affine_select(out=bt[:], in_=bt[:], pattern=[[1, 128]], compare_op=ALU.is_ge,
                                                                     fill=0.0, base=0, channel_multiplier=-msz), reads=[bt], writes=[bt])
                P.op("pool", lambda e, bt=bt, msz=msz: e.affine_select(out=bt[:], in_=bt[:], pattern=[[-1, 128]], compare_op=ALU.is_ge,
                                                                     fill=0.0, base=msz - 1, channel_multiplier=msz), reads=[bt], writes=[bt])
                sm_ = P.sb("same%d" % msz, [128, 128], F32)
                P.op("pe", lambda e, bt=bt, nb_=nb_: e.matmul(PB[0][:, 0:128], bt[0:nb_, :], bt[0:nb_, :], start=True, stop=True),
                     reads=[bt], writes=[PB[0]])
                P.op("dve", lambda e, sm_=sm_: e.tensor_copy(out=sm_[:], in_=PB[0][:, 0:128]), reads=[PB[0]], writes=[sm_])
                SAME[msz] = sm_
            SYM = {}
            for msz, big in ((16, 32), (32, 64)):
                t_ = P.sb("sym%d" % msz, [128, 128], F32)
                P.op("dve", lambda e, t_=t_, msz=msz, big=big: e.tensor_tensor(out=t_[:], in0=SAME[big][:], in1=SAME[msz][:], op=ALU.subtract),
                     reads=[SAME[big], SAME[msz]], writes=[t_])
                SYM[msz] = t_
            t_ = P.sb("sym64", [128, 128], F32)
            P.op("dve", lambda e, t_=t_: e.tensor_tensor(out=t_[:], in0=ones[:], in1=SAME[64][:], op=ALU.subtract),
                 reads=[ones, SAME[64]], writes=[t_])
            SYM[64] = t_
            dA = [P.sb("dA%d" % i, [128, 128], F32) for i in range(NU)]
            dB = [P.sb("dB%d" % i, [128, 128], F32) for i in range(NU)]
            qk = [P.sb("qk%d" % i, [128, 128], F32) for i in range(NU)]
            tb = [P.sb("tb%d" % i, [128, 128], F32) for i in range(NU)]
            kd = [P.sb("kd%d" % i, [128, 128], F32) for i in range(NU)]
            vt = [P.sb("vt%d" % i, [128, 128], F32) for i in range(2)]
            uc = [0]
            if kind == "s":
                hin, hout, thin, thout = self.pair_buf("HALO", 1536, 2)
                P.dma("sp", lambda e: e.dma_start(out=hin, in_=QKVT[:, row0 + T - 2:row0 + T]), reads=[tQKVT], writes=[thin])
                self.pair_gather(hin, hout, thin, thout)
                hal = P.sb("hal", [128, 12, 2, 2], F32)
                for rk in range(2):
                    P.dma("sp", lambda e, rk=rk: e.dma_start(out=hal[:, :, rk, :], in_=hout[rk * 1536:(rk + 1) * 1536, :].rearrange("(m p) c -> p m c", p=128)),
                          reads=[thout], writes=[(hal, rk)])
                selw = P.sb("selw", [128, 2], F32)
                P.dma("sp", lambda e: e.dma_start(out=selw[:], in_=di["selw"]), writes=[selw])
                hsel = P.sb("hsel", [128, 12, 2], F32)
                P.op("dve", lambda e: e.tensor_scalar(out=hsel[:], in0=hal[:, :, 0, :], scalar1=selw[:, 0:1], scalar2=None, op0=ALU.mult),
                     reads=[hal, selw], writes=[hsel])
                P.op("dve", lambda e: e.scalar_tensor_tensor(out=hsel[:], in0=hal[:, :, 1, :], scalar=selw[:, 1:2], in1=hsel[:], op0=ALU.mult,
                                                             op1=ALU.add), reads=[hal, selw, hsel], writes=[hsel])
            for hh in range(4):
                for part in range(3):
                    X = XT[part % 2]
                    m = part * 4 + hh
                    P.op("pool", lambda e, X=X: e.memset(X[:, 0:2], 0.0), writes=[(X, "h0")])
                    if kind == "s":
                        P.op("pool", lambda e, X=X, m=m: e.tensor_copy(out=X[:, T + 2:T + 3], in_=hsel[:, m, 1:2]), reads=[hsel], writes=[(X, "h1")])
                        P.op("pool", lambda e, X=X, m=m: e.tensor_copy(out=X[:, T + 3:T + 4], in_=hsel[:, m, 0:1]), reads=[hsel], writes=[(X, "h2")])
                    else:
                        P.op("pool", lambda e, X=X: e.memset(X[:, T + 2:T + 4], 0.0), writes=[(X, "h1")])
                    P.dma("sp", lambda e, X=X, m=m: e.dma_start(out=X[:, 2:T + 2], in_=QKVT[m * 128:(m + 1) * 128, row0:row0 + T]),
                          reads=[tQKVT], writes=[(X, "b")])
                    A = AC[part]
                    for tap in range(5):
                        if tap == 0:
                            P.op("dve", lambda e, X=X, A=A, m=m: e.tensor_scalar(out=A[:], in0=X[:, 0:T], scalar1=cw[:, m:m + 1],
                                                                                 scalar2=None, op0=ALU.mult),
                                 reads=[X, cw], writes=[A])
                        else:
                            P.op("dve", lambda e, X=X, A=A, m=m, tap=tap: e.scalar_tensor_tensor(
                                out=A[:], in0=X[:, tap:tap + T], scalar=cw[:, tap * 12 + m:tap * 12 + m + 1], in1=A[:],
                                op0=ALU.mult, op1=ALU.add), reads=[X, cw, A], writes=[A])
                    P.op("act", lambda e, A=A: e.activation(out=A[:], in_=A[:], func=AF.Silu), reads=[A], writes=[A])
                    if part < 2:
                        for b0 in range(0, T, 512):
                            bw = min(512, T - b0)
                            P.op("act", lambda e, A=A, b0=b0, bw=bw: e.activation(out=sq[:, 0:bw], in_=A[:, b0:b0 + bw], func=AF.Square),
                                 reads=[A], writes=[sq])
                            P.op("pe", lambda e, bw=bw: e.matmul(PB[0][:, 0:bw], ones[:], sq[:, 0:bw], start=True, stop=True),
                                 reads=[ones, sq], writes=[PB[0]])
                            P.op("act", lambda e, bw=bw: e.activation(out=rin[:, 0:bw], in_=PB[0][:, 0:bw], func=AF.Sqrt,
                                                                      bias=self.epsc[:, 0:1]), reads=[PB[0], self.epsc], writes=[rin])
                            P.op("dve", lambda e, bw=bw: e.reciprocal(out=rin[:, 0:bw], in_=rin[:, 0:bw]), reads=[rin], writes=[rin])
                            if part == 0:
                                P.op("dve", lambda e, A=A, b0=b0, bw=bw: e.scalar_tensor_tensor(
                                    out=A[:, b0:b0 + bw], in0=A[:, b0:b0 + bw], scalar=128.0 ** -0.5, in1=rin[:, 0:bw],
                                    op0=ALU.mult, op1=ALU.mult), reads=[A, rin], writes=[A])
                            else:
                                P.op("dve", lambda e, A=A, b0=b0, bw=bw: e.tensor_tensor(
                                    out=A[:, b0:b0 + bw], in0=A[:, b0:b0 + bw], in1=rin[:, 0:bw], op=ALU.mult),
                                    reads=[A, rin], writes=[A])
                qn, kn, vv = AC
                P.dma("sp", lambda e, hh=hh: e.dma_start(out=QNT[hh, :, row0:row0 + T], in_=qn[:]), reads=[qn],
                      writes=[(tQNT, (hh, row0))])
                P.dma("sp", lambda e, hh=hh: e.dma_start(out=KNT[hh, :, row0:row0 + T], in_=kn[:]), reads=[kn],
                      writes=[(tKNT, (hh, row0))])
                for cc0 in range(0, NTl, TB):
                    units = []
                    if kind == "s":
                        self.cast_piece(e2cq)
                        self.cast_piece(e2cq)
                    for bi, cc in enumerate(range(cc0, min(NTl, cc0 + TB))):
                        dg = dgs[bi]
                        tl = tile0 + cc
                        csl = slice(cc * 128, (cc + 1) * 128)
                        pmisc = PB[1 + bi]
                        P.op("pe", lambda e, csl=csl, pmisc=pmisc: e.transpose(pmisc[:, 0:128], kn[:, csl], ident[:]), reads=[kn, ident], writes=[pmisc])
                        P.op("pe", lambda e, csl=csl, pmisc=pmisc: e.transpose(pmisc[:, 128:256], vv[:, csl], ident[:]), reads=[vv, ident], writes=[pmisc])
                        P.op("pe", lambda e, csl=csl, pmisc=pmisc: e.matmul(pmisc[:, 256:384], kn[:, csl], kn[:, csl], start=True, stop=True),
                             reads=[kn], writes=[pmisc])
                        P.op("pe", lambda e, csl=csl, pmisc=pmisc: e.matmul(pmisc[:, 384:512], kn[:, csl], qn[:, csl], start=True, stop=True),
                             reads=[kn, qn], writes=[pmisc])
                        v_ = vt[bi]
                        P.op("dve", lambda e, v_=v_, pmisc=pmisc: e.tensor_copy(out=v_[:], in_=pmisc[:, 128:256]), reads=[pmisc], writes=[v_])
                        P.dma("sp", lambda e, v_=v_, tl=tl, hh=hh: e.dma_start(out=VTM[tl, hh], in_=v_[:]), reads=[v_],
                              writes=[(tVTM, (tl, hh))])
                        for d in range(2):
                            P.op("dve", lambda e, d=d, cc=cc, hh=hh, dg=dg: e.tensor_scalar(out=dg[:, d * 128:(d + 1) * 128], in0=ident[:],
                                                                                     scalar1=gc[:, d, cc, hh:hh + 1], scalar2=None,
                                                                                     op0=ALU.mult), reads=[ident, gc], writes=[(dg, d)])
                        pgb = PB[3]
                        pg0 = bi * 256
                        P.op("pe", lambda e, dg=dg, pg0=pg0: e.matmul(pgb[:, pg0:pg0 + 256], ones[:], dg[:], start=True, stop=True), reads=[ones, dg], writes=[pgb])
                        for d in range(2):
                            u = bi * 2 + d
                            dh = d * 4 + hh
                            col = lambda t_, d=d, cc=cc, hh=hh: t_[:, d, cc, hh:hh + 1]
                            P.op("act", lambda e, u=u, d=d, cc=cc, hh=hh, pmisc=pmisc: e.activation(out=kd[u][:], in_=pmisc[:, 0:128], func=AF.Identity,
                                                                                       scale=kds[:, d, cc, hh:hh + 1]),
                                 reads=[pmisc, kds], writes=[kd[u]])
                            P.dma("sp", lambda e, u=u, tl=tl, dh=dh: e.dma_start(out=KDEC[tl, dh], in_=kd[u][:]), reads=[kd[u]],
                                  writes=[(tKDEC, (tl, dh))])
                            P.op("dve", lambda e, u=u, d=d, pg0=pg0: e.scalar_tensor_tensor(out=dA[u][:], in0=pgb[:, pg0 + d * 128:pg0 + (d + 1) * 128], scalar=-1.0,
                                                                                   in1=MA[d][:], op0=ALU.mult, op1=ALU.add),
                                 reads=[pgb, MA[d]], writes=[dA[u]])
                            P.op("act", lambda e, u=u, d=d, cc=cc, hh=hh: e.activation(out=dA[u][:], in_=dA[u][:], func=AF.Exp,
                                                                                       bias=gc[:, d, cc, hh:hh + 1]),
                                 reads=[dA[u], gc], writes=[dA[u]])
                            P.op("dve", lambda e, u=u, d=d, pg0=pg0: e.tensor_tensor(out=dB[u][:], in0=pgb[:, pg0 + d * 128:pg0 + (d + 1) * 128], in1=MB[d][:],
                                                                            op=ALU.add), reads=[pgb, MB[d]], writes=[dB[u]])
                            P.op("act", lambda e, u=u, d=d, cc=cc, hh=hh: e.activation(out=dB[u][:], in_=dB[u][:], func=AF.Exp,
                                                                                       bias=ngc[:, d, cc, hh:hh + 1]),
                                 reads=[dB[u], ngc], writes=[dB[u]])
                            n0 = Nt[2 * u]
                            P.op("dve", lambda e, u=u, n0=n0, cc=cc, dh=dh, pmisc=pmisc: e.scalar_tensor_tensor(
                                out=n0[:], in0=pmisc[:, 256:384], scalar=nbeta[:, cc, dh:dh + 1], in1=dA[u][:], op0=ALU.mult, op1=ALU.mult),
                                reads=[pmisc, nbeta, dA[u]], writes=[n0])
                            P.op("dve", lambda e, u=u, pmisc=pmisc: e.tensor_tensor(out=qk[u][:], in0=pmisc[:, 384:512], in1=dB[u][:], op=ALU.mult),
                                 reads=[pmisc, dB[u]], writes=[qk[u]])
                            P.dma("sp", lambda e, u=u, tl=tl, dh=dh: e.dma_start(out=QKT[tl, dh], in_=qk[u][:]), reads=[qk[u]],
                                  writes=[(tQKT, (tl, dh))])
                            units.append((u, d, dh, cc, tl))
                    pw = {u_[0]: PB[4 + u_[0]] for u_ in units}
                    U = [(u, pw[u], Nt[2 * u], cc, tl, dh) for (u, d, dh, cc, tl) in units]
                    for (u, p_, Nf, cc, tl, dh) in U:
                        P.op("pe", lambda e, p_=p_, Nf=Nf: e.transpose(p_[:, 0:128].bitcast(F32R), Nf[:], identr[:]), reads=[Nf, identr], writes=[p_])
                    for (u, p_, Nf, cc, tl, dh) in U:
                        P.op("act", lambda e, u=u, p_=p_: e.activation(out=Qf[u][:], in_=p_[:, 0:128], func=AF.Copy), reads=[p_], writes=[Qf[u]])
                        P.op("dve", lambda e, u=u, p_=p_: e.tensor_tensor(out=Ql[u][0][:], in0=p_[:, 0:128], in1=SAME[16][:], op=ALU.mult),
                             reads=[p_, SAME[16]], writes=[Ql[u][0]])
                        P.op("pool", lambda e, u=u, Nf=Nf: e.tensor_tensor(out=Nl[u][0][:], in0=Nf[:], in1=SAME[16][:], op=ALU.mult),
                             reads=[Nf, SAME[16]], writes=[Nl[u][0]])
                    for (u, p_, Nf, cc, tl, dh) in U:
                        P.op("pool", lambda e, u=u: e.tensor_tensor(out=Dm[u][:], in0=Nl[u][0][:], in1=ident[:], op=ALU.add),
                             reads=[Nl[u][0], ident], writes=[Dm[u]])
                        P.op("dve", lambda e, u=u: e.tensor_tensor(out=Em[u][:], in0=Ql[u][0][:], in1=ident[:], op=ALU.add),
                             reads=[Ql[u][0], ident], writes=[Em[u]])
                    for lvl in range(1, 4):
                        for (u, p_, Nf, cc, tl, dh) in U:
                            nc_, qc_ = Nl[u][(lvl - 1) % 2], Ql[u][(lvl - 1) % 2]
                            P.op("pe", lambda e, p_=p_, nc_=nc_, qc_=qc_: e.matmul(p_[:, 0:128], qc_[:], nc_[:], start=True, stop=True),
                                 reads=[nc_, qc_], writes=[p_])
                            P.op("pe", lambda e, p_=p_, nc_=nc_, qc_=qc_: e.matmul(p_[:, 128:256], nc_[:], qc_[:], start=True, stop=True),
                                 reads=[nc_, qc_], writes=[p_])
                        for (u, p_, Nf, cc, tl, dh) in U:
                            nn_, qn_ = Nl[u][lvl % 2], Ql[u][lvl % 2]
                            P.op("act", lambda e, p_=p_, nn_=nn_: e.activation(out=nn_[:], in_=p_[:, 0:128], func=AF.Copy), reads=[p_], writes=[nn_])
                            P.op("dve", lambda e, p_=p_, qn_=qn_: e.tensor_copy(out=qn_[:], in_=p_[:, 128:256]), reads=[p_], writes=[qn_])
                        for (u, p_, Nf, cc, tl, dh) in U:
                            nn_, qn_ = Nl[u][lvl % 2], Ql[u][lvl % 2]
                            P.op("pe", lambda e, p_=p_, nn_=nn_, u=u: e.matmul(p_[:, 256:384], nn_[:], Em[u][:], start=True, stop=True),
                                 reads=[nn_, Em[u]], writes=[p_])
                            P.op("pe", lambda e, p_=p_, qn_=qn_, u=u: e.matmul(p_[:, 384:512], qn_[:], Dm[u][:], start=True, stop=True),
                                 reads=[qn_, Dm[u]], writes=[p_])
                        for (u, p_, Nf, cc, tl, dh) in U:
                            P.op("dve", lambda e, p_=p_, u=u: e.tensor_tensor(out=Em[u][:], in0=Em[u][:], in1=p_[:, 256:384], op=ALU.add),
                                 reads=[p_, Em[u]], writes=[Em[u]])
                            P.op("dve", lambda e, p_=p_, u=u: e.tensor_tensor(out=Dm[u][:], in0=Dm[u][:], in1=p_[:, 384:512], op=ALU.add),
                                 reads=[p_, Dm[u]], writes=[Dm[u]])
                    for mi, msz in enumerate((16, 32, 64)):
                        last = (mi == 2)
                        for (u, p_, Nf, cc, tl, dh) in U:
                            P.op("pool", lambda e, u=u, Nf=Nf, msz=msz: e.tensor_tensor(out=No[u][:], in0=Nf[:], in1=SYM[msz][:], op=ALU.mult),
                                 reads=[Nf, SYM[msz]], writes=[No[u]])
                            if not last:
                                P.op("pool", lambda e, u=u, msz=msz: e.tensor_tensor(out=Qo[u][:], in0=Qf[u][:], in1=SYM[msz][:], op=ALU.mult),
                                     reads=[Qf[u], SYM[msz]], writes=[Qo[u]])
                        for (u, p_, Nf, cc, tl, dh) in U:
                            if not last:
                                P.op("pe", lambda e, p_=p_, u=u: e.matmul(p_[:, 0:128], Qo[u][:], Dm[u][:], start=True, stop=True),
                                     reads=[Qo[u], Dm[u]], writes=[p_])
                            P.op("pe", lambda e, p_=p_, u=u: e.matmul(p_[:, 128:256], No[u][:], Em[u][:], start=True, stop=True),
                                 reads=[No[u], Em[u]], writes=[p_])
                        for (u, p_, Nf, cc, tl, dh) in U:
                            if not last:
                                P.op("act", lambda e, p_=p_, u=u: e.activation(out=Wt[u][:], in_=p_[:, 0:128], func=AF.Copy), reads=[p_], writes=[Wt[u]])
                            P.op("dve", lambda e, p_=p_, u=u: e.tensor_copy(out=Zt[u][:], in_=p_[:, 128:256]), reads=[p_], writes=[Zt[u]])
                        for (u, p_, Nf, cc, tl, dh) in U:
                            if not last:
                                P.op("pe", lambda e, p_=p_, u=u: e.matmul(p_[:, 256:384], Em[u][:], Wt[u][:], start=True, stop=True),
                                     reads=[Em[u], Wt[u]], writes=[p_])
                            P.op("pe", lambda e, p_=p_, u=u: e.matmul(p_[:, 384:512], Dm[u][:], Zt[u][:], start=True, stop=True),
                                 reads=[Dm[u], Zt[u]], writes=[p_])
                        for (u, p_, Nf, cc, tl, dh) in U:
                            if not last:
                                P.op("dve", lambda e, p_=p_, u=u: e.tensor_tensor(out=Dm[u][:], in0=Dm[u][:], in1=p_[:, 256:384], op=ALU.add),
                                     reads=[p_, Dm[u]], writes=[Dm[u]])
                            P.op("dve", lambda e, p_=p_, u=u: e.tensor_tensor(out=Em[u][:], in0=Em[u][:], in1=p_[:, 384:512], op=ALU.add),
                                 reads=[p_, Em[u]], writes=[Em[u]])
                    for (u, p_, Nf, cc, tl, dh) in U:
                        P.op("act", lambda e, u=u, cc=cc, dh=dh: e.activation(out=tb[u][:], in_=Em[u][:], func=AF.Identity,
                                                                              scale=GBt[:, cc, 8 + dh:9 + dh]), reads=[Em[u], GBt],
                             writes=[tb[u]])
                        P.dma("sp", lambda e, u=u, tl=tl, dh=dh: e.dma_start(out=TBT[tl, dh], in_=tb[u][:]), reads=[tb[u]],
                              writes=[(tTBT, (tl, dh))])
            if kind == "s":
                while e2cq:
                    self.cast_piece(e2cq)
            P.end_phase()
            P.begin_phase()
            PB = [P.ps("pb%d" % i, [128, 512], F32) for i in range(8)]
            S = [[P.sb("S%d_%d" % (dh, i), [128, 128], F32) for i in range(2)] for dh in range(8)]
            for dh in range(8):
                if kind == "p":
                    P.op("pool", lambda e, dh=dh: e.memset(S[dh][0][:], 0.0), writes=[S[dh][0]])
                else:
                    P.dma("sp", lambda e, dh=dh: e.dma_start(out=S[dh][0][:], in_=di["state_gdn"][dh // 4, dh % 4]), writes=[S[dh][0]])
            NB = 2
            L_tb = [[P.sb("Ltb%d_%d" % (dh, i), [128, 128], F32) for i in range(NB)] for dh in range(8)]
            L_qk = [[P.sb("Lqk%d_%d" % (dh, i), [128, 128], F32) for i in range(NB)] for dh in range(8)]
            L_kd = [[P.sb("Lkd%d_%d" % (dh, i), [128, 128], F32) for i in range(NB)] for dh in range(8)]
            L_kn = [[P.sb("Lkn%d_%d" % (dh, i), [128, 128], F32) for i in range(NB)] for dh in range(8)]
            L_qn = [[P.sb("Lqn%d_%d" % (dh, i), [128, 128], F32) for i in range(NB)] for dh in range(8)]
            L_v = [[P.sb("Lv%d_%d" % (dh, i), [128, 128], F32) for i in range(NB)] for dh in range(8)]
            Rt = [P.sb("Rt%d" % dh, [128, 128], F32) for dh in range(8)]
            vn = [P.sb("vn%d" % dh, [128, 128], F32) for dh in range(8)]
            o1 = [P.sb("o1_%d" % dh, [128, 128], F32) for dh in range(8)]
            ob = [[P.sb("ob%d_%d" % (dh, i), [128, 128], F32) for i in range(2)] for dh in range(8)]

            def sload(i, dirs):
                for d in dirs:
                    cc = i if d == 0 else NTl - 1 - i
                    tl = tile0 + cc
                    rows = slice(row0 + cc * 128, row0 + (cc + 1) * 128)
                    for hh in range(4):
                        dh = d * 4 + hh
                        b = i % NB
                        P.dma("sp", lambda e, dh=dh, b=b, tl=tl: e.dma_start(out=L_tb[dh][b][:], in_=TBT[tl, dh]), reads=[tTBT],
                              writes=[L_tb[dh][b]])
                        P.dma("sp", lambda e, dh=dh, b=b, tl=tl: e.dma_start(out=L_qk[dh][b][:], in_=QKT[tl, dh]), reads=[tQKT],
                              writes=[L_qk[dh][b]])
                        P.dma("sp", lambda e, dh=dh, b=b, tl=tl: e.dma_start(out=L_kd[dh][b][:], in_=KDEC[tl, dh]), reads=[tKDEC],
                              writes=[L_kd[dh][b]])
                        P.dma("sp", lambda e, dh=dh, b=b, hh=hh, rows=rows: e.dma_start(out=L_kn[dh][b][:], in_=KNT[hh, :, rows]),
                              reads=[tKNT], writes=[L_kn[dh][b]])
                        P.dma("sp", lambda e, dh=dh, b=b, hh=hh, rows=rows: e.dma_start(out=L_qn[dh][b][:], in_=QNT[hh, :, rows]),
                              reads=[tQNT], writes=[L_qn[dh][b]])
                        P.dma("sp", lambda e, dh=dh, b=b, tl=tl, hh=hh: e.dma_start(out=L_v[dh][b][:], in_=VTM[tl, hh]), reads=[tVTM],
                              writes=[L_v[dh][b]])

            def scan_pass(dirs):
                sload(0, dirs)
                for i in range(NTl):
                    if i + 1 < NTl:
                        sload(i + 1, dirs)
                    b = i % NB
                    CH = []
                    for d in dirs:
                        cc = i if d == 0 else NTl - 1 - i
                        for hh in range(4):
                            dh = d * 4 + hh
                            CH.append((d, cc, hh, dh, S[dh][i % 2], S[dh][(i + 1) % 2], PB[dh], row0 + cc * 128))
                    for (d, cc, hh, dh, Sc, Sn, pp, rows0) in CH:
                        P.op("pe", lambda e, dh=dh, b=b, Sc=Sc, pp=pp: e.matmul(pp[:, 0:128], L_kn[dh][b][:], Sc[:], start=True, stop=True),
                             reads=[L_kn[dh][b], Sc], writes=[pp])
                        P.op("pe", lambda e, dh=dh, b=b, Sc=Sc, pp=pp: e.matmul(pp[:, 128:256], L_qn[dh][b][:], Sc[:], start=True, stop=True),
                             reads=[L_qn[dh][b], Sc], writes=[pp])
                    for (d, cc, hh, dh, Sc, Sn, pp, rows0) in CH:
                        P.op("dve", lambda e, dh=dh, b=b, pp=pp, d=d, cc=cc, hh=hh: e.scalar_tensor_tensor(
                            out=Rt[dh][:], in0=pp[:, 0:128], scalar=negc[:, d, cc, hh:hh + 1], in1=L_v[dh][b][:], op0=ALU.mult, op1=ALU.add),
                            reads=[pp, negc, L_v[dh][b]], writes=[Rt[dh]])
                    for (d, cc, hh, dh, Sc, Sn, pp, rows0) in CH:
                        P.op("pe", lambda e, dh=dh, b=b, pp=pp: e.matmul(pp[:, 256:384], L_tb[dh][b][:], Rt[dh][:], start=True, stop=True),
                             reads=[L_tb[dh][b], Rt[dh]], writes=[pp])
                    for (d, cc, hh, dh, Sc, Sn, pp, rows0) in CH:
                        P.op("act", lambda e, dh=dh, pp=pp: e.activation(out=vn[dh][:], in_=pp[:, 256:384], func=AF.Copy), reads=[pp],
                             writes=[vn[dh]])
                        P.op("act", lambda e, dh=dh, pp=pp, d=d, cc=cc, hh=hh: e.activation(out=o1[dh][:], in_=pp[:, 128:256], func=AF.Identity,
                                                                                            scale=egc[:, d, cc, hh:hh + 1]),
                             reads=[pp, egc], writes=[o1[dh]])
                    for (d, cc, hh, dh, Sc, Sn, pp, rows0) in CH:
                        P.op("pe", lambda e, dh=dh, b=b, pp=pp: e.matmul(pp[:, 0:128], L_kd[dh][b][:], vn[dh][:], start=True, stop=True),
                             reads=[L_kd[dh][b], vn[dh]], writes=[pp])
                        P.op("pe", lambda e, dh=dh, b=b, pp=pp: e.matmul(pp[:, 384:512], L_qk[dh][b][:], vn[dh][:], start=True, stop=True),
                             reads=[L_qk[dh][b], vn[dh]], writes=[pp])
                    for (d, cc, hh, dh, Sc, Sn, pp, rows0) in CH:
                        P.op("dve", lambda e, dh=dh, pp=pp, Sc=Sc, Sn=Sn, d=d, cc=cc, hh=hh: e.scalar_tensor_tensor(
                            out=Sn[:], in0=Sc[:], scalar=egl[:, d, cc, hh:hh + 1], in1=pp[:, 0:128], op0=ALU.mult, op1=ALU.add),
                            reads=[pp, Sc, egl], writes=[Sn])
                    for (d, cc, hh, dh, Sc, Sn, pp, rows0) in CH:
                        o_ = ob[dh][i % 2]
                        P.op("dve", lambda e, dh=dh, pp=pp, o_=o_: e.tensor_tensor(out=o_[:], in0=o1[dh][:], in1=pp[:, 384:512], op=ALU.add),
                             reads=[pp, o1[dh]], writes=[o_])
                        P.dma("sp", lambda e, o_=o_, d=d, hh=hh, rows0=rows0: e.dma_start(
                            out=OD[d, rows0:rows0 + 128, hh * 128:(hh + 1) * 128], in_=o_[:]), reads=[o_], writes=[(tOD, (d, rows0, hh))])
            if kind == "p":
                scan_pass((0, 1))
            else:
                scan_pass((0,))
                gin, gout, tgin, tgout = self.pair_buf("GST", 512, 128)
                for hh in range(4):
                    Sf = S[hh][NTl % 2]
                    P.dma("sp", lambda e, hh=hh, Sf=Sf: e.dma_start(out=gin[hh * 128:(hh + 1) * 128, :], in_=Sf[:]), reads=[Sf],
                          writes=[(tgin, hh)])
                self.pair_gather(gin, gout, tgin, tgout)
                cand = P.sb("cand", [128, 2, 4, 128], F32)
                for rk in range(2):
                    P.dma("sp", lambda e, rk=rk: e.dma_start(out=cand[:, rk], in_=gout[rk * 512:(rk + 1) * 512, :].rearrange("(h p) c -> p h c", p=128)),
                          reads=[tgout], writes=[(cand, rk)])
                selw2 = P.sb("selw2", [128, 2], F32)
                P.dma("sp", lambda e: e.dma_start(out=selw2[:], in_=di["selw"]), writes=[selw2])
                for hh in range(4):
                    P.op("dve", lambda e, hh=hh: e.tensor_scalar(out=S[4 + hh][0][:], in0=cand[:, 0, hh, :], scalar1=selw2[:, 0:1], scalar2=None,
                                                                 op0=ALU.mult), reads=[cand, selw2], writes=[S[4 + hh][0]])
                    P.op("dve", lambda e, hh=hh: e.scalar_tensor_tensor(out=S[4 + hh][0][:], in0=cand[:, 1, hh, :], scalar=selw2[:, 1:2],
                                                                        in1=S[4 + hh][0][:], op0=ALU.mult, op1=ALU.add),
                         reads=[cand, selw2, S[4 + hh][0]], writes=[S[4 + hh][0]])
                scan_pass((1,))
            if kind == "p":
                si = row0 // c.TP
                for dh in range(8):
                    Sf = S[dh][NTl % 2]
                    P.dma("sp", lambda e, dh=dh, Sf=Sf, si=si: e.dma_start(out=self.dout["gdn_out"][si, dh // 4, dh % 4], in_=Sf[:]),
                          reads=[Sf], writes=[(self.dtr["gdn_out"], (si, dh))])
            P.end_phase()
            P.mid_es.close()

        P.begin_phase()
        PB = [P.ps("pb%d" % i, [128, 512], F32) for i in range(8)]
        gn = P.sb("gn", [128, 128], F32)
        P.dma("sp", lambda e: e.dma_start(out=gn[:], in_=di["ev_gdn_norm"].partition_broadcast(128)), writes=[gn])
        oa = [P.sb("oa%d" % i, [128, 512], F32) for i in range(2)]
        obb = [P.sb("obb%d" % i, [128, 512], F32) for i in range(2)]
        zz = [P.sb("zz%d" % i, [128, 512], F32) for i in range(2)]
        st8 = [P.sb("st8%d" % i, [128, 16], F32) for i in range(2)]
        junk = P.sb("junk", [128, 512], F32)
        otb = [P.sb("otb%d" % i, [128, 4, 128], BF16) for i in range(2)]
        for tl in range(NTILE):
            rows = slice(tl * 128, (tl + 1) * 128)
            a, b_, z_, s8, ot_ = oa[tl % 2], obb[tl % 2], zz[tl % 2], st8[tl % 2], otb[tl % 2]
            P.dma("sp", lambda e, a=a, rows=rows: e.dma_start(out=a[:], in_=OD[0, rows, :]), reads=[tOD], writes=[a])
            P.dma("sp", lambda e, b_=b_, rows=rows: e.dma_start(out=b_[:], in_=OD[1, rows, :]), reads=[tOD], writes=[b_])
            P.dma("sp", lambda e, z_=z_, rows=rows: e.dma_start(out=z_[:], in_=ZS[rows, :]), reads=[tZS], writes=[z_])
            P.op("dve", lambda e, a=a, b_=b_: e.tensor_tensor(out=a[:], in0=a[:], in1=b_[:], op=ALU.add), reads=[a, b_], writes=[a])
            P.op("pool", lambda e, s8=s8: e.memset(s8[:], 0.0), writes=[s8])
            for hh in range(4):
                P.op("act", lambda e, a=a, s8=s8, hh=hh: e.activation(out=junk[:, 0:128], in_=a[:, hh * 128:(hh + 1) * 128], func=AF.Square,
                                                                      accum_out=s8[:, hh:hh + 1]), reads=[a, s8], writes=[junk, s8])
            P.op("act", lambda e, s8=s8: e.activation(out=s8[:, 4:8], in_=s8[:, 0:4], func=AF.Sqrt, scale=1.0 / 128, bias=self.epsc[:, 0:1]),
                 reads=[s8, self.epsc], writes=[s8])
            P.op("dve", lambda e, s8=s8: e.reciprocal(out=s8[:, 8:12], in_=s8[:, 4:8]), reads=[s8], writes=[s8])
            for hh in range(4):
                P.op("dve", lambda e, a=a, s8=s8, hh=hh: e.scalar_tensor_tensor(
                    out=a[:, hh * 128:(hh + 1) * 128], in0=a[:, hh * 128:(hh + 1) * 128], scalar=s8[:, 8 + hh:9 + hh], in1=gn[:],
                    op0=ALU.mult, op1=ALU.mult), reads=[a, s8, gn], writes=[a])
            P.op("pool", lambda e, a=a, z_=z_: e.tensor_tensor(out=a[:], in0=a[:], in1=z_[:], op=ALU.mult), reads=[a, z_], writes=[a])
            pt = PB[tl % 2]
            for hh in range(4):
                P.op("pe", lambda e, a=a, hh=hh, pt=pt: e.transpose(pt[:, hh * 128:(hh + 1) * 128], a[:, hh * 128:(hh + 1) * 128], ident[:]),
                     reads=[a, ident], writes=[pt])
            P.op("act", lambda e, pt=pt, ot_=ot_: e.activation(out=ot_[:], in_=pt[:].rearrange("p (a b) -> p a b", a=4), func=AF.Copy),
                 reads=[pt], writes=[ot_])
            P.dma("sp", lambda e, ot_=ot_, tl=tl: e.dma_start(out=OT[0:512, tl * 128:(tl + 1) * 128].rearrange("(a p) t -> p a t", p=128),
                                                             in_=ot_[:]), reads=[ot_], writes=[(tOT, ("a", tl))])
        P.end_phase()

    def ev_e3(self):
        c = self.cfg
        P = self.P
        NT, NP = c.NT, c.NP
        ones = self.ones
        QT, CKVT, KRT, OT = self.QT, self.CKVT, self.KRT, self.OT
        tQT, tCKVT, tKRT, tOT = [self.dtr[n] for n in ("QT", "CKVT", "KRT", "OT")]
        wukv = self.din["ev_mla_w_ukv"]
        SCALE = 192.0 ** -0.5
        SMAX = 2 * c.TS + 256
        P.begin_phase()
        PB = [P.ps("pb%d" % i, [128, 512], F32) for i in range(8)]
        wsb = P.sb("wukv", [128, 1024], BF16)
        P.dma("pool", lambda e: e.dma_start(out=wsb[:], in_=wukv), writes=[wsb])
        sel = P.sb("sel65", [128, 65], F32)
        P.op("pool", lambda e: e.memset(sel[:], 0.0), writes=[sel])
        P.op("pool", lambda e: e.memset(sel[:, 64:65], 1.0), reads=[sel], writes=[sel])
        onesb = P.sb("onesb", [128, 128], BF16)
        P.op("pool", lambda e: e.memset(onesb[:], 1.0), writes=[onesb])
        CK = P.sb("CK", [128, SMAX], BF16)
        KR = P.sb("KR", [65, SMAX], BF16)
        KN = P.sb("KN", [128, SMAX], BF16)
        Vp = P.sb("Vp", [128, SMAX // 128, 128], BF16)
        sq1 = P.sb("sq1", [128, 512], F32)
        sq2 = P.sb("sq2", [64, 512], F32)
        kkm = P.sb("kkm", [65, 16], F32)
        row = P.sb("row", [65, 512], F32)
        QN = [P.sb("QN%d" % i, [128, 512], BF16) for i in range(2)]
        QR = [P.sb("QR%d" % i, [65, 512], BF16) for i in range(2)]
        PT = [P.sb("PT%d" % i, [128, 512], BF16) for i in range(3)]
        rec = P.sb("rec", [128, 512], F32)
        obf = [P.sb("obf%d" % i, [128, 512], BF16) for i in range(2)]
        qctr = [0]
        pctr = [0]
        for (row0, T, r, kind) in self.seqs():
            if kind == "p":
                k0 = self.kcol(row0)
                S = T
                P.dma("sp", lambda e, k0=k0, S=S: e.dma_start(out=CK[:, 0:S], in_=CKVT[:, k0:k0 + S]), reads=[tCKVT], writes=[CK])
                P.dma("sp", lambda e, k0=k0, S=S: e.dma_start(out=KR[:, 0:S], in_=KRT[:, k0:k0 + S]), reads=[tKRT], writes=[KR])
            else:
                S = 256 + 2 * T
                own0 = self.kcol(row0)
                kin, kout, tkin, tkout = self.pair_buf("KVX", 193, T, BF16)
                P.dma("sp", lambda e, own0=own0, T=T: e.dma_start(out=kin[0:128, :], in_=CKVT[:, own0:own0 + T]), reads=[tCKVT], writes=[(tkin, 0)])
                P.dma("sp", lambda e, own0=own0, T=T: e.dma_start(out=kin[128:193, :], in_=KRT[:, own0:own0 + T]), reads=[tKRT], writes=[(tkin, 1)])
                self.pair_gather(kin, kout, tkin, tkout)
                P.dma("sp", lambda e: e.dma_start(out=CK[:, 0:256], in_=CKVT[:, NP:NP + 256]), reads=[tCKVT], writes=[(CK, "c")])
                P.dma("sp", lambda e: e.dma_start(out=KR[:, 0:256], in_=KRT[:, NP:NP + 256]), reads=[tKRT], writes=[(KR, "c")])
                for rk in range(2):
                    P.dma("sp", lambda e, rk=rk, T=T: e.dma_start(out=CK[:, 256 + rk * T:256 + (rk + 1) * T], in_=kout[rk * 193:rk * 193 + 128, :]),
                          reads=[tkout], writes=[(CK, rk)])
                    P.dma("sp", lambda e, rk=rk, T=T: e.dma_start(out=KR[:, 256 + rk * T:256 + (rk + 1) * T], in_=kout[rk * 193 + 128:rk * 193 + 193, :]),
                          reads=[tkout], writes=[(KR, rk)])
            nkt = S // 128
            for hh in range(4):
                for b0 in range(0, S, 512):
                    bw = min(512, S - b0)
                    pk = PB[0]
                    P.op("pe", lambda e, hh=hh, b0=b0, bw=bw, pk=pk: e.matmul(pk[:, 0:bw], wsb[:, hh * 256:hh * 256 + 128], CK[:, b0:b0 + bw],
                                                                              start=True, stop=True), reads=[wsb, CK], writes=[pk])
                    P.op("act", lambda e, b0=b0, bw=bw, pk=pk: e.activation(out=KN[:, b0:b0 + bw], in_=pk[:, 0:bw], func=AF.Copy),
                         reads=[pk], writes=[(KN, b0)])
                    P.op("act", lambda e, bw=bw, pk=pk: e.activation(out=sq1[:, 0:bw], in_=pk[:, 0:bw], func=AF.Square), reads=[pk],
                         writes=[sq1])
                    P.op("act", lambda e, b0=b0, bw=bw: e.activation(out=sq2[:, 0:bw], in_=KR[0:64, b0:b0 + bw], func=AF.Square),
                         reads=[KR], writes=[sq2])
                    pr = PB[1]
                    P.op("pe", lambda e, bw=bw, pr=pr: e.matmul(pr[0:65, 0:bw], sel[:], sq1[:, 0:bw], start=True, stop=False),
                         reads=[sel, sq1], writes=[pr])
                    P.op("pe", lambda e, bw=bw, pr=pr: e.matmul(pr[0:65, 0:bw], sel[0:64, :], sq2[:, 0:bw], start=False, stop=True),
                         reads=[sel, sq2], writes=[pr])
                    P.op("dve", lambda e, b0=b0, bw=bw, pr=pr: e.reduce_max(out=kkm[64:65, b0 // 512:b0 // 512 + 1], in_=pr[64:65, 0:bw],
                                                                            axis=AX.X), reads=[pr], writes=[(kkm, b0)])
                    for k4 in range(0, bw // 128, 4):
                        pv = PB[2]
                        n4 = min(4, bw // 128 - k4)
                        for q in range(n4):
                            kt = b0 // 128 + k4 + q
                            P.op("pe", lambda e, hh=hh, kt=kt, q=q, pv=pv: e.matmul(
                                pv[:, q * 128:(q + 1) * 128], CK[:, kt * 128:(kt + 1) * 128], wsb[:, hh * 256 + 128:hh * 256 + 256],
                                start=True, stop=True), reads=[wsb, CK], writes=[pv])
                        kt0 = b0 // 128 + k4
                        P.op("dve", lambda e, kt0=kt0, n4=n4, pv=pv: e.tensor_copy(
                            out=Vp[:, kt0:kt0 + n4, :], in_=pv[:, 0:n4 * 128].rearrange("p (a b) -> p a b", a=n4)),
                            reads=[pv], writes=[(Vp, kt0)])
                nblk = (S + 511) // 512
                P.op("dve", lambda e, nblk=nblk: e.reduce_max(out=kkm[64:65, 15:16], in_=kkm[64:65, 0:nblk], axis=AX.X),
                     reads=[kkm], writes=[(kkm, "max")])
                for q0 in range(0, T, 512):
                    qw = min(512, T - q0)
                    qn_, qr_ = QN[qctr[0] % 2], QR[qctr[0] % 2]
                    pacc, psm = PB[4 + (qctr[0] % 2) * 2], PB[5 + (qctr[0] % 2) * 2]
                    qctr[0] += 1
                    rows = slice(row0 + q0, row0 + q0 + qw)
                    P.dma("sp", lambda e, hh=hh, rows=rows, qw=qw, qn_=qn_: e.dma_start(out=qn_[:, 0:qw], in_=QT[hh, 0:128, rows]),
                          reads=[tQT], writes=[qn_])
                    P.dma("sp", lambda e, hh=hh, rows=rows, qw=qw, qr_=qr_: e.dma_start(out=qr_[0:64, 0:qw], in_=QT[hh, 128:192, rows]),
                          reads=[tQT], writes=[(qr_, "a")])
                    P.op("act", lambda e, qw=qw, qn_=qn_: e.activation(out=sq1[:, 0:qw], in_=qn_[:, 0:qw], func=AF.Square), reads=[qn_],
                         writes=[sq1])
                    P.op("act", lambda e, qw=qw, qr_=qr_: e.activation(out=sq2[:, 0:qw], in_=qr_[0:64, 0:qw], func=AF.Square),
                         reads=[(qr_, "a")], writes=[sq2])
                    pr = PB[1]
                    P.op("pe", lambda e, qw=qw, pr=pr: e.matmul(pr[0:65, 0:qw], sel[:], sq1[:, 0:qw], start=True, stop=False),
                         reads=[sel, sq1], writes=[pr])
                    P.op("pe", lambda e, qw=qw, pr=pr: e.matmul(pr[0:65, 0:qw], sel[0:64, :], sq2[:, 0:qw], start=False, stop=True),
                         reads=[sel, sq2], writes=[pr])
                    P.op("dve", lambda e, qw=qw, pr=pr: e.tensor_scalar(out=row[64:65, 0:qw], in0=pr[64:65, 0:qw], scalar1=kkm[64:65, 15:16],
                                                                        scalar2=None, op0=ALU.mult), reads=[pr, (kkm, "max")], writes=[row])
                    P.op("act", lambda e, qw=qw: e.activation(out=row[64:65, 0:qw], in_=row[64:65, 0:qw], func=AF.Sqrt), reads=[row],
                         writes=[row])
                    P.op("dve", lambda e, qw=qw, qr_=qr_: e.tensor_scalar(out=qr_[64:65, 0:qw], in0=row[64:65, 0:qw], scalar1=-1.0,
                                                                          scalar2=None, op0=ALU.mult), reads=[row], writes=[(qr_, "b")])
                    ptmap = {}

                    def scores(kt, qw=qw, qn_=qn_, qr_=qr_, ptmap=ptmap):
                        ps = PB[2 + (pctr[0] % 2)]
                        pt_ = PT[pctr[0] % 3]
                        pctr[0] += 1
                        ptmap[kt] = pt_
                        ksl = slice(kt * 128, (kt + 1) * 128)
                        P.op("pe", lambda e: e.matmul(ps[:, 0:qw], KN[:, ksl], qn_[:, 0:qw], start=True, stop=False), reads=[KN, qn_], writes=[ps])
                        P.op("pe", lambda e: e.matmul(ps[:, 0:qw], KR[0:65, ksl], qr_[0:65, 0:qw], start=False, stop=True), reads=[KR, qr_],
                             writes=[ps])
                        P.op("act", lambda e: e.activation(out=pt_[:, 0:qw], in_=ps[:, 0:qw], func=AF.Exp, scale=SCALE), reads=[ps], writes=[pt_])

                    def accum(kt, qw=qw, pacc=pacc, psm=psm, nkt=nkt, ptmap=ptmap):
                        pt_ = ptmap[kt]
                        P.op("pe", lambda e: e.matmul(pacc[:, 0:qw], Vp[:, kt, :], pt_[:, 0:qw], start=(kt == 0), stop=(kt == nkt - 1)),
                             reads=[Vp, pt_], writes=[pacc])
                        P.op("pe", lambda e: e.matmul(psm[:, 0:qw], onesb[:], pt_[:, 0:qw], start=(kt == 0), stop=(kt == nkt - 1)),
                             reads=[onesb, pt_], writes=[psm])

                    scores(0)
                    for kt in range(nkt):
                        if kt + 1 < nkt:
                            scores(kt + 1)
                        accum(kt)
                    P.op("dve", lambda e, qw=qw, psm=psm: e.reciprocal(out=rec[:, 0:qw], in_=psm[:, 0:qw]), reads=[psm], writes=[rec])
                    ob_ = obf[qctr[0] % 2]
                    P.op("dve", lambda e, qw=qw, pacc=pacc, ob_=ob_: e.tensor_tensor(out=ob_[:, 0:qw], in0=pacc[:, 0:qw], in1=rec[:, 0:qw],
                                                                                     op=ALU.mult), reads=[pacc, rec], writes=[ob_])
                    P.dma("sp", lambda e, hh=hh, rows=rows, qw=qw, ob_=ob_: e.dma_start(out=OT[512 + hh * 128:512 + (hh + 1) * 128, rows],
                                                                                       in_=ob_[:, 0:qw]), reads=[ob_],
                          writes=[(tOT, ("b", hh, row0 + q0))])
        P.end_phase()

    def mix_out(self, l, src, dst, wname):
        c = self.cfg
        P = self.P
        s = 1
        srcap, dstap = self.dr(src), self.dr(dst)
        srct, dstt = self.dtr[src], self.dtr[dst]
        OT, tOT = self.OT, self.dtr["OT"]
        wo_d = self.din[wname]
        P.begin_phase()
        PB = [P.ps("pb%d" % i, [128, 512], F32) for i in range(8)]
        cbc = self.make_cbc(l, s, PB)
        wo = P.sb("wo", [128, 8, 1024], BF16)
        P.dma("pool", lambda e: e.dma_start(out=wo[:], in_=wo_d.rearrange("(k p) n -> p k n", p=128)), writes=[wo])
        xg = [P.sb("xg%d" % i, [128, 4, 1024], F32) for i in range(2)]
        og = [P.sb("og%d" % i, [128, 8, 512], BF16) for i in range(2)]
        junk = P.sb("junk", [128, 1024], F32)
        tmp = [P.sb("tmp%d" % i, [128, 1024], F32) for i in range(2)]
        ss = [P.sb("ss%d" % i, [128, 24], F32) for i in range(2)]
        gs = self.groups()
        for gi, (r0, r) in enumerate(gs):
            x, o_, sst = xg[gi % 2], og[gi % 2], ss[gi % 2]
            P.dma("sp", lambda e, x=x, r0=r0: e.dma_start(out=x[:], in_=srcap[r0:r0 + 512, :].rearrange("(t p) d -> p t d", p=128)),
                  reads=[srct], writes=[x])
            for hf in range(2):
                P.dma("sp", lambda e, o_=o_, r0=r0, hf=hf: e.dma_start(
                    out=o_[:, hf * 4:(hf + 1) * 4, :], in_=OT[hf * 512:(hf + 1) * 512, r0:r0 + 512].rearrange("(k p) t -> p k t", p=128)),
                    reads=[tOT], writes=[(o_, hf)])
            P.op("pool", lambda e, sst=sst: e.memset(sst[:], 0.0), writes=[sst])
            for t in range(4):
                py = [PB[4 + (t % 2) * 2], PB[5 + (t % 2) * 2]]
                for nb in range(2):
                    for k in range(8):
                        P.op("pe", lambda e, nb=nb, k=k, t=t, py=py, o_=o_: e.matmul(
                            py[nb][:], o_[:, k, t * 128:(t + 1) * 128], wo[:, k, nb * 512:(nb + 1) * 512], start=(k == 0), stop=(k == 7)),
                            reads=[o_, wo], writes=[py[nb]])
                self.epi(py, x[:, t, :], [(x, None)], r, sst, 4 * t, cbc, tmp[t % 2], junk, dstap, dstt, r0 + t * 128)
        P.end_phase()

    def odd_mixer(self, src, dst):
        c = self.cfg
        NT = c.NT
        if not hasattr(self, "OT"):
            self.OT = self.scratch("OT", (1024, NT), BF16)
        self.QK2 = self.scratch("QK2", (NT, 1024))
        self.V2 = self.scratch("V2", (NT, 1024), BF16)
        self.RS = self.scratch("RS", (NT, 1024))
        self.G2 = self.scratch("G2", (2, NT, 512))
        self.ODG = self.scratch("ODG", (2, NT, 1024))
        self.od_o1(src)
        self.od_o2()
        self.mix_out(1, src, dst, "od_w_out")

    def od_o1(self, src):
        c = self.cfg
        P = self.P
        di = self.din
        l, s = 1, 1
        NT, NP = c.NT, c.NP
        srcap, srct = self.dr(src), self.dtr[src]
        win = di["od_w_in"]
        QK2, V2, RS, G2 = self.QK2, self.V2, self.RS, self.G2
        tQK2, tV2, tRS, tG2 = [self.dtr[n] for n in ("QK2", "V2", "RS", "G2")]
        P.begin_phase()
        PB = [P.ps("pb%d" % i, [128, 512], F32) for i in range(8)]
        wsb = P.sb("wsb", [128, 8, 3104], BF16)
        for q in range(4):
            P.dma("pool", lambda e, q=q: e.dma_start(
                out=wsb[:, q * 2:(q + 1) * 2, :], in_=win[q * 256:(q + 1) * 256, :].rearrange("(kc p) n -> p kc n", p=128)),
                writes=[(wsb, q)])
        wg = [P.sb("wg%d" % d, [17, 512], F32) for d in range(2)]
        gdT = [P.sb("gdT%d" % d, [17, 512], F32) for d in range(2)]
        for d in range(2):
            P.dma("sp", lambda e, d=d: e.dma_start(out=wg[d][0:16, :], in_=di["od_gla_w_gup"][d]), writes=[(wg[d], 0)])
            P.dma("sp", lambda e, d=d: e.dma_start(out=wg[d][16:17, :], in_=di["od_gla_b_g"][d:d + 1, :]), writes=[(wg[d], 1)])
            P.op("pool", lambda e, d=d: e.memset(gdT[d][:], 1.0), writes=[gdT[d]])
        xg = [P.sb("xg%d" % i, [128, 4, 1024], F32) for i in range(2)]
        hT = [P.sb("hT%d" % i, [128, 8, 512], BF16) for i in range(2)]
        xn = [P.sb("xn%d" % i, [128, 1024], F32) for i in range(2)]
        junk = P.sb("junk", [128, 1024], F32)
        ss = [P.sb("ss%d" % i, [128, 24], F32) for i in range(2)]
        sqk = [P.sb("sqk%d" % i, [128, 1024], F32) for i in range(2)]
        sv = [P.sb("sv%d" % i, [128, 1024], BF16) for i in range(2)]
        sr = [P.sb("sr%d" % i, [128, 1024], F32) for i in range(2)]
        sg = [P.sb("sg%d" % i, [128, 512], F32) for i in range(2)]
        gs = self.groups()

        def load(gi):
            r0, r = gs[gi]
            x = xg[gi % 2]
            P.dma("sp", lambda e: e.dma_start(out=x[:], in_=srcap[r0:r0 + 512, :].rearrange("(t p) d -> p t d", p=128)),
                  reads=[srct], writes=[x])

        def body(gi):
            r0, r = gs[gi]
            h = hT[gi % 2]
            for d in range(2):
                pg = PB[2 + d]
                for k in range(8):
                    P.op("pe", lambda e, k=k, d=d, pg=pg: e.matmul(pg[0:16, :], wsb[:, k, 3072 + 16 * d:3072 + 16 * (d + 1)], h[:, k, :],
                                                                     start=(k == 0), stop=(k == 7)), reads=[wsb, h], writes=[pg])
                P.op("dve", lambda e, d=d, pg=pg: e.tensor_copy(out=gdT[d][0:16, :], in_=pg[0:16, :]), reads=[pg], writes=[gdT[d]])
            for t in range(4):
                row = r0 + t * 128
                tsl = slice(t * 128, (t + 1) * 128)
                banks = {"qk": (PB[0], PB[1], 0), "v": (PB[2], PB[3], 1024), "r": (PB[6], PB[7], 2048)}
                for name in ("qk", "v", "r"):
                    pa, pb_, c0 = banks[name]
                    for hf, pp in enumerate((pa, pb_)):
                        for k in range(8):
                            P.op("pe", lambda e, k=k, pp=pp, c0=c0, hf=hf, tsl=tsl: e.matmul(
                                pp[:], h[:, k, tsl], wsb[:, k, c0 + hf * 512:c0 + (hf + 1) * 512], start=(k == 0), stop=(k == 7)),
                                reads=[wsb, h], writes=[pp])
                    if name == "qk":
                        st = sqk[t % 2]
                        P.op("act", lambda e, st=st, pa=pa: e.activation(out=st[:, 0:512], in_=pa[:], func=AF.Copy), reads=[pa], writes=[(st, 0)])
                        P.op("dve", lambda e, st=st, pb_=pb_: e.tensor_copy(out=st[:, 512:1024], in_=pb_[:]), reads=[pb_], writes=[(st, 1)])
                        P.dma("sp", lambda e, st=st, row=row: e.dma_start(out=QK2[row:row + 128, :], in_=st[:]), reads=[st], writes=[(tQK2, row)])
                    elif name == "v":
                        st = sv[t % 2]
                        P.op("act", lambda e, st=st, pa=pa: e.activation(out=st[:, 0:512], in_=pa[:], func=AF.Copy), reads=[pa], writes=[(st, 0)])
                        P.op("dve", lambda e, st=st, pb_=pb_: e.tensor_copy(out=st[:, 512:1024], in_=pb_[:]), reads=[pb_], writes=[(st, 1)])
                        P.dma("sp", lambda e, st=st, row=row: e.dma_start(out=V2[row:row + 128, :], in_=st[:]), reads=[st], writes=[(tV2, row)])
                    else:
                        st = sr[t % 2]
                        P.op("act", lambda e, st=st, pa=pa: e.activation(out=st[:, 0:512], in_=pa[:], func=AF.Silu), reads=[pa], writes=[(st, 0)])
                        P.op("act", lambda e, st=st, pb_=pb_: e.activation(out=st[:, 512:1024], in_=pb_[:], func=AF.Silu), reads=[pb_], writes=[(st, 1)])
                        P.dma("sp", lambda e, st=st, row=row: e.dma_start(out=RS[row:row + 128, :], in_=st[:]), reads=[st], writes=[(tRS, row)])
                for d in range(2):
                    pg = PB[d]
                    g_ = sg[d]
                    P.op("pe", lambda e, d=d, pg=pg, tsl=tsl: e.matmul(pg[:], gdT[d][0:17, tsl], wg[d][0:17, :], start=True, stop=True),
                         reads=[gdT[d], wg[d]], writes=[pg])
                    P.op("act", lambda e, pg=pg, g_=g_: e.activation(out=g_[:], in_=pg[:], func=AF.Exp, scale=-1.0), reads=[pg], writes=[g_])
                    P.op("act", lambda e, g_=g_: e.activation(out=g_[:], in_=g_[:], func=AF.Ln, bias=self.ones[:, 0:1]), reads=[g_, self.ones],
                         writes=[g_])
                    P.op("dve", lambda e, g_=g_: e.tensor_scalar(out=g_[:], in0=g_[:], scalar1=-1.0 / 16.0, scalar2=None, op0=ALU.mult),
                         reads=[g_], writes=[g_])
                    P.dma("sp", lambda e, g_=g_, d=d, row=row: e.dma_start(out=G2[d, row:row + 128, :], in_=g_[:]), reads=[g_],
                          writes=[(tG2, (d, row))])

        ng = len(gs)
        load(0)
        for gi in range(ng):
            r0, r = gs[gi]
            self.pro_hT(xg[gi % 2], ss[gi % 2], hT[gi % 2], l, s, r, [PB[4], PB[5]], xn, junk)
            if gi + 1 < ng:
                load(gi + 1)
            body(gi)
        P.end_phase()

    def od_o2(self):
        c = self.cfg
        P = self.P
        di = self.din
        NT, NP = c.NT, c.NP
        NTILE = NT // 128
        ident, ones = self.ident, self.ones
        QK2, V2, RS, G2, ODG, OT = self.QK2, self.V2, self.RS, self.G2, self.ODG, self.OT
        tQK2, tV2, tRS, tG2, tODG, tOT = [self.dtr[n] for n in ("QK2", "V2", "RS", "G2", "ODG", "OT")]
        P.begin_phase()
        PB = [P.ps("pb%d" % i, [128, 512], F32) for i in range(8)]

        def mask(name, op, flip):
            m = P.sb(name, [128, 128], F32)
            P.op("pool", lambda e: e.memset(m[:], 1.0), writes=[m])
            P.op("pool", lambda e: e.affine_select(out=m[:], in_=m[:], pattern=[[1 if flip else -1, 128]], compare_op=op,
                                                   fill=0.0, base=0, channel_multiplier=(-1 if flip else 1)), reads=[m], writes=[m])
            return m
        Uincl = mask("Uincl", ALU.is_ge, True)
        Lincl = mask("Lincl", ALU.is_ge, False)
        CUM = [Uincl, Lincl]
        AM = [Uincl, Lincl]
        qk_t = [P.sb("qk_t%d" % i, [128, 1024], F32) for i in range(2)]
        v_t = [P.sb("v_t%d" % i, [128, 1024], BF16) for i in range(2)]
        g_t = [P.sb("g_t%d" % i, [128, 512], F32) for i in range(2)]
        bsb = P.sb("bsb", [128, 512], F32)
        eb = P.sb("eb", [128, 512], F32)
        enb = P.sb("enb", [128, 512], F32)
        ekd = P.sb("ekd", [128, 512], F32)
        qe = P.sb("qe", [128, 512], F32)
        ke = P.sb("ke", [128, 512], F32)
        kd2 = [P.sb("kd%d" % i, [128, 512], BF16) for i in range(2)]
        qeT2 = [P.sb("qeT%d" % i, [128, 4, 128], BF16) for i in range(2)]
        keT2 = [P.sb("keT%d" % i, [128, 4, 128], BF16) for i in range(2)]
        attT = [P.sb("attT%d" % i, [128, 128], BF16) for i in range(4)]
        ebl2 = [P.sb("ebl%d" % i, [128, 8], F32) for i in range(2)]
        osb = [P.sb("osb%d" % i, [128, 1024], F32) for i in range(2)]
        S = [[[P.sb("S%d_%d_%d" % (d, hh, i), [128, 256], F32) for i in range(2)] for hh in range(4)] for d in range(2)]
        Sb = [[P.sb("Sb%d_%d" % (d, hh), [128, 256], BF16) for hh in range(4)] for d in range(2)]
        cnt = [0]
        for (row0, T, r, kind) in self.seqs():
            NTl = T // 128
            for d in range(2):
                for hh in range(4):
                    if kind == "p":
                        P.op("pool", lambda e, d=d, hh=hh: e.memset(S[d][hh][0][:], 0.0), writes=[S[d][hh][0]])
                    else:
                        P.dma("sp", lambda e, d=d, hh=hh: e.dma_start(out=S[d][hh][0][:], in_=di["state_gla"][d, hh]), writes=[S[d][hh][0]])
                    P.op("pool", lambda e, d=d, hh=hh: e.tensor_copy(out=Sb[d][hh][:], in_=S[d][hh][0][:]), reads=[S[d][hh][0]],
                         writes=[Sb[d][hh]])
            def gla_pass(dirs):
                its = [(i, d) for i in range(NTl) for d in dirs]
                ctx = {}

                def front(n):
                    i, d = its[n]
                    cc = i if d == 0 else NTl - 1 - i
                    rows = slice(row0 + cc * 128, row0 + (cc + 1) * 128)
                    k_ = cnt[0] % 2
                    cnt[0] += 1
                    qk_, vv_, gg_ = qk_t[k_], v_t[k_], g_t[k_]
                    qeT_, keT_, kd_, ebl_ = qeT2[k_], keT2[k_], kd2[k_], ebl2[k_]
                    ctx[n] = (i, d, cc, rows, k_, vv_, qeT_, keT_, kd_, ebl_)
                    P.dma("sp", lambda e: e.dma_start(out=qk_[:], in_=QK2[rows, :]), reads=[tQK2], writes=[qk_])
                    P.dma("sp", lambda e: e.dma_start(out=vv_[:], in_=V2[rows, :]), reads=[tV2], writes=[vv_])
                    P.dma("sp", lambda e: e.dma_start(out=gg_[:], in_=G2[d, rows, :]), reads=[tG2], writes=[gg_])
                    pb_c, pb_t = PB[0], PB[1]
                    P.op("pe", lambda e: e.matmul(pb_c[:], CUM[d][:], gg_[:], start=True, stop=True), reads=[CUM[d], gg_], writes=[pb_c])
                    P.op("pe", lambda e: e.matmul(pb_t[:], ones[:], gg_[:], start=True, stop=True), reads=[ones, gg_], writes=[pb_t])
                    for hh in range(4):
                        P.op("pe", lambda e, hh=hh: e.matmul(PB[2][:, hh:hh + 1], gg_[:, hh * 128:(hh + 1) * 128], ones[:, 0:1],
                                                             start=True, stop=True), reads=[gg_, ones], writes=[PB[2]])
                    P.op("act", lambda e: e.activation(out=ebl_[:, 0:4], in_=PB[2][:, 0:4], func=AF.Exp), reads=[PB[2]], writes=[ebl_])
                    P.op("dve", lambda e: e.tensor_copy(out=bsb[:], in_=pb_c[:]), reads=[pb_c], writes=[bsb])
                    P.op("act", lambda e: e.activation(out=eb[:], in_=bsb[:], func=AF.Exp), reads=[bsb], writes=[eb])
                    P.op("act", lambda e: e.activation(out=enb[:], in_=bsb[:], func=AF.Exp, scale=-1.0), reads=[bsb], writes=[enb])
                    P.op("dve", lambda e: e.tensor_tensor(out=ekd[:], in0=pb_t[:], in1=bsb[:], op=ALU.subtract), reads=[pb_t, bsb], writes=[ekd])
                    P.op("act", lambda e: e.activation(out=ekd[:], in_=ekd[:], func=AF.Exp), reads=[ekd], writes=[ekd])
                    P.op("dve", lambda e: e.scalar_tensor_tensor(out=qe[:], in0=qk_[:, 0:512], scalar=128.0 ** -0.5, in1=eb[:],
                                                                 op0=ALU.mult, op1=ALU.mult), reads=[qk_, eb], writes=[qe])
                    P.op("dve", lambda e: e.tensor_tensor(out=ke[:], in0=qk_[:, 512:1024], in1=enb[:], op=ALU.mult), reads=[qk_, enb], writes=[ke])
                    P.op("pool", lambda e: e.tensor_tensor(out=kd_[:], in0=qk_[:, 512:1024], in1=ekd[:], op=ALU.mult), reads=[qk_, ekd],
                         writes=[kd_])
                    for hh in range(4):
                        hs = slice(hh * 128, (hh + 1) * 128)
                        P.op("pe", lambda e, hs=hs: e.transpose(PB[3][:, hs], qe[:, hs], ident[:]), reads=[qe, ident], writes=[PB[3]])
                        P.op("pe", lambda e, hs=hs: e.transpose(PB[4][:, hs], ke[:, hs], ident[:]), reads=[ke, ident], writes=[PB[4]])
                    P.op("act", lambda e: e.activation(out=qeT_[:], in_=PB[3][:].rearrange("p (a b) -> p a b", a=4), func=AF.Copy), reads=[PB[3]],
                         writes=[qeT_])
                    P.op("dve", lambda e: e.tensor_copy(out=keT_[:], in_=PB[4][:].rearrange("p (a b) -> p a b", a=4)), reads=[PB[4]], writes=[keT_])

                def back(n):
                    (i, d, cc, rows, k_, vv_, qeT_, keT_, kd_, ebl_) = ctx.pop(n)
                    o_ = osb[k_]
                    for hh in range(4):
                        P.op("pe", lambda e, hh=hh: e.matmul(PB[5][:, hh * 128:(hh + 1) * 128], keT_[:, hh, :], qeT_[:, hh, :], start=True, stop=True),
                             reads=[keT_, qeT_], writes=[PB[5]])
                    for hh in range(4):
                        P.op("dve", lambda e, hh=hh: e.tensor_tensor(out=attT[hh][:], in0=PB[5][:, hh * 128:(hh + 1) * 128], in1=AM[d][:],
                                                                     op=ALU.mult), reads=[PB[5], AM[d]], writes=[attT[hh]])
                    for hh in range(4):
                        hs = slice(hh * 128, (hh + 1) * 128)
                        po = PB[6 + (hh // 2)]
                        osl = slice((hh % 2) * 256, (hh % 2 + 1) * 256)
                        pu = PB[1] if hh < 2 else PB[0]
                        usl = slice((hh % 2) * 256, (hh % 2 + 1) * 256)
                        P.op("pe", lambda e, hh=hh, hs=hs, pu=pu, usl=usl: e.matmul(pu[:, usl], kd_[:, hs], vv_[:, hh * 256:(hh + 1) * 256],
                                                                                     start=True, stop=True), reads=[kd_, vv_], writes=[pu])
                        P.op("pe", lambda e, hh=hh, po=po, osl=osl: e.matmul(po[:, osl], qeT_[:, hh, :], Sb[d][hh][:], start=True, stop=False),
                             reads=[qeT_, Sb[d][hh]], writes=[po])
                        P.op("pe", lambda e, hh=hh, po=po, osl=osl: e.matmul(po[:, osl], attT[hh][:], vv_[:, hh * 256:(hh + 1) * 256],
                                                                             start=False, stop=True), reads=[attT[hh], vv_], writes=[po])
                    for hh in range(4):
                        Sc, Sn = S[d][hh][i % 2], S[d][hh][(i + 1) % 2]
                        pu = PB[1] if hh < 2 else PB[0]
                        usl = slice((hh % 2) * 256, (hh % 2 + 1) * 256)
                        P.op("dve", lambda e, hh=hh, pu=pu, usl=usl, Sc=Sc, Sn=Sn: e.scalar_tensor_tensor(
                            out=Sn[:], in0=Sc[:], scalar=ebl_[:, hh:hh + 1], in1=pu[:, usl], op0=ALU.mult, op1=ALU.add),
                            reads=[pu, Sc, ebl_], writes=[Sn])
                        P.op("pool", lambda e, hh=hh, Sn=Sn: e.tensor_copy(out=Sb[d][hh][:], in_=Sn[:]), reads=[Sn], writes=[Sb[d][hh]])
                    P.op("act", lambda e: e.activation(out=o_[:, 0:512], in_=PB[6][:], func=AF.Copy), reads=[PB[6]], writes=[(o_, 0)])
                    P.op("act", lambda e: e.activation(out=o_[:, 512:1024], in_=PB[7][:], func=AF.Copy), reads=[PB[7]], writes=[(o_, 1)])
                    P.dma("sp", lambda e: e.dma_start(out=ODG[d, rows, :], in_=o_[:]), reads=[o_], writes=[(tODG, (d, rows.start))])

                front(0)
                for n in range(len(its)):
                    if n + 1 < len(its):
                        front(n + 1)
                    back(n)
            if kind == "p":
                gla_pass((0, 1))
            else:
                gla_pass((0,))
                gin, gout, tgin, tgout = self.pair_buf("LST", 512, 256)
                for hh in range(4):
                    Sf = S[0][hh][NTl % 2]
                    P.dma("sp", lambda e, hh=hh, Sf=Sf: e.dma_start(out=gin[hh * 128:(hh + 1) * 128, :], in_=Sf[:]), reads=[Sf],
                          writes=[(tgin, hh)])
                self.pair_gather(gin, gout, tgin, tgout)
                cand = P.sb("cand", [128, 2, 4, 256], F32)
                for rk in range(2):
                    P.dma("sp", lambda e, rk=rk: e.dma_start(out=cand[:, rk], in_=gout[rk * 512:(rk + 1) * 512, :].rearrange("(h p) c -> p h c", p=128)),
                          reads=[tgout], writes=[(cand, rk)])
                selw = P.sb("selw", [128, 2], F32)
                P.dma("sp", lambda e: e.dma_start(out=selw[:], in_=di["selw"]), writes=[selw])
                for hh in range(4):
                    P.op("dve", lambda e, hh=hh: e.tensor_scalar(out=S[1][hh][0][:], in0=cand[:, 0, hh, :], scalar1=selw[:, 0:1], scalar2=None,
                                                                 op0=ALU.mult), reads=[cand, selw], writes=[S[1][hh][0]])
                    P.op("dve", lambda e, hh=hh: e.scalar_tensor_tensor(out=S[1][hh][0][:], in0=cand[:, 1, hh, :], scalar=selw[:, 1:2],
                                                                        in1=S[1][hh][0][:], op0=ALU.mult, op1=ALU.add),
                         reads=[cand, selw, S[1][hh][0]], writes=[S[1][hh][0]])
                    P.op("pool", lambda e, hh=hh: e.tensor_copy(out=Sb[1][hh][:], in_=S[1][hh][0][:]), reads=[S[1][hh][0]], writes=[Sb[1][hh]])
                gla_pass((1,))
            if kind == "p":
                si = row0 // c.TP
                for d in range(2):
                    for hh in range(4):
                        Sf = S[d][hh][NTl % 2]
                        P.dma("sp", lambda e, d=d, hh=hh, Sf=Sf, si=si: e.dma_start(out=self.dout["gla_out"][si, d, hh], in_=Sf[:]), reads=[Sf],
                              writes=[(self.dtr["gla_out"], (si, d, hh))])
        P.end_phase()
        P.begin_phase()
        PB = [P.ps("pb%d" % i, [128, 512], F32) for i in range(8)]
        gn = P.sb("gn", [128, 256], F32)
        P.dma("sp", lambda e: e.dma_start(out=gn[:], in_=di["od_gla_norm"].partition_broadcast(128)), writes=[gn])
        oa = [P.sb("oa%d" % i, [128, 1024], F32) for i in range(2)]
        obb = [P.sb("obb%d" % i, [128, 1024], F32) for i in range(2)]
        zz = [P.sb("zz%d" % i, [128, 1024], F32) for i in range(2)]
        st8 = [P.sb("st8%d" % i, [128, 16], F32) for i in range(2)]
        junk = P.sb("junk", [128, 512], F32)
        otb = [P.sb("otb%d" % i, [128, 8, 128], BF16) for i in range(2)]
        for tl in range(NTILE):
            rows = slice(tl * 128, (tl + 1) * 128)
            a, b_, z_, s8, ot_ = oa[tl % 2], obb[tl % 2], zz[tl % 2], st8[tl % 2], otb[tl % 2]
            P.dma("sp", lambda e, a=a, rows=rows: e.dma_start(out=a[:], in_=ODG[0, rows, :]), reads=[tODG], writes=[a])
            P.dma("sp", lambda e, b_=b_, rows=rows: e.dma_start(out=b_[:], in_=ODG[1, rows, :]), reads=[tODG], writes=[b_])
            P.dma("sp", lambda e, z_=z_, rows=rows: e.dma_start(out=z_[:], in_=RS[rows, :]), reads=[tRS], writes=[z_])
            P.op("dve", lambda e, a=a, b_=b_: e.tensor_tensor(out=a[:], in0=a[:], in1=b_[:], op=ALU.add), reads=[a, b_], writes=[a])
            P.op("pool", lambda e, s8=s8: e.memset(s8[:], 0.0), writes=[s8])
            for hh in range(4):
                P.op("act", lambda e, a=a, s8=s8, hh=hh: e.activation(out=junk[:, 0:256], in_=a[:, hh * 256:(hh + 1) * 256], func=AF.Square,
                                                                      accum_out=s8[:, hh:hh + 1]), reads=[a, s8], writes=[junk, s8])
            P.op("act", lambda e, s8=s8: e.activation(out=s8[:, 4:8], in_=s8[:, 0:4], func=AF.Sqrt, scale=1.0 / 256, bias=self.epsc[:, 0:1]),
                 reads=[s8, self.epsc], writes=[s8])
            P.op("dve", lambda e, s8=s8: e.reciprocal(out=s8[:, 8:12], in_=s8[:, 4:8]), reads=[s8], writes=[s8])
            for hh in range(4):
                P.op("dve", lambda e, a=a, s8=s8, hh=hh: e.scalar_tensor_tensor(
                    out=a[:, hh * 256:(hh + 1) * 256], in0=a[:, hh * 256:(hh + 1) * 256], scalar=s8[:, 8 + hh:9 + hh], in1=gn[:],
                    op0=ALU.mult, op1=ALU.mult), reads=[a, s8, gn], writes=[a])
            P.op("pool", lambda e, a=a, z_=z_: e.tensor_tensor(out=a[:], in0=a[:], in1=z_[:], op=ALU.mult), reads=[a, z_], writes=[a])
            for hf in range(2):
                pt = PB[(tl % 2) * 2 + hf]
                for q in range(4):
                    cc = hf * 4 + q
                    P.op("pe", lambda e, a=a, q=q, cc=cc, pt=pt: e.transpose(pt[:, q * 128:(q + 1) * 128], a[:, cc * 128:(cc + 1) * 128], ident[:]),
                         reads=[a, ident], writes=[pt])
                if hf == 0:
                    P.op("act", lambda e, pt=pt, ot_=ot_: e.activation(out=ot_[:, 0:4, :], in_=pt[:].rearrange("p (a b) -> p a b", a=4),
                                                                       func=AF.Copy), reads=[pt], writes=[(ot_, 0)])
                else:
                    P.op("dve", lambda e, pt=pt, ot_=ot_: e.tensor_copy(out=ot_[:, 4:8, :], in_=pt[:].rearrange("p (a b) -> p a b", a=4)),
                         reads=[pt], writes=[(ot_, 1)])
            P.dma("sp", lambda e, ot_=ot_, tl=tl: e.dma_start(out=OT[:, tl * 128:(tl + 1) * 128].rearrange("(a p) t -> p a t", p=128),
                                                             in_=ot_[:]), reads=[ot_], writes=[(tOT, ("a", tl))])
        P.end_phase()


def rope_tables(pos):
    nf = 16
    inv = (10000.0 ** (-np.arange(nf, dtype=np.float32) / nf)).astype(np.float32)
    TS = len(pos)
    row = (pos // 64).astype(np.float32)
    col = (pos % 64).astype(np.float32)
    cos = np.zeros((64, TS), np.float32)
    sin = np.zeros((64, TS), np.float32)
    for a in range(2):
        p_ = row if a == 0 else col
        ang = p_[None, :] * inv[:, None]
        for half in range(2):
            d0 = a * 32 + half * 16
            cos[d0:d0 + 16] = np.cos(ang)
            sin[d0:d0 + 16] = np.sin(ang) * (-1.0 if half == 0 else 1.0)
    return cos, sin


def rope_perm():
    p = np.zeros(64, np.int64)
    for a in range(2):
        for half in range(2):
            for f in range(16):
                p[a * 32 + half * 16 + f] = a * 32 + (1 - half) * 16 + f
    return p


def shared_weights(inputs, mirror):
    f32 = lambda a: np.ascontiguousarray(np.asarray(a, dtype=np.float32))
    perm = rope_perm()
    ev_w_in = np.asarray(inputs["ev_w_in"][0])
    conv_w = np.asarray(inputs["ev_conv_w"][0])
    a_log = np.asarray(inputs["ev_gdn_a_log"][0])
    dtb = np.asarray(inputs["ev_gdn_dt_bias"][0])
    od_w_in = np.asarray(inputs["od_w_in"][0])
    w_gup = np.asarray(inputs["od_gla_w_gup"][0])
    b_g = np.asarray(inputs["od_gla_b_g"][0])
    if mirror:
        idx = np.arange(2512)
        idx[2048:2052], idx[2052:2056] = np.arange(2052, 2056), np.arange(2048, 2052)
        idx[2056:2060], idx[2060:2064] = np.arange(2060, 2064), np.arange(2056, 2060)
        ev_w_in = ev_w_in[:, idx]
        conv_w = conv_w[::-1]
        a_log = a_log[::-1]
        dtb = dtb[::-1]
        idx2 = np.arange(3104)
        idx2[3072:3088], idx2[3088:3104] = np.arange(3088, 3104), np.arange(3072, 3088)
        od_w_in = od_w_in[:, idx2]
        w_gup = w_gup[::-1]
        b_g = b_g[::-1]
    w_uq = np.asarray(inputs["ev_mla_w_uq"][0])
    return {
        "mod_w": f32(inputs["mod_w"]), "mod_b": f32(inputs["mod_b"]),
        "norm_pre": f32(inputs["norm_pre"]), "norm_post": f32(inputs["norm_post"]),
        "ffn_w_in": f32(inputs["ffn_w_in"]), "ffn_w_out": f32(inputs["ffn_w_out"]),
        "ev_w_in": f32(ev_w_in), "ev_w_in_perm": f32(ev_w_in[:, 2448:2512][:, perm]),
        "ev_conv_w": f32(conv_w), "ev_gdn_a_log": f32(a_log.reshape(8)), "ev_gdn_dt_bias": f32(dtb.reshape(8)),
        "ev_gdn_norm": f32(inputs["ev_gdn_norm"][0]), "ev_mla_q_norm": f32(inputs["ev_mla_q_norm"][0]),
        "ev_mla_w_uq": f32(w_uq),
        "ev_mla_w_uq_perm": f32(np.concatenate([w_uq[:, h * 192 + 128:h * 192 + 192][:, perm] for h in range(4)], axis=1)),
        "ev_mla_kv_norm": f32(inputs["ev_mla_kv_norm"][0]), "ev_mla_w_ukv": f32(inputs["ev_mla_w_ukv"][0]),
        "ev_w_out": f32(inputs["ev_w_out"][0]),
        "od_w_in": f32(od_w_in), "od_gla_w_gup": f32(w_gup), "od_gla_b_g": f32(b_g),
        "od_gla_norm": f32(inputs["od_gla_norm"][0]), "od_w_out": f32(inputs["od_w_out"][0]),
    }


def make_in_maps(cfg, inputs, cores):
    c = cfg
    f32 = lambda a: np.ascontiguousarray(np.asarray(a, dtype=np.float32))
    sh = [shared_weights(inputs, False), shared_weights(inputs, True)]
    maps = []
    for core in cores:
        sb, half = core // 2, core % 2
        mirror = (half == 1)
        xp = np.asarray(inputs["x_prompt"])[core * c.NPS:(core + 1) * c.NPS, :c.TP]
        xs = np.asarray(inputs["x_sample"])[sb, half * c.TS:(half + 1) * c.TS]
        pos = np.arange(half * c.TS, (half + 1) * c.TS)
        if mirror:
            xp = xp[:, ::-1]
            xs = xs[::-1]
            pos = pos[::-1]
        cos, sin = rope_tables(pos)
        m = dict(sh[half])
        m["rope_cos"], m["rope_sin"] = cos, sin
        m["xin"] = f32(np.concatenate([xp.reshape(c.NP, D), xs], axis=0))
        m["cond"] = f32(np.stack([np.asarray(inputs["c_ctx"]), np.asarray(inputs["c"])[sb]], axis=0))
        m["cache_ckv"] = f32(inputs["cache_mla_ckv"][sb, 0])
        m["cache_krope"] = f32(inputs["cache_mla_krope"][sb, 0])
        sg = np.asarray(inputs["state_gdn"][sb, 0])
        sl = np.asarray(inputs["state_gla"][sb, 0])
        m["state_gdn"] = f32(sg[::-1] if mirror else sg)
        m["state_gla"] = f32(sl[::-1] if mirror else sl)
        w = np.zeros((128, 2), np.float32)
        w[:, 1 - half] = 1.0
        m["selw"] = w
        maps.append(m)
    return maps


def assemble(cfg, rs, cores):
    c = cfg
    yp, ck, kr, gd, gl = [], [], [], [], []
    ys = {}
    for r, core in zip(rs, cores):
        sb, half = core // 2, core % 2
        mirror = (half == 1)
        y = r["y"]
        p_ = y[:c.NP].reshape(c.NPS, c.TP, D)
        s_ = y[c.NP:]
        a = r["ckv_out"].reshape(c.NPS, 1, c.TP, 128)
        b = r["krope_out"].reshape(c.NPS, 1, c.TP, 64)
        g1 = r["gdn_out"].reshape(c.NPS, 1, 2, 4, 128, 128)
        g2 = r["gla_out"].reshape(c.NPS, 1, 2, 4, 128, 256)
        if mirror:
            p_, s_ = p_[:, ::-1], s_[::-1]
            a, b = a[:, :, ::-1], b[:, :, ::-1]
            g1, g2 = g1[:, :, ::-1], g2[:, :, ::-1]
        yp.append(p_); ck.append(a); kr.append(b); gd.append(g1); gl.append(g2)
        ys.setdefault(sb, [None, None])[half] = s_
    y_sample = np.stack([np.concatenate(ys[k], axis=0) for k in sorted(ys)], axis=0)
    cat = lambda l: np.ascontiguousarray(np.concatenate(l, axis=0), dtype=np.float32)
    return (cat(yp), np.ascontiguousarray(y_sample, dtype=np.float32), cat(ck), cat(kr), cat(gd), cat(gl))


_NC_CACHE = {}


def kernel(**inputs):
    cfg = Cfg()
    cores = list(range(8))
    if "full" not in _NC_CACHE:
        _NC_CACHE["full"] = Builder(cfg).build()
    nc = _NC_CACHE["full"]
    maps = make_in_maps(cfg, inputs, cores)
    res = run_bass_kernel_spmd(nc, maps, core_ids=cores)
    return assemble(cfg, res.results, cores)
```

```python
import numpy as np
from contextlib import ExitStack
import concourse.bass as bass
import concourse.mybir as mybir
from concourse.bass_utils import run_bass_kernel_spmd

F32 = mybir.dt.float32
BF16 = mybir.dt.bfloat16
F32R = mybir.dt.float32r
AF = mybir.ActivationFunctionType
ALU = mybir.AluOpType
AX = mybir.AxisListType

ENGS = ("pe", "act", "dve", "pool", "sp")
NDMA = 20
D = 1024
DFF = 2816
EPS = 1e-6


class T:
    _n = 0

    def __init__(self, h, name, psum=False):
        self.h = h
        self.name = name
        self.recs = {}
        self.psum = psum

    def __getitem__(self, idx):
        return self.h[idx]


class Prog:
    def __init__(self, nc):
        self.nc = nc
        self.es = ExitStack()
        self.sem = {}
        for e in ENGS:
            self.sem["E:" + e] = self.es.enter_context(nc.semaphore("s_" + e))
        self.dsem = {}
        for q in ("sp", "pool", "act"):
            self.dsem[q] = []
            for i in range(56 if q == "pool" else (NDMA if q == "sp" else 2)):
                k = "D:%s%d" % (q, i)
                self.sem[k] = self.es.enter_context(nc.semaphore("d_%s%d" % (q, i)))
                self.dsem[q].append(k)
        self.csem = []
        for i in range(6):
            k = "C:%d" % i
            self.sem[k] = self.es.enter_context(nc.semaphore("c_%d" % i))
            self.csem.append(k)
        self.cctr = 0
        self.val = {k: 0 for k in self.sem}
        self.drr = {q: 0 for q in self.dsem}
        self.known = {e: {} for e in ENGS}
        self.ops = {e: [] for e in ENGS}
        self.nops = 0
        self.phase_es = None
        self.uid = 0

    def begin_phase(self):
        self.phase_es = ExitStack()

    def sb(self, name, shape, dt=F32, persistent=False, mid=False):
        es = self.es if persistent else (self.mid_es if mid else self.phase_es)
        self.uid += 1
        h = es.enter_context(self.nc.sbuf_tensor("%s_%d" % (name, self.uid), list(shape), dt))
        return T(h, name)

    def ps(self, name, shape, dt=F32, persistent=False):
        es = self.es if persistent else self.phase_es
        self.uid += 1
        h = es.enter_context(self.nc.psum_tensor("%s_%d" % (name, self.uid), list(shape), dt))
        return T(h, name, psum=True)

    def _conf(self, t, key):
        if key is None or t.psum:
            return list(t.recs.values())
        out = []
        if key in t.recs:
            out.append(t.recs[key])
        if None in t.recs:
            out.append(t.recs[None])
        return out

    def _deps(self, reads, writes, eng=None):
        deps = []
        for (t, key) in reads:
            for r in self._conf(t, key):
                if r["w"] is not None:
                    deps.append(r["w"])
                if t.psum and eng is not None:
                    for tok in r["r"]:
                        if tok[0] != "E:" + eng:
                            deps.append(tok)
        for (t, key) in writes:
            for r in self._conf(t, key):
                if r["w"] is not None:
                    deps.append(r["w"])
                deps.extend(r["r"])
        return deps

    def _commit(self, tok, reads, writes):
        for (t, key) in reads:
            r = t.recs.get(key)
            if r is None:
                r = t.recs[key] = {"w": None, "r": []}
            r["r"].append(tok)
            if len(r["r"]) > 48:
                best = {}
                for (s, v) in r["r"]:
                    if best.get(s, -1) < v:
                        best[s] = v
                r["r"] = list(best.items())
        for (t, key) in writes:
            if key is None:
                t.recs = {None: {"w": tok, "r": []}}
            else:
                t.recs[key] = {"w": tok, "r": []}

    def _waits(self, eng, deps):
        best = {}
        for (s, v) in deps:
            if best.get(s, -1) < v:
                best[s] = v
        kn = self.known[eng]
        out = []
        for s, v in best.items():
            if kn.get(s, 0) >= v:
                continue
            kn[s] = v
            out.append((s, v))
        return out

    @staticmethod
    def _norm(lst):
        out = []
        for a in lst:
            if isinstance(a, T):
                out.append((a, None))
            else:
                out.append(a)
        return out

    def op(self, eng, fn, reads=(), writes=()):
        reads = self._norm(reads)
        writes = self._norm(writes)
        deps = self._deps(reads, writes, eng)
        if eng == "pe":
            deps = [d for d in deps if d[0] != "E:pe"]
        waits = self._waits(eng, deps)
        sk = "E:" + eng
        self.val[sk] += 1
        tok = (sk, self.val[sk])
        self.ops[eng].append((waits, fn, (sk, 1)))
        self._commit(tok, reads, writes)
        self.nops += 1
        return tok

    def dma(self, q, fn, reads=(), writes=()):
        reads = self._norm(reads)
        writes = self._norm(writes)
        deps = self._deps(reads, writes)
        i = self.drr[q]
        self.drr[q] = (i + 1) % len(self.dsem[q])
        sk = self.dsem[q][i]
        if self.val[sk] > 0:
            deps.append((sk, self.val[sk]))
        waits = self._waits(q, deps)
        self.val[sk] += 16
        tok = (sk, self.val[sk])
        self.ops[q].append((waits, fn, (sk, 16)))
        self._commit(tok, reads, writes)
        self.nops += 1
        return tok

    def coll(self, fn, reads=(), writes=()):
        reads = self._norm(reads)
        writes = self._norm(writes)
        deps = self._deps(reads, writes)
        sk = self.csem[self.cctr % len(self.csem)]
        self.cctr += 1
        if self.val[sk] > 0:
            deps.append((sk, self.val[sk]))
        waits = self._waits("pool", deps)
        self.val[sk] += 1
        tok = (sk, self.val[sk])
        self.ops["pool"].append((waits, fn, (sk, 1)))
        self._commit(tok, reads, writes)
        self.nops += 1
        return tok

    def _replay(self, eng, e):
        for (waits, fn, inc) in self.ops[eng]:
            if fn is None:
                for (s, v) in waits:
                    e.wait_ge(self.sem[s], v)
                continue
            for (s, v) in waits[1:]:
                e.wait_ge(self.sem[s], v)
            ins = fn(e)
            if waits:
                ins._wait_ge(self.sem[waits[0][0]], waits[0][1])
            if inc is not None:
                ins.then_inc(self.sem[inc[0]], inc[1])

    def end_phase(self, final=False):
        allv = [(s, v) for s, v in self.val.items() if v > 0 and (final or not s.startswith("D:pool"))]
        for eng in ENGS:
            waits = self._waits(eng, allv)
            if waits:
                self.ops[eng].append((waits, None, None))
        nc = self.nc
        with nc.Block() as block:
            @block.tensor
            def _(e):
                self._replay("pe", e)

            @block.scalar
            def _(e):
                self._replay("act", e)

            @block.vector
            def _(e):
                self._replay("dve", e)

            @block.gpsimd
            def _(e):
                self._replay("pool", e)

            @block.sync
            def _(e):
                self._replay("sp", e)
        self.ops = {e: [] for e in ENGS}
        if self.phase_es is not None:
            self.phase_es.close()
            self.phase_es = None

    def close(self):
        self.es.close()


class Cfg:
    def __init__(self, TS=2048, NPS=4, TP=256, phases=None, dbg=False, groups=((0, 1), (2, 3), (4, 5), (6, 7))):
        self.rg = [list(g) for g in groups]
        self.TS = TS
        self.NPS = NPS
        self.TP = TP
        self.NP = NPS * TP
        self.NT = self.NP + TS
        self.phases = phases
        self.dbg = dbg


INPUT_SPECS = {
    "mod_w": (2, 1024, 9216), "mod_b": (2, 9216), "norm_pre": (2, 3, 1024), "norm_post": (2, 3, 1024),
    "ffn_w_in": (2, 2, 1024, 5632), "ffn_w_out": (2, 2, 2816, 1024),
    "ev_w_in": (1024, 2512), "ev_w_in_perm": (1024, 64), "ev_conv_w": (5, 1536), "ev_gdn_a_log": (8,), "ev_gdn_dt_bias": (8,),
    "ev_gdn_norm": (128,), "ev_mla_q_norm": (256,), "ev_mla_w_uq": (256, 768), "ev_mla_w_uq_perm": (256, 256),
    "ev_mla_kv_norm": (128,), "ev_mla_w_ukv": (128, 1024), "ev_w_out": (1024, 1024),
    "od_w_in": (1024, 3104), "od_gla_w_gup": (2, 16, 512), "od_gla_b_g": (2, 512), "od_gla_norm": (256,),
    "od_w_out": (1024, 1024),
}


class Builder:
    def __init__(self, cfg):
        self.cfg = cfg
        nc = self.nc = bass.Bass("TRN2", target_bir_lowering=False)
        self.P = Prog(nc)
        self.din = {}
        self.dout = {}
        self.dtr = {}
        c = cfg
        self.inp("xin", (c.NT, D))
        self.inp("cond", (2, D))
        self.inp("cache_ckv", (256, 128))
        self.inp("cache_krope", (256, 64))
        self.inp("state_gdn", (2, 4, 128, 128))
        self.inp("state_gla", (2, 4, 128, 256))
        self.inp("rope_cos", (64, c.TS))
        self.inp("rope_sin", (64, c.TS))
        self.inp("selw", (128, 2))
        for k, shp in INPUT_SPECS.items():
            self.inp(k, shp)
        self.out("y", (c.NT, D))
        self.out("ckv_out", (c.NP, 128))
        self.out("krope_out", (c.NP, 64))
        self.out("gdn_out", (c.NPS, 2, 4, 128, 128))
        self.out("gla_out", (c.NPS, 2, 4, 128, 256))

    def inp(self, name, shape):
        self.din[name] = self.nc.dram_tensor(name, list(shape), F32, kind="ExternalInput").ap()
        self.dtr[name] = T(None, name)

    def out(self, name, shape):
        self.dout[name] = self.nc.dram_tensor(name, list(shape), F32, kind="ExternalOutput").ap()
        self.dtr[name] = T(None, name)

    def scratch(self, name, shape, dt=F32):
        ap = self.nc.dram_tensor(name, list(shape), dt).ap()
        self.dtr[name] = T(None, name)
        self.din[name] = ap
        return ap

    def build(self):
        c = self.cfg
        P = self.P
        self.XS = self.scratch("XS", (c.NT, D))
        self.wbf = {}
        self.phase0()
        src = "xin"
        plan = [("ffn", 0, 0), ("mix", 0), ("ffn", 0, 1), ("ffn", 1, 0), ("mix", 1), ("ffn", 1, 1)]
        if c.phases is not None:
            plan = c.phases
        for i, ph in enumerate(plan):
            last = (i == len(plan) - 1)
            dst = "y" if last else "XS"
            if ph[0] == "ffn":
                self.ffn_phase(ph[1], ph[2], src, dst)
            else:
                self.mix_phase(ph[1], src, dst)
            src = dst
        P.begin_phase()
        P.end_phase(final=True)
        P.close()
        return self.nc

    def dr(self, name):
        return self.din[name] if name in self.din else self.dout[name]

    def cast_begin(self, name, ap2d, piece=64):
        R, C = ap2d.shape
        dst = self.scratch(name + "_bf", (R, C), BF16)
        self.wbf[name] = dst
        q = []
        for r0 in range(0, R, piece):
            q.append((name, ap2d, dst, r0, min(R, r0 + piece)))
        return q

    def cast_piece(self, q):
        if not q:
            return
        name, ap2d, dst, r0, r1 = q.pop(0)
        tr = self.dtr[name + "_bf"]
        self.P.dma("pool", lambda e: e.dma_start(out=dst[r0:r1, :], in_=ap2d[r0:r1, :]), writes=[(tr, r0)])

    def flush_deferred(self, upto=1):
        for k in sorted(getattr(self, "deferred", {})):
            if k <= upto:
                for (nm_, ap_, rp_) in self.deferred.pop(k):
                    self.cast_weight(nm_, ap_, rows_per=rp_)

    def cast_weight(self, name, ap2d, rows_per=256):
        P = self.P
        R, C = ap2d.shape
        dst = self.scratch(name + "_bf", (R, C), BF16)
        tr = self.dtr[name + "_bf"]
        r0 = 0
        while r0 < R:
            r1 = min(R, r0 + rows_per)
            P.dma("pool", lambda e, r0=r0, r1=r1: e.dma_start(out=dst[r0:r1, :], in_=ap2d[r0:r1, :]),
                  writes=[(tr, r0)])
            r0 = r1
        self.wbf[name] = dst
        return dst

    def phase0(self):
        c = self.cfg
        P = self.P
        nc = self.nc
        di = self.din
        P.begin_phase()
        self.ident = P.sb("ident", [128, 128], F32, persistent=True)
        self.ones = P.sb("ones", [128, 128], F32, persistent=True)
        self.epsc = P.sb("epsc", [128, 1], F32, persistent=True)
        ident, ones = self.ident, self.ones
        P.op("pool", lambda e: e.memset(ident[:], 1.0), writes=[ident])
        P.op("pool", lambda e: e.affine_select(out=ident[:], in_=ident[:], pattern=[[-1, 128]],
                                               compare_op=ALU.is_equal, fill=0.0, base=0, channel_multiplier=1),
             reads=[ident], writes=[ident])
        P.op("pool", lambda e: e.memset(ones[:], 1.0), writes=[ones])
        P.op("pool", lambda e: e.memset(self.epsc[:], EPS), writes=[self.epsc])
        self.cast_weight("ffn_in_00", di["ffn_w_in"][0, 0])
        self.castq = {}
        self.npre = P.sb("npre", [128, 48], F32, persistent=True)
        self.npost = P.sb("npost", [128, 48], F32, persistent=True)
        self.modbT = P.sb("modbT", [128, 144], F32, persistent=True)
        self.modsT = P.sb("modsT", [128, 2, 72, 2], F32, persistent=True)
        self.ABC = P.sb("ABC", [128, 2, 3, 2, 3, 8], F32, persistent=True)
        ld = P.sb("ld", [128, 128], F32)
        tps = P.ps("tps", [128, 128], F32)

        def load_T(src_ap, nrows, dst_tile, dst_off):
            P.dma("sp", lambda e: e.dma_start(out=ld[0:nrows, :], in_=src_ap), writes=[ld])
            P.op("pe", lambda e: e.transpose(tps[:, 0:nrows], ld[0:nrows, :], ident[0:nrows, 0:nrows]),
                 reads=[ld, ident], writes=[tps])
            P.op("dve", lambda e: e.tensor_copy(out=dst_tile[:, dst_off:dst_off + nrows], in_=tps[:, 0:nrows]),
                 reads=[tps], writes=[dst_tile])

        load_T(di["norm_pre"].rearrange("l s (c p) -> (l s c) p", p=128), 48, self.npre, 0)
        load_T(di["norm_post"].rearrange("l s (c p) -> (l s c) p", p=128), 48, self.npost, 0)
        for l in range(2):
            load_T(di["mod_b"][l].rearrange("(m p) -> m p", p=128), 72, self.modbT, l * 72)
        scT = P.sb("scT", [128, 16], F32)
        sc2 = P.sb("sc2", [128, 8, 2], F32)
        load_T(di["cond"].rearrange("r (c p) -> (r c) p", p=128), 16, scT, 0)
        P.op("act", lambda e: e.activation(out=scT[:], in_=scT[:], func=AF.Silu), reads=[scT], writes=[scT])
        for r in range(2):
            P.op("dve", lambda e, r=r: e.tensor_copy(out=sc2[:, :, r], in_=scT[:, r * 8:(r + 1) * 8]),
                 reads=[scT], writes=[sc2])
        wmb = [P.sb("wmb%d" % i, [128, 8, 512], F32) for i in range(2)]
        mps = [P.ps("mps%d" % l, [128, 144], F32) for l in range(2)]
        mrow = [P.sb("mrow%d" % l, [2, 9216], F32) for l in range(2)]
        pacc = [P.ps("pacc%d" % i, [2, 512], F32) for i in range(2)]
        bi = 0
        for l in range(2):
            for cb in range(18):
                w = wmb[bi % 2]
                pa = pacc[bi % 2]
                bi += 1
                for h2 in range(2):
                    P.dma("sp", lambda e, w=w, l=l, cb=cb, h2=h2: e.dma_start(
                        out=w[:, h2 * 4:(h2 + 1) * 4, :],
                        in_=di["mod_w"][l, h2 * 512:(h2 + 1) * 512, cb * 512:(cb + 1) * 512].rearrange(
                            "(kc p) n -> p kc n", p=128)), writes=[(w, h2)])
                for k in range(8):
                    P.op("pe", lambda e, w=w, k=k, pa=pa: e.matmul(pa[0:2, :], sc2[:, k, :], w[:, k, :], start=(k == 0), stop=(k == 7)),
                         reads=[(w, k // 4), sc2], writes=[pa])
                P.op("act", lambda e, l=l, cb=cb, pa=pa: e.activation(out=mrow[l][0:2, cb * 512:(cb + 1) * 512], in_=pa[0:2, :], func=AF.Copy),
                     reads=[pa], writes=[(mrow[l], cb)])
        for l in range(2):
            for m in range(72):
                P.op("pe", lambda e, l=l, m=m: e.transpose(mps[l][:, m * 2:m * 2 + 2], mrow[l][0:2, m * 128:(m + 1) * 128], ident[0:2, 0:2]),
                     reads=[(mrow[l], m // 4), ident], writes=[mps[l]])
        modsT, modbT = self.modsT, self.modbT
        for l in range(2):
            P.op("dve", lambda e, l=l: e.tensor_tensor(
                out=modsT[:, l], in0=mps[l][:].rearrange("p (m r) -> p m r", r=2),
                in1=modbT[:, l * 72:(l + 1) * 72].unsqueeze(2).to_broadcast([128, 72, 2]), op=ALU.add),
                reads=[mps[l], modbT], writes=[modsT])
        ABC = self.ABC
        for l in range(2):
            for s in range(3):
                res = 1.0 if s == 1 else 0.5
                for r in range(2):
                    gp = self.npre[:, (l * 3 + s) * 8:(l * 3 + s) * 8 + 8]
                    gq = self.npost[:, (l * 3 + s) * 8:(l * 3 + s) * 8 + 8]
                    sh = modsT[:, l, (3 * s) * 8:(3 * s) * 8 + 8, r]
                    scl = modsT[:, l, (3 * s + 1) * 8:(3 * s + 1) * 8 + 8, r]
                    gt = modsT[:, l, (3 * s + 2) * 8:(3 * s + 2) * 8 + 8, r]
                    P.op("dve", lambda e, l=l, s=s, r=r, gp=gp, scl=scl: e.scalar_tensor_tensor(
                        out=ABC[:, l, s, r, 0, :], in0=scl, scalar=1.0, in1=gp, op0=ALU.add, op1=ALU.mult),
                        reads=[modsT, self.npre], writes=[(ABC, (l, s, r, 0))])
                    P.op("dve", lambda e, l=l, s=s, r=r, sh=sh: e.tensor_copy(out=ABC[:, l, s, r, 1, :], in_=sh),
                         reads=[modsT], writes=[(ABC, (l, s, r, 1))])
                    P.op("dve", lambda e, l=l, s=s, r=r, gt=gt, gq=gq, res=res: e.scalar_tensor_tensor(
                        out=ABC[:, l, s, r, 2, :], in0=gt, scalar=res, in1=gq, op0=ALU.mult, op1=ALU.mult),
                        reads=[modsT, self.npost], writes=[(ABC, (l, s, r, 2))])
        P.end_phase()

    def groups(self):
        c = self.cfg
        gs = []
        for r0 in range(0, c.NP, 512):
            gs.append((r0, 0))
        for r0 in range(c.NP, c.NT, 512):
            gs.append((r0, 1))
        return gs

    def make_cbc(self, l, s, PB):
        P = self.P
        ABC, ident, ones = self.ABC, self.ident, self.ones
        out = []
        dg = P.sb("dg", [128, 128], F32)
        for r in range(2):
            cb = P.sb("cbc%d" % r, [128, 1024], F32)
            pb = [PB[2 * r], PB[2 * r + 1]]
            for cc in range(8):
                P.op("dve", lambda e, cc=cc, r=r: e.tensor_scalar(
                    out=dg[:], in0=ident[:], scalar1=ABC[:, l, s, r, 2, cc:cc + 1], scalar2=None, op0=ALU.mult),
                    reads=[ident, ABC], writes=[dg])
                P.op("pe", lambda e, cc=cc, r=r, pb=pb: e.matmul(
                    pb[cc // 4][:, (cc % 4) * 128:(cc % 4 + 1) * 128], ones[:], dg[:], start=True, stop=True),
                    reads=[ones, dg], writes=[(pb[cc // 4], cc % 4)])
            for i in range(2):
                P.op("act", lambda e, i=i, pb=pb, cb=cb: e.activation(
                    out=cb[:, i * 512:(i + 1) * 512], in_=pb[i][:], func=AF.Copy), reads=[pb[i]], writes=[(cb, i)])
            out.append(cb)
        return out

    def ffn_phase(self, l, f, src, dst):
        c = self.cfg
        P = self.P
        s = 0 if f == 0 else 2
        ABC, ident = self.ABC, self.ident
        srcap, dstap = self.dr(src), self.dr(dst)
        srct, dstt = self.dtr[src], self.dtr[dst]
        win = self.wbf["ffn_in_%d%d" % (l, f)]
        wint = self.dtr["ffn_in_%d%d_bf" % (l, f)]
        P.begin_phase()
        PB = [P.ps("pb%d" % i, [128, 512], F32) for i in range(8)]
        import os
        parts = os.environ.get("FFN_PARTS", "cbc,wo,load,pro,in,out").split(",")
        if "cbc" in parts:
            cbc = self.make_cbc(l, s, PB)
        wo = P.sb("wo", [128, 22, 1024], BF16)
        wout32 = self.din["ffn_w_out"][l, f]
        for q in range(2 if "wo" in parts else 0):
            P.dma("pool", lambda e, q=q: e.dma_start(
                out=wo[:, q * 11:(q + 1) * 11, :],
                in_=wout32[q * 1408:(q + 1) * 1408, :].rearrange("(j p) n -> p j n", p=128)),
                writes=[(wo, q)])
        wi = [P.sb("wi%d" % i, [128, 8, 512], BF16) for i in range(3)]
        xg = [P.sb("xg%d" % i, [128, 4, 1024], F32) for i in range(2)]
        hT = [P.sb("hT%d" % i, [128, 8, 512], BF16) for i in range(2)]
        gT = [P.sb("gT%d" % i, [128, 22, 512], BF16) for i in range(2)]
        xn = [P.sb("xn%d" % i, [128, 1024], F32) for i in range(2)]
        su = [P.sb("su%d" % i, [128, 512], F32) for i in range(2)]
        junk = P.sb("junk", [128, 1024], F32)
        junk2 = P.sb("junk2", [128, 1024], F32)
        tmp = [P.sb("tmp%d" % i, [128, 1024], F32) for i in range(2)]
        ss = [P.sb("ss%d" % i, [128, 24], F32) for i in range(2)]
        gs = self.groups()
        wctr = [0]

        def load(gi):
            r0, r = gs[gi]
            x = xg[gi % 2]
            P.dma("sp", lambda e: e.dma_start(out=x[:], in_=srcap[r0:r0 + 512, :].rearrange("(t p) d -> p t d", p=128)),
                  reads=[srct], writes=[x])

        def pro_a(gi):
            r0, r = gs[gi]
            x = xg[gi % 2]
            sst = ss[gi % 2]
            P.op("dve", lambda e: e.memset(sst[:], 0.0), writes=[sst])
            for t in range(4):
                P.op("act", lambda e, t=t: e.activation(out=junk2[:], in_=x[:, t, :], func=AF.Square,
                                                        accum_out=sst[:, t:t + 1]),
                     reads=[(x, None)], writes=[junk2, (sst, t)])
            P.op("act", lambda e: e.activation(out=sst[:, 4:8], in_=sst[:, 0:4], func=AF.Sqrt, scale=1.0 / D,
                                               bias=self.epsc[:, 0:1]), reads=[sst, self.epsc], writes=[(sst, "sq")])
            P.op("dve", lambda e: e.reciprocal(out=sst[:, 8:12], in_=sst[:, 4:8]), reads=[(sst, "sq")],
                 writes=[(sst, "rs")])
            for t in range(2):
                xx = xn[t % 2]
                P.op("act", lambda e, t=t, xx=xx: e.activation(out=xx[:], in_=x[:, t, :], func=AF.Identity,
                                                               scale=sst[:, 8 + t:9 + t]),
                     reads=[(x, None), (sst, "rs")], writes=[xx])

        def pro_b(gi):
            r0, r = gs[gi]
            x = xg[gi % 2]
            sst = ss[gi % 2]
            h = hT[gi % 2]
            for t in range(4):
                xx = xn[t % 2]
                if t >= 2:
                    P.op("act", lambda e, t=t, xx=xx: e.activation(out=xx[:], in_=x[:, t, :], func=AF.Identity,
                                                                   scale=sst[:, 8 + t:9 + t]),
                         reads=[(x, None), (sst, "rs")], writes=[xx])
                for half in range(2):
                    pb = PB[4 + half]
                    for q in range(4):
                        cc = half * 4 + q
                        P.op("pe", lambda e, xx=xx, cc=cc, q=q, pb=pb: e.transpose(
                            pb[:, q * 128:(q + 1) * 128], xx[:, cc * 128:(cc + 1) * 128], ident[:]),
                            reads=[xx, ident], writes=[(pb, q)])
                    for q in range(4):
                        cc = half * 4 + q
                        if half == 0:
                            P.op("act", lambda e, cc=cc, q=q, pb=pb, t=t: e.activation(
                                out=h[:, cc, t * 128:(t + 1) * 128], in_=pb[:, q * 128:(q + 1) * 128],
                                func=AF.Identity, scale=ABC[:, l, s, r, 0, cc:cc + 1],
                                bias=ABC[:, l, s, r, 1, cc:cc + 1]),
                                reads=[(pb, q), ABC], writes=[(h, (cc, t))])
                        else:
                            P.op("dve", lambda e, cc=cc, q=q, pb=pb, t=t: e.tensor_scalar(
                                out=h[:, cc, t * 128:(t + 1) * 128], in0=pb[:, q * 128:(q + 1) * 128],
                                scalar1=ABC[:, l, s, r, 0, cc:cc + 1], scalar2=ABC[:, l, s, r, 1, cc:cc + 1],
                                op0=ALU.mult, op1=ALU.add),
                                reads=[(pb, q), ABC], writes=[(h, (cc, t))])

        wmap = {}

        def wload(gi, b):
            if (gi, b) in wmap:
                return
            w = wi[wctr[0] % 3]
            wctr[0] += 1
            wmap[(gi, b)] = w
            for uv in range(2):
                P.dma("sp", lambda e, w=w, b=b, uv=uv: e.dma_start(
                    out=w[:, :, uv * 256:(uv + 1) * 256],
                    in_=win[:, uv * DFF + b * 256: uv * DFF + (b + 1) * 256].rearrange("(kc p) n -> p kc n", p=128)),
                    reads=[wint], writes=[(w, uv)])

        def inproj(gi, hook=None):
            h = hT[gi % 2]
            g = gT[gi % 2]
            for b in range(11):
                if b == 4 and hook is not None:
                    hook()
                wload(gi, b)
                w = wmap.pop((gi, b))
                self.cast_piece(self.cq_cur)
                for jj in range(2):
                    j = 2 * b + jj
                    pu = PB[(j % 2) * 2]
                    pv = PB[(j % 2) * 2 + 1]
                    for k in range(8):
                        P.op("pe", lambda e, w=w, k=k, jj=jj, pu=pu: e.matmul(
                            pu[:], w[:, k, jj * 128:(jj + 1) * 128], h[:, k, :], start=(k == 0), stop=(k == 7)),
                            reads=[(w, 0), h], writes=[pu])
                    for k in range(8):
                        P.op("pe", lambda e, w=w, k=k, jj=jj, pv=pv: e.matmul(
                            pv[:], w[:, k, 256 + jj * 128:256 + (jj + 1) * 128], h[:, k, :], start=(k == 0),
                            stop=(k == 7)),
                            reads=[(w, 1), h], writes=[pv])
                    sj = su[j % 2]
                    P.op("act", lambda e, sj=sj, pu=pu: e.activation(out=sj[:], in_=pu[:], func=AF.Silu),
                         reads=[pu], writes=[sj])
                    P.op("dve", lambda e, sj=sj, pv=pv, j=j: e.tensor_tensor(out=g[:, j, :], in0=sj[:], in1=pv[:],
                                                                             op=ALU.mult),
                         reads=[sj, pv], writes=[(g, j)])

        def outproj(gi):
            r0, r = gs[gi]
            x = xg[gi % 2]
            g = gT[gi % 2]
            sst = ss[gi % 2]
            for t in range(4):
                py = [PB[6], PB[7]] if t % 2 == 0 else [PB[4], PB[5]]
                for nb in range(2):
                    for j in range(22):
                        P.op("pe", lambda e, nb=nb, j=j, t=t, py=py: e.matmul(
                            py[nb][:], g[:, j, t * 128:(t + 1) * 128], wo[:, j, nb * 512:(nb + 1) * 512],
                            start=(j == 0), stop=(j == 21)),
                            reads=[g, (wo, j // 11)], writes=[py[nb]])
                for nb in range(2):
                    P.op("act", lambda e, nb=nb, t=t, py=py: e.activation(
                        out=junk[:, 0:512], in_=py[nb][:], func=AF.Square, accum_out=sst[:, 12 + 2 * t + nb:13 + 2 * t + nb]),
                        reads=[py[nb]], writes=[junk, (sst, ("y", t, nb))])
                P.op("dve", lambda e, t=t: e.tensor_tensor(out=sst[:, 20 + (t % 2) * 2:21 + (t % 2) * 2], in0=sst[:, 12 + 2 * t:13 + 2 * t],
                                                           in1=sst[:, 13 + 2 * t:14 + 2 * t], op=ALU.add),
                     reads=[(sst, ("y", t, 0)), (sst, ("y", t, 1))], writes=[(sst, ("ys", t % 2))])
                P.op("act", lambda e, t=t: e.activation(out=sst[:, 20 + (t % 2) * 2:21 + (t % 2) * 2], in_=sst[:, 20 + (t % 2) * 2:21 + (t % 2) * 2],
                                                        func=AF.Sqrt, scale=1.0 / D, bias=self.epsc[:, 0:1]),
                     reads=[(sst, ("ys", t % 2)), self.epsc], writes=[(sst, ("ys", t % 2))])
                P.op("dve", lambda e, t=t: e.reciprocal(out=sst[:, 21 + (t % 2) * 2:22 + (t % 2) * 2], in_=sst[:, 20 + (t % 2) * 2:21 + (t % 2) * 2]),
                     reads=[(sst, ("ys", t % 2))], writes=[(sst, ("yr", t % 2))])
                tm = tmp[t % 2]
                for nb in range(2):
                    P.op("dve", lambda e, nb=nb, t=t, py=py, tm=tm: e.scalar_tensor_tensor(
                        out=tm[:, nb * 512:(nb + 1) * 512], in0=py[nb][:], scalar=sst[:, 21 + (t % 2) * 2:22 + (t % 2) * 2],
                        in1=cbc[r][:, nb * 512:(nb + 1) * 512], op0=ALU.mult, op1=ALU.mult),
                        reads=[py[nb], (sst, ("yr", t % 2)), cbc[r]], writes=[(tm, nb)])
                P.op("pool", lambda e, t=t, tm=tm: e.tensor_tensor(out=tm[:], in0=tm[:], in1=x[:, t, :], op=ALU.add),
                     reads=[tm, (x, None)], writes=[tm])
                P.dma("sp", lambda e, t=t, tm=tm: e.dma_start(out=dstap[r0 + t * 128:r0 + (t + 1) * 128, :], in_=tm[:]),
                      reads=[tm], writes=[(dstt, r0 + t * 128)])

        ng = len(gs)
        di_ = self.din
        cq = []
        for (l_, f_) in ((0, 1), (1, 0), (1, 1)):
            if (l, f) == (l_, f_) and ("ffn_in_%d%d" % (l_, f_)) not in self.wbf:
                q_ = self.cast_begin("ffn_in_%d%d" % (l_, f_), di_["ffn_w_in"][l_, f_], piece=256)
                while q_:
                    self.cast_piece(q_)
        self.cq_cur = cq
        load(0)
        pro_a(0)
        pro_b(0)
        for gi in range(ng):
            if gi + 1 < ng:
                load(gi + 1)
                inproj(gi, hook=lambda gi=gi: pro_a(gi + 1))
                pro_b(gi + 1)
                for b_ in range(3):
                    wload(gi + 1, b_)
            else:
                inproj(gi)
            outproj(gi)
        while self.cq_cur:
            self.cast_piece(self.cq_cur)
        P.end_phase()

    def mix_phase(self, l, src, dst):
        if l == 0:
            self.even_mixer(src, dst)
        else:
            self.odd_mixer(src, dst)

    def pro_hT(self, x, sst, h, l, s, r, pbs, xn, junk):
        P = self.P
        ABC, ident = self.ABC, self.ident
        P.op("pool", lambda e: e.memset(sst[:], 0.0), writes=[sst])
        for t in range(4):
            P.op("act", lambda e, t=t: e.activation(out=junk[:], in_=x[:, t, :], func=AF.Square,
                                                    accum_out=sst[:, t:t + 1]),
                 reads=[(x, None)], writes=[junk, (sst, t)])
        P.op("act", lambda e: e.activation(out=sst[:, 4:8], in_=sst[:, 0:4], func=AF.Sqrt, scale=1.0 / D,
                                           bias=self.epsc[:, 0:1]), reads=[sst, self.epsc], writes=[(sst, "sq")])
        P.op("dve", lambda e: e.reciprocal(out=sst[:, 8:12], in_=sst[:, 4:8]), reads=[(sst, "sq")],
             writes=[(sst, "rs")])
        for t in range(4):
            xx = xn[t % 2]
            P.op("act", lambda e, t=t, xx=xx: e.activation(out=xx[:], in_=x[:, t, :], func=AF.Identity,
                                                           scale=sst[:, 8 + t:9 + t]),
                 reads=[(x, None), (sst, "rs")], writes=[xx])
            for half in range(2):
                pb = pbs[half]
                for q in range(4):
                    cc = half * 4 + q
                    P.op("pe", lambda e, xx=xx, cc=cc, q=q, pb=pb: e.transpose(
                        pb[:, q * 128:(q + 1) * 128], xx[:, cc * 128:(cc + 1) * 128], ident[:]),
                        reads=[xx, ident], writes=[(pb, q)])
                for q in range(4):
                    cc = half * 4 + q
                    P.op("act", lambda e, cc=cc, q=q, pb=pb, t=t: e.activation(
                        out=h[:, cc, t * 128:(t + 1) * 128], in_=pb[:, q * 128:(q + 1) * 128],
                        func=AF.Identity, scale=ABC[:, l, s, r, 0, cc:cc + 1],
                        bias=ABC[:, l, s, r, 1, cc:cc + 1]),
                        reads=[(pb, q), ABC], writes=[(h, (cc, t))])

    def epi(self, py, xap, xdeps, r, sst, c0, cbc, tm, junk, dstap, dstt, row0):
        P = self.P
        for nb in range(2):
            P.op("act", lambda e, nb=nb: e.activation(out=junk[:, 0:512], in_=py[nb][:], func=AF.Square,
                                                      accum_out=sst[:, c0 + nb:c0 + nb + 1]),
                 reads=[py[nb]], writes=[junk, (sst, ("y", c0, nb))])
        P.op("dve", lambda e: e.tensor_tensor(out=sst[:, c0 + 2:c0 + 3], in0=sst[:, c0:c0 + 1], in1=sst[:, c0 + 1:c0 + 2],
                                              op=ALU.add),
             reads=[(sst, ("y", c0, 0)), (sst, ("y", c0, 1))], writes=[(sst, ("ys", c0))])
        P.op("act", lambda e: e.activation(out=sst[:, c0 + 2:c0 + 3], in_=sst[:, c0 + 2:c0 + 3], func=AF.Sqrt,
                                           scale=1.0 / D, bias=self.epsc[:, 0:1]),
             reads=[(sst, ("ys", c0)), self.epsc], writes=[(sst, ("ys", c0))])
        P.op("dve", lambda e: e.reciprocal(out=sst[:, c0 + 3:c0 + 4], in_=sst[:, c0 + 2:c0 + 3]),
             reads=[(sst, ("ys", c0))], writes=[(sst, ("yr", c0))])
        for nb in range(2):
            P.op("dve", lambda e, nb=nb: e.scalar_tensor_tensor(
                out=tm[:, nb * 512:(nb + 1) * 512], in0=py[nb][:], scalar=sst[:, c0 + 3:c0 + 4],
                in1=cbc[r][:, nb * 512:(nb + 1) * 512], op0=ALU.mult, op1=ALU.mult),
                reads=[py[nb], (sst, ("yr", c0)), cbc[r]], writes=[(tm, nb)])
        P.op("pool", lambda e: e.tensor_tensor(out=tm[:], in0=tm[:], in1=xap, op=ALU.add),
             reads=[tm] + xdeps, writes=[tm])
        P.dma("sp", lambda e: e.dma_start(out=dstap[row0:row0 + 128, :], in_=tm[:]),
              reads=[tm], writes=[(dstt, row0)])

    def pair_buf(self, name, rows, cols, dt=F32):
        gin = self.scratch(name + "_in", (rows, cols), dt)
        gout = self.scratch(name + "_out", (2 * rows, cols), dt)
        return gin, gout, self.dtr[name + "_in"], self.dtr[name + "_out"]

    def pair_gather(self, gin, gout, tin, tout):
        rg = self.cfg.rg
        self.P.coll(lambda e: e.collective_compute("AllGather", ALU.bypass, replica_groups=rg, ins=[gin.opt()], outs=[gout.opt()]),
                    reads=[tin], writes=[tout])

    def kcol(self, row):
        return row if row < self.cfg.NP else row + 256

    def seqs(self):
        c = self.cfg
        out = [(i * c.TP, c.TP, 0, "p") for i in range(c.NPS)]
        out.append((c.NP, c.TS, 1, "s"))
        return out

    def even_mixer(self, src, dst):
        c = self.cfg
        NT, NP = c.NT, c.NP
        NKV = NT + 256
        self.QKVT = self.scratch("QKVT", (1536, NT))
        self.ZS = self.scratch("ZS", (NT, 512))
        self.GB = self.scratch("GB", (NT, 16))
        self.QT = self.scratch("QT", (4, 192, NT), BF16)
        self.CKVT = self.scratch("CKVT", (128, NKV), BF16)
        self.KRT = self.scratch("KRT", (65, NKV), BF16)
        self.OT = self.scratch("OT", (1024, NT), BF16)
        self.OD = self.scratch("OD", (2, NT, 512))
        import os
        parts = os.environ.get("EV_PARTS", "e1,e2,e3,e4").split(",")
        if "e1" in parts:
            self.ev_e1(src)
        if "e2" in parts:
            self.ev_e2()
        if "e3" in parts:
            self.ev_e3()
        if "e4" in parts:
            self.mix_out(0, src, dst, "ev_w_out")

    def ev_e1(self, src):
        c = self.cfg
        P = self.P
        di = self.din
        l, s = 0, 1
        NT, NP = c.NT, c.NP
        ident = self.ident
        srcap, srct = self.dr(src), self.dtr[src]
        win = di["ev_w_in"]
        winp = di["ev_w_in_perm"]
        QKVT, ZS, GB, QT, CKVT, KRT = self.QKVT, self.ZS, self.GB, self.QT, self.CKVT, self.KRT
        tQKVT, tZS, tGB, tQT, tCKVT, tKRT = [self.dtr[n] for n in ("QKVT", "ZS", "GB", "QT", "CKVT", "KRT")]
        P.begin_phase()
        PB = [P.ps("pb%d" % i, [128, 512], F32) for i in range(8)]
        wsb = P.sb("wsb", [128, 8, 2512], BF16)
        for q in range(4):
            P.dma("pool", lambda e, q=q: e.dma_start(
                out=wsb[:, q * 2:(q + 1) * 2, :], in_=win[q * 256:(q + 1) * 256, :].rearrange("(kc p) n -> p kc n", p=128)),
                writes=[(wsb, q)])
        wpsb = P.sb("wpsb", [128, 8, 64], BF16)
        P.dma("pool", lambda e: e.dma_start(out=wpsb[:], in_=winp.rearrange("(kc p) n -> p kc n", p=128)),
              writes=[wpsb])
        ld = P.sb("ld", [128, 128], F32)
        qnT = P.sb("qnT", [128, 2], F32)
        P.dma("sp", lambda e: e.dma_start(out=ld[0:2, :], in_=di["ev_mla_q_norm"].rearrange("(c p) -> c p", p=128)),
              writes=[ld])
        P.op("pe", lambda e: e.transpose(PB[0][:, 0:2], ld[0:2, :], ident[0:2, 0:2]), reads=[ld, ident], writes=[PB[0]])
        P.op("dve", lambda e: e.tensor_copy(out=qnT[:], in_=PB[0][:, 0:2]), reads=[PB[0]], writes=[qnT])
        wq32 = P.sb("wq32", [128, 2, 1024], F32)
        P.dma("sp", lambda e: e.dma_start(out=wq32[:, :, 0:768], in_=di["ev_mla_w_uq"].rearrange("(kc p) n -> p kc n", p=128)),
              writes=[(wq32, 0)])
        P.dma("sp", lambda e: e.dma_start(out=wq32[:, :, 768:1024], in_=di["ev_mla_w_uq_perm"].rearrange("(kc p) n -> p kc n", p=128)),
              writes=[(wq32, 1)])
        wuq = P.sb("wuq", [128, 2, 1024], BF16)
        for kc in range(2):
            P.op("dve", lambda e, kc=kc: e.tensor_scalar(out=wuq[:, kc, :], in0=wq32[:, kc, :], scalar1=qnT[:, kc:kc + 1],
                                                         scalar2=None, op0=ALU.mult),
                 reads=[wq32, qnT], writes=[(wuq, kc)])
        dtb = P.sb("dtb", [128, 8], F32)
        nea = P.sb("nea", [128, 8], F32)
        kvn = P.sb("kvn", [128, 128], F32)
        P.dma("sp", lambda e: e.dma_start(out=dtb[:], in_=di["ev_gdn_dt_bias"].partition_broadcast(128)), writes=[dtb])
        P.dma("sp", lambda e: e.dma_start(out=nea[:], in_=di["ev_gdn_a_log"].partition_broadcast(128)), writes=[nea])
        P.dma("sp", lambda e: e.dma_start(out=kvn[:], in_=di["ev_mla_kv_norm"].partition_broadcast(128)), writes=[kvn])
        P.op("act", lambda e: e.activation(out=nea[:], in_=nea[:], func=AF.Exp), reads=[nea], writes=[nea])
        P.op("dve", lambda e: e.tensor_scalar(out=nea[:], in0=nea[:], scalar1=-1.0, scalar2=None, op0=ALU.mult),
             reads=[nea], writes=[nea])
        cst = P.sb("cst", [128, 256], BF16)
        for i in range(2):
            P.dma("sp", lambda e, i=i: e.dma_start(out=ld[:], in_=di["cache_ckv"][i * 128:(i + 1) * 128, :]), writes=[ld])
            P.op("pe", lambda e: e.transpose(PB[1][:, 0:128], ld[:], ident[:]), reads=[ld, ident], writes=[PB[1]])
            P.op("dve", lambda e, i=i: e.tensor_copy(out=cst[:, i * 128:(i + 1) * 128], in_=PB[1][:, 0:128]),
                 reads=[PB[1]], writes=[(cst, i)])
        P.dma("sp", lambda e: e.dma_start(out=CKVT[:, NP:NP + 256], in_=cst[:]), reads=[cst], writes=[(tCKVT, "c")])
        cst2 = P.sb("cst2", [64, 256], BF16)
        for i in range(2):
            P.dma("sp", lambda e, i=i: e.dma_start(out=ld[:, 0:64], in_=di["cache_krope"][i * 128:(i + 1) * 128, :]),
                  writes=[ld])
            P.op("pe", lambda e: e.transpose(PB[1][0:64, 0:128], ld[:, 0:64], ident[:]), reads=[ld, ident],
                 writes=[PB[1]])
            P.op("dve", lambda e, i=i: e.tensor_copy(out=cst2[:, i * 128:(i + 1) * 128], in_=PB[1][0:64, 0:128]),
                 reads=[PB[1]], writes=[(cst2, i)])
        P.dma("sp", lambda e: e.dma_start(out=KRT[0:64, NP:NP + 256], in_=cst2[:]), reads=[cst2], writes=[(tKRT, "c")])
        onesrow = P.sb("onesrow", [1, NT + 256], BF16)
        P.op("pool", lambda e: e.memset(onesrow[:], 1.0), writes=[onesrow])
        P.dma("sp", lambda e: e.dma_start(out=KRT[64:65, :], in_=onesrow[:]), reads=[onesrow], writes=[(tKRT, "o")])

        xg = [P.sb("xg%d" % i, [128, 4, 1024], F32) for i in range(2)]
        hT = [P.sb("hT%d" % i, [128, 8, 512], BF16) for i in range(2)]
        xn = [P.sb("xn%d" % i, [128, 1024], F32) for i in range(2)]
        junk = P.sb("junk", [128, 1024], F32)
        ss = [P.sb("ss%d" % i, [128, 24], F32) for i in range(2)]
        stg = [P.sb("stg%d" % i, [128, 512], F32) for i in range(3)]
        zst = [P.sb("zst%d" % i, [128, 512], F32) for i in range(2)]
        sm = [P.sb("sm%d" % i, [128, 64], F32) for i in range(2)]
        gb = [P.sb("gb%d" % i, [128, 16], F32) for i in range(2)]
        cqn = [P.sb("cqn%d" % i, [128, 256], F32) for i in range(2)]
        ckn = [P.sb("ckn%d" % i, [128, 128], F32) for i in range(2)]
        krt = [P.sb("krt%d" % i, [128, 64], F32) for i in range(2)]
        cqnT = P.sb("cqnT", [128, 2, 512], BF16)
        ckT = P.sb("ckT", [128, 512], BF16)
        cs = [P.sb("cos%d" % i, [64, 512], F32) for i in range(2)]
        sn = [P.sb("sin%d" % i, [64, 512], F32) for i in range(2)]
        rt = [P.sb("rt%d" % i, [64, 512], F32) for i in range(2)]
        rb = [P.sb("rb%d" % i, [128, 512], BF16) for i in range(3)]
        gs = self.groups()
        ctr = {"stg": 0, "rb": 0, "pb": 0}

        def rope_or_copy(pa, pbp, kind, gi, outbf, nrow=64):
            if kind == "p":
                P.op("act", lambda e: e.activation(out=outbf[0:64, :], in_=pa[0:64, :], func=AF.Copy), reads=[pa],
                     writes=[outbf])
            else:
                a, b = rt[0], rt[1]
                P.op("dve", lambda e: e.tensor_tensor(out=a[:], in0=pa[0:64, :], in1=cs[gi % 2][:], op=ALU.mult),
                     reads=[pa, cs[gi % 2]], writes=[a])
                P.op("dve", lambda e: e.tensor_tensor(out=b[:], in0=pbp[0:64, :], in1=sn[gi % 2][:], op=ALU.mult),
                     reads=[pbp, sn[gi % 2]], writes=[b])
                P.op("pool", lambda e: e.tensor_tensor(out=outbf[0:64, :], in0=a[:], in1=b[:], op=ALU.add),
                     reads=[a, b], writes=[outbf])

        def load(gi):
            r0, r = gs[gi]
            x = xg[gi % 2]
            P.dma("sp", lambda e: e.dma_start(out=x[:], in_=srcap[r0:r0 + 512, :].rearrange("(t p) d -> p t d", p=128)),
                  reads=[srct], writes=[x])
            if r == 1:
                t0 = r0 - NP
                P.dma("sp", lambda e: e.dma_start(out=cs[gi % 2][:], in_=di["rope_cos"][:, t0:t0 + 512]), writes=[cs[gi % 2]])
                P.dma("sp", lambda e: e.dma_start(out=sn[gi % 2][:], in_=di["rope_sin"][:, t0:t0 + 512]), writes=[sn[gi % 2]])

        def body(gi):
            r0, r = gs[gi]
            kind = "p" if r == 0 else "s"
            x = xg[gi % 2]
            h = hT[gi % 2]
            sst = ss[gi % 2]
            kc0 = self.kcol(r0)
            for m in range(12):
                pb = PB[m % 4]
                for k in range(8):
                    P.op("pe", lambda e, m=m, k=k, pb=pb: e.matmul(pb[:], wsb[:, k, m * 128:(m + 1) * 128], h[:, k, :],
                                                                     start=(k == 0), stop=(k == 7)),
                         reads=[wsb, h], writes=[pb])
                st = stg[ctr["stg"] % 3]
                ctr["stg"] += 1
                if m % 2 == 0:
                    P.op("act", lambda e, st=st, pb=pb: e.activation(out=st[:], in_=pb[:], func=AF.Copy), reads=[pb], writes=[st])
                else:
                    P.op("dve", lambda e, st=st, pb=pb: e.tensor_copy(out=st[:], in_=pb[:]), reads=[pb], writes=[st])
                P.dma("sp", lambda e, st=st, m=m: e.dma_start(out=QKVT[m * 128:(m + 1) * 128, r0:r0 + 512], in_=st[:]),
                      reads=[st], writes=[(tQKVT, (m, r0))])
            pa, pbp = PB[0], PB[1]
            for k in range(8):
                P.op("pe", lambda e, k=k, pa=pa: e.matmul(pa[0:64, :], wsb[:, k, 2448:2512], h[:, k, :], start=(k == 0), stop=(k == 7)),
                     reads=[wsb, h], writes=[pa])
            if kind == "s":
                for k in range(8):
                    P.op("pe", lambda e, k=k, pbp=pbp: e.matmul(pbp[0:64, :], wpsb[:, k, :], h[:, k, :], start=(k == 0), stop=(k == 7)),
                         reads=[wpsb, h], writes=[pbp])
            ob = rb[ctr["rb"] % 3]
            ctr["rb"] += 1
            rope_or_copy(pa, pbp, kind, gi, ob)
            P.dma("sp", lambda e, ob=ob: e.dma_start(out=KRT[0:64, kc0:kc0 + 512], in_=ob[0:64, :]), reads=[ob],
                  writes=[(tKRT, r0)])
            for t in range(4):
                pz, pm = PB[6], PB[7]
                for k in range(8):
                    P.op("pe", lambda e, k=k, t=t: e.matmul(pz[:], h[:, k, t * 128:(t + 1) * 128], wsb[:, k, 1536:2048],
                                                            start=(k == 0), stop=(k == 7)), reads=[wsb, h], writes=[pz])
                for k in range(8):
                    P.op("pe", lambda e, k=k, t=t: e.matmul(pm[:, 0:464], h[:, k, t * 128:(t + 1) * 128], wsb[:, k, 2048:2512],
                                                            start=(k == 0), stop=(k == 7)), reads=[wsb, h], writes=[pm])
                row = r0 + t * 128
                zt = zst[t % 2]
                P.op("act", lambda e, zt=zt: e.activation(out=zt[:], in_=pz[:], func=AF.Silu), reads=[pz], writes=[zt])
                P.dma("sp", lambda e, zt=zt, row=row: e.dma_start(out=ZS[row:row + 128, :], in_=zt[:]), reads=[zt],
                      writes=[(tZS, row)])
                s_ = sm[t % 2]
                g_ = gb[t % 2]
                P.op("dve", lambda e, s_=s_: e.tensor_tensor(out=s_[:, 0:8], in0=pm[:, 0:8], in1=dtb[:], op=ALU.add),
                     reads=[pm, dtb], writes=[(s_, "a")])
                P.op("act", lambda e, s_=s_: e.activation(out=s_[:, 0:8], in_=s_[:, 0:8], func=AF.Exp), reads=[(s_, "a")],
                     writes=[(s_, "a")])
                P.op("act", lambda e, s_=s_: e.activation(out=s_[:, 0:8], in_=s_[:, 0:8], func=AF.Ln, bias=self.ones[:, 0:1]),
                     reads=[(s_, "a"), self.ones], writes=[(s_, "a")])
                P.op("dve", lambda e, s_=s_, g_=g_: e.tensor_tensor(out=g_[:, 0:8], in0=s_[:, 0:8], in1=nea[:], op=ALU.mult),
                     reads=[(s_, "a"), nea], writes=[(g_, 0)])
                P.op("act", lambda e, g_=g_: e.activation(out=g_[:, 8:16], in_=pm[:, 8:16], func=AF.Sigmoid), reads=[pm],
                     writes=[(g_, 1)])
                P.dma("sp", lambda e, g_=g_, row=row: e.dma_start(out=GB[row:row + 128, :], in_=g_[:]), reads=[g_],
                      writes=[(tGB, row)])
                cq = cqn[t % 2]
                P.op("pool", lambda e, s_=s_: e.memset(s_[:, 16:24], 0.0), writes=[(s_, "n")])
                P.op("act", lambda e, s_=s_: e.activation(out=junk[:, 0:256], in_=pm[:, 16:272], func=AF.Square,
                                                          accum_out=s_[:, 16:17]), reads=[pm, (s_, "n")],
                     writes=[junk, (s_, "n")])
                P.op("act", lambda e, s_=s_: e.activation(out=junk[:, 256:384], in_=pm[:, 272:400], func=AF.Square,
                                                          accum_out=s_[:, 17:18]), reads=[pm, (s_, "n")],
                     writes=[junk, (s_, "n")])
                P.op("act", lambda e, s_=s_: e.activation(out=s_[:, 18:19], in_=s_[:, 16:17], func=AF.Sqrt, scale=1.0 / 256,
                                                          bias=self.epsc[:, 0:1]), reads=[(s_, "n"), self.epsc],
                     writes=[(s_, "n")])
                P.op("act", lambda e, s_=s_: e.activation(out=s_[:, 19:20], in_=s_[:, 17:18], func=AF.Sqrt, scale=1.0 / 128,
                                                          bias=self.epsc[:, 0:1]), reads=[(s_, "n"), self.epsc],
                     writes=[(s_, "n")])
                P.op("dve", lambda e, s_=s_: e.reciprocal(out=s_[:, 20:22], in_=s_[:, 18:20]), reads=[(s_, "n")],
                     writes=[(s_, "n")])
                P.op("act", lambda e, s_=s_, cq=cq: e.activation(out=cq[:], in_=pm[:, 16:272], func=AF.Identity,
                                                                 scale=s_[:, 20:21]), reads=[pm, (s_, "n")], writes=[cq])
                ck = ckn[t % 2]
                P.op("dve", lambda e, s_=s_, ck=ck: e.scalar_tensor_tensor(out=ck[:], in0=pm[:, 272:400], scalar=s_[:, 21:22],
                                                                           in1=kvn[:], op0=ALU.mult, op1=ALU.mult),
                     reads=[pm, (s_, "n"), kvn], writes=[ck])
                if kind == "p":
                    P.dma("sp", lambda e, ck=ck, row=row: e.dma_start(out=self.dout["ckv_out"][row:row + 128, :], in_=ck[:]),
                          reads=[ck], writes=[(self.dtr["ckv_out"], row)])
                    kr = krt[t % 2]
                    P.op("dve", lambda e, kr=kr: e.tensor_copy(out=kr[:], in_=pm[:, 400:464]), reads=[pm], writes=[kr])
                    P.dma("sp", lambda e, kr=kr, row=row: e.dma_start(out=self.dout["krope_out"][row:row + 128, :], in_=kr[:]),
                          reads=[kr], writes=[(self.dtr["krope_out"], row)])
                pt = PB[4 + (t % 2)]
                for kc in range(2):
                    P.op("pe", lambda e, kc=kc, cq=cq, pt=pt: e.transpose(pt[:, kc * 128:(kc + 1) * 128],
                                                                           cq[:, kc * 128:(kc + 1) * 128], ident[:]),
                         reads=[cq, ident], writes=[pt])
                P.op("pe", lambda e, ck=ck, pt=pt: e.transpose(pt[:, 256:384], ck[:], ident[:]), reads=[ck, ident], writes=[pt])
                P.op("act", lambda e, pt=pt, t=t: e.activation(out=cqnT[:, :, t * 128:(t + 1) * 128],
                                                               in_=pt[:, 0:256].rearrange("p (a b) -> p a b", a=2),
                                                               func=AF.Copy), reads=[pt], writes=[(cqnT, t)])
                P.op("dve", lambda e, pt=pt, t=t: e.tensor_copy(out=ckT[:, t * 128:(t + 1) * 128], in_=pt[:, 256:384]),
                     reads=[pt], writes=[(ckT, t)])
            P.dma("sp", lambda e: e.dma_start(out=CKVT[:, kc0:kc0 + 512], in_=ckT[:]), reads=[ckT], writes=[(tCKVT, r0)])
            for hh in range(4):
                pq = PB[hh % 4]
                for kc in range(2):
                    P.op("pe", lambda e, kc=kc, hh=hh, pq=pq: e.matmul(pq[:], wuq[:, kc, hh * 192:hh * 192 + 128], cqnT[:, kc, :],
                                                                         start=(kc == 0), stop=(kc == 1)),
                         reads=[wuq, cqnT], writes=[pq])
                ob = rb[ctr["rb"] % 3]
                ctr["rb"] += 1
                P.op("act", lambda e, ob=ob, pq=pq: e.activation(out=ob[:], in_=pq[:], func=AF.Copy), reads=[pq], writes=[ob])
                P.dma("sp", lambda e, ob=ob, hh=hh: e.dma_start(out=QT[hh, 0:128, r0:r0 + 512], in_=ob[:]), reads=[ob],
                      writes=[(tQT, (hh, 0, r0))])
            for hh in range(4):
                pa, pbp = PB[(2 * hh) % 4], PB[(2 * hh + 1) % 4]
                for kc in range(2):
                    P.op("pe", lambda e, kc=kc, hh=hh, pa=pa: e.matmul(pa[0:64, :], wuq[:, kc, hh * 192 + 128:hh * 192 + 192],
                                                                         cqnT[:, kc, :], start=(kc == 0), stop=(kc == 1)),
                         reads=[wuq, cqnT], writes=[pa])
                if kind == "s":
                    for kc in range(2):
                        P.op("pe", lambda e, kc=kc, hh=hh, pbp=pbp: e.matmul(pbp[0:64, :], wuq[:, kc, 768 + hh * 64:768 + (hh + 1) * 64],
                                                                               cqnT[:, kc, :], start=(kc == 0), stop=(kc == 1)),
                             reads=[wuq, cqnT], writes=[pbp])
                ob = rb[ctr["rb"] % 3]
                ctr["rb"] += 1
                rope_or_copy(pa, pbp, kind, gi, ob)
                P.dma("sp", lambda e, ob=ob, hh=hh: e.dma_start(out=QT[hh, 128:192, r0:r0 + 512], in_=ob[0:64, :]), reads=[ob],
                      writes=[(tQT, (hh, 1, r0))])

        ng = len(gs)
        load(0)
        for gi in range(ng):
            r0, r = gs[gi]
            self.pro_hT(xg[gi % 2], ss[gi % 2], hT[gi % 2], l, s, r, [PB[4], PB[5]], xn, junk)
            if gi + 1 < ng:
                load(gi + 1)
            body(gi)
        P.end_phase()

    def ev_e2(self):
        c = self.cfg
        P = self.P
        di = self.din
        NT, NP = c.NT, c.NP
        NTILE = NT // 128
        ident, ones = self.ident, self.ones
        QKVT, GB, ZS, OT, OD = self.QKVT, self.GB, self.ZS, self.OT, self.OD
        tQKVT, tGB, tZS, tOT, tOD = [self.dtr[n] for n in ("QKVT", "GB", "ZS", "OT", "OD")]
        TBT = self.scratch("TBT", (NTILE, 8, 128, 128))
        QKT = self.scratch("QKT", (NTILE, 8, 128, 128))
        KDEC = self.scratch("KDEC", (NTILE, 8, 128, 128))
        VTM = self.scratch("VTM", (NTILE, 4, 128, 128))
        KNT = self.scratch("KNT", (4, 128, NT))
        QNT = self.scratch("QNT", (4, 128, NT))
        tTBT, tQKT, tKDEC, tVTM, tKNT, tQNT = [self.dtr[n] for n in ("TBT", "QKT", "KDEC", "VTM", "KNT", "QNT")]
        BIG = 30000.0
        e2cq = []
        for (l_, f_) in ((0, 1), (1, 0), (1, 1)):
            e2cq += self.cast_begin("ffn_in_%d%d" % (l_, f_), di["ffn_w_in"][l_, f_])

        for (row0, T, r, kind) in self.seqs():
            NTl = T // 128
            tile0 = row0 // 128
            P.mid_es = ExitStack()
            egc = P.sb("egc", [128, 2, NTl, 4], F32, mid=True)
            negc = P.sb("negc", [128, 2, NTl, 4], F32, mid=True)
            egl = P.sb("egl", [128, 2, NTl, 4], F32, mid=True)
            P.begin_phase()
            PB = [P.ps("pb%d" % i, [128, 512], F32) for i in range(8)]
            def mask(name, op, fill_where_false, flip=False):
                m = P.sb(name, [128, 128], F32)
                P.op("pool", lambda e: e.memset(m[:], 0.0 if fill_where_false != 0.0 else 1.0), writes=[m])
                P.op("pool", lambda e: e.affine_select(out=m[:], in_=m[:], pattern=[[1 if flip else -1, 128]], compare_op=op,
                                                       fill=fill_where_false, base=0, channel_multiplier=(-1 if flip else 1)),
                     reads=[m], writes=[m])
                return m
            Uincl = mask("Uincl", ALU.is_ge, 0.0, flip=True)
            Lincl = mask("Lincl", ALU.is_ge, 0.0)
            MA = [mask("MAf", ALU.is_gt, -BIG), mask("MAb", ALU.is_gt, -BIG, flip=True)]
            MB = [mask("MBf", ALU.is_ge, -BIG, flip=True), mask("MBb", ALU.is_ge, -BIG)]
            ld = P.sb("ld", [128, 128], F32)
            cw = P.sb("cw", [128, 60], F32)
            P.dma("sp", lambda e: e.dma_start(out=ld[0:60, :], in_=di["ev_conv_w"].rearrange("t (m p) -> (t m) p", p=128)),
                  writes=[ld])
            P.op("pe", lambda e: e.transpose(PB[0][:, 0:60], ld[0:60, :], ident[0:60, 0:60]), reads=[ld, ident], writes=[PB[0]])
            P.op("dve", lambda e: e.tensor_copy(out=cw[:], in_=PB[0][:, 0:60]), reads=[PB[0]], writes=[cw])
            GBt = P.sb("GBt", [128, NTl, 16], F32)
            nq = max(1, NTl // 8)
            for q in range(0, NTl, nq):
                P.dma("sp", lambda e, q=q: e.dma_start(
                    out=GBt[:, q:q + nq, :], in_=GB[row0 + q * 128:row0 + (q + nq) * 128, :].rearrange("(c p) f -> p c f", p=128)),
                    reads=[tGB], writes=[(GBt, q)])
            gfb = P.sb("gfb", [128, 2, NTl, 4], F32)
            for d in range(2):
                P.op("dve", lambda e, d=d: e.tensor_copy(out=gfb[:, d], in_=GBt[:, :, d * 4:(d + 1) * 4]), reads=[GBt],
                     writes=[(gfb, d)])
            gc = P.sb("gc", [128, 2, NTl, 4], F32)
            ngc = P.sb("ngc", [128, 2, NTl, 4], F32)
            kds = P.sb("kds", [128, 2, NTl, 4], F32)
            nbeta = P.sb("nbeta", [128, NTl, 8], F32)
            W = NTl * 4
            for d in range(2):
                P.op("pe", lambda e, d=d: e.matmul(PB[0][:, d * W:(d + 1) * W], (Uincl if d == 0 else Lincl)[:],
                                                   gfb[:, d].rearrange("p c h -> p (c h)"), start=True, stop=True),
                     reads=[Uincl, Lincl, gfb], writes=[PB[0]])
            P.op("pe", lambda e: e.matmul(PB[1][:, 0:2 * W], ones[:], gfb[:].rearrange("p d c h -> p (d c h)"), start=True,
                                          stop=True), reads=[ones, gfb], writes=[PB[1]])
            fl = lambda t_: t_[:].rearrange("p d c h -> p (d c h)")
            P.op("dve", lambda e: e.tensor_copy(out=fl(gc), in_=PB[0][:, 0:2 * W]), reads=[PB[0]], writes=[gc])
            P.op("dve", lambda e: e.tensor_scalar(out=fl(ngc), in0=PB[0][:, 0:2 * W], scalar1=-1.0, scalar2=None, op0=ALU.mult),
                 reads=[PB[0]], writes=[ngc])
            P.op("act", lambda e: e.activation(out=fl(egc), in_=PB[0][:, 0:2 * W], func=AF.Exp), reads=[PB[0]], writes=[egc])
            P.op("dve", lambda e: e.tensor_scalar(out=fl(negc), in0=fl(egc), scalar1=-1.0, scalar2=None, op0=ALU.mult),
                 reads=[egc], writes=[negc])
            P.op("act", lambda e: e.activation(out=fl(egl), in_=PB[1][:, 0:2 * W], func=AF.Exp), reads=[PB[1]], writes=[egl])
            P.op("dve", lambda e: e.tensor_tensor(out=fl(kds), in0=PB[1][:, 0:2 * W], in1=fl(gc), op=ALU.subtract),
                 reads=[PB[1], gc], writes=[kds])
            P.op("act", lambda e: e.activation(out=fl(kds), in_=fl(kds), func=AF.Exp), reads=[kds], writes=[kds])
            P.op("dve", lambda e: e.tensor_scalar(out=nbeta[:], in0=GBt[:, :, 8:16], scalar1=-1.0, scalar2=None, op0=ALU.mult),
                 reads=[GBt], writes=[nbeta])

            XT = [P.sb("XT%d" % i, [128, T + 4], F32) for i in range(2)]
            AC = [P.sb("AC%d" % i, [128, T], F32) for i in range(3)]
            sq = P.sb("sq", [128, 512], F32)
            rin = P.sb("rin", [128, 512], F32)
            TB = 2
            NU = 2 * TB
            dgs = [P.sb("dg%d" % i, [128, 256], F32) for i in range(TB)]
            Nt = [P.sb("Nt%d" % i, [128, 128], F32R) for i in range(2 * NU)]
            identr = P.sb("identr", [128, 128], F32R)
            P.op("dve", lambda e: e.tensor_copy(out=identr[:], in_=ident[:]), reads=[ident], writes=[identr])
            Qf = [P.sb("Qf%d" % i, [128, 128], F32R) for i in range(NU)]
            Nl = [[P.sb("Nl%d_%d" % (i, j), [128, 128], F32R) for j in range(2)] for i in range(NU)]
            Ql = [[P.sb("Ql%d_%d" % (i, j), [128, 128], F32R) for j in range(2)] for i in range(NU)]
            Dm = [P.sb("Dm%d" % i, [128, 128], F32R) for i in range(NU)]
            Em = [P.sb("Em%d" % i, [128, 128], F32R) for i in range(NU)]
            Wt = [P.sb("Wt%d" % i, [128, 128], F32R) for i in range(NU)]
            Zt = [P.sb("Zt%d" % i, [128, 128], F32R) for i in range(NU)]
            No = [P.sb("No%d" % i, [128, 128], F32R) for i in range(NU)]
            Qo = [P.sb("Qo%d" % i, [128, 128], F32R) for i in range(NU)]
            SAME = {}
            for msz in (16, 32, 64):
                nb_ = 128 // msz
                bt = P.sb("bt%d" % msz, [8, 128], F32)
                P.op("pool", lambda e, bt=bt: e.memset(bt[:], 1.0), writes=[bt])
                P.op("pool", lambda e, bt=bt, msz=msz: e.affine_select(out=bt[:], in_=bt[:], pattern=[[1, 128]], compare_op=ALU.is_ge,
                                                                     fill=0.0, base=0, channel_multiplier=-msz), reads=[bt], writes=[bt])
                P.op("pool", lambda e, bt=bt, msz=msz: e.affine_select(out=bt[:], in_=bt[:], pattern=[[-1, 128]], compare_op=ALU.is_ge,
                                                                     fill=0.0, base=msz - 1, channel_multiplier=msz), reads=[bt], writes=[bt])
                sm_ = P.sb("same%d" % msz, [128, 128], F32)
                P.op("pe", lambda e, bt=bt, nb_=nb_: e.matmul(PB[0][:, 0:128], bt[0:nb_, :], bt[0:nb_, :], start=True, stop=True),
                     reads=[bt], writes=[PB[0]])
                P.op("dve", lambda e, sm_=sm_: e.tensor_copy(out=sm_[:], in_=PB[0][:, 0:128]), reads=[PB[0]], writes=[sm_])
                SAME[msz] = sm_
            SYM = {}
            for msz, big in ((16, 32), (32, 64)):
                t_ = P.sb("sym%d" % msz, [128, 128], F32)
                P.op("dve", lambda e, t_=t_, msz=msz, big=big: e.tensor_tensor(out=t_[:], in0=SAME[big][:], in1=SAME[msz][:], op=ALU.subtract),
                     reads=[SAME[big], SAME[msz]], writes=[t_])
                SYM[msz] = t_
            t_ = P.sb("sym64", [128, 128], F32)
            P.op("dve", lambda e, t_=t_: e.tensor_tensor(out=t_[:], in0=ones[:], in1=SAME[64][:], op=ALU.subtract),
                 reads=[ones, SAME[64]], writes=[t_])
            SYM[64] = t_
            dA = [P.sb("dA%d" % i, [128, 128], F32) for i in range(NU)]
            dB = [P.sb("dB%d" % i, [128, 128], F32) for i in range(NU)]
            qk = [P.sb("qk%d" % i, [128, 128], F32) for i in range(NU)]
            tb = [P.sb("tb%d" % i, [128, 128], F32) for i in range(NU)]
            kd = [P.sb("kd%d" % i, [128, 128], F32) for i in range(NU)]
            vt = [P.sb("vt%d" % i, [128, 128], F32) for i in range(2)]
            uc = [0]
            if kind == "s":
                hin, hout, thin, thout = self.pair_buf("HALO", 1536, 2)
                P.dma("sp", lambda e: e.dma_start(out=hin, in_=QKVT[:, row0 + T - 2:row0 + T]), reads=[tQKVT], writes=[thin])
                self.pair_gather(hin, hout, thin, thout)
                hal = P.sb("hal", [128, 12, 2, 2], F32)
                for rk in range(2):
                    P.dma("sp", lambda e, rk=rk: e.dma_start(out=hal[:, :, rk, :], in_=hout[rk * 1536:(rk + 1) * 1536, :].rearrange("(m p) c -> p m c", p=128)),
                          reads=[thout], writes=[(hal, rk)])
                selw = P.sb("selw", [128, 2], F32)
                P.dma("sp", lambda e: e.dma_start(out=selw[:], in_=di["selw"]), writes=[selw])
                hsel = P.sb("hsel", [128, 12, 2], F32)
                P.op("dve", lambda e: e.tensor_scalar(out=hsel[:], in0=hal[:, :, 0, :], scalar1=selw[:, 0:1], scalar2=None, op0=ALU.mult),
                     reads=[hal, selw], writes=[hsel])
                P.op("dve", lambda e: e.scalar_tensor_tensor(out=hsel[:], in0=hal[:, :, 1, :], scalar=selw[:, 1:2], in1=hsel[:], op0=ALU.mult,
                                                             op1=ALU.add), reads=[hal, selw, hsel], writes=[hsel])
            for hh in range(4):
                for part in range(3):
                    X = XT[part % 2]
                    m = part * 4 + hh
                    P.op("pool", lambda e, X=X: e.memset(X[:, 0:2], 0.0), writes=[(X, "h0")])
                    if kind == "s":
                        P.op("pool", lambda e, X=X, m=m: e.tensor_copy(out=X[:, T + 2:T + 3], in_=hsel[:, m, 1:2]), reads=[hsel], writes=[(X, "h1")])
                        P.op("pool", lambda e, X=X, m=m: e.tensor_copy(out=X[:, T + 3:T + 4], in_=hsel[:, m, 0:1]), reads=[hsel], writes=[(X, "h2")])
                    else:
                        P.op("pool", lambda e, X=X: e.memset(X[:, T + 2:T + 4], 0.0), writes=[(X, "h1")])
                    P.dma("sp", lambda e, X=X, m=m: e.dma_start(out=X[:, 2:T + 2], in_=QKVT[m * 128:(m + 1) * 128, row0:row0 + T]),
                          reads=[tQKVT], writes=[(X, "b")])
                    A = AC[part]
                    for tap in range(5):
                        if tap == 0:
                            P.op("dve", lambda e, X=X, A=A, m=m: e.tensor_scalar(out=A[:], in0=X[:, 0:T], scalar1=cw[:, m:m + 1],
                                                                                 scalar2=None, op0=ALU.mult),
                                 reads=[X, cw], writes=[A])
                        else:
                            P.op("dve", lambda e, X=X, A=A, m=m, tap=tap: e.scalar_tensor_tensor(
                                out=A[:], in0=X[:, tap:tap + T], scalar=cw[:, tap * 12 + m:tap * 12 + m + 1], in1=A[:],
                                op0=ALU.mult, op1=ALU.add), reads=[X, cw, A], writes=[A])
                    P.op("act", lambda e, A=A: e.activation(out=A[:], in_=A[:], func=AF.Silu), reads=[A], writes=[A])
                    if part < 2:
                        for b0 in range(0, T, 512):
                            bw = min(512, T - b0)
                            P.op("act", lambda e, A=A, b0=b0, bw=bw: e.activation(out=sq[:, 0:bw], in_=A[:, b0:b0 + bw], func=AF.Square),
                                 reads=[A], writes=[sq])
                            P.op("pe", lambda e, bw=bw: e.matmul(PB[0][:, 0:bw], ones[:], sq[:, 0:bw], start=True, stop=True),
                                 reads=[ones, sq], writes=[PB[0]])
                            P.op("act", lambda e, bw=bw: e.activation(out=rin[:, 0:bw], in_=PB[0][:, 0:bw], func=AF.Sqrt,
                                                                      bias=self.epsc[:, 0:1]), reads=[PB[0], self.epsc], writes=[rin])
                            P.op("dve", lambda e, bw=bw: e.reciprocal(out=rin[:, 0:bw], in_=rin[:, 0:bw]), reads=[rin], writes=[rin])
                            if part == 0:
                                P.op("dve", lambda e, A=A, b0=b0, bw=bw: e.scalar_tensor_tensor(
                                    out=A[:, b0:b0 + bw], in0=A[:, b0:b0 + bw], scalar=128.0 ** -0.5, in1=rin[:, 0:bw],
                                    op0=ALU.mult, op1=ALU.mult), reads=[A, rin], writes=[A])
                            else:
                                P.op("dve", lambda e, A=A, b0=b0, bw=bw: e.tensor_tensor(
                                    out=A[:, b0:b0 + bw], in0=A[:, b0:b0 + bw], in1=rin[:, 0:bw], op=ALU.mult),
                                    reads=[A, rin], writes=[A])
                qn, kn, vv = AC
                P.dma("sp", lambda e, hh=hh: e.dma_start(out=QNT[hh, :, row0:row0 + T], in_=qn[:]), reads=[qn],
                      writes=[(tQNT, (hh, row0))])
                P.dma("sp", lambda e, hh=hh: e.dma_start(out=KNT[hh, :, row0:row0 + T], in_=kn[:]), reads=[kn],
                      writes=[(tKNT, (hh, row0))])
                for cc0 in range(0, NTl, TB):
                    units = []
                    if kind == "s":
                        self.cast_piece(e2cq)
                        self.cast_piece(e2cq)
                    for bi, cc in enumerate(range(cc0, min(NTl, cc0 + TB))):
                        dg = dgs[bi]
                        tl = tile0 + cc
                        csl = slice(cc * 128, (cc + 1) * 128)
                        pmisc = PB[1 + bi]
                        P.op("pe", lambda e, csl=csl, pmisc=pmisc: e.transpose(pmisc[:, 0:128], kn[:, csl], ident[:]), reads=[kn, ident], writes=[pmisc])
                        P.op("pe", lambda e, csl=csl, pmisc=pmisc: e.transpose(pmisc[:, 128:256], vv[:, csl], ident[:]), reads=[vv, ident], writes=[pmisc])
                        P.op("pe", lambda e, csl=csl, pmisc=pmisc: e.matmul(pmisc[:, 256:384], kn[:, csl], kn[:, csl], start=True, stop=True),
                             reads=[kn], writes=[pmisc])
                        P.op("pe", lambda e, csl=csl, pmisc=pmisc: e.matmul(pmisc[:, 384:512], kn[:, csl], qn[:, csl], start=True, stop=True),
                             reads=[kn, qn], writes=[pmisc])
                        v_ = vt[bi]
                        P.op("dve", lambda e, v_=v_, pmisc=pmisc: e.tensor_copy(out=v_[:], in_=pmisc[:, 128:256]), reads=[pmisc], writes=[v_])
                        P.dma("sp", lambda e, v_=v_, tl=tl, hh=hh: e.dma_start(out=VTM[tl, hh], in_=v_[:]), reads=[v_],
                              writes=[(tVTM, (tl, hh))])
                        for d in range(2):
                            P.op("dve", lambda e, d=d, cc=cc, hh=hh, dg=dg: e.tensor_scalar(out=dg[:, d * 128:(d + 1) * 128], in0=ident[:],
                                                                                     scalar1=gc[:, d, cc, hh:hh + 1], scalar2=None,
                                                                                     op0=ALU.mult), reads=[ident, gc], writes=[(dg, d)])
                        pgb = PB[3]
                        pg0 = bi * 256
                        P.op("pe", lambda e, dg=dg, pg0=pg0: e.matmul(pgb[:, pg0:pg0 + 256], ones[:], dg[:], start=True, stop=True), reads=[ones, dg], writes=[pgb])
                        for d in range(2):
                            u = bi * 2 + d
                            dh = d * 4 + hh
                            col = lambda t_, d=d, cc=cc, hh=hh: t_[:, d, cc, hh:hh + 1]
                            P.op("act", lambda e, u=u, d=d, cc=cc, hh=hh, pmisc=pmisc: e.activation(out=kd[u][:], in_=pmisc[:, 0:128], func=AF.Identity,
                                                                                       scale=kds[:, d, cc, hh:hh + 1]),
                                 reads=[pmisc, kds], writes=[kd[u]])
                            P.dma("sp", lambda e, u=u, tl=tl, dh=dh: e.dma_start(out=KDEC[tl, dh], in_=kd[u][:]), reads=[kd[u]],
                                  writes=[(tKDEC, (tl, dh))])
                            P.op("dve", lambda e, u=u, d=d, pg0=pg0: e.scalar_tensor_tensor(out=dA[u][:], in0=pgb[:, pg0 + d * 128:pg0 + (d + 1) * 128], scalar=-1.0,
                                                                                   in1=MA[d][:], op0=ALU.mult, op1=ALU.add),
                                 reads=[pgb, MA[d]], writes=[dA[u]])
                            P.op("act", lambda e, u=u, d=d, cc=cc, hh=hh: e.activation(out=dA[u][:], in_=dA[u][:], func=AF.Exp,
                                                                                       bias=gc[:, d, cc, hh:hh + 1]),
                                 reads=[dA[u], gc], writes=[dA[u]])
                            P.op("dve", lambda e, u=u, d=d, pg0=pg0: e.tensor_tensor(out=dB[u][:], in0=pgb[:, pg0 + d * 128:pg0 + (d + 1) * 128], in1=MB[d][:],
                                                                            op=ALU.add), reads=[pgb, MB[d]], writes=[dB[u]])
                            P.op("act", lambda e, u=u, d=d, cc=cc, hh=hh: e.activation(out=dB[u][:], in_=dB[u][:], func=AF.Exp,
                                                                                       bias=ngc[:, d, cc, hh:hh + 1]),
                                 reads=[dB[u], ngc], writes=[dB[u]])
                            n0 = Nt[2 * u]
                            P.op("dve", lambda e, u=u, n0=n0, cc=cc, dh=dh, pmisc=pmisc: e.scalar_tensor_tensor(
                                out=n0[:], in0=pmisc[:, 256:384], scalar=nbeta[:, cc, dh:dh + 1], in1=dA[u][:], op0=ALU.mult, op1=ALU.mult),
                                reads=[pmisc, nbeta, dA[u]], writes=[n0])
                            P.op("dve", lambda e, u=u, pmisc=pmisc: e.tensor_tensor(out=qk[u][:], in0=pmisc[:, 384:512], in1=dB[u][:], op=ALU.mult),
                                 reads=[pmisc, dB[u]], writes=[qk[u]])
                            P.dma("sp", lambda e, u=u, tl=tl, dh=dh: e.dma_start(out=QKT[tl, dh], in_=qk[u][:]), reads=[qk[u]],
                                  writes=[(tQKT, (tl, dh))])
                            units.append((u, d, dh, cc, tl))
                    pw = {u_[0]: PB[4 + u_[0]] for u_ in units}
                    U = [(u, pw[u], Nt[2 * u], cc, tl, dh) for (u, d, dh, cc, tl) in units]
                    for (u, p_, Nf, cc, tl, dh) in U:
                        P.op("pe", lambda e, p_=p_, Nf=Nf: e.transpose(p_[:, 0:128].bitcast(F32R), Nf[:], identr[:]), reads=[Nf, identr], writes=[p_])
                    for (u, p_, Nf, cc, tl, dh) in U:
                        P.op("act", lambda e, u=u, p_=p_: e.activation(out=Qf[u][:], in_=p_[:, 0:128], func=AF.Copy), reads=[p_], writes=[Qf[u]])
                        P.op("dve", lambda e, u=u, p_=p_: e.tensor_tensor(out=Ql[u][0][:], in0=p_[:, 0:128], in1=SAME[16][:], op=ALU.mult),
                             reads=[p_, SAME[16]], writes=[Ql[u][0]])
                        P.op("pool", lambda e, u=u, Nf=Nf: e.tensor_tensor(out=Nl[u][0][:], in0=Nf[:], in1=SAME[16][:], op=ALU.mult),
                             reads=[Nf, SAME[16]], writes=[Nl[u][0]])
                    for (u, p_, Nf, cc, tl, dh) in U:
                        P.op("pool", lambda e, u=u: e.tensor_tensor(out=Dm[u][:], in0=Nl[u][0][:], in1=ident[:], op=ALU.add),
                             reads=[Nl[u][0], ident], writes=[Dm[u]])
                        P.op("dve", lambda e, u=u: e.tensor_tensor(out=Em[u][:], in0=Ql[u][0][:], in1=ident[:], op=ALU.add),
                             reads=[Ql[u][0], ident], writes=[Em[u]])
                    for lvl in range(1, 4):
                        for (u, p_, Nf, cc, tl, dh) in U:
                            nc_, qc_ = Nl[u][(lvl - 1) % 2], Ql[u][(lvl - 1) % 2]
                            P.op("pe", lambda e, p_=p_, nc_=nc_, qc_=qc_: e.matmul(p_[:, 0:128], qc_[:], nc_[:], start=True, stop=True),
                                 reads=[nc_, qc_], writes=[p_])
                            P.op("pe", lambda e, p_=p_, nc_=nc_, qc_=qc_: e.matmul(p_[:, 128:256], nc_[:], qc_[:], start=True, stop=True),
                                 reads=[nc_, qc_], writes=[p_])
                        for (u, p_, Nf, cc, tl, dh) in U:
                            nn_, qn_ = Nl[u][lvl % 2], Ql[u][lvl % 2]
                            P.op("act", lambda e, p_=p_, nn_=nn_: e.activation(out=nn_[:], in_=p_[:, 0:128], func=AF.Copy), reads=[p_], writes=[nn_])
                            P.op("dve", lambda e, p_=p_, qn_=qn_: e.tensor_copy(out=qn_[:], in_=p_[:, 128:256]), reads=[p_], writes=[qn_])
                        for (u, p_, Nf, cc, tl, dh) in U:
                            nn_, qn_ = Nl[u][lvl % 2], Ql[u][lvl % 2]
                            P.op("pe", lambda e, p_=p_, nn_=nn_, u=u: e.matmul(p_[:, 256:384], nn_[:], Em[u][:], start=True, stop=True),
                                 reads=[nn_, Em[u]], writes=[p_])
                            P.op("pe", lambda e, p_=p_, qn_=qn_, u=u: e.matmul(p_[:, 384:512], qn_[:], Dm[u][:], start=True, stop=True),
                                 reads=[qn_, Dm[u]], writes=[p_])
                        for (u, p_, Nf, cc, tl, dh) in U:
                            P.op("dve", lambda e, p_=p_, u=u: e.tensor_tensor(out=Em[u][:], in0=Em[u][:], in1=p_[:, 256:384], op=ALU.add),
                                 reads=[p_, Em[u]], writes=[Em[u]])
                            P.op("dve", lambda e, p_=p_, u=u: e.tensor_tensor(out=Dm[u][:], in0=Dm[u][:], in1=p_[:, 384:512], op=ALU.add),
                                 reads=[p_, Dm[u]], writes=[Dm[u]])
                    for mi, msz in enumerate((16, 32, 64)):
                        last = (mi == 2)
                        for (u, p_, Nf, cc, tl, dh) in U:
                            P.op("pool", lambda e, u=u, Nf=Nf, msz=msz: e.tensor_tensor(out=No[u][:], in0=Nf[:], in1=SYM[msz][:], op=ALU.mult),
                                 reads=[Nf, SYM[msz]], writes=[No[u]])
                            if not last:
                                P.op("pool", lambda e, u=u, msz=msz: e.tensor_tensor(out=Qo[u][:], in0=Qf[u][:], in1=SYM[msz][:], op=ALU.mult),
                                     reads=[Qf[u], SYM[msz]], writes=[Qo[u]])
                        for (u, p_, Nf, cc, tl, dh) in U:
                            if not last:
                                P.op("pe", lambda e, p_=p_, u=u: e.matmul(p_[:, 0:128], Qo[u][:], Dm[u][:], start=True, stop=True),
                                     reads=[Qo[u], Dm[u]], writes=[p_])
                            P.op("pe", lambda e, p_=p_, u=u: e.matmul(p_[:, 128:256], No[u][:], Em[u][:], start=True, stop=True),
                                 reads=[No[u], Em[u]], writes=[p_])
                        for (u, p_, Nf, cc, tl, dh) in U:
                            if not last:
                                P.op("act", lambda e, p_=p_, u=u: e.activation(out=Wt[u][:], in_=p_[:, 0:128], func=AF.Copy), reads=[p_], writes=[Wt[u]])
                            P.op("dve", lambda e, p_=p_, u=u: e.tensor_copy(out=Zt[u][:], in_=p_[:, 128:256]), reads=[p_], writes=[Zt[u]])
                        for (u, p_, Nf, cc, tl, dh) in U:
                            if not last:
                                P.op("pe", lambda e, p_=p_, u=u: e.matmul(p_[:, 256:384], Em[u][:], Wt[u][:], start=True, stop=True),
                                     reads=[Em[u], Wt[u]], writes=[p_])
                            P.op("pe", lambda e, p_=p_, u=u: e.matmul(p_[:, 384:512], Dm[u][:], Zt[u][:], start=True, stop=True),
                                 reads=[Dm[u], Zt[u]], writes=[p_])
                        for (u, p_, Nf, cc, tl, dh) in U:
                            if not last:
                                P.op("dve", lambda e, p_=p_, u=u: e.tensor_tensor(out=Dm[u][:], in0=Dm[u][:], in1=p_[:, 256:384], op=ALU.add),
                                     reads=[p_, Dm[u]], writes=[Dm[u]])
                            P.op("dve", lambda e, p_=p_, u=u: e.tensor_tensor(out=Em[u][:], in0=Em[u][:], in1=p_[:, 384:512], op=ALU.add),
                                 reads=[p_, Em[u]], writes=[Em[u]])
                    for (u, p_, Nf, cc, tl, dh) in U:
                        P.op("act", lambda e, u=u, cc=cc, dh=dh: e.activation(out=tb[u][:], in_=Em[u][:], func=AF.Identity,
                                                                              scale=GBt[:, cc, 8 + dh:9 + dh]), reads=[Em[u], GBt],
                             writes=[tb[u]])
                        P.dma("sp", lambda e, u=u, tl=tl, dh=dh: e.dma_start(out=TBT[tl, dh], in_=tb[u][:]), reads=[tb[u]],
                              writes=[(tTBT, (tl, dh))])
            if kind == "s":
                while e2cq:
                    self.cast_piece(e2cq)
            P.end_phase()
            P.begin_phase()
            PB = [P.ps("pb%d" % i, [128, 512], F32) for i in range(8)]
            S = [[P.sb("S%d_%d" % (dh, i), [128, 128], F32) for i in range(2)] for dh in range(8)]
            for dh in range(8):
                if kind == "p":
                    P.op("pool", lambda e, dh=dh: e.memset(S[dh][0][:], 0.0), writes=[S[dh][0]])
                else:
                    P.dma("sp", lambda e, dh=dh: e.dma_start(out=S[dh][0][:], in_=di["state_gdn"][dh // 4, dh % 4]), writes=[S[dh][0]])
            NB = 2
            L_tb = [[P.sb("Ltb%d_%d" % (dh, i), [128, 128], F32) for i in range(NB)] for dh in range(8)]
            L_qk = [[P.sb("Lqk%d_%d" % (dh, i), [128, 128], F32) for i in range(NB)] for dh in range(8)]
            L_kd = [[P.sb("Lkd%d_%d" % (dh, i), [128, 128], F32) for i in range(NB)] for dh in range(8)]
            L_kn = [[P.sb("Lkn%d_%d" % (dh, i), [128, 128], F32) for i in range(NB)] for dh in range(8)]
            L_qn = [[P.sb("Lqn%d_%d" % (dh, i), [128, 128], F32) for i in range(NB)] for dh in range(8)]
            L_v = [[P.sb("Lv%d_%d" % (dh, i), [128, 128], F32) for i in range(NB)] for dh in range(8)]
            Rt = [P.sb("Rt%d" % dh, [128, 128], F32) for dh in range(8)]
            vn = [P.sb("vn%d" % dh, [128, 128], F32) for dh in range(8)]
            o1 = [P.sb("o1_%d" % dh, [128, 128], F32) for dh in range(8)]
            ob = [[P.sb("ob%d_%d" % (dh, i), [128, 128], F32) for i in range(2)] for dh in range(8)]

            def sload(i, dirs):
                for d in dirs:
                    cc = i if d == 0 else NTl - 1 - i
                    tl = tile0 + cc
                    rows = slice(row0 + cc * 128, row0 + (cc + 1) * 128)
                    for hh in range(4):
                        dh = d * 4 + hh
                        b = i % NB
                        P.dma("sp", lambda e, dh=dh, b=b, tl=tl: e.dma_start(out=L_tb[dh][b][:], in_=TBT[tl, dh]), reads=[tTBT],
                              writes=[L_tb[dh][b]])
                        P.dma("sp", lambda e, dh=dh, b=b, tl=tl: e.dma_start(out=L_qk[dh][b][:], in_=QKT[tl, dh]), reads=[tQKT],
                              writes=[L_qk[dh][b]])
                        P.dma("sp", lambda e, dh=dh, b=b, tl=tl: e.dma_start(out=L_kd[dh][b][:], in_=KDEC[tl, dh]), reads=[tKDEC],
                              writes=[L_kd[dh][b]])
                        P.dma("sp", lambda e, dh=dh, b=b, hh=hh, rows=rows: e.dma_start(out=L_kn[dh][b][:], in_=KNT[hh, :, rows]),
                              reads=[tKNT], writes=[L_kn[dh][b]])
                        P.dma("sp", lambda e, dh=dh, b=b, hh=hh, rows=rows: e.dma_start(out=L_qn[dh][b][:], in_=QNT[hh, :, rows]),
                              reads=[tQNT], writes=[L_qn[dh][b]])
                        P.dma("sp", lambda e, dh=dh, b=b, tl=tl, hh=hh: e.dma_start(out=L_v[dh][b][:], in_=VTM[tl, hh]), reads=[tVTM],
                              writes=[L_v[dh][b]])

            def scan_pass(dirs):
                sload(0, dirs)
                for i in range(NTl):
                    if i + 1 < NTl:
                        sload(i + 1, dirs)
                    b = i % NB
                    CH = []
                    for d in dirs:
                        cc = i if d == 0 else NTl - 1 - i
                        for hh in range(4):
                            dh = d * 4 + hh
                            CH.append((d, cc, hh, dh, S[dh][i % 2], S[dh][(i + 1) % 2], PB[dh], row0 + cc * 128))
                    for (d, cc, hh, dh, Sc, Sn, pp, rows0) in CH:
                        P.op("pe", lambda e, dh=dh, b=b, Sc=Sc, pp=pp: e.matmul(pp[:, 0:128], L_kn[dh][b][:], Sc[:], start=True, stop=True),
                             reads=[L_kn[dh][b], Sc], writes=[pp])
                        P.op("pe", lambda e, dh=dh, b=b, Sc=Sc, pp=pp: e.matmul(pp[:, 128:256], L_qn[dh][b][:], Sc[:], start=True, stop=True),
                             reads=[L_qn[dh][b], Sc], writes=[pp])
                    for (d, cc, hh, dh, Sc, Sn, pp, rows0) in CH:
                        P.op("dve", lambda e, dh=dh, b=b, pp=pp, d=d, cc=cc, hh=hh: e.scalar_tensor_tensor(
                            out=Rt[dh][:], in0=pp[:, 0:128], scalar=negc[:, d, cc, hh:hh + 1], in1=L_v[dh][b][:], op0=ALU.mult, op1=ALU.add),
                            reads=[pp, negc, L_v[dh][b]], writes=[Rt[dh]])
                    for (d, cc, hh, dh, Sc, Sn, pp, rows0) in CH:
                        P.op("pe", lambda e, dh=dh, b=b, pp=pp: e.matmul(pp[:, 256:384], L_tb[dh][b][:], Rt[dh][:], start=True, stop=True),
                             reads=[L_tb[dh][b], Rt[dh]], writes=[pp])
                    for (d, cc, hh, dh, Sc, Sn, pp, rows0) in CH:
                        P.op("act", lambda e, dh=dh, pp=pp: e.activation(out=vn[dh][:], in_=pp[:, 256:384], func=AF.Copy), reads=[pp],
                             writes=[vn[dh]])
                        P.op("act", lambda e, dh=dh, pp=pp, d=d, cc=cc, hh=hh: e.activation(out=o1[dh][:], in_=pp[:, 128:256], func=AF.Identity,
                                                                                            scale=egc[:, d, cc, hh:hh + 1]),
                             reads=[pp, egc], writes=[o1[dh]])
                    for (d, cc, hh, dh, Sc, Sn, pp, rows0) in CH:
                        P.op("pe", lambda e, dh=dh, b=b, pp=pp: e.matmul(pp[:, 0:128], L_kd[dh][b][:], vn[dh][:], start=True, stop=True),
                             reads=[L_kd[dh][b], vn[dh]], writes=[pp])
                        P.op("pe", lambda e, dh=dh, b=b, pp=pp: e.matmul(pp[:, 384:512], L_qk[dh][b][:], vn[dh][:], start=True, stop=True),
                             reads=[L_qk[dh][b], vn[dh]], writes=[pp])
                    for (d, cc, hh, dh, Sc, Sn, pp, rows0) in CH:
                        P.op("dve", lambda e, dh=dh, pp=pp, Sc=Sc, Sn=Sn, d=d, cc=cc, hh=hh: e.scalar_tensor_tensor(
                            out=Sn[:], in0=Sc[:], scalar=egl[:, d, cc, hh:hh + 1], in1=pp[:, 0:128], op0=ALU.mult, op1=ALU.add),
                            reads=[pp, Sc, egl], writes=[Sn])
                    for (d, cc, hh, dh, Sc, Sn, pp, rows0) in CH:
                        o_ = ob[dh][i % 2]
                        P.op("dve", lambda e, dh=dh, pp=pp, o_=o_: e.tensor_tensor(out=o_[:], in0=o1[dh][:], in1=pp[:, 384:512], op=ALU.add),
                             reads=[pp, o1[dh]], writes=[o_])
                        P.dma("sp", lambda e, o_=o_, d=d, hh=hh, rows0=rows0: e.dma_start(
                            out=OD[d, rows0:rows0 + 128, hh * 128:(hh + 1) * 128], in_=o_[:]), reads=[o_], writes=[(tOD, (d, rows0, hh))])
            if kind == "p":
                scan_pass((0, 1))
            else:
                scan_pass((0,))
                gin, gout, tgin, tgout = self.pair_buf("GST", 512, 128)
                for hh in range(4):
                    Sf = S[hh][NTl % 2]
                    P.dma("sp", lambda e, hh=hh, Sf=Sf: e.dma_start(out=gin[hh * 128:(hh + 1) * 128, :], in_=Sf[:]), reads=[Sf],
                          writes=[(tgin, hh)])
                self.pair_gather(gin, gout, tgin, tgout)
                cand = P.sb("cand", [128, 2, 4, 128], F32)
                for rk in range(2):
                    P.dma("sp", lambda e, rk=rk: e.dma_start(out=cand[:, rk], in_=gout[rk * 512:(rk + 1) * 512, :].rearrange("(h p) c -> p h c", p=128)),
                          reads=[tgout], writes=[(cand, rk)])
                selw2 = P.sb("selw2", [128, 2], F32)
                P.dma("sp", lambda e: e.dma_start(out=selw2[:], in_=di["selw"]), writes=[selw2])
                for hh in range(4):
                    P.op("dve", lambda e, hh=hh: e.tensor_scalar(out=S[4 + hh][0][:], in0=cand[:, 0, hh, :], scalar1=selw2[:, 0:1], scalar2=None,
                                                                 op0=ALU.mult), reads=[cand, selw2], writes=[S[4 + hh][0]])
                    P.op("dve", lambda e, hh=hh: e.scalar_tensor_tensor(out=S[4 + hh][0][:], in0=cand[:, 1, hh, :], scalar=selw2[:, 1:2],
                                                                        in1=S[4 + hh][0][:], op0=ALU.mult, op1=ALU.add),
                         reads=[cand, selw2, S[4 + hh][0]], writes=[S[4 + hh][0]])
                scan_pass((1,))
            if kind == "p":
                si = row0 // c.TP
                for dh in range(8):
                    Sf = S[dh][NTl % 2]
                    P.dma("sp", lambda e, dh=dh, Sf=Sf, si=si: e.dma_start(out=self.dout["gdn_out"][si, dh // 4, dh % 4], in_=Sf[:]),
                          reads=[Sf], writes=[(self.dtr["gdn_out"], (si, dh))])
            P.end_phase()
            P.mid_es.close()

        P.begin_phase()
        PB = [P.ps("pb%d" % i, [128, 512], F32) for i in range(8)]
        gn = P.sb("gn", [128, 128], F32)
        P.dma("sp", lambda e: e.dma_start(out=gn[:], in_=di["ev_gdn_norm"].partition_broadcast(128)), writes=[gn])
        oa = [P.sb("oa%d" % i, [128, 512], F32) for i in range(2)]
        obb = [P.sb("obb%d" % i, [128, 512], F32) for i in range(2)]
        zz = [P.sb("zz%d" % i, [128, 512], F32) for i in range(2)]
        st8 = [P.sb("st8%d" % i, [128, 16], F32) for i in range(2)]
        junk = P.sb("junk", [128, 512], F32)
        otb = [P.sb("otb%d" % i, [128, 4, 128], BF16) for i in range(2)]
        def gload(tl):
            rows = slice(tl * 128, (tl + 1) * 128)
            a, b_, z_ = oa[tl % 2], obb[tl % 2], zz[tl % 2]
            P.dma("sp", lambda e, a=a, rows=rows: e.dma_start(out=a[:], in_=OD[0, rows, :]), reads=[tOD], writes=[a])
            P.dma("sp", lambda e, b_=b_, rows=rows: e.dma_start(out=b_[:], in_=OD[1, rows, :]), reads=[tOD], writes=[b_])
            P.dma("sp", lambda e, z_=z_, rows=rows: e.dma_start(out=z_[:], in_=ZS[rows, :]), reads=[tZS], writes=[z_])

        gload(0)
        for tl in range(NTILE):
            rows = slice(tl * 128, (tl + 1) * 128)
            a, b_, z_, s8, ot_ = oa[tl % 2], obb[tl % 2], zz[tl % 2], st8[tl % 2], otb[tl % 2]
            if tl + 1 < NTILE:
                gload(tl + 1)
            P.op("dve", lambda e, a=a, b_=b_: e.tensor_tensor(out=a[:], in0=a[:], in1=b_[:], op=ALU.add), reads=[a, b_], writes=[a])
            P.op("pool", lambda e, s8=s8: e.memset(s8[:], 0.0), writes=[s8])
            for hh in range(4):
                P.op("act", lambda e, a=a, s8=s8, hh=hh: e.activation(out=junk[:, 0:128], in_=a[:, hh * 128:(hh + 1) * 128], func=AF.Square,
                                                                      accum_out=s8[:, hh:hh + 1]), reads=[a, s8], writes=[junk, s8])
            P.op("act", lambda e, s8=s8: e.activation(out=s8[:, 4:8], in_=s8[:, 0:4], func=AF.Sqrt, scale=1.0 / 128, bias=self.epsc[:, 0:1]),
                 reads=[s8, self.epsc], writes=[s8])
            P.op("dve", lambda e, s8=s8: e.reciprocal(out=s8[:, 8:12], in_=s8[:, 4:8]), reads=[s8], writes=[s8])
            for hh in range(4):
                P.op("dve", lambda e, a=a, s8=s8, hh=hh: e.scalar_tensor_tensor(
                    out=a[:, hh * 128:(hh + 1) * 128], in0=a[:, hh * 128:(hh + 1) * 128], scalar=s8[:, 8 + hh:9 + hh], in1=gn[:],
                    op0=ALU.mult, op1=ALU.mult), reads=[a, s8, gn], writes=[a])
            P.op("pool", lambda e, a=a, z_=z_: e.tensor_tensor(out=a[:], in0=a[:], in1=z_[:], op=ALU.mult), reads=[a, z_], writes=[a])
            pt = PB[tl % 2]
            for hh in range(4):
                P.op("pe", lambda e, a=a, hh=hh, pt=pt: e.transpose(pt[:, hh * 128:(hh + 1) * 128], a[:, hh * 128:(hh + 1) * 128], ident[:]),
                     reads=[a, ident], writes=[pt])
            P.op("act", lambda e, pt=pt, ot_=ot_: e.activation(out=ot_[:], in_=pt[:].rearrange("p (a b) -> p a b", a=4), func=AF.Copy),
                 reads=[pt], writes=[ot_])
            P.dma("sp", lambda e, ot_=ot_, tl=tl: e.dma_start(out=OT[0:512, tl * 128:(tl + 1) * 128].rearrange("(a p) t -> p a t", p=128),
                                                             in_=ot_[:]), reads=[ot_], writes=[(tOT, ("a", tl))])
        P.end_phase()

    def ev_e3(self):
        c = self.cfg
        P = self.P
        NT, NP = c.NT, c.NP
        ones = self.ones
        QT, CKVT, KRT, OT = self.QT, self.CKVT, self.KRT, self.OT
        tQT, tCKVT, tKRT, tOT = [self.dtr[n] for n in ("QT", "CKVT", "KRT", "OT")]
        wukv = self.din["ev_mla_w_ukv"]
        SCALE = 192.0 ** -0.5
        SMAX = 2 * c.TS + 256
        P.begin_phase()
        PB = [P.ps("pb%d" % i, [128, 512], F32) for i in range(8)]
        wsb = P.sb("wukv", [128, 1024], BF16)
        P.dma("pool", lambda e: e.dma_start(out=wsb[:], in_=wukv), writes=[wsb])
        sel = P.sb("sel65", [128, 65], F32)
        P.op("pool", lambda e: e.memset(sel[:], 0.0), writes=[sel])
        P.op("pool", lambda e: e.memset(sel[:, 64:65], 1.0), reads=[sel], writes=[sel])
        onesb = P.sb("onesb", [128, 128], BF16)
        P.op("pool", lambda e: e.memset(onesb[:], 1.0), writes=[onesb])
        CK = P.sb("CK", [128, SMAX], BF16)
        KR = P.sb("KR", [65, SMAX], BF16)
        KN = P.sb("KN", [128, SMAX], BF16)
        Vp = P.sb("Vp", [128, SMAX // 128, 128], BF16)
        sq1 = P.sb("sq1", [128, 512], F32)
        sq2 = P.sb("sq2", [64, 512], F32)
        kkm = P.sb("kkm", [65, 16], F32)
        row = P.sb("row", [65, 512], F32)
        QN = [P.sb("QN%d" % i, [128, 512], BF16) for i in range(2)]
        QR = [P.sb("QR%d" % i, [65, 512], BF16) for i in range(2)]
        PT = [P.sb("PT%d" % i, [128, 512], BF16) for i in range(3)]
        rec = P.sb("rec", [128, 512], F32)
        obf = [P.sb("obf%d" % i, [128, 512], BF16) for i in range(2)]
        qctr = [0]
        pctr = [0]
        for (row0, T, r, kind) in self.seqs():
            if kind == "p":
                k0 = self.kcol(row0)
                S = T
                P.dma("sp", lambda e, k0=k0, S=S: e.dma_start(out=CK[:, 0:S], in_=CKVT[:, k0:k0 + S]), reads=[tCKVT], writes=[CK])
                P.dma("sp", lambda e, k0=k0, S=S: e.dma_start(out=KR[:, 0:S], in_=KRT[:, k0:k0 + S]), reads=[tKRT], writes=[KR])
            else:
                S = 256 + 2 * T
                own0 = self.kcol(row0)
                kin, kout, tkin, tkout = self.pair_buf("KVX", 193, T, BF16)
                P.dma("sp", lambda e, own0=own0, T=T: e.dma_start(out=kin[0:128, :], in_=CKVT[:, own0:own0 + T]), reads=[tCKVT], writes=[(tkin, 0)])
                P.dma("sp", lambda e, own0=own0, T=T: e.dma_start(out=kin[128:193, :], in_=KRT[:, own0:own0 + T]), reads=[tKRT], writes=[(tkin, 1)])
                self.pair_gather(kin, kout, tkin, tkout)
                P.dma("sp", lambda e: e.dma_start(out=CK[:, 0:256], in_=CKVT[:, NP:NP + 256]), reads=[tCKVT], writes=[(CK, "c")])
                P.dma("sp", lambda e: e.dma_start(out=KR[:, 0:256], in_=KRT[:, NP:NP + 256]), reads=[tKRT], writes=[(KR, "c")])
                for rk in range(2):
                    P.dma("sp", lambda e, rk=rk, T=T: e.dma_start(out=CK[:, 256 + rk * T:256 + (rk + 1) * T], in_=kout[rk * 193:rk * 193 + 128, :]),
                          reads=[tkout], writes=[(CK, rk)])
                    P.dma("sp", lambda e, rk=rk, T=T: e.dma_start(out=KR[:, 256 + rk * T:256 + (rk + 1) * T], in_=kout[rk * 193 + 128:rk * 193 + 193, :]),
                          reads=[tkout], writes=[(KR, rk)])
            nkt = S // 128
            for hh in range(4):
                for b0 in range(0, S, 512):
                    bw = min(512, S - b0)
                    pk = PB[0]
                    P.op("pe", lambda e, hh=hh, b0=b0, bw=bw, pk=pk: e.matmul(pk[:, 0:bw], wsb[:, hh * 256:hh * 256 + 128], CK[:, b0:b0 + bw],
                                                                              start=True, stop=True), reads=[wsb, CK], writes=[pk])
                    P.op("act", lambda e, b0=b0, bw=bw, pk=pk: e.activation(out=KN[:, b0:b0 + bw], in_=pk[:, 0:bw], func=AF.Copy),
                         reads=[pk], writes=[(KN, b0)])
                    P.op("act", lambda e, bw=bw, pk=pk: e.activation(out=sq1[:, 0:bw], in_=pk[:, 0:bw], func=AF.Square), reads=[pk],
                         writes=[sq1])
                    P.op("act", lambda e, b0=b0, bw=bw: e.activation(out=sq2[:, 0:bw], in_=KR[0:64, b0:b0 + bw], func=AF.Square),
                         reads=[KR], writes=[sq2])
                    pr = PB[1]
                    P.op("pe", lambda e, bw=bw, pr=pr: e.matmul(pr[0:65, 0:bw], sel[:], sq1[:, 0:bw], start=True, stop=False),
                         reads=[sel, sq1], writes=[pr])
                    P.op("pe", lambda e, bw=bw, pr=pr: e.matmul(pr[0:65, 0:bw], sel[0:64, :], sq2[:, 0:bw], start=False, stop=True),
                         reads=[sel, sq2], writes=[pr])
                    P.op("dve", lambda e, b0=b0, bw=bw, pr=pr: e.reduce_max(out=kkm[64:65, b0 // 512:b0 // 512 + 1], in_=pr[64:65, 0:bw],
                                                                            axis=AX.X), reads=[pr], writes=[(kkm, b0)])
                    for k4 in range(0, bw // 128, 4):
                        pv = PB[2]
                        n4 = min(4, bw // 128 - k4)
                        for q in range(n4):
                            kt = b0 // 128 + k4 + q
                            P.op("pe", lambda e, hh=hh, kt=kt, q=q, pv=pv: e.matmul(
                                pv[:, q * 128:(q + 1) * 128], CK[:, kt * 128:(kt + 1) * 128], wsb[:, hh * 256 + 128:hh * 256 + 256],
                                start=True, stop=True), reads=[wsb, CK], writes=[pv])
                        kt0 = b0 // 128 + k4
                        P.op("dve", lambda e, kt0=kt0, n4=n4, pv=pv: e.tensor_copy(
                            out=Vp[:, kt0:kt0 + n4, :], in_=pv[:, 0:n4 * 128].rearrange("p (a b) -> p a b", a=n4)),
                            reads=[pv], writes=[(Vp, kt0)])
                nblk = (S + 511) // 512
                P.op("dve", lambda e, nblk=nblk: e.reduce_max(out=kkm[64:65, 15:16], in_=kkm[64:65, 0:nblk], axis=AX.X),
                     reads=[kkm], writes=[(kkm, "max")])
                for q0 in range(0, T, 512):
                    qw = min(512, T - q0)
                    qn_, qr_ = QN[qctr[0] % 2], QR[qctr[0] % 2]
                    pacc, psm = PB[4 + (qctr[0] % 2) * 2], PB[5 + (qctr[0] % 2) * 2]
                    qctr[0] += 1
                    rows = slice(row0 + q0, row0 + q0 + qw)
                    P.dma("sp", lambda e, hh=hh, rows=rows, qw=qw, qn_=qn_: e.dma_start(out=qn_[:, 0:qw], in_=QT[hh, 0:128, rows]),
                          reads=[tQT], writes=[qn_])
                    P.dma("sp", lambda e, hh=hh, rows=rows, qw=qw, qr_=qr_: e.dma_start(out=qr_[0:64, 0:qw], in_=QT[hh, 128:192, rows]),
                          reads=[tQT], writes=[(qr_, "a")])
                    P.op("act", lambda e, qw=qw, qn_=qn_: e.activation(out=sq1[:, 0:qw], in_=qn_[:, 0:qw], func=AF.Square), reads=[qn_],
                         writes=[sq1])
                    P.op("act", lambda e, qw=qw, qr_=qr_: e.activation(out=sq2[:, 0:qw], in_=qr_[0:64, 0:qw], func=AF.Square),
                         reads=[(qr_, "a")], writes=[sq2])
                    pr = PB[1]
                    P.op("pe", lambda e, qw=qw, pr=pr: e.matmul(pr[0:65, 0:qw], sel[:], sq1[:, 0:qw], start=True, stop=False),
                         reads=[sel, sq1], writes=[pr])
                    P.op("pe", lambda e, qw=qw, pr=pr: e.matmul(pr[0:65, 0:qw], sel[0:64, :], sq2[:, 0:qw], start=False, stop=True),
                         reads=[sel, sq2], writes=[pr])
                    P.op("dve", lambda e, qw=qw, pr=pr: e.tensor_scalar(out=row[64:65, 0:qw], in0=pr[64:65, 0:qw], scalar1=kkm[64:65, 15:16],
                                                                        scalar2=None, op0=ALU.mult), reads=[pr, (kkm, "max")], writes=[row])
                    P.op("act", lambda e, qw=qw: e.activation(out=row[64:65, 0:qw], in_=row[64:65, 0:qw], func=AF.Sqrt), reads=[row],
                         writes=[row])
                    P.op("dve", lambda e, qw=qw, qr_=qr_: e.tensor_scalar(out=qr_[64:65, 0:qw], in0=row[64:65, 0:qw], scalar1=-1.0,
                                                                          scalar2=None, op0=ALU.mult), reads=[row], writes=[(qr_, "b")])
                    ptmap = {}

                    def scores(kt, qw=qw, qn_=qn_, qr_=qr_, ptmap=ptmap):
                        ps = PB[2 + (pctr[0] % 2)]
                        pt_ = PT[pctr[0] % 3]
                        pctr[0] += 1
                        ptmap[kt] = pt_
                        ksl = slice(kt * 128, (kt + 1) * 128)
                        P.op("pe", lambda e: e.matmul(ps[:, 0:qw], KN[:, ksl], qn_[:, 0:qw], start=True, stop=False), reads=[KN, qn_], writes=[ps])
                        P.op("pe", lambda e: e.matmul(ps[:, 0:qw], KR[0:65, ksl], qr_[0:65, 0:qw], start=False, stop=True), reads=[KR, qr_],
                             writes=[ps])
                        P.op("act", lambda e: e.activation(out=pt_[:, 0:qw], in_=ps[:, 0:qw], func=AF.Exp, scale=SCALE), reads=[ps], writes=[pt_])

                    def accum(kt, qw=qw, pacc=pacc, psm=psm, nkt=nkt, ptmap=ptmap):
                        pt_ = ptmap[kt]
                        P.op("pe", lambda e: e.matmul(pacc[:, 0:qw], Vp[:, kt, :], pt_[:, 0:qw], start=(kt == 0), stop=(kt == nkt - 1)),
                             reads=[Vp, pt_], writes=[pacc])
                        P.op("pe", lambda e: e.matmul(psm[:, 0:qw], onesb[:], pt_[:, 0:qw], start=(kt == 0), stop=(kt == nkt - 1)),
                             reads=[onesb, pt_], writes=[psm])

                    scores(0)
                    for kt in range(nkt):
                        if kt + 1 < nkt:
                            scores(kt + 1)
                        accum(kt)
                    P.op("dve", lambda e, qw=qw, psm=psm: e.reciprocal(out=rec[:, 0:qw], in_=psm[:, 0:qw]), reads=[psm], writes=[rec])
                    ob_ = obf[qctr[0] % 2]
                    P.op("dve", lambda e, qw=qw, pacc=pacc, ob_=ob_: e.tensor_tensor(out=ob_[:, 0:qw], in0=pacc[:, 0:qw], in1=rec[:, 0:qw],
                                                                                     op=ALU.mult), reads=[pacc, rec], writes=[ob_])
                    P.dma("sp", lambda e, hh=hh, rows=rows, qw=qw, ob_=ob_: e.dma_start(out=OT[512 + hh * 128:512 + (hh + 1) * 128, rows],
                                                                                       in_=ob_[:, 0:qw]), reads=[ob_],
                          writes=[(tOT, ("b", hh, row0 + q0))])
        P.end_phase()

    def mix_out(self, l, src, dst, wname):
        c = self.cfg
        P = self.P
        s = 1
        srcap, dstap = self.dr(src), self.dr(dst)
        srct, dstt = self.dtr[src], self.dtr[dst]
        OT, tOT = self.OT, self.dtr["OT"]
        wo_d = self.din[wname]
        P.begin_phase()
        PB = [P.ps("pb%d" % i, [128, 512], F32) for i in range(8)]
        cbc = self.make_cbc(l, s, PB)
        wo = P.sb("wo", [128, 8, 1024], BF16)
        P.dma("pool", lambda e: e.dma_start(out=wo[:], in_=wo_d.rearrange("(k p) n -> p k n", p=128)), writes=[wo])
        xg = [P.sb("xg%d" % i, [128, 4, 1024], F32) for i in range(2)]
        og = [P.sb("og%d" % i, [128, 8, 512], BF16) for i in range(2)]
        junk = P.sb("junk", [128, 1024], F32)
        tmp = [P.sb("tmp%d" % i, [128, 1024], F32) for i in range(2)]
        ss = [P.sb("ss%d" % i, [128, 24], F32) for i in range(2)]
        gs = self.groups()
        def mload(gi):
            r0, r = gs[gi]
            x, o_ = xg[gi % 2], og[gi % 2]
            P.dma("sp", lambda e, x=x, r0=r0: e.dma_start(out=x[:], in_=srcap[r0:r0 + 512, :].rearrange("(t p) d -> p t d", p=128)),
                  reads=[srct], writes=[x])
            for hf in range(2):
                P.dma("sp", lambda e, o_=o_, r0=r0, hf=hf: e.dma_start(
                    out=o_[:, hf * 4:(hf + 1) * 4, :], in_=OT[hf * 512:(hf + 1) * 512, r0:r0 + 512].rearrange("(k p) t -> p k t", p=128)),
                    reads=[tOT], writes=[(o_, hf)])

        mload(0)
        for gi, (r0, r) in enumerate(gs):
            x, o_, sst = xg[gi % 2], og[gi % 2], ss[gi % 2]
            if gi + 1 < len(gs):
                mload(gi + 1)
            P.op("pool", lambda e, sst=sst: e.memset(sst[:], 0.0), writes=[sst])
            for t in range(4):
                py = [PB[4 + (t % 2) * 2], PB[5 + (t % 2) * 2]]
                for nb in range(2):
                    for k in range(8):
                        P.op("pe", lambda e, nb=nb, k=k, t=t, py=py, o_=o_: e.matmul(
                            py[nb][:], o_[:, k, t * 128:(t + 1) * 128], wo[:, k, nb * 512:(nb + 1) * 512], start=(k == 0), stop=(k == 7)),
                            reads=[o_, wo], writes=[py[nb]])
                self.epi(py, x[:, t, :], [(x, None)], r, sst, 4 * t, cbc, tmp[t % 2], junk, dstap, dstt, r0 + t * 128)
        P.end_phase()

    def odd_mixer(self, src, dst):
        c = self.cfg
        NT = c.NT
        if not hasattr(self, "OT"):
            self.OT = self.scratch("OT", (1024, NT), BF16)
        self.QK2 = self.scratch("QK2", (NT, 1024))
        self.V2 = self.scratch("V2", (NT, 1024), BF16)
        self.RS = self.scratch("RS", (NT, 1024))
        self.G2 = self.scratch("G2", (2, NT, 512))
        self.ODG = self.scratch("ODG", (2, NT, 1024))
        self.od_o1(src)
        self.od_o2()
        self.mix_out(1, src, dst, "od_w_out")

    def od_o1(self, src):
        c = self.cfg
        P = self.P
        di = self.din
        l, s = 1, 1
        NT, NP = c.NT, c.NP
        srcap, srct = self.dr(src), self.dtr[src]
        win = di["od_w_in"]
        QK2, V2, RS, G2 = self.QK2, self.V2, self.RS, self.G2
        tQK2, tV2, tRS, tG2 = [self.dtr[n] for n in ("QK2", "V2", "RS", "G2")]
        P.begin_phase()
        PB = [P.ps("pb%d" % i, [128, 512], F32) for i in range(8)]
        wsb = P.sb("wsb", [128, 8, 3104], BF16)
        for q in range(4):
            P.dma("pool", lambda e, q=q: e.dma_start(
                out=wsb[:, q * 2:(q + 1) * 2, :], in_=win[q * 256:(q + 1) * 256, :].rearrange("(kc p) n -> p kc n", p=128)),
                writes=[(wsb, q)])
        wg = [P.sb("wg%d" % d, [17, 512], F32) for d in range(2)]
        gdT = [P.sb("gdT%d" % d, [17, 512], F32) for d in range(2)]
        for d in range(2):
            P.dma("sp", lambda e, d=d: e.dma_start(out=wg[d][0:16, :], in_=di["od_gla_w_gup"][d]), writes=[(wg[d], 0)])
            P.dma("sp", lambda e, d=d: e.dma_start(out=wg[d][16:17, :], in_=di["od_gla_b_g"][d:d + 1, :]), writes=[(wg[d], 1)])
            P.op("pool", lambda e, d=d: e.memset(gdT[d][:], 1.0), writes=[gdT[d]])
        xg = [P.sb("xg%d" % i, [128, 4, 1024], F32) for i in range(2)]
        hT = [P.sb("hT%d" % i, [128, 8, 512], BF16) for i in range(2)]
        xn = [P.sb("xn%d" % i, [128, 1024], F32) for i in range(2)]
        junk = P.sb("junk", [128, 1024], F32)
        ss = [P.sb("ss%d" % i, [128, 24], F32) for i in range(2)]
        sqk = [P.sb("sqk%d" % i, [128, 1024], F32) for i in range(2)]
        sv = [P.sb("sv%d" % i, [128, 1024], BF16) for i in range(2)]
        sr = [P.sb("sr%d" % i, [128, 1024], F32) for i in range(2)]
        sg = [P.sb("sg%d" % i, [128, 512], F32) for i in range(2)]
        gs = self.groups()

        def load(gi):
            r0, r = gs[gi]
            x = xg[gi % 2]
            P.dma("sp", lambda e: e.dma_start(out=x[:], in_=srcap[r0:r0 + 512, :].rearrange("(t p) d -> p t d", p=128)),
                  reads=[srct], writes=[x])

        def body(gi):
            r0, r = gs[gi]
            h = hT[gi % 2]
            for d in range(2):
                pg = PB[2 + d]
                for k in range(8):
                    P.op("pe", lambda e, k=k, d=d, pg=pg: e.matmul(pg[0:16, :], wsb[:, k, 3072 + 16 * d:3072 + 16 * (d + 1)], h[:, k, :],
                                                                     start=(k == 0), stop=(k == 7)), reads=[wsb, h], writes=[pg])
                P.op("dve", lambda e, d=d, pg=pg: e.tensor_copy(out=gdT[d][0:16, :], in_=pg[0:16, :]), reads=[pg], writes=[gdT[d]])
            for t in range(4):
                row = r0 + t * 128
                tsl = slice(t * 128, (t + 1) * 128)
                banks = {"qk": (PB[0], PB[1], 0), "v": (PB[2], PB[3], 1024), "r": (PB[6], PB[7], 2048)}
                for name in ("qk", "v", "r"):
                    pa, pb_, c0 = banks[name]
                    for hf, pp in enumerate((pa, pb_)):
                        for k in range(8):
                            P.op("pe", lambda e, k=k, pp=pp, c0=c0, hf=hf, tsl=tsl: e.matmul(
                                pp[:], h[:, k, tsl], wsb[:, k, c0 + hf * 512:c0 + (hf + 1) * 512], start=(k == 0), stop=(k == 7)),
                                reads=[wsb, h], writes=[pp])
                    if name == "qk":
                        st = sqk[t % 2]
                        P.op("act", lambda e, st=st, pa=pa: e.activation(out=st[:, 0:512], in_=pa[:], func=AF.Copy), reads=[pa], writes=[(st, 0)])
                        P.op("dve", lambda e, st=st, pb_=pb_: e.tensor_copy(out=st[:, 512:1024], in_=pb_[:]), reads=[pb_], writes=[(st, 1)])
                        P.dma("sp", lambda e, st=st, row=row: e.dma_start(out=QK2[row:row + 128, :], in_=st[:]), reads=[st], writes=[(tQK2, row)])
                    elif name == "v":
                        st = sv[t % 2]
                        P.op("act", lambda e, st=st, pa=pa: e.activation(out=st[:, 0:512], in_=pa[:], func=AF.Copy), reads=[pa], writes=[(st, 0)])
                        P.op("dve", lambda e, st=st, pb_=pb_: e.tensor_copy(out=st[:, 512:1024], in_=pb_[:]), reads=[pb_], writes=[(st, 1)])
                        P.dma("sp", lambda e, st=st, row=row: e.dma_start(out=V2[row:row + 128, :], in_=st[:]), reads=[st], writes=[(tV2, row)])
                    else:
                        st = sr[t % 2]
                        P.op("act", lambda e, st=st, pa=pa: e.activation(out=st[:, 0:512], in_=pa[:], func=AF.Silu), reads=[pa], writes=[(st, 0)])
                        P.op("act", lambda e, st=st, pb_=pb_: e.activation(out=st[:, 512:1024], in_=pb_[:], func=AF.Silu), reads=[pb_], writes=[(st, 1)])
                        P.dma("sp", lambda e, st=st, row=row: e.dma_start(out=RS[row:row + 128, :], in_=st[:]), reads=[st], writes=[(tRS, row)])
                for d in range(2):
                    pg = PB[d]
                    g_ = sg[d]
                    P.op("pe", lambda e, d=d, pg=pg, tsl=tsl: e.matmul(pg[:], gdT[d][0:17, tsl], wg[d][0:17, :], start=True, stop=True),
                         reads=[gdT[d], wg[d]], writes=[pg])
                    P.op("act", lambda e, pg=pg, g_=g_: e.activation(out=g_[:], in_=pg[:], func=AF.Exp, scale=-1.0), reads=[pg], writes=[g_])
                    P.op("act", lambda e, g_=g_: e.activation(out=g_[:], in_=g_[:], func=AF.Ln, bias=self.ones[:, 0:1]), reads=[g_, self.ones],
                         writes=[g_])
                    P.op("dve", lambda e, g_=g_: e.tensor_scalar(out=g_[:], in0=g_[:], scalar1=-1.0 / 16.0, scalar2=None, op0=ALU.mult),
                         reads=[g_], writes=[g_])
                    P.dma("sp", lambda e, g_=g_, d=d, row=row: e.dma_start(out=G2[d, row:row + 128, :], in_=g_[:]), reads=[g_],
                          writes=[(tG2, (d, row))])

        ng = len(gs)
        load(0)
        for gi in range(ng):
            r0, r = gs[gi]
            self.pro_hT(xg[gi % 2], ss[gi % 2], hT[gi % 2], l, s, r, [PB[4], PB[5]], xn, junk)
            if gi + 1 < ng:
                load(gi + 1)
            body(gi)
        P.end_phase()

    def od_o2(self):
        c = self.cfg
        P = self.P
        di = self.din
        NT, NP = c.NT, c.NP
        NTILE = NT // 128
        ident, ones = self.ident, self.ones
        QK2, V2, RS, G2, ODG, OT = self.QK2, self.V2, self.RS, self.G2, self.ODG, self.OT
        tQK2, tV2, tRS, tG2, tODG, tOT = [self.dtr[n] for n in ("QK2", "V2", "RS", "G2", "ODG", "OT")]
        P.begin_phase()
        PB = [P.ps("pb%d" % i, [128, 512], F32) for i in range(8)]

        def mask(name, op, flip):
            m = P.sb(name, [128, 128], F32)
            P.op("pool", lambda e: e.memset(m[:], 1.0), writes=[m])
            P.op("pool", lambda e: e.affine_select(out=m[:], in_=m[:], pattern=[[1 if flip else -1, 128]], compare_op=op,
                                                   fill=0.0, base=0, channel_multiplier=(-1 if flip else 1)), reads=[m], writes=[m])
            return m
        Uincl = mask("Uincl", ALU.is_ge, True)
        Lincl = mask("Lincl", ALU.is_ge, False)
        CUM = [Uincl, Lincl]
        AM = [Uincl, Lincl]
        qk_t = [P.sb("qk_t%d" % i, [128, 1024], F32) for i in range(2)]
        v_t = [P.sb("v_t%d" % i, [128, 1024], BF16) for i in range(2)]
        g_t = [P.sb("g_t%d" % i, [128, 512], F32) for i in range(2)]
        bsb = P.sb("bsb", [128, 512], F32)
        eb = P.sb("eb", [128, 512], F32)
        enb = P.sb("enb", [128, 512], F32)
        ekd = P.sb("ekd", [128, 512], F32)
        qe = P.sb("qe", [128, 512], F32)
        ke = P.sb("ke", [128, 512], F32)
        kd2 = [P.sb("kd%d" % i, [128, 512], BF16) for i in range(2)]
        qeT2 = [P.sb("qeT%d" % i, [128, 4, 128], BF16) for i in range(2)]
        keT2 = [P.sb("keT%d" % i, [128, 4, 128], BF16) for i in range(2)]
        attT = [P.sb("attT%d" % i, [128, 128], BF16) for i in range(4)]
        ebl2 = [P.sb("ebl%d" % i, [128, 8], F32) for i in range(2)]
        osb = [P.sb("osb%d" % i, [128, 1024], F32) for i in range(2)]
        S = [[[P.sb("S%d_%d_%d" % (d, hh, i), [128, 256], F32) for i in range(2)] for hh in range(4)] for d in range(2)]
        Sb = [[P.sb("Sb%d_%d" % (d, hh), [128, 256], BF16) for hh in range(4)] for d in range(2)]
        cnt = [0]
        for (row0, T, r, kind) in self.seqs():
            NTl = T // 128
            for d in range(2):
                for hh in range(4):
                    if kind == "p":
                        P.op("pool", lambda e, d=d, hh=hh: e.memset(S[d][hh][0][:], 0.0), writes=[S[d][hh][0]])
                    else:
                        P.dma("sp", lambda e, d=d, hh=hh: e.dma_start(out=S[d][hh][0][:], in_=di["state_gla"][d, hh]), writes=[S[d][hh][0]])
                    P.op("pool", lambda e, d=d, hh=hh: e.tensor_copy(out=Sb[d][hh][:], in_=S[d][hh][0][:]), reads=[S[d][hh][0]],
                         writes=[Sb[d][hh]])
            def gla_pass(dirs):
                its = [(i, d) for i in range(NTl) for d in dirs]
                ctx = {}

                def front(n):
                    i, d = its[n]
                    cc = i if d == 0 else NTl - 1 - i
                    rows = slice(row0 + cc * 128, row0 + (cc + 1) * 128)
                    k_ = cnt[0] % 2
                    cnt[0] += 1
                    qk_, vv_, gg_ = qk_t[k_], v_t[k_], g_t[k_]
                    qeT_, keT_, kd_, ebl_ = qeT2[k_], keT2[k_], kd2[k_], ebl2[k_]
                    ctx[n] = (i, d, cc, rows, k_, vv_, qeT_, keT_, kd_, ebl_)
                    P.dma("sp", lambda e: e.dma_start(out=qk_[:], in_=QK2[rows, :]), reads=[tQK2], writes=[qk_])
                    P.dma("sp", lambda e: e.dma_start(out=vv_[:], in_=V2[rows, :]), reads=[tV2], writes=[vv_])
                    P.dma("sp", lambda e: e.dma_start(out=gg_[:], in_=G2[d, rows, :]), reads=[tG2], writes=[gg_])
                    pb_c, pb_t = PB[0], PB[1]
                    P.op("pe", lambda e: e.matmul(pb_c[:], CUM[d][:], gg_[:], start=True, stop=True), reads=[CUM[d], gg_], writes=[pb_c])
                    P.op("pe", lambda e: e.matmul(pb_t[:], ones[:], gg_[:], start=True, stop=True), reads=[ones, gg_], writes=[pb_t])
                    for hh in range(4):
                        P.op("pe", lambda e, hh=hh: e.matmul(PB[2][:, hh:hh + 1], gg_[:, hh * 128:(hh + 1) * 128], ones[:, 0:1],
                                                             start=True, stop=True), reads=[gg_, ones], writes=[PB[2]])
                    P.op("act", lambda e: e.activation(out=ebl_[:, 0:4], in_=PB[2][:, 0:4], func=AF.Exp), reads=[PB[2]], writes=[ebl_])
                    P.op("dve", lambda e: e.tensor_copy(out=bsb[:], in_=pb_c[:]), reads=[pb_c], writes=[bsb])
                    P.op("act", lambda e: e.activation(out=eb[:], in_=bsb[:], func=AF.Exp), reads=[bsb], writes=[eb])
                    P.op("act", lambda e: e.activation(out=enb[:], in_=bsb[:], func=AF.Exp, scale=-1.0), reads=[bsb], writes=[enb])
                    P.op("dve", lambda e: e.tensor_tensor(out=ekd[:], in0=pb_t[:], in1=bsb[:], op=ALU.subtract), reads=[pb_t, bsb], writes=[ekd])
                    P.op("act", lambda e: e.activation(out=ekd[:], in_=ekd[:], func=AF.Exp), reads=[ekd], writes=[ekd])
                    P.op("dve", lambda e: e.scalar_tensor_tensor(out=qe[:], in0=qk_[:, 0:512], scalar=128.0 ** -0.5, in1=eb[:],
                                                                 op0=ALU.mult, op1=ALU.mult), reads=[qk_, eb], writes=[qe])
                    P.op("dve", lambda e: e.tensor_tensor(out=ke[:], in0=qk_[:, 512:1024], in1=enb[:], op=ALU.mult), reads=[qk_, enb], writes=[ke])
                    P.op("pool", lambda e: e.tensor_tensor(out=kd_[:], in0=qk_[:, 512:1024], in1=ekd[:], op=ALU.mult), reads=[qk_, ekd],
                         writes=[kd_])
                    for hh in range(4):
                        hs = slice(hh * 128, (hh + 1) * 128)
                        P.op("pe", lambda e, hs=hs: e.transpose(PB[3][:, hs], qe[:, hs], ident[:]), reads=[qe, ident], writes=[PB[3]])
                        P.op("pe", lambda e, hs=hs: e.transpose(PB[4][:, hs], ke[:, hs], ident[:]), reads=[ke, ident], writes=[PB[4]])
                    P.op("act", lambda e: e.activation(out=qeT_[:], in_=PB[3][:].rearrange("p (a b) -> p a b", a=4), func=AF.Copy), reads=[PB[3]],
                         writes=[qeT_])
                    P.op("dve", lambda e: e.tensor_copy(out=keT_[:], in_=PB[4][:].rearrange("p (a b) -> p a b", a=4)), reads=[PB[4]], writes=[keT_])

                def back(n):
                    (i, d, cc, rows, k_, vv_, qeT_, keT_, kd_, ebl_) = ctx.pop(n)
                    o_ = osb[k_]
                    for hh in range(4):
                        P.op("pe", lambda e, hh=hh: e.matmul(PB[5][:, hh * 128:(hh + 1) * 128], keT_[:, hh, :], qeT_[:, hh, :], start=True, stop=True),
                             reads=[keT_, qeT_], writes=[PB[5]])
                    for hh in range(4):
                        P.op("dve", lambda e, hh=hh: e.tensor_tensor(out=attT[hh][:], in0=PB[5][:, hh * 128:(hh + 1) * 128], in1=AM[d][:],
                                                                     op=ALU.mult), reads=[PB[5], AM[d]], writes=[attT[hh]])
                    for hh in range(4):
                        hs = slice(hh * 128, (hh + 1) * 128)
                        po = PB[6 + (hh // 2)]
                        osl = slice((hh % 2) * 256, (hh % 2 + 1) * 256)
                        pu = PB[1] if hh < 2 else PB[0]
                        usl = slice((hh % 2) * 256, (hh % 2 + 1) * 256)
                        P.op("pe", lambda e, hh=hh, hs=hs, pu=pu, usl=usl: e.matmul(pu[:, usl], kd_[:, hs], vv_[:, hh * 256:(hh + 1) * 256],
                                                                                     start=True, stop=True), reads=[kd_, vv_], writes=[pu])
                        P.op("pe", lambda e, hh=hh, po=po, osl=osl: e.matmul(po[:, osl], qeT_[:, hh, :], Sb[d][hh][:], start=True, stop=False),
                             reads=[qeT_, Sb[d][hh]], writes=[po])
                        P.op("pe", lambda e, hh=hh, po=po, osl=osl: e.matmul(po[:, osl], attT[hh][:], vv_[:, hh * 256:(hh + 1) * 256],
                                                                             start=False, stop=True), reads=[attT[hh], vv_], writes=[po])
                    for hh in range(4):
                        Sc, Sn = S[d][hh][i % 2], S[d][hh][(i + 1) % 2]
                        pu = PB[1] if hh < 2 else PB[0]
                        usl = slice((hh % 2) * 256, (hh % 2 + 1) * 256)
                        P.op("dve", lambda e, hh=hh, pu=pu, usl=usl, Sc=Sc, Sn=Sn: e.scalar_tensor_tensor(
                            out=Sn[:], in0=Sc[:], scalar=ebl_[:, hh:hh + 1], in1=pu[:, usl], op0=ALU.mult, op1=ALU.add),
                            reads=[pu, Sc, ebl_], writes=[Sn])
                        P.op("pool", lambda e, hh=hh, Sn=Sn: e.tensor_copy(out=Sb[d][hh][:], in_=Sn[:]), reads=[Sn], writes=[Sb[d][hh]])
                    P.op("act", lambda e: e.activation(out=o_[:, 0:512], in_=PB[6][:], func=AF.Copy), reads=[PB[6]], writes=[(o_, 0)])
                    P.op("act", lambda e: e.activation(out=o_[:, 512:1024], in_=PB[7][:], func=AF.Copy), reads=[PB[7]], writes=[(o_, 1)])
                    P.dma("sp", lambda e: e.dma_start(out=ODG[d, rows, :], in_=o_[:]), reads=[o_], writes=[(tODG, (d, rows.start))])

                front(0)
                for n in range(len(its)):
                    if n + 1 < len(its):
                        front(n + 1)
                    back(n)
            if kind == "p":
                gla_pass((0, 1))
            else:
                gla_pass((0,))
                gin, gout, tgin, tgout = self.pair_buf("LST", 512, 256)
                for hh in range(4):
                    Sf = S[0][hh][NTl % 2]
                    P.dma("sp", lambda e, hh=hh, Sf=Sf: e.dma_start(out=gin[hh * 128:(hh + 1) * 128, :], in_=Sf[:]), reads=[Sf],
                          writes=[(tgin, hh)])
                self.pair_gather(gin, gout, tgin, tgout)
                cand = P.sb("cand", [128, 2, 4, 256], F32)
                for rk in range(2):
                    P.dma("sp", lambda e, rk=rk: e.dma_start(out=cand[:, rk], in_=gout[rk * 512:(rk + 1) * 512, :].rearrange("(h p) c -> p h c", p=128)),
                          reads=[tgout], writes=[(cand, rk)])
                selw = P.sb("selw", [128, 2], F32)
                P.dma("sp", lambda e: e.dma_start(out=selw[:], in_=di["selw"]), writes=[selw])
                for hh in range(4):
                    P.op("dve", lambda e, hh=hh: e.tensor_scalar(out=S[1][hh][0][:], in0=cand[:, 0, hh, :], scalar1=selw[:, 0:1], scalar2=None,
                                                                 op0=ALU.mult), reads=[cand, selw], writes=[S[1][hh][0]])
                    P.op("dve", lambda e, hh=hh: e.scalar_tensor_tensor(out=S[1][hh][0][:], in0=cand[:, 1, hh, :], scalar=selw[:, 1:2],
                                                                        in1=S[1][hh][0][:], op0=ALU.mult, op1=ALU.add),
                         reads=[cand, selw, S[1][hh][0]], writes=[S[1][hh][0]])
                    P.op("pool", lambda e, hh=hh: e.tensor_copy(out=Sb[1][hh][:], in_=S[1][hh][0][:]), reads=[S[1][hh][0]], writes=[Sb[1][hh]])
                gla_pass((1,))
            if kind == "p":
                si = row0 // c.TP
                for d in range(2):
                    for hh in range(4):
                        Sf = S[d][hh][NTl % 2]
                        P.dma("sp", lambda e, d=d, hh=hh, Sf=Sf, si=si: e.dma_start(out=self.dout["gla_out"][si, d, hh], in_=Sf[:]), reads=[Sf],
                              writes=[(self.dtr["gla_out"], (si, d, hh))])
        P.end_phase()
        P.begin_phase()
        PB = [P.ps("pb%d" % i, [128, 512], F32) for i in range(8)]
        gn = P.sb("gn", [128, 256], F32)
        P.dma("sp", lambda e: e.dma_start(out=gn[:], in_=di["od_gla_norm"].partition_broadcast(128)), writes=[gn])
        oa = [P.sb("oa%d" % i, [128, 1024], F32) for i in range(2)]
        obb = [P.sb("obb%d" % i, [128, 1024], F32) for i in range(2)]
        zz = [P.sb("zz%d" % i, [128, 1024], F32) for i in range(2)]
        st8 = [P.sb("st8%d" % i, [128, 16], F32) for i in range(2)]
        junk = P.sb("junk", [128, 512], F32)
        otb = [P.sb("otb%d" % i, [128, 8, 128], BF16) for i in range(2)]
        def gload2(tl):
            rows = slice(tl * 128, (tl + 1) * 128)
            a, b_, z_ = oa[tl % 2], obb[tl % 2], zz[tl % 2]
            P.dma("sp", lambda e, a=a, rows=rows: e.dma_start(out=a[:], in_=ODG[0, rows, :]), reads=[tODG], writes=[a])
            P.dma("sp", lambda e, b_=b_, rows=rows: e.dma_start(out=b_[:], in_=ODG[1, rows, :]), reads=[tODG], writes=[b_])
            P.dma("sp", lambda e, z_=z_, rows=rows: e.dma_start(out=z_[:], in_=RS[rows, :]), reads=[tRS], writes=[z_])

        gload2(0)
        for tl in range(NTILE):
            rows = slice(tl * 128, (tl + 1) * 128)
            a, b_, z_, s8, ot_ = oa[tl % 2], obb[tl % 2], zz[tl % 2], st8[tl % 2], otb[tl % 2]
            if tl + 1 < NTILE:
                gload2(tl + 1)
            P.op("dve", lambda e, a=a, b_=b_: e.tensor_tensor(out=a[:], in0=a[:], in1=b_[:], op=ALU.add), reads=[a, b_], writes=[a])
            P.op("pool", lambda e, s8=s8: e.memset(s8[:], 0.0), writes=[s8])
            for hh in range(4):
                P.op("act", lambda e, a=a, s8=s8, hh=hh: e.activation(out=junk[:, 0:256], in_=a[:, hh * 256:(hh + 1) * 256], func=AF.Square,
                                                                      accum_out=s8[:, hh:hh + 1]), reads=[a, s8], writes=[junk, s8])
            P.op("act", lambda e, s8=s8: e.activation(out=s8[:, 4:8], in_=s8[:, 0:4], func=AF.Sqrt, scale=1.0 / 256, bias=self.epsc[:, 0:1]),
                 reads=[s8, self.epsc], writes=[s8])
            P.op("dve", lambda e, s8=s8: e.reciprocal(out=s8[:, 8:12], in_=s8[:, 4:8]), reads=[s8], writes=[s8])
            for hh in range(4):
                P.op("dve", lambda e, a=a, s8=s8, hh=hh: e.scalar_tensor_tensor(
                    out=a[:, hh * 256:(hh + 1) * 256], in0=a[:, hh * 256:(hh + 1) * 256], scalar=s8[:, 8 + hh:9 + hh], in1=gn[:],
                    op0=ALU.mult, op1=ALU.mult), reads=[a, s8, gn], writes=[a])
            P.op("pool", lambda e, a=a, z_=z_: e.tensor_tensor(out=a[:], in0=a[:], in1=z_[:], op=ALU.mult), reads=[a, z_], writes=[a])
            for hf in range(2):
                pt = PB[(tl % 2) * 2 + hf]
                for q in range(4):
                    cc = hf * 4 + q
                    P.op("pe", lambda e, a=a, q=q, cc=cc, pt=pt: e.transpose(pt[:, q * 128:(q + 1) * 128], a[:, cc * 128:(cc + 1) * 128], ident[:]),
                         reads=[a, ident], writes=[pt])
                if hf == 0:
                    P.op("act", lambda e, pt=pt, ot_=ot_: e.activation(out=ot_[:, 0:4, :], in_=pt[:].rearrange("p (a b) -> p a b", a=4),
                                                                       func=AF.Copy), reads=[pt], writes=[(ot_, 0)])
                else:
                    P.op("dve", lambda e, pt=pt, ot_=ot_: e.tensor_copy(out=ot_[:, 4:8, :], in_=pt[:].rearrange("p (a b) -> p a b", a=4)),
                         reads=[pt], writes=[(ot_, 1)])
            P.dma("sp", lambda e, ot_=ot_, tl=tl: e.dma_start(out=OT[:, tl * 128:(tl + 1) * 128].rearrange("(a p) t -> p a t", p=128),
                                                             in_=ot_[:]), reads=[ot_], writes=[(tOT, ("a", tl))])
        P.end_phase()


def rope_tables(pos):
    nf = 16
    inv = (10000.0 ** (-np.arange(nf, dtype=np.float32) / nf)).astype(np.float32)
    TS = len(pos)
    row = (pos // 64).astype(np.float32)
    col = (pos % 64).astype(np.float32)
    cos = np.zeros((64, TS), np.float32)
    sin = np.zeros((64, TS), np.float32)
    for a in range(2):
        p_ = row if a == 0 else col
        ang = p_[None, :] * inv[:, None]
        for half in range(2):
            d0 = a * 32 + half * 16
            cos[d0:d0 + 16] = np.cos(ang)
            sin[d0:d0 + 16] = np.sin(ang) * (-1.0 if half == 0 else 1.0)
    return cos, sin


def rope_perm():
    p = np.zeros(64, np.int64)
    for a in range(2):
        for half in range(2):
            for f in range(16):
                p[a * 32 + half * 16 + f] = a * 32 + (1 - half) * 16 + f
    return p


def shared_weights(inputs, mirror):
    f32 = lambda a: np.ascontiguousarray(np.asarray(a, dtype=np.float32))
    perm = rope_perm()
    ev_w_in = np.asarray(inputs["ev_w_in"][0])
    conv_w = np.asarray(inputs["ev_conv_w"][0])
    a_log = np.asarray(inputs["ev_gdn_a_log"][0])
    dtb = np.asarray(inputs["ev_gdn_dt_bias"][0])
    od_w_in = np.asarray(inputs["od_w_in"][0])
    w_gup = np.asarray(inputs["od_gla_w_gup"][0])
    b_g = np.asarray(inputs["od_gla_b_g"][0])
    if mirror:
        idx = np.arange(2512)
        idx[2048:2052], idx[2052:2056] = np.arange(2052, 2056), np.arange(2048, 2052)
        idx[2056:2060], idx[2060:2064] = np.arange(2060, 2064), np.arange(2056, 2060)
        ev_w_in = ev_w_in[:, idx]
        conv_w = conv_w[::-1]
        a_log = a_log[::-1]
        dtb = dtb[::-1]
        idx2 = np.arange(3104)
        idx2[3072:3088], idx2[3088:3104] = np.arange(3088, 3104), np.arange(3072, 3088)
        od_w_in = od_w_in[:, idx2]
        w_gup = w_gup[::-1]
        b_g = b_g[::-1]
    w_uq = np.asarray(inputs["ev_mla_w_uq"][0])
    return {
        "mod_w": f32(inputs["mod_w"]), "mod_b": f32(inputs["mod_b"]),
        "norm_pre": f32(inputs["norm_pre"]), "norm_post": f32(inputs["norm_post"]),
        "ffn_w_in": f32(inputs["ffn_w_in"]), "ffn_w_out": f32(inputs["ffn_w_out"]),
        "ev_w_in": f32(ev_w_in), "ev_w_in_perm": f32(ev_w_in[:, 2448:2512][:, perm]),
        "ev_conv_w": f32(conv_w), "ev_gdn_a_log": f32(a_log.reshape(8)), "ev_gdn_dt_bias": f32(dtb.reshape(8)),
        "ev_gdn_norm": f32(inputs["ev_gdn_norm"][0]), "ev_mla_q_norm": f32(inputs["ev_mla_q_norm"][0]),
        "ev_mla_w_uq": f32(w_uq),
        "ev_mla_w_uq_perm": f32(np.concatenate([w_uq[:, h * 192 + 128:h * 192 + 192][:, perm] for h in range(4)], axis=1)),
        "ev_mla_kv_norm": f32(inputs["ev_mla_kv_norm"][0]), "ev_mla_w_ukv": f32(inputs["ev_mla_w_ukv"][0]),
        "ev_w_out": f32(inputs["ev_w_out"][0]),
        "od_w_in": f32(od_w_in), "od_gla_w_gup": f32(w_gup), "od_gla_b_g": f32(b_g),
        "od_gla_norm": f32(inputs["od_gla_norm"][0]), "od_w_out": f32(inputs["od_w_out"][0]),
    }


def make_in_maps(cfg, inputs, cores):
    c = cfg
    f32 = lambda a: np.ascontiguousarray(np.asarray(a, dtype=np.float32))
    sh = [shared_weights(inputs, False), shared_weights(inputs, True)]
    maps = []
    for core in cores:
        sb, half = core // 2, core % 2
        mirror = (half == 1)
        xp = np.asarray(inputs["x_prompt"])[core * c.NPS:(core + 1) * c.NPS, :c.TP]
        xs = np.asarray(inputs["x_sample"])[sb, half * c.TS:(half + 1) * c.TS]
        pos = np.arange(half * c.TS, (half + 1) * c.TS)
        if mirror:
            xp = xp[:, ::-1]
            xs = xs[::-1]
            pos = pos[::-1]
        cos, sin = rope_tables(pos)
        m = dict(sh[half])
        m["rope_cos"], m["rope_sin"] = cos, sin
        m["xin"] = f32(np.concatenate([xp.reshape(c.NP, D), xs], axis=0))
        m["cond"] = f32(np.stack([np.asarray(inputs["c_ctx"]), np.asarray(inputs["c"])[sb]], axis=0))
        m["cache_ckv"] = f32(inputs["cache_mla_ckv"][sb, 0])
        m["cache_krope"] = f32(inputs["cache_mla_krope"][sb, 0])
        sg = np.asarray(inputs["state_gdn"][sb, 0])
        sl = np.asarray(inputs["state_gla"][sb, 0])
        m["state_gdn"] = f32(sg[::-1] if mirror else sg)
        m["state_gla"] = f32(sl[::-1] if mirror else sl)
        w = np.zeros((128, 2), np.float32)
        w[:, 1 - half] = 1.0
        m["selw"] = w
        maps.append(m)
    return maps


def assemble(cfg, rs, cores):
    c = cfg
    yp, ck, kr, gd, gl = [], [], [], [], []
    ys = {}
    for r, core in zip(rs, cores):
        sb, half = core // 2, core % 2
        mirror = (half == 1)
        y = r["y"]
        p_ = y[:c.NP].reshape(c.NPS, c.TP, D)
        s_ = y[c.NP:]
        a = r["ckv_out"].reshape(c.NPS, 1, c.TP, 128)
        b = r["krope_out"].reshape(c.NPS, 1, c.TP, 64)
        g1 = r["gdn_out"].reshape(c.NPS, 1, 2, 4, 128, 128)
        g2 = r["gla_out"].reshape(c.NPS, 1, 2, 4, 128, 256)
        if mirror:
            p_, s_ = p_[:, ::-1], s_[::-1]
            a, b = a[:, :, ::-1], b[:, :, ::-1]
            g1, g2 = g1[:, :, ::-1], g2[:, :, ::-1]
        yp.append(p_); ck.append(a); kr.append(b); gd.append(g1); gl.append(g2)
        ys.setdefault(sb, [None, None])[half] = s_
    y_sample = np.stack([np.concatenate(ys[k], axis=0) for k in sorted(ys)], axis=0)
    cat = lambda l: np.ascontiguousarray(np.concatenate(l, axis=0), dtype=np.float32)
    return (cat(yp), np.ascontiguousarray(y_sample, dtype=np.float32), cat(ck), cat(kr), cat(gd), cat(gl))


_NC_CACHE = {}


def kernel(**inputs):
    cfg = Cfg()
    cores = list(range(8))
    if "full" not in _NC_CACHE:
        _NC_CACHE["full"] = Builder(cfg).build()
    nc = _NC_CACHE["full"]
    maps = make_in_maps(cfg, inputs, cores)
    res = run_bass_kernel_spmd(nc, maps, core_ids=cores)
    return assemble(cfg, res.results, cores)
```
